# Optimizing a Trainium2 kernel written in Bass

```python
import math
import jax
import jax.numpy as jnp
from jax import lax
import numpy as np

D_MODEL = 1024
BATCH = 8
SEQ = 2048
DEPTH = 4
DEC_BATCH = 128
DEC_SEQ = 4
PAST_LEN = 2048
PAGE_SIZE = 128

HEAD_DIM = 64
A_GROUPS = 4
A_WIDTH = A_GROUPS * HEAD_DIM
A_CONV = 3
B_HEADS = 6
B_DK = HEAD_DIM
B_DV = HEAD_DIM
B_WIDTH = B_HEADS * B_DV
B_QKV = 2 * B_HEADS * B_DK + B_WIDTH
B_CONV = 4
GDN_CHUNK = 64
C_PAIRS = ((128, 1), (512, 4), (2048, 16))
C_HPG = 2
C_HEADS = len(C_PAIRS) * C_HPG
C_WIDTH = C_HEADS * HEAD_DIM
ATTN_BLOCK = 128
ROT_DIM = HEAD_DIM // 4
ROPE_THETA = 500000.0
MIX_WIDTH = A_WIDTH + B_WIDTH + C_WIDTH
EPS = 1e-6
NEG = -1e30
IN_SPLITS = (A_WIDTH, A_WIDTH, A_WIDTH, A_WIDTH,
             B_HEADS * B_DK, B_HEADS * B_DK, B_WIDTH, B_WIDTH, B_HEADS, B_HEADS,
             C_WIDTH, C_WIDTH, C_WIDTH, C_WIDTH)
IN_WIDTH = sum(IN_SPLITS)

kernel_name = 'hybrid_conv_deltanet_dilated_step'


def rmsnorm(x, w):
    xf = x.astype(jnp.float32)
    y = xf * lax.rsqrt(jnp.mean(xf * xf, axis=-1, keepdims=True) + EPS) * w.astype(jnp.float32)
    return y.astype(x.dtype)


def l2norm(x):
    return x * lax.rsqrt(jnp.sum(x * x, axis=-1, keepdims=True) + 1e-6)


def split_cols(u):
    out, start = [], 0
    for n in IN_SPLITS:
        out.append(u[..., start:start + n])
        start += n
    return out


def partial_rope(x, pos):
    half = ROT_DIM // 2
    inv_freq = ROPE_THETA ** (-jnp.arange(half, dtype=jnp.float32) * (2.0 / ROT_DIM))
    ang = pos.astype(jnp.float32)[:, None] * inv_freq[None, :]
    cos = jnp.cos(ang)[None, :, None, :]
    sin = jnp.sin(ang)[None, :, None, :]
    xf = x.astype(jnp.float32)
    x1, x2, rest = xf[..., :half], xf[..., half:ROT_DIM], xf[..., ROT_DIM:]
    return jnp.concatenate([x1 * cos - x2 * sin, x2 * cos + x1 * sin, rest], axis=-1).astype(x.dtype)


def causal_dwconv(x, buf, w):
    k_w = w.shape[0]
    t = x.shape[1]
    xp = jnp.concatenate([buf.astype(x.dtype), x], axis=1)
    y = xp[:, 0:t] * w[0]
    for j in range(1, k_w):
        y = y + xp[:, j:j + t] * w[j]
    return y, xp[:, -(k_w - 1):]


def gated_delta_rule(q, k, v, g, beta, s0):
    bsz, t, h, dk = k.shape
    dv = v.shape[-1]
    c = min(GDN_CHUNK, t)
    n = -(-t // c)
    pad = n * c - t

    def chunks(a):
        a = jnp.pad(a, [(0, 0), (0, pad)] + [(0, 0)] * (a.ndim - 2))
        a = a.reshape((bsz, n, c) + a.shape[2:])
        return jnp.moveaxis(a, 3, 1)

    qc, kc, vc, gc, bc = chunks(q), chunks(k), chunks(v), chunks(g), chunks(beta)
    gcum = jnp.cumsum(gc, axis=-1)
    tri = jnp.tril(jnp.ones((c, c), bool))
    strict = jnp.tril(jnp.ones((c, c), bool), -1)
    diff = gcum[..., :, None] - gcum[..., None, :]
    decay = jnp.where(tri, jnp.exp(jnp.where(tri, diff, 0.0)), 0.0)
    kb = kc * bc[..., None]
    lmat = jnp.where(strict, jnp.einsum('bhnid,bhnjd->bhnij', kb, kc) * decay, 0.0)
    eye = jnp.eye(c, dtype=jnp.float32)
    tinv = lax.linalg.triangular_solve(lmat + eye, jnp.broadcast_to(eye, lmat.shape),
                                       left_side=True, lower=True)
    u = tinv @ (vc * bc[..., None])
    w = tinv @ (kb * jnp.exp(gcum)[..., None])
    a_intra = jnp.einsum('bhnid,bhnjd->bhnij', qc, kc) * decay

    def step(s, xs):
        q_i, k_i, u_i, w_i, a_i, g_i = xs
        v_new = u_i - w_i @ s
        o = (q_i * jnp.exp(g_i)[..., None]) @ s + a_i @ v_new
        g_last = g_i[..., -1]
        s = s * jnp.exp(g_last)[..., None, None] + jnp.einsum(
            'bhcd,bhce->bhde', k_i * jnp.exp(g_last[..., None] - g_i)[..., None], v_new)
        return s, o

    xs = tuple(jnp.moveaxis(a, 2, 0) for a in (qc, kc, u, w, a_intra, gcum))
    s_fin, o = lax.scan(step, s0.astype(jnp.float32), xs)
    o = o.transpose(1, 0, 3, 2, 4).reshape(bsz, n * c, h, dv)[:, :t]
    return o, s_fin


def dilated_band_attention(q, k, v, dil, taps):
    bsz, t, h, d = q.shape
    ls = t // dil
    nb = -(-ls // ATTN_BLOCK)
    pad = nb * ATTN_BLOCK - ls

    def streams(a):
        a = a.reshape(bsz, ls, dil, h, d).transpose(0, 2, 1, 3, 4)
        a = jnp.pad(a, ((0, 0), (0, 0), (0, pad), (0, 0), (0, 0)))
        return a.reshape(bsz, dil, nb, ATTN_BLOCK, h, d).astype(jnp.float32)

    def with_prev(a):
        prev = jnp.pad(a[:, :, :-1], ((0, 0), (0, 0), (1, 0), (0, 0), (0, 0), (0, 0)))
        return jnp.concatenate([prev, a], axis=3)

    qs = streams(q)
    kk, vv = with_prev(streams(k)), with_prev(streams(v))
    s = jnp.einsum('brnqhd,brnkhd->brnqhk', qs, kk) * (HEAD_DIM ** -0.5)
    qi = jnp.arange(ATTN_BLOCK)[:, None]
    kj = jnp.arange(2 * ATTN_BLOCK)[None, :]
    dist = ATTN_BLOCK + qi - kj
    band = (dist >= 0) & (dist <= taps)
    blk = jnp.arange(nb)[:, None, None]
    valid = band[None] & ((blk > 0) | (kj >= ATTN_BLOCK)[None])
    s = jnp.where(valid[None, None, :, :, None, :], s, NEG)
    m = jnp.max(s, axis=-1, keepdims=True)
    p = jnp.exp(s - m)
    den = jnp.sum(p, axis=-1)
    o = jnp.einsum('brnqhk,brnkhd->brnqhd', p, vv) / den[..., None]
    lse = m[..., 0] + jnp.log(den)
    o = o.reshape(bsz, dil, nb * ATTN_BLOCK, h, d)[:, :, :ls].transpose(0, 2, 1, 3, 4).reshape(bsz, t, h, d)
    lse = lse.reshape(bsz, dil, nb * ATTN_BLOCK, h)[:, :, :ls].transpose(0, 2, 1, 3).reshape(bsz, t, h)
    return o, lse


def dilated_gather_attention(q, k, v, buf, dil, taps):
    bsz, td, h, d = q.shape
    lb = buf.shape[1]
    rel = jnp.arange(td)[:, None] - jnp.arange(taps + 1)[None, :] * dil
    in_new = rel >= 0
    idx_b = lb + rel
    valid = in_new | (idx_b >= 0)
    past = buf[:, jnp.clip(idx_b, 0, lb - 1)]
    ridx = jnp.clip(rel, 0, td - 1)
    sel = in_new[None, :, :, None, None]
    keys = jnp.where(sel, k[:, ridx], past[:, :, :, 0].astype(k.dtype)).astype(jnp.float32)
    vals = jnp.where(sel, v[:, ridx], past[:, :, :, 1].astype(v.dtype)).astype(jnp.float32)
    s = jnp.einsum('bqhd,bqkhd->bqhk', q.astype(jnp.float32), keys) * (HEAD_DIM ** -0.5)
    s = jnp.where(valid[None, :, None, :], s, NEG)
    m = jnp.max(s, axis=-1, keepdims=True)
    p = jnp.exp(s - m)
    den = jnp.sum(p, axis=-1)
    o = jnp.einsum('bqhk,bqkhd->bqhd', p, vals) / den[..., None]
    lse = m[..., 0] + jnp.log(den)
    return o, lse


def mixer_sublayer(hn, pos, buf_a, buf_b, s0, kv_bufs, w_in_l, w_out_l, conv_a_w_l, conv_b_w_l,
                   a_log_l, dt_bias_l, gdn_norm_w_l):
    bsz, t, _ = hn.shape
    u = hn @ w_in_l
    (a_x, a_cg, a_bg, a_z, b_q, b_k, b_v, b_z, b_a, b_b, c_q, c_k, c_v, c_z) = split_cols(u)

    a_conv, buf_a_new = causal_dwconv(a_cg * a_x, buf_a, conv_a_w_l)
    y_a = a_bg * a_conv * jax.nn.silu(a_z)

    qkv, buf_b_new = causal_dwconv(jnp.concatenate([b_q, b_k, b_v], axis=-1), buf_b, conv_b_w_l)
    qkv = jax.nn.silu(qkv).astype(jnp.float32)
    nqk = B_HEADS * B_DK
    gq = l2norm(qkv[..., :nqk].reshape(bsz, t, B_HEADS, B_DK)) * (B_DK ** -0.5)
    gk = l2norm(qkv[..., nqk:2 * nqk].reshape(bsz, t, B_HEADS, B_DK))
    gv = qkv[..., 2 * nqk:].reshape(bsz, t, B_HEADS, B_DV)
    g = -jnp.exp(a_log_l.astype(jnp.float32)) * jax.nn.softplus(
        b_a.astype(jnp.float32) + dt_bias_l.astype(jnp.float32))
    beta = jax.nn.sigmoid(b_b.astype(jnp.float32))
    o_b, s_new = gated_delta_rule(gq, gk, gv, g, beta, s0)
    y_b = rmsnorm(o_b, gdn_norm_w_l).reshape(bsz, t, B_WIDTH).astype(hn.dtype) * jax.nn.silu(b_z)

    cq = partial_rope(c_q.reshape(bsz, t, C_HEADS, HEAD_DIM), pos)
    ck = partial_rope(c_k.reshape(bsz, t, C_HEADS, HEAD_DIM), pos)
    cv = c_v.reshape(bsz, t, C_HEADS, HEAD_DIM)
    outs, lses, kv_new = [], [], []
    for gi, (win, dil) in enumerate(C_PAIRS):
        taps = win // dil
        qg = cq[:, :, gi * C_HPG:(gi + 1) * C_HPG]
        kg = ck[:, :, gi * C_HPG:(gi + 1) * C_HPG]
        vg = cv[:, :, gi * C_HPG:(gi + 1) * C_HPG]
        kv_rows = jnp.stack([kg, vg], axis=2)
        if kv_bufs is None:
            o, lse = dilated_band_attention(qg, kg, vg, dil, taps)
            kv_new.append(kv_rows[:, -min(win, t):])
        else:
            o, lse = dilated_gather_attention(qg, kg, vg, kv_bufs[gi], dil, taps)
            kv_new.append(kv_rows)
        outs.append(o)
        lses.append(lse)
    alpha = jax.nn.softmax(jnp.stack(lses, axis=2), axis=2)
    o_c = (jnp.stack(outs, axis=2) * alpha[..., None]).reshape(bsz, t, C_WIDTH).astype(hn.dtype)
    y_c = o_c * jax.nn.silu(c_z)

    mix = jnp.concatenate([y_a, y_b, y_c], axis=-1)
    return mix @ w_out_l, buf_a_new, buf_b_new, s_new, kv_new


def trunk(x, c, pos, conv_a_state, conv_b_state, gdn_state, kv_caches, w_in, w_out, w_ada, b_ada,
          norm_w, conv_a_w, conv_b_w, a_log, dt_bias, gdn_norm_w, final_norm_w, w_ada_final, b_ada_final):
    bsz = x.shape[0]
    new_a, new_b, new_g = [], [], []
    new_kv = [[] for _ in C_PAIRS]
    for l in range(DEPTH):
        if conv_a_state is None:
            buf_a = jnp.zeros((bsz, A_CONV - 1, A_WIDTH), x.dtype)
            buf_b = jnp.zeros((bsz, B_CONV - 1, B_QKV), x.dtype)
            s0 = jnp.zeros((bsz, B_HEADS, B_DK, B_DV), jnp.float32)
            kv_bufs = None
        else:
            buf_a = conv_a_state[l]
            buf_b = conv_b_state[l]
            s0 = gdn_state[l]
            kv_bufs = tuple(cc[l] for cc in kv_caches)
        shift, scale, gate = jnp.split(c @ w_ada[l] + b_ada[l], 3, axis=-1)
        hn = rmsnorm(x, norm_w[l]) * (1 + scale[:, None]) + shift[:, None]
        out, buf_a, buf_b, s_new, kv_rows = mixer_sublayer(
            hn, pos, buf_a, buf_b, s0, kv_bufs, w_in[l], w_out[l], conv_a_w[l], conv_b_w[l],
            a_log[l], dt_bias[l], gdn_norm_w[l])
        x = x + (1 + gate[:, None]) * out
        new_a.append(buf_a)
        new_b.append(buf_b)
        new_g.append(s_new)
        for gi in range(len(C_PAIRS)):
            new_kv[gi].append(kv_rows[gi])
    shift, scale = jnp.split(c @ w_ada_final + b_ada_final, 2, axis=-1)
    y = rmsnorm(x, final_norm_w) * (1 + scale[:, None]) + shift[:, None]
    return (y, jnp.stack(new_a), jnp.stack(new_b), jnp.stack(new_g),
            [jnp.stack(r) for r in new_kv])


def setup_inputs(seed: int = 0) -> dict:
    key = jax.random.key(seed)
    ks = jax.random.split(key, 24)
    f32 = jnp.float32

    def nrm(k, shape, s):
        return jax.random.normal(k, shape, f32) * s

    def kv(k, win):
        return nrm(k, (DEPTH, DEC_BATCH, min(win, PAST_LEN), 2, C_HPG, HEAD_DIM), 1.0)

    dt = jnp.exp(jax.random.uniform(ks[18], (DEPTH, B_HEADS), f32, math.log(1e-3), math.log(1e-1)))
    return {
        'x_prompt': nrm(ks[0], (BATCH, SEQ, D_MODEL), 1.0),
        'x_sample': nrm(ks[1], (DEC_BATCH, DEC_SEQ, D_MODEL), 1.0),
        'state_conv_a': nrm(ks[2], (DEPTH, DEC_BATCH, A_CONV - 1, A_WIDTH), 1.0),
        'state_conv_b': nrm(ks[3], (DEPTH, DEC_BATCH, B_CONV - 1, B_QKV), 1.0),
        'state_gdn': nrm(ks[4], (DEPTH, DEC_BATCH, B_HEADS, B_DK, B_DV), 0.1),
        'cache_kv_w128': kv(ks[5], C_PAIRS[0][0]),
        'cache_kv_w512': kv(ks[6], C_PAIRS[1][0]),
        'cache_kv_w2048': kv(ks[7], C_PAIRS[2][0]),
        'c_prompt': nrm(ks[8], (BATCH, D_MODEL), 1.0),
        'c_sample': nrm(ks[9], (DEC_BATCH, D_MODEL), 1.0),
        'w_in': nrm(ks[10], (DEPTH, D_MODEL, IN_WIDTH), D_MODEL ** -0.5),
        'w_out': nrm(ks[11], (DEPTH, MIX_WIDTH, D_MODEL), MIX_WIDTH ** -0.5),
        'w_ada': nrm(ks[12], (DEPTH, D_MODEL, 3 * D_MODEL), 0.1 * D_MODEL ** -0.5),
        'b_ada': nrm(ks[13], (DEPTH, 3 * D_MODEL), 0.01),
        'norm_w': 1.0 + nrm(ks[14], (DEPTH, D_MODEL), 0.02),
        'conv_a_w': nrm(ks[15], (DEPTH, A_CONV, A_WIDTH), A_CONV ** -0.5),
        'conv_b_w': nrm(ks[16], (DEPTH, B_CONV, B_QKV), B_CONV ** -0.5),
        'a_log': jnp.log(jax.random.uniform(ks[17], (DEPTH, B_HEADS), f32, 1.0, 16.0)),
        'dt_bias': dt + jnp.log(-jnp.expm1(-dt)),
        'gdn_norm_w': 1.0 + nrm(ks[19], (DEPTH, B_DV), 0.02),
        'final_norm_w': 1.0 + nrm(ks[20], (D_MODEL,), 0.02),
        'w_ada_final': nrm(ks[21], (D_MODEL, 2 * D_MODEL), 0.1 * D_MODEL ** -0.5),
        'b_ada_final': nrm(ks[22], (2 * D_MODEL,), 0.01),
    }


def reference(x_prompt, x_sample, state_conv_a, state_conv_b, state_gdn, cache_kv_w128, cache_kv_w512,
              cache_kv_w2048, c_prompt, c_sample, w_in, w_out, w_ada, b_ada, norm_w, conv_a_w, conv_b_w,
              a_log, dt_bias, gdn_norm_w, final_norm_w, w_ada_final, b_ada_final):
    pos_p = jnp.arange(x_prompt.shape[1])
    y_prompt, ca_p, cb_p, g_p, kv_p = trunk(
        x_prompt, c_prompt, pos_p, None, None, None, None, w_in, w_out, w_ada, b_ada, norm_w,
        conv_a_w, conv_b_w, a_log, dt_bias, gdn_norm_w, final_norm_w, w_ada_final, b_ada_final)
    pos_s = PAST_LEN + jnp.arange(x_sample.shape[1])
    y_sample, ca_s, cb_s, g_s, kv_s = trunk(
        x_sample, c_sample, pos_s, state_conv_a, state_conv_b, state_gdn,
        (cache_kv_w128, cache_kv_w512, cache_kv_w2048), w_in, w_out, w_ada, b_ada, norm_w,
        conv_a_w, conv_b_w, a_log, dt_bias, gdn_norm_w, final_norm_w, w_ada_final, b_ada_final)
    return (y_prompt, y_sample, ca_p, ca_s, cb_p, cb_s, g_p, g_s,
            kv_p[0], kv_s[0], kv_p[1], kv_s[1], kv_p[2], kv_s[2])
```

```python
import contextlib
import numpy as np
import concourse.bass as bass
import concourse.mybir as mybir
from concourse.bass_utils import run_bass_kernel_spmd

F32 = mybir.dt.float32
F32R = mybir.dt.float32r
BF16 = mybir.dt.bfloat16
AF = mybir.ActivationFunctionType
ALU = mybir.AluOpType
AX = mybir.AxisListType


class Prog:
    COMPUTE = ("pe", "act", "dve", "pool")
    ALL = ("pe", "act", "dve", "pool", "sp")

    def __init__(self, nc, stack, n_dma_sems=24):
        self.nc = nc
        self.stack = stack
        self.streams = {e: [] for e in self.ALL}
        self.esem = {e: stack.enter_context(nc.semaphore("prog_" + e)) for e in self.COMPUTE}
        self.ecount = {e: 0 for e in self.COMPUTE}
        self.known = {e: {} for e in self.ALL}
        self.res = {}
        self.dsem = {}
        for q in ("sp", "act", "pool"):
            self.dsem[q] = [[stack.enter_context(nc.semaphore("dq_%s_%d" % (q, i))), 0] for i in range(n_dma_sems)]
        self.dnext = {q: 0 for q in self.dsem}
        self.final_tokens = []

    def _deps(self, eng, reads, writes, same_engine_sem=None):
        toks = []
        for k in reads:
            r = self.res.get(k)
            if r and r["w"] is not None:
                toks.append(r["w"])
        for k in writes:
            r = self.res.get(k)
            if r:
                if r["w"] is not None:
                    toks.append(r["w"])
                toks.extend(r["r"])
        return toks

    def _record(self, tok, reads, writes):
        for k in reads:
            r = self.res.setdefault(k, {"w": None, "r": []})
            r["r"].append(tok)
        for k in writes:
            self.res[k] = {"w": tok, "r": []}

    def _waits(self, eng, toks, skip_sem=None):
        need = {}
        for (sem, val) in toks:
            if skip_sem is not None and sem is skip_sem:
                continue
            sid = id(sem)
            if self.known[eng].get(sid, 0) >= val:
                continue
            if sid not in need or need[sid][1] < val:
                need[sid] = (sem, val)
        for sid, (sem, val) in need.items():
            self.known[eng][sid] = val
        return list(need.values())

    def op(self, eng, fn, reads=(), writes=()):
        toks = self._deps(eng, reads, writes)
        own = self.esem[eng]
        if eng == "pe":
            waits = self._waits(eng, toks, skip_sem=own)
        else:
            waits = self._waits(eng, toks)
        self.ecount[eng] += 1
        tok = (own, self.ecount[eng])
        self.streams[eng].append((waits, fn, (own, 1)))
        self._record(tok, reads, writes)
        return tok

    def dma(self, q, out, in_, reads=(), writes=(), final=False, **kw):
        pool = self.dsem[q]
        i = self.dnext[q]
        self.dnext[q] = (i + 1) % len(pool)
        sem, val = pool[i]
        toks = self._deps(q, reads, writes)
        if val > 0:
            toks = toks + [(sem, val)]
        eng = {"sp": "sp", "act": "act", "pool": "pool"}[q]
        waits = self._waits(eng, toks)
        pool[i][1] = val + 16
        tok = (sem, val + 16)

        def fn(e, out=out, in_=in_, kw=kw):
            return e.dma_start(out, in_, **kw)
        self.streams[eng].append((waits, fn, (sem, 16)))
        self._record(tok, reads, writes)
        if final:
            self.final_tokens.append(tok)
        return tok


    @staticmethod
    def _k(*xs):
        out = []
        for x in xs:
            if isinstance(x, tuple):
                out.extend(x[1])
        return out

    @staticmethod
    def _a(x):
        return x[0] if isinstance(x, tuple) else x

    def mm(self, out, lhsT, rhs, start=True, stop=True):
        o, l, r = out[0], lhsT[0], rhs[0]
        return self.op("pe", lambda e: e.matmul(o, l, r, start=start, stop=stop),
                       reads=self._k(lhsT, rhs), writes=self._k(out))

    def transpose(self, out, in_, ident):
        o, i, d = out[0], in_[0], ident[0]
        return self.op("pe", lambda e: e.transpose(o, i, d), reads=self._k(in_, ident), writes=self._k(out))

    def act(self, out, in_, func, bias=0.0, scale=1.0, eng="act"):
        o, i, b, sc = out[0], in_[0], self._a(bias), self._a(scale)
        return self.op(eng, lambda e: e.activation(o, i, func, bias=b, scale=sc),
                       reads=self._k(in_, bias, scale), writes=self._k(out))

    def copy(self, eng, out, in_):
        o, i = out[0], in_[0]
        if eng == "act":
            return self.op(eng, lambda e: e.copy(o, i), reads=self._k(in_), writes=self._k(out))
        return self.op(eng, lambda e: e.tensor_copy(o, i), reads=self._k(in_), writes=self._k(out))

    def tt(self, eng, out, in0, in1, op):
        o, a, b = out[0], in0[0], in1[0]
        return self.op(eng, lambda e: e.tensor_tensor(o, a, b, op), reads=self._k(in0, in1), writes=self._k(out))

    def ts(self, eng, out, in0, s1, s2, op0, op1=None):
        o, a, x1, x2 = out[0], in0[0], self._a(s1), self._a(s2)
        if op1 is None:
            return self.op(eng, lambda e: e.tensor_scalar(o, a, x1, None, op0), reads=self._k(in0, s1), writes=self._k(out))
        return self.op(eng, lambda e: e.tensor_scalar(o, a, x1, x2, op0, op1), reads=self._k(in0, s1, s2), writes=self._k(out))

    def stt(self, eng, out, in0, scalar, in1, op0, op1):
        o, a, sc, b = out[0], in0[0], self._a(scalar), in1[0]
        return self.op(eng, lambda e: e.scalar_tensor_tensor(o, a, sc, b, op0, op1),
                       reads=self._k(in0, scalar, in1), writes=self._k(out))

    def memset(self, eng, out, val):
        o = out[0]
        return self.op(eng, lambda e: e.memset(o, val), writes=self._k(out))

    def barrier(self):
        toks = [(self.esem[e], self.ecount[e]) for e in self.COMPUTE if self.ecount[e] > 0]
        for q in self.dsem:
            for sem, val in self.dsem[q]:
                if val > 0:
                    toks.append((sem, val))
        for e in self.ALL:
            waits = self._waits(e, toks, skip_sem=self.esem.get(e))
            if waits:
                self.streams[e].append((waits, None, None))
        self.res = {}

    def new_epoch(self):
        self.epoch = getattr(self, "epoch", 0) + 1
        for e in self.COMPUTE:
            self.esem[e] = self.stack.enter_context(self.nc.semaphore("prog_%s_%d" % (e, self.epoch)))
            self.ecount[e] = 0

    def emit(self):
        nc = self.nc
        fin = self._waits("sp", self.final_tokens)
        streams = self.streams

        def run(ename, e):
            for (waits, fn, inc) in streams[ename]:
                for (sem, val) in waits:
                    e.wait_ge(sem, val)
                if fn is None:
                    continue
                ins = fn(e)
                if inc is not None:
                    ins.then_inc(inc[0], inc[1])
            if ename == "sp":
                for (sem, val) in fin:
                    e.wait_ge(sem, val)

        with nc.Block() as block:
            @block.sync
            def _(e):
                run("sp", e)

            @block.tensor
            def _(e):
                run("pe", e)

            @block.scalar
            def _(e):
                run("act", e)

            @block.vector
            def _(e):
                run("dve", e)

            @block.gpsimd
            def _(e):
                run("pool", e)


D = 1024
KC = 8
T = 2048
NSQ = 16
SQ = 4
NS = NSQ * SQ
NTOK = T + NS
DEPTH = 4
INW = 4108
NCORE = 8
GROUPS = [(0, 512), (512, 512), (1024, 512), (1536, 512), (2048, 64)]
WSLOT = 384
EPS = 1e-6

V_BADA = 0
V_NORMW = 96
V_FNORM = 128
V_BADAF = 136
V_CAW = 152
V_CBW = 176
V_GNW = 320
V_ALOG = 324
V_DTB = 328
NV = 332

C_IDENT = 0
C_SHIFT = 128
C_ONES = 192
C_MPREV = 320
C_MCUR = 448
C_BLK = 576
C_EH = 704
C_OFFD = 1088
C_MSEQ = 1152
C_SEQM = 1216
C_SCAN64 = 1232
C_SCAN4 = 1360
C_MDIAG = 1424
C_MS128 = 1488
NCST = 1492
NEGM = -240000.0


def R(ap, *keys):
    return (ap, keys)


class Ctx:
    pass


def build_program(phases=("A",), depth=DEPTH):
    nc = bass.Bass("TRN2", target_bir_lowering=False)
    st = contextlib.ExitStack()
    with st:
        P = Prog(nc, st)
        c = Ctx()
        c.nc, c.P, c.st = nc, P, st

        def din(name, shape):
            return nc.dram_tensor(name, list(shape), F32, kind="ExternalInput").ap()

        def dout(name, shape):
            return nc.dram_tensor(name, list(shape), F32, kind="ExternalOutput").ap()

        c.xp = din("xp", (T, D)); c.xs = din("xs", (NS, D))
        c.cT = din("cT", (128, KC, 17))
        c.sca = din("sca", (DEPTH, NSQ * 2, 256)); c.scb = din("scb", (DEPTH, NSQ * 3, 1152))
        c.sgd = din("sgd", (DEPTH, NSQ, 6, 64, 64))
        c.ck128 = din("ck128", (DEPTH, NSQ, 128, 256)); c.ck512 = din("ck512", (DEPTH, NSQ, 512, 256))
        c.ck2048 = din("ck2048", (DEPTH, NSQ, 2048, 256))
        c.w_in = din("w_in", (DEPTH, D, INW)); c.w_out = din("w_out", (DEPTH, D, D))
        c.w_ada = din("w_ada", (DEPTH, D, 3 * D)); c.w_adaf = din("w_adaf", (D, 2 * D))
        c.vecT = din("vecT", (128, NV)); c.cst = din("cst", (128, NCST))
        c.rope = din("rope", (128, 17, 16))
        c.yp = dout("yp", (T, D)); c.ys = dout("ys", (NS, D))
        c.ca_p = dout("ca_p", (DEPTH, 2, 256)); c.ca_s = dout("ca_s", (DEPTH, NSQ * 2, 256))
        c.cb_p = dout("cb_p", (DEPTH, 3, 1152)); c.cb_s = dout("cb_s", (DEPTH, NSQ * 3, 1152))
        c.gd_p = dout("gd_p", (DEPTH, 6, 64, 64)); c.gd_s = dout("gd_s", (DEPTH, NSQ, 6, 64, 64))
        c.kv_p = [dout("kv128_p", (DEPTH, 128, 256)), dout("kv512_p", (DEPTH, 512, 256)), dout("kv2048_p", (DEPTH, 2048, 256))]
        c.kv_s = [dout("kv128_s", (DEPTH, NS, 256)), dout("kv512_s", (DEPTH, NS, 256)), dout("kv2048_s", (DEPTH, NS, 256))]

        def sb(name, shape, dt):
            return st.enter_context(nc.sbuf_tensor(name, list(shape), dt))

        c.xT = sb("xT", (128, KC, NTOK), F32)
        c.hnT = sb("hnT", (128, KC, NTOK), BF16)
        c.ring = [sb("ring%d" % i, (128, KC, WSLOT), BF16) for i in range(4)]
        c.wab = sb("wab", (128, KC, 12), BF16)
        c.wo = sb("wo", (128, 6, D), BF16)
        c.vec = sb("vec", (128, NV), F32)
        c.cstf = sb("cstf", (128, NCST), F32)
        c.cstb = sb("cstb", (128, 836), BF16)
        c.ropet = sb("ropet", (128, 17, 16), F32)
        c.cTf = sb("cTf", (128, KC, 17), F32)
        c.cTb = sb("cTb", (128, KC, 17), BF16)
        c.ada = sb("ada", (128, 24, 17), F32)
        c.m1 = sb("m1", (128, KC, 17), F32)
        c.g1 = sb("g1", (128, KC, 17), F32)
        ARENA_F32 = 14592
        c.arena = sb("arena", (128, ARENA_F32), F32)
        c.ps = [st.enter_context(nc.psum_tensor("ps%d" % i, [128, 512], F32)) for i in range(8)]
        c.ps_i = 0
        c.ring_i = 0
        c.phases = phases

        c.ps_n = 8

        def next_ps():
            i = c.ps_i % c.ps_n
            c.ps_i = (i + 1) % c.ps_n
            return i
        c.next_ps = next_ps

        class Arena:
            def __init__(self):
                self.off = 0

            def f32(self, n):
                o = self.off
                self.off += n
                c.arena_max = max(getattr(c, "arena_max", 0), self.off)
                assert self.off <= ARENA_F32, ("arena overflow", self.off)
                return c.arena[:, o:o + n]

            def bf16(self, n):
                assert n % 2 == 0
                return self.f32(n // 2).bitcast(BF16)

            def f32r(self, n):
                return self.f32(n).bitcast(F32R)
        c.Arena = Arena

        def wload(src, ncols):
            i = c.ring_i
            c.ring_i = (i + 1) % 4
            key = ("ring", i)
            P.dma("pool", c.ring[i][:, :, 0:ncols], src.rearrange("(k p) c -> p k c", p=128), writes=[key])
            return c.ring[i], key
        c.wload = wload

        P.dma("sp", c.vec[:], c.vecT, writes=["vec"])
        P.dma("sp", c.cstf[:], c.cst, writes=["cstf"])
        P.dma("sp", c.ropet[:], c.rope, writes=["rope"])
        P.dma("sp", c.cTf[:], c.cT, writes=["cTf"])
        P.copy("dve", R(c.cstb[:, 0:704], "cstb"), R(c.cstf[:, 0:704], "cstf"))
        P.copy("dve", R(c.cstb[:, 704:768], "cstb"), R(c.cstf[:, C_MSEQ:C_MSEQ + 64], "cstf"))
        P.copy("dve", R(c.cstb[:, 768:836], "cstb"), R(c.cstf[:, C_MDIAG:C_MDIAG + 68], "cstf"))
        P.copy("dve", R(c.cTb[:], "cTb"), R(c.cTf[:], "cTf"))
        for i in range(8):
            P.memset("dve", R(c.ps[i][:], ("ps", i)), 0.0)
        c.zero = sb("zero", (128, 384), F32)
        P.memset("dve", R(c.zero[:], "zero"), 0.0)
        c.ident = R(c.cstf[:, C_IDENT:C_IDENT + 128], "cstf")
        c.identb = R(c.cstb[:, C_IDENT:C_IDENT + 128], "cstb")
        c.onesb = R(c.cstb[:, C_ONES:C_ONES + 128], "cstb")

        load_x(c)
        for l in range(depth):
            layer(c, l)
        final(c)
        P.emit()
    return nc


def xkey(g):
    return ("x", g)


def hkey(g):
    return ("hn", g)


def load_x(c):
    P = c.P
    A = c.Arena()
    stg = [A.f32(D) for _ in range(2)]
    for tt in range(17):
        rows = 128 if tt < 16 else NS
        g = min(tt // 4, 4)
        s = stg[tt % 2]
        skey = ("xstg", tt % 2)
        src = c.xp[tt * 128:(tt + 1) * 128, :] if tt < 16 else c.xs
        P.dma("sp", s[0:rows, :], src, writes=[skey])
        for half in range(2):
            pi = c.next_ps()
            for kk in range(4):
                k = half * 4 + kk
                P.transpose(R(c.ps[pi][:, kk * 128:kk * 128 + rows], ("ps", pi)),
                            R(s[0:rows, k * 128:(k + 1) * 128], skey), R(c.cstf[0:rows, C_IDENT:C_IDENT + rows], "cstf"))
            col0 = tt * 128
            dst = c.xT[:, half * 4:half * 4 + 4, col0:col0 + rows]
            srcp = c.ps[pi][:].rearrange("p (k t) -> p k t", k=4)[:, :, 0:rows]
            P.copy("act" if half == 0 else "dve", R(dst, xkey(g)), R(srcp, ("ps", pi)))
    P.barrier()


def ada_vectors(c, l):
    P = c.P
    final_ = (l == DEPTH)
    ncol = 2 * D if final_ else 3 * D
    nj = ncol // 128
    pi = c.next_ps()
    pst = c.ps[pi][:, 0:24 * 17].rearrange("p (j s) -> p j s", s=17)
    for t0 in range(0, ncol, WSLOT):
        nc_ = min(WSLOT, ncol - t0)
        src = (c.w_adaf if final_ else c.w_ada[l])[:, t0:t0 + nc_]
        wt, wk = c.wload(src, nc_)
        for jj in range(nc_ // 128):
            j = t0 // 128 + jj
            for k in range(KC):
                P.mm(R(pst[:, j, :], ("ps", pi)), R(wt[:, k, jj * 128:(jj + 1) * 128], wk), R(c.cTb[:, k, :], "cTb"),
                     start=(k == 0), stop=(k == KC - 1))
    vb = V_BADAF if final_ else V_BADA + l * 24
    bias = c.vec[:, vb:vb + nj].unsqueeze(2).to_broadcast([128, nj, 17])
    P.tt("dve", R(c.ada[:, 0:nj, :], "ada"), R(pst[:, 0:nj, :], ("ps", pi)), R(bias, "vec"), ALU.add)
    nw0 = V_FNORM if final_ else V_NORMW + l * 8
    nw = c.vec[:, nw0:nw0 + KC].unsqueeze(2).to_broadcast([128, KC, 17])
    P.stt("dve", R(c.m1[:], "m1"), R(c.ada[:, 8:16, :], "ada"), 1.0, R(nw, "vec"), ALU.add, ALU.mult)
    if not final_:
        P.ts("dve", R(c.g1[:], "g1"), R(c.ada[:, 16:24, :], "ada"), 1.0, None, ALU.add)


def rsqrt(c, out, in_, scale, bias):
    P = c.P
    P.ts("dve", out, in_, scale, bias, ALU.mult, ALU.add)
    P.act(out, out, AF.Sqrt)
    o = out[0]
    P.op("dve", lambda e: e.reciprocal(o, o), reads=list(out[1]), writes=list(out[1]))


def rms_stats(c, A, g, tag):
    P = c.P
    col0, ncol = GROUPS[g]
    sq = A["sq"]
    P.act(R(sq[:, :, 0:ncol], "sq"), R(c.xT[:, :, col0:col0 + ncol], xkey(g)), AF.Square)
    pi = c.next_ps()
    for k in range(KC):
        P.mm(R(c.ps[pi][:, 0:ncol], ("ps", pi)), c.onesb, R(sq[:, k, 0:ncol], "sq"), start=(k == 0), stop=(k == KC - 1))
    rstd = A["rstd"]
    rsqrt(c, R(rstd[:, 0:ncol], "rstd"), R(c.ps[pi][:, 0:ncol], ("ps", pi)), 1.0 / D, EPS)
    return rstd


def norm_phase(c, l, out_fn):
    P = c.P
    A_ = c.Arena()
    A = {"sq": A_.bf16(KC * 512).rearrange("p (k t) -> p k t", k=KC), "rstd": A_.f32(512),
         "tmp": [A_.f32(512) for _ in range(2)], "tmps": A_.f32(KC * NS).rearrange("p (k t) -> p k t", k=KC)}
    for g in range(5):
        col0, ncol = GROUPS[g]
        rstd = rms_stats(c, A, g, "n")
        if g < 4:
            for k in range(KC):
                tmp = A["tmp"][k % 2]
                tk = ("ntmp", k % 2)
                P.tt("dve", R(tmp[:, 0:ncol], tk), R(c.xT[:, k, col0:col0 + ncol], xkey(g)), R(rstd[:, 0:ncol], "rstd"), ALU.mult)
                P.act(out_fn(g, k, ncol), R(tmp[:, 0:ncol], tk), AF.Identity,
                      bias=R(c.ada[:, k, 0:1], "ada"), scale=R(c.m1[:, k, 0:1], "m1"))
        else:
            ts_ = A["tmps"]
            P.tt("dve", R(ts_[:], "ntmps"), R(c.xT[:, :, col0:col0 + ncol], xkey(g)),
                 R(rstd[:, 0:ncol].unsqueeze(1).to_broadcast([128, KC, NS]), "rstd"), ALU.mult)
            v4 = ts_[:].rearrange("p k (s i) -> p k s i", i=SQ)
            m1b = c.m1[:, :, 1:17].unsqueeze(3).to_broadcast([128, KC, NSQ, SQ])
            shb = c.ada[:, 0:8, 1:17].unsqueeze(3).to_broadcast([128, KC, NSQ, SQ])
            P.tt("dve", R(v4, "ntmps"), R(v4, "ntmps"), R(m1b, "m1"), ALU.mult)
            for k in range(KC):
                o = out_fn(g, k, ncol)
                P.tt("dve", (o[0].rearrange("p (s i) -> p s i", i=SQ), o[1]), R(v4[:, k], "ntmps"), R(shb[:, k], "ada"), ALU.add)
    P.barrier()


def proj(c, pi, wt, wk, wc0, m, g, ncol_override=None, cols=None):
    P = c.P
    col0, ncol = GROUPS[g] if cols is None else cols
    for k in range(KC):
        P.mm(R(c.ps[pi][0:m, 0:ncol], ("ps", pi)), R(wt[:, k, wc0:wc0 + m], wk), R(c.hnT[:, k, col0:col0 + ncol], hkey(g)),
             start=(k == 0), stop=(k == KC - 1))


def outproj(c, l, g, mix_fn, nchunk, kpart, cols=None):
    P = c.P
    col0, ncol = GROUPS[g] if cols is None else cols
    for dc in range(KC):
        pi = c.next_ps()
        for mc in range(nchunk):
            P.mm(R(c.ps[pi][:, 0:ncol], ("ps", pi)), R(c.wo[0:kpart, mc, dc * 128:(dc + 1) * 128], "wo"), mix_fn(mc),
                 start=(mc == 0), stop=(mc == nchunk - 1))
        xs = c.xT[:, dc, col0:col0 + ncol]
        if g < 4:
            P.stt("dve", R(xs, xkey(g)), R(c.ps[pi][:, 0:ncol], ("ps", pi)), R(c.g1[:, dc, 0:1], "g1"), R(xs, xkey(g)), ALU.mult, ALU.add)
        else:
            x3 = xs.rearrange("p (s i) -> p s i", i=SQ)
            p3 = c.ps[pi][:, 0:ncol].rearrange("p (s i) -> p s i", i=SQ)
            g1b = c.g1[:, dc, 1:17].unsqueeze(2).to_broadcast([128, NSQ, SQ])
            tmp = c.optmp
            P.tt("dve", R(tmp, "optmp"), R(p3, ("ps", pi)), R(g1b, "g1"), ALU.mult)
            P.tt("dve", R(x3, xkey(g)), R(x3, xkey(g)), R(tmp, "optmp"), ALU.add)


def load_wo(c, l, r0, nchunk, kpart):
    P = c.P
    src = c.w_out[l][r0:r0 + nchunk * kpart, :].rearrange("(j p) d -> p j d", p=kpart)
    P.dma("pool", c.wo[0:kpart, 0:nchunk, :], src, writes=["wo"])


def layer(c, l):
    P = c.P
    ada_vectors(c, l)
    norm_phase(c, l, lambda g, k, ncol: R(c.hnT[:, k, GROUPS[g][0]:GROUPS[g][0] + ncol], hkey(g)))
    if "A" in c.phases:
        branch_a(c, l)
    if "B" in c.phases:
        branch_b(c, l)
    if "C" in c.phases:
        branch_c(c, l)


def branch_a(c, l):
    P = c.P
    A_ = c.Arena()
    CI = A_.f32(2 * (2 + T)).rearrange("p (j t) -> p j t", j=2)
    CIs = A_.f32(2 * NSQ * 6).rearrange("p (j s t) -> p j s t", j=2, s=NSQ)
    tmpx = A_.f32(512); sz = A_.f32(512); acc = A_.f32(512); tz = A_.f32(512)
    mixA = A_.bf16(2 * 512).rearrange("p (j t) -> p j t", j=2)
    c.optmp = A_.f32(NS).rearrange("p (s i) -> p s i", i=SQ)
    sin_ = A_.f32(256)
    gat = A_.f32(2 * 34).rearrange("p (j t) -> p j t", j=2)
    outa = A_.f32(256)
    load_wo(c, l, 0, 2, 128)
    tiles = []
    for t0 in (0, 384, 768):
        ncols = min(384, 1024 - t0)
        tiles.append(c.wload(c.w_in[l][:, t0:t0 + ncols], ncols))

    def wsel(col):
        ti = col // 384
        return tiles[ti][0], tiles[ti][1], col - ti * 384

    P.memset("dve", R(CI[:, :, 0:2], "CIh"), 0.0)
    P.dma("sp", sin_[0:NSQ * 2, :], c.sca[l], writes=["sin"])
    for j in range(2):
        pi = c.next_ps()
        P.transpose(R(c.ps[pi][:, 0:32], ("ps", pi)), R(sin_[0:32, j * 128:(j + 1) * 128], "sin"), R(c.cstf[0:32, 0:32], "cstf"))
        P.copy("act", R(CIs[:, j, :, 0:2], ("CIs", j)), R(c.ps[pi][:, 0:32].rearrange("p (s r) -> p s r", r=2), ("ps", pi)))
    for g in range(5):
        col0, ncol = GROUPS[g]
        for j in range(2):
            p0 = c.next_ps(); wt, wk, wc = wsel(j * 128); proj(c, p0, wt, wk, wc, 128, g)
            p1 = c.next_ps(); wt, wk, wc = wsel(256 + j * 128); proj(c, p1, wt, wk, wc, 128, g)
            P.copy("act", R(tmpx[:, 0:ncol], "tmpx"), R(c.ps[p0][:, 0:ncol], ("ps", p0)))
            vb = V_CAW + (l * 3) * 2 + j
            w0 = R(c.vec[:, vb:vb + 1], "vec"); w1 = R(c.vec[:, vb + 2:vb + 3], "vec"); w2 = R(c.vec[:, vb + 4:vb + 5], "vec")
            if g < 4:
                ck = ("CI", j, g)
                P.tt("dve", R(CI[:, j, 2 + col0:2 + col0 + ncol], ck), R(tmpx[:, 0:ncol], "tmpx"), R(c.ps[p1][:, 0:ncol], ("ps", p1)), ALU.mult)
                rd = [ck, ("CI", j, g - 1), "CIh"]
                P.ts("dve", R(acc[:, 0:ncol], "acc"), (CI[:, j, col0 + 2:col0 + 2 + ncol], rd), w2, None, ALU.mult)
                P.stt("dve", R(acc[:, 0:ncol], "acc"), (CI[:, j, col0 + 1:col0 + 1 + ncol], rd), w1, R(acc[:, 0:ncol], "acc"), ALU.mult, ALU.add)
                P.stt("dve", R(acc[:, 0:ncol], "acc"), (CI[:, j, col0:col0 + ncol], rd), w0, R(acc[:, 0:ncol], "acc"), ALU.mult, ALU.add)
                accv = acc[:, 0:ncol]
            else:
                ck = ("CIs", j)
                P.tt("dve", R(CIs[:, j, :, 2:6], ck), R(tmpx[:, 0:ncol].rearrange("p (s i) -> p s i", i=SQ), "tmpx"),
                     R(c.ps[p1][:, 0:ncol].rearrange("p (s i) -> p s i", i=SQ), ("ps", p1)), ALU.mult)
                a3 = acc[:, 0:ncol].rearrange("p (s i) -> p s i", i=SQ)
                P.ts("dve", R(a3, "acc"), R(CIs[:, j, :, 2:6], ck), w2, None, ALU.mult)
                P.stt("dve", R(a3, "acc"), R(CIs[:, j, :, 1:5], ck), w1, R(a3, "acc"), ALU.mult, ALU.add)
                P.stt("dve", R(a3, "acc"), R(CIs[:, j, :, 0:4], ck), w0, R(a3, "acc"), ALU.mult, ALU.add)
                accv = acc[:, 0:ncol]
            p2 = c.next_ps(); wt, wk, wc = wsel(512 + j * 128); proj(c, p2, wt, wk, wc, 128, g)
            p3 = c.next_ps(); wt, wk, wc = wsel(768 + j * 128); proj(c, p3, wt, wk, wc, 128, g)
            P.act(R(sz[:, 0:ncol], "sz"), R(c.ps[p3][:, 0:ncol], ("ps", p3)), AF.Silu)
            P.tt("dve", R(tz[:, 0:ncol], "tz"), R(accv, "acc"), R(sz[:, 0:ncol], "sz"), ALU.mult)
            P.tt("dve", R(mixA[:, j, 0:ncol], ("mixA", j)), R(tz[:, 0:ncol], "tz"), R(c.ps[p2][:, 0:ncol], ("ps", p2)), ALU.mult)
        outproj(c, l, g, lambda mc: R(mixA[:, mc, 0:GROUPS[g][1]], ("mixA", mc)), 2, 128)
    for j in range(2):
        P.copy("act", R(gat[:, j, 0:2], ("gat", j)), R(CI[:, j, T:T + 2], ("CI", j, 3)))
        P.copy("act", R(gat[:, j, 2:34].rearrange("p (s r) -> p s r", r=2), ("gat", j)), R(CIs[:, j, :, 4:6], ("CIs", j)))
        pi = c.next_ps()
        P.transpose(R(c.ps[pi][0:34, 0:128], ("ps", pi)), R(gat[:, j, :], ("gat", j)), c.ident)
        P.copy("dve", R(outa[0:34, j * 128:(j + 1) * 128], "outa"), R(c.ps[pi][0:34, 0:128], ("ps", pi)))
    P.dma("sp", c.ca_p[l], outa[0:2, :], reads=["outa"], final=True)
    P.dma("sp", c.ca_s[l], outa[2:34, :], reads=["outa"], final=True)
    P.barrier()


def final(c):
    P = c.P
    ada_vectors(c, DEPTH)
    A_ = c.Arena()
    yT = A_.f32(KC * 512).rearrange("p (k t) -> p k t", k=KC)
    A = {"sq": A_.bf16(KC * 512).rearrange("p (k t) -> p k t", k=KC), "rstd": A_.f32(512),
         "tmp": [A_.f32(512) for _ in range(2)], "tmps": A_.f32(KC * NS).rearrange("p (k t) -> p k t", k=KC)}
    ystg = [A_.f32(D) for _ in range(2)]
    si = 0
    for g in range(5):
        col0, ncol = GROUPS[g]
        rstd = rms_stats(c, A, g, "f")
        yk = ("yT",)
        if g < 4:
            for k in range(KC):
                tmp = A["tmp"][k % 2]; tk = ("ntmp", k % 2)
                P.tt("dve", R(tmp[:, 0:ncol], tk), R(c.xT[:, k, col0:col0 + ncol], xkey(g)), R(rstd[:, 0:ncol], "rstd"), ALU.mult)
                P.act(R(yT[:, k, 0:ncol], "yT"), R(tmp[:, 0:ncol], tk), AF.Identity,
                      bias=R(c.ada[:, k, 0:1], "ada"), scale=R(c.m1[:, k, 0:1], "m1"))
        else:
            ts_ = A["tmps"]
            P.tt("dve", R(ts_[:], "ntmps"), R(c.xT[:, :, col0:col0 + ncol], xkey(g)),
                 R(rstd[:, 0:ncol].unsqueeze(1).to_broadcast([128, KC, NS]), "rstd"), ALU.mult)
            v4 = ts_[:].rearrange("p k (s i) -> p k s i", i=SQ)
            m1b = c.m1[:, :, 1:17].unsqueeze(3).to_broadcast([128, KC, NSQ, SQ])
            shb = c.ada[:, 0:8, 1:17].unsqueeze(3).to_broadcast([128, KC, NSQ, SQ])
            P.tt("dve", R(v4, "ntmps"), R(v4, "ntmps"), R(m1b, "m1"), ALU.mult)
            P.tt("dve", R(yT[:, :, 0:NS].rearrange("p k (s i) -> p k s i", i=SQ), "yT"), R(v4, "ntmps"), R(shb, "ada"), ALU.add)
        for tt in range((ncol + 127) // 128):
            rows = min(128, ncol - tt * 128)
            stg = ystg[si % 2]; sk = ("ystg", si % 2); si += 1
            for half in range(2):
                pi = c.next_ps()
                for kk in range(4):
                    k = half * 4 + kk
                    P.transpose(R(c.ps[pi][0:rows, kk * 128:(kk + 1) * 128], ("ps", pi)),
                                R(yT[:, k, tt * 128:tt * 128 + rows], "yT"), c.ident)
                P.copy("act" if half == 0 else "dve", R(stg[0:rows, half * 512:(half + 1) * 512], sk), R(c.ps[pi][0:rows, :], ("ps", pi)))
            if g < 4:
                dst = c.yp[col0 + tt * 128:col0 + tt * 128 + rows, :]
            else:
                dst = c.ys
            P.dma("sp", dst, stg[0:rows, :], reads=[sk], final=True)


_PHASES = ("A", "B", "C")
_NC_CACHE = {}


def _host_consts():
    cst = np.zeros((128, NCST), np.float32)
    cst[:, C_IDENT:C_IDENT + 128] = np.eye(128, dtype=np.float32)
    for m in range(64):
        cst[64 + m, C_SHIFT + m] = 1.0
    cst[:, C_ONES:C_ONES + 128] = 1.0
    k = np.arange(128)[:, None]
    q = np.arange(128)[None, :]
    cst[:, C_MPREV:C_MPREV + 128] = np.where(k >= q, 0.0, NEGM)
    cst[:, C_MCUR:C_MCUR + 128] = np.where(k <= q, 0.0, NEGM)
    cst[0:64, C_BLK:C_BLK + 64] = 1.0
    cst[64:128, C_BLK + 64:C_BLK + 128] = 1.0
    for h in range(6):
        cst[h, C_EH + h * 64:C_EH + (h + 1) * 64] = 1.0
    cst[0:64, C_OFFD:C_OFFD + 64] = 1.0 - np.eye(64, dtype=np.float32)
    j64 = np.arange(64)[:, None]
    i64 = np.arange(64)[None, :]
    cst[0:64, C_MSEQ:C_MSEQ + 64] = np.where((j64 // 4 == i64 // 4) & (j64 <= i64), 0.0, NEGM)
    cst[0:64, C_SEQM:C_SEQM + 16] = (j64 // 4 == np.arange(16)[None, :]).astype(np.float32)
    cst[:, C_SCAN64:C_SCAN64 + 128] = (np.arange(128) % 64 != 0).astype(np.float32)[None, :]
    cst[:, C_SCAN4:C_SCAN4 + 64] = (np.arange(64) % 4 != 0).astype(np.float32)[None, :]
    cst[0:64, C_MDIAG:C_MDIAG + 64] = np.where(j64 == i64, 0.0, NEGM)
    cst[:, C_MS128:C_MS128 + 4] = np.where(np.arange(128)[:, None] >= np.arange(4)[None, :], 0.0, NEGM)
    half = 8
    inv_freq = (500000.0 ** (-np.arange(half, dtype=np.float32) * np.float32(2.0 / 16))).astype(np.float32)
    rope = np.zeros((128, 17, 16), np.float32)
    for tt in range(17):
        if tt < 16:
            pos = (tt * 128 + np.arange(128)).astype(np.float32)
        else:
            pos = (T + (np.arange(128) % SQ)).astype(np.float32)
        ang = pos[:, None] * inv_freq[None, :]
        rope[:, tt, 0:8] = np.cos(ang)
        rope[:, tt, 8:16] = np.sin(ang)
    return cst, rope


def _fm(v):
    v = np.asarray(v, np.float32)
    return np.ascontiguousarray(v.reshape(-1, 128).T)


def _host_vecT(b_ada, norm_w, final_norm_w, b_ada_final, conv_a_w, conv_b_w, gdn_norm_w, a_log, dt_bias):
    vt = np.zeros((128, NV), np.float32)
    for l in range(DEPTH):
        vt[:, V_BADA + l * 24:V_BADA + (l + 1) * 24] = _fm(b_ada[l])
        vt[:, V_NORMW + l * 8:V_NORMW + (l + 1) * 8] = _fm(norm_w[l])
        for tap in range(3):
            vt[:, V_CAW + (l * 3 + tap) * 2:V_CAW + (l * 3 + tap) * 2 + 2] = _fm(conv_a_w[l, tap])
        for tap in range(4):
            vt[:, V_CBW + (l * 4 + tap) * 9:V_CBW + (l * 4 + tap) * 9 + 9] = _fm(conv_b_w[l, tap])
        vt[:, V_GNW + l] = np.tile(np.asarray(gdn_norm_w[l], np.float32), 2)
        vt[0:6, V_ALOG + l] = a_log[l]
        vt[0:6, V_DTB + l] = dt_bias[l]
    vt[:, V_FNORM:V_FNORM + 8] = _fm(final_norm_w)
    vt[:, V_BADAF:V_BADAF + 16] = _fm(b_ada_final)
    return vt


def kernel(x_prompt, x_sample, state_conv_a, state_conv_b, state_gdn, cache_kv_w128, cache_kv_w512,
           cache_kv_w2048, c_prompt, c_sample, w_in, w_out, w_ada, b_ada, norm_w, conv_a_w, conv_b_w,
           a_log, dt_bias, gdn_norm_w, final_norm_w, w_ada_final, b_ada_final, _phases=None, _depth=DEPTH):
    phases = tuple(_phases) if _phases is not None else _PHASES
    f = lambda a: np.ascontiguousarray(np.asarray(a, dtype=np.float32))
    key = (phases, _depth)
    if key not in _NC_CACHE:
        _NC_CACHE[key] = build_program(phases, _depth)
    nc = _NC_CACHE[key]
    cst, rope = _host_consts()
    vt = _host_vecT(f(b_ada), f(norm_w), f(final_norm_w), f(b_ada_final), f(conv_a_w), f(conv_b_w), f(gdn_norm_w),
                    f(a_log), f(dt_bias))
    w_in, w_out, w_ada, w_adaf = f(w_in), f(w_out), f(w_ada), f(w_ada_final)
    x_prompt, x_sample = f(x_prompt), f(x_sample)
    c_prompt, c_sample = f(c_prompt), f(c_sample)
    sca, scb, sgd = f(state_conv_a), f(state_conv_b), f(state_gdn)
    k128, k512, k2048 = f(cache_kv_w128), f(cache_kv_w512), f(cache_kv_w2048)
    in_maps = []
    for i in range(NCORE):
        ss = slice(i * NSQ, (i + 1) * NSQ)
        call = np.concatenate([c_prompt[i:i + 1], c_sample[ss]], axis=0)
        cT = np.ascontiguousarray(call.reshape(17, KC, 128).transpose(2, 1, 0))
        in_maps.append({
            "xp": x_prompt[i], "xs": np.ascontiguousarray(x_sample[ss].reshape(NS, D)), "cT": cT,
            "sca": np.ascontiguousarray(sca[:, ss].reshape(DEPTH, NSQ * 2, 256)),
            "scb": np.ascontiguousarray(scb[:, ss].reshape(DEPTH, NSQ * 3, 1152)),
            "sgd": np.ascontiguousarray(sgd[:, ss]),
            "ck128": np.ascontiguousarray(k128[:, ss].reshape(DEPTH, NSQ, 128, 256)),
            "ck512": np.ascontiguousarray(k512[:, ss].reshape(DEPTH, NSQ, 512, 256)),
            "ck2048": np.ascontiguousarray(k2048[:, ss].reshape(DEPTH, NSQ, 2048, 256)),
            "w_in": w_in, "w_out": w_out, "w_ada": w_ada, "w_adaf": w_adaf,
            "vecT": vt, "cst": cst, "rope": rope,
        })
    res = run_bass_kernel_spmd(nc, in_maps, core_ids=list(range(NCORE)))
    rs = res.results
    cat = lambda name: np.stack([r[name] for r in rs], axis=0)
    y_p = cat("yp")
    y_s = cat("ys").reshape(NCORE * NSQ, SQ, D)
    ca_p = cat("ca_p").transpose(1, 0, 2, 3)
    ca_s = cat("ca_s").reshape(NCORE, DEPTH, NSQ, 2, 256).transpose(1, 0, 2, 3, 4).reshape(DEPTH, NCORE * NSQ, 2, 256)
    cb_p = cat("cb_p").transpose(1, 0, 2, 3)
    cb_s = cat("cb_s").reshape(NCORE, DEPTH, NSQ, 3, 1152).transpose(1, 0, 2, 3, 4).reshape(DEPTH, NCORE * NSQ, 3, 1152)
    gd_p = cat("gd_p").transpose(1, 0, 2, 3, 4)
    gd_s = cat("gd_s").transpose(1, 0, 2, 3, 4, 5).reshape(DEPTH, NCORE * NSQ, 6, 64, 64)
    outs = [y_p, y_s, ca_p, ca_s, cb_p, cb_s, gd_p, gd_s]
    for gi, win in enumerate((128, 512, 2048)):
        name = "kv%d" % win
        kp = cat(name + "_p").transpose(1, 0, 2, 3).reshape(DEPTH, NCORE, win, 2, 2, 64)
        ks = cat(name + "_s").reshape(NCORE, DEPTH, NSQ, SQ, 256).transpose(1, 0, 2, 3, 4).reshape(DEPTH, NCORE * NSQ, SQ, 2, 2, 64)
        outs += [kp, ks]
    return tuple(np.ascontiguousarray(o.astype(np.float32)) for o in outs)


def branch_b(c, l):
    P = c.P
    A_ = c.Arena()
    f32 = A_.f32

    def t3(n_mid, n_in, dt=F32):
        a = f32(n_mid * n_in)[0:64]
        return a.rearrange("p (a b) -> p a b", a=n_mid)

    pre = [f32(131) for _ in range(2)]
    pres = f32(NSQ * 7).rearrange("p (s t) -> p s t", t=7)
    halo = f32(27).rearrange("p (b t) -> p b t", t=3)
    acc = f32(128); act_ = f32(128); rinv = f32(128)
    sqb = A_.bf16(128)
    nrm = f32(128)
    hq_raw = f32(768)[0:64]; hk_raw = f32(768)[0:64]
    HQ = hq_raw.rearrange("p (a b) -> p a b", a=6)
    HK = hk_raw.rearrange("p (a b) -> p a b", a=6)
    HV = t3(6, 128, F32R)
    HZ = t3(6, 128)
    szt = f32(128)
    G = f32(128); BETA = f32(128); GC = f32(128); EG = f32(128); DL = f32(128); NGC = f32(128); tmpd = f32(128)
    EGL = f32(16); nA = f32(1)
    EGLB = t3(6, 16)
    OT = t3(6, 128)
    mixB = A_.bf16(6 * 128)[0:64].rearrange("p (h t) -> p h t", h=6)
    sqo = hk_raw[:, 0:384].bitcast(BF16).rearrange("p (h t) -> p h t", h=6)
    rso = hq_raw.rearrange("p (a b) -> p a b", a=6)
    S = t3(6, 64, F32R)
    SS = t3(NSQ, 64)
    KDblk = t3(NSQ, 64, F32R)
    U = {}
    for nm in ("decT", "LT0", "RT"):
        U[nm] = t3(3, 64)
    for nm in ("kbT", "kbgT", "qgT", "kdT", "vbT", "LT", "aT", "L", "P0", "P1", "PT0", "PT1", "X0", "X1", "VB", "KD", "Rr", "VN"):
        U[nm] = t3(3, 64, F32R)
    stg = f32(384)
    gatB = f32(9 * 51).rearrange("p (b t) -> p b t", b=9)
    c.optmp = f32(NS).rearrange("p (s i) -> p s i", i=SQ)

    def F(ap):
        return ap

    identr = R(c.cstf[0:64, 0:64], "cstf")
    shiftr = R(c.cstf[:, C_SHIFT:C_SHIFT + 64], "cstf")
    ident64 = R(c.cstf[0:64, 0:64], "cstf")
    identb64 = R(c.cstb[0:64, 0:64], "cstb")

    def EH(h):
        return R(c.cstf[0:6, C_EH + h * 64:C_EH + (h + 1) * 64], "cstf")

    load_wo(c, l, 256, 6, 64)
    wt = [c.wload(c.w_in[l][:, 1024 + i * 384:1024 + (i + 1) * 384], 384) for i in range(4)]
    P.dma("pool", c.wab[:], c.w_in[l][:, 2560:2572].rearrange("(k p) c -> p k c", p=128), writes=["wab"])

    P.copy("dve", R(S[:], "S"), R(c.zero[0:64, 0:384].rearrange("p (h t) -> p h t", h=6), "zero"))
    P.memset("dve", R(halo[:], "halo"), 0.0)
    P.act(R(nA[0:6, :], "nA"), R(c.vec[0:6, V_ALOG + l:V_ALOG + l + 1], "vec"), AF.Exp)
    P.ts("dve", R(nA[0:6, :], "nA"), R(nA[0:6, :], "nA"), -1.0, None, ALU.mult)
    for b3 in range(3):
        P.dma("sp", stg[0:NSQ * 3, :], c.scb[l][:, b3 * 384:(b3 + 1) * 384], writes=["stgB"])
        for bb in range(3):
            blk = b3 * 3 + bb
            pi = c.next_ps()
            P.transpose(R(c.ps[pi][:, 0:48], ("ps", pi)), R(stg[0:48, bb * 128:(bb + 1) * 128], "stgB"), R(c.cstf[0:48, 0:48], "cstf"))
            P.copy("act", R(gatB[:, blk, 0:48], ("gatB", blk)), R(c.ps[pi][:, 0:48], ("ps", pi)))

    groups = [(gb * 128, 128, gb // 4) for gb in range(16)] + [(T, NS, 4)]
    for gi, (col0, ncol, g5) in enumerate(groups):
        smp = (g5 == 4)
        nch = 1 if smp else 2
        cols = (col0, ncol)
        for blk in range(9):
            typ, sub = blk // 3, blk % 3
            wtile, wkey = wt[typ]
            pi = c.next_ps()
            proj(c, pi, wtile, wkey, sub * 128, 128, g5, cols=cols)
            vb = V_CBW + (l * 4) * 9 + blk
            wtap = [R(c.vec[:, vb + 9 * tap:vb + 9 * tap + 1], "vec") for tap in range(4)]
            if not smp:
                pr = pre[blk % 2]; pk = ("pre", blk % 2)
                P.copy("dve", R(pr[:, 0:3], pk), R(halo[:, blk, :], ("halo", blk)))
                P.copy("act", R(pr[:, 3:3 + ncol], pk), R(c.ps[pi][:, 0:ncol], ("ps", pi)))
                P.copy("dve", R(halo[:, blk, :], ("halo", blk)), R(pr[:, ncol:ncol + 3], pk))
                P.ts("dve", R(acc[:, 0:ncol], "accB"), R(pr[:, 3:3 + ncol], pk), wtap[3], None, ALU.mult)
                for tap in range(3):
                    P.stt("dve", R(acc[:, 0:ncol], "accB"), R(pr[:, tap:tap + ncol], pk), wtap[tap], R(acc[:, 0:ncol], "accB"), ALU.mult, ALU.add)
            else:
                pk = ("pres",)
                P.copy("dve", R(pres[:, :, 0:3], pk), R(gatB[:, blk, 0:48].rearrange("p (s r) -> p s r", r=3), ("gatB", blk)))
                P.copy("act", R(pres[:, :, 3:7], pk), R(c.ps[pi][:, 0:ncol].rearrange("p (s i) -> p s i", i=SQ), ("ps", pi)))
                a3 = acc[:, 0:ncol].rearrange("p (s i) -> p s i", i=SQ)
                P.ts("dve", R(a3, "accB"), R(pres[:, :, 3:7], pk), wtap[3], None, ALU.mult)
                for tap in range(3):
                    P.stt("dve", R(a3, "accB"), R(pres[:, :, tap:tap + 4], pk), wtap[tap], R(a3, "accB"), ALU.mult, ALU.add)
                P.copy("act", R(gatB[:, blk, 3:51].rearrange("p (s r) -> p s r", r=3), ("gatB", blk)), R(pres[:, :, 4:7], pk))
                P.copy("act", R(gatB[:, blk, 0:3], ("gatB", blk)), R(halo[:, blk, :], ("halo", blk)))
            P.act(R(act_[:, 0:ncol], "actB"), R(acc[:, 0:ncol], "accB"), AF.Silu)
            if typ < 2:
                P.tt("dve", R(sqb[:, 0:ncol], "sqb"), R(act_[:, 0:ncol], "actB"), R(act_[:, 0:ncol], "actB"), ALU.mult)
                p2 = c.next_ps()
                P.mm(R(c.ps[p2][:, 0:ncol], ("ps", p2)), R(c.cstb[:, C_BLK:C_BLK + 128], "cstb"), R(sqb[:, 0:ncol], "sqb"))
                rsqrt(c, R(rinv[:, 0:ncol], "rinvB"), R(c.ps[p2][:, 0:ncol], ("ps", p2)), 1.0, 1e-6)
                P.stt("dve", R(nrm[:, 0:ncol], "nrm"), R(act_[:, 0:ncol], "actB"), 0.125 if typ == 0 else 1.0,
                      R(rinv[:, 0:ncol], "rinvB"), ALU.mult, ALU.mult)
            else:
                P.copy("dve", R(nrm[:, 0:ncol], "nrm"), R(act_[:, 0:ncol], "actB"))
            H = (HQ, HK, HV)[typ]
            hk = ("H", typ)
            P.copy("act", R(H[:, 2 * sub, 0:ncol], hk), R(nrm[0:64, 0:ncol], "nrm"))
            p3 = c.next_ps()
            P.mm(R(c.ps[p3][0:64, 0:ncol], ("ps", p3)), shiftr, R(nrm[:, 0:ncol], "nrm"))
            P.copy("act", R(H[:, 2 * sub + 1, 0:ncol], hk), R(c.ps[p3][0:64, 0:ncol], ("ps", p3)))
        for sub in range(3):
            pi = c.next_ps()
            proj(c, pi, wt[3][0], wt[3][1], sub * 128, 128, g5, cols=cols)
            P.act(R(szt[:, 0:ncol], "szt"), R(c.ps[pi][:, 0:ncol], ("ps", pi)), AF.Silu)
            P.copy("dve", R(HZ[:, 2 * sub, 0:ncol], "HZ"), R(F(szt[0:64, 0:ncol]), "szt"))
            p3 = c.next_ps()
            P.mm(R(c.ps[p3][0:64, 0:ncol], ("ps", p3)), shiftr, R(szt[:, 0:ncol], "szt"))
            P.copy("act", R(HZ[:, 2 * sub + 1, 0:ncol], "HZ"), R(c.ps[p3][0:64, 0:ncol], ("ps", p3)))
        pa = c.next_ps(); pb = c.next_ps()
        for k in range(KC):
            P.mm(R(c.ps[pa][0:6, 0:ncol], ("ps", pa)), R(c.wab[:, k, 0:6], "wab"), R(c.hnT[:, k, col0:col0 + ncol], hkey(g5)),
                 start=(k == 0), stop=(k == KC - 1))
        for k in range(KC):
            P.mm(R(c.ps[pb][0:6, 0:ncol], ("ps", pb)), R(c.wab[:, k, 6:12], "wab"), R(c.hnT[:, k, col0:col0 + ncol], hkey(g5)),
                 start=(k == 0), stop=(k == KC - 1))
        rk = "rowsB"
        P.act(R(G[0:6, 0:ncol], rk), R(c.ps[pa][0:6, 0:ncol], ("ps", pa)), AF.Exp, bias=R(c.vec[0:6, V_DTB + l:V_DTB + l + 1], "vec"))
        P.act(R(G[0:6, 0:ncol], rk), R(G[0:6, 0:ncol], rk), AF.Ln, bias=1.0)
        P.ts("dve", R(G[0:6, 0:ncol], rk), R(G[0:6, 0:ncol], rk), R(nA[0:6, 0:1], "nA"), None, ALU.mult)
        P.act(R(BETA[0:6, 0:ncol], rk), R(c.ps[pb][0:6, 0:ncol], ("ps", pb)), AF.Sigmoid)
        scm = c.cstf[0:6, C_SCAN4:C_SCAN4 + 64] if smp else c.cstf[0:6, C_SCAN64:C_SCAN64 + 128]
        gco, go, sco = GC[0:6, 0:ncol], G[0:6, 0:ncol], scm
        P.op("dve", lambda e, gco=gco, go=go, sco=sco: e.tensor_tensor_scan(gco, sco, go, 0.0, ALU.mult, ALU.add), reads=[rk, "cstf"], writes=[rk])
        P.act(R(EG[0:6, 0:ncol], rk), R(GC[0:6, 0:ncol], rk), AF.Exp)
        P.ts("dve", R(NGC[0:6, 0:ncol], rk), R(GC[0:6, 0:ncol], rk), -1.0, None, ALU.mult)
        clen = SQ if smp else 64
        nseg = ncol // clen
        gc3 = GC[0:6, 0:ncol].rearrange("p (n t) -> p n t", t=clen)
        glb = gc3[:, :, clen - 1:clen].to_broadcast([6, nseg, clen])
        P.tt("dve", R(tmpd[0:6, 0:ncol].rearrange("p (n t) -> p n t", t=clen), rk), R(glb, rk), R(gc3, rk), ALU.subtract)
        P.act(R(DL[0:6, 0:ncol], rk), R(tmpd[0:6, 0:ncol], rk), AF.Exp)
        P.act(R(EGL[0:6, 0:nseg], rk), R(gc3[:, :, clen - 1], rk), AF.Exp)
        pe_ = c.next_ps()
        for h in range(6):
            P.mm(R(c.ps[pe_][0:64, h * 16:h * 16 + nseg], ("ps", pe_)), EH(h), R(EGL[0:6, 0:nseg], rk))
        P.copy("dve", R(EGLB[:, :, 0:nseg], "EGLB"), R(c.ps[pe_][0:64, 0:96].rearrange("p (h n) -> p h n", h=6)[:, :, 0:nseg], ("ps", pe_)))

        for ch in range(nch):
            cc = slice(ch * 64, ch * 64 + 64)
            for hb in range(2):
                heads = [hb * 3 + i for i in range(3)]
                uk = lambda nm: ("U", nm)
                pd = c.next_ps()
                maskap = c.cstb[0:64, 704:768] if smp else c.cstb[0:64, C_MCUR:C_MCUR + 64]
                for i, h in enumerate(heads):
                    o = R(c.ps[pd][0:64, i * 64:(i + 1) * 64], ("ps", pd))
                    P.mm(o, identb64, R(maskap, "cstb"), start=True, stop=False)
                    P.mm(o, EH(h), R(GC[0:6, cc], rk), start=False, stop=False)
                    P.mm(o, R(NGC[0:6, cc], rk), EH(h), start=False, stop=True)
                P.act(R(U["decT"][:].rearrange("p a b -> p (a b)"), uk("decT")), R(c.ps[pd][0:64, 0:192], ("ps", pd)), AF.Exp)
                pbb = c.next_ps(); pb2 = c.next_ps()
                for qi, row in enumerate((BETA, EG, DL)):
                    pq_ = pbb if qi < 2 else pb2
                    for i, h in enumerate(heads):
                        P.mm(R(c.ps[pq_][0:64, ((qi % 2) * 3 + i) * 64:((qi % 2) * 3 + i + 1) * 64], ("ps", pq_)), EH(h), R(row[0:6, cc], rk))

                def bview(qi, pbb=pbb, pb2=pb2):
                    pq_ = pbb if qi < 2 else pb2
                    return R(c.ps[pq_][0:64, (qi % 2) * 192:(qi % 2 + 1) * 192].rearrange("p (a b) -> p a b", a=3), ("ps", pq_))
                hs = slice(hb * 3, hb * 3 + 3)
                P.tt("dve", R(U["kbT"][:], uk("kbT")), R(F(HK[:, hs, cc]), ("H", 1)), bview(0), ALU.mult)
                P.tt("dve", R(U["vbT"][:], uk("vbT")), R(F(HV[:, hs, cc]), ("H", 2)), bview(0), ALU.mult)
                P.tt("dve", R(U["kbgT"][:], uk("kbgT")), R(F(U["kbT"][:]), uk("kbT")), bview(1), ALU.mult)
                P.tt("dve", R(U["qgT"][:], uk("qgT")), R(F(HQ[:, hs, cc]), ("H", 0)), bview(1), ALU.mult)
                P.tt("dve", R(U["kdT"][:], uk("kdT")), R(F(HK[:, hs, cc]), ("H", 1)), bview(2), ALU.mult)
                pk_ = c.next_ps()
                for i, h in enumerate(heads):
                    P.mm(R(c.ps[pk_][0:64, i * 64:(i + 1) * 64], ("ps", pk_)), R(HK[:, h, cc], ("H", 1)), R(U["kbT"][:, i, :], uk("kbT")))
                    P.mm(R(c.ps[pk_][0:64, 192 + i * 64:192 + (i + 1) * 64], ("ps", pk_)), R(HK[:, h, cc], ("H", 1)), R(HQ[:, h, cc], ("H", 0)))
                kk3 = R(c.ps[pk_][0:64, 0:192].rearrange("p (a b) -> p a b", a=3), ("ps", pk_))
                qk3 = R(c.ps[pk_][0:64, 192:384].rearrange("p (a b) -> p a b", a=3), ("ps", pk_))
                P.tt("dve", R(U["LT0"][:], uk("LT0")), kk3, R(U["decT"][:], uk("decT")), ALU.mult)
                offd = c.cstf[0:64, C_OFFD:C_OFFD + 64].unsqueeze(1).to_broadcast([64, 3, 64])
                P.tt("dve", R(U["LT"][:], uk("LT")), R(U["LT0"][:], uk("LT0")), R(offd, "cstf"), ALU.mult)
                P.tt("dve", R(U["aT"][:], uk("aT")), qk3, R(U["decT"][:], uk("decT")), ALU.mult)
                pl = c.next_ps()
                for i in range(3):
                    P.transpose(R(c.ps[pl][0:64, i * 64:(i + 1) * 64], ("ps", pl)), R(U["LT0"][:, i, :], uk("LT0")), ident64)
                P.tt("dve", R(U["L"][:], uk("L")), R(c.ps[pl][0:64, 0:192].rearrange("p (a b) -> p a b", a=3), ("ps", pl)), R(offd, "cstf"), ALU.mult)
                idb = c.cstf[0:64, 0:64].unsqueeze(1).to_broadcast([64, 3, 64])
                P.stt("dve", R(U["X0"][:], uk("X0")), R(F(U["LT"][:]), uk("LT")), -1.0, R(idb, "cstf"), ALU.mult, ALU.add)
                Pc, PTc, Xc = "L", "LT", "X0"
                nlev = 1 if smp else 5
                for lev in range(nlev):
                    Pn = "P%d" % (lev % 2); PTn = "PT%d" % (lev % 2); Xn = "X%d" % ((lev + 1) % 2)
                    pp = c.next_ps()
                    for i in range(3):
                        P.mm(R(c.ps[pp][0:64, i * 64:(i + 1) * 64], ("ps", pp)), R(U[PTc][:, i, :], uk(PTc)), R(U[Pc][:, i, :], uk(Pc)))
                    P.copy("act", R(U[Pn][:], uk(Pn)), R(c.ps[pp][0:64, 0:192].rearrange("p (a b) -> p a b", a=3), ("ps", pp)))
                    if lev < nlev - 1:
                        pt_ = c.next_ps()
                        for i in range(3):
                            P.mm(R(c.ps[pt_][0:64, i * 64:(i + 1) * 64], ("ps", pt_)), R(U[Pc][:, i, :], uk(Pc)), R(U[PTc][:, i, :], uk(PTc)))
                        P.copy("act", R(U[PTn][:], uk(PTn)), R(c.ps[pt_][0:64, 0:192].rearrange("p (a b) -> p a b", a=3), ("ps", pt_)))
                    px = c.next_ps()
                    for i in range(3):
                        o = R(c.ps[px][0:64, i * 64:(i + 1) * 64], ("ps", px))
                        P.mm(o, identr, R(U[Xc][:, i, :], uk(Xc)), start=True, stop=False)
                        P.mm(o, R(U[Pn][:, i, :], uk(Pn)), R(U[Xc][:, i, :], uk(Xc)), start=False, stop=True)
                    P.copy("dve", R(U[Xn][:], uk(Xn)), R(c.ps[px][0:64, 0:192].rearrange("p (a b) -> p a b", a=3), ("ps", px)))
                    Pc, PTc, Xc = Pn, PTn, Xn
                TinvT = Xc
                pv = c.next_ps()
                for i in range(3):
                    P.transpose(R(c.ps[pv][0:64, i * 64:(i + 1) * 64], ("ps", pv)), R(F(U["vbT"][:, i, :]), uk("vbT")), ident64)
                    P.transpose(R(c.ps[pv][0:64, 192 + i * 64:192 + (i + 1) * 64], ("ps", pv)), R(F(U["kdT"][:, i, :]), uk("kdT")), ident64)
                P.copy("act", R(U["VB"][:], uk("VB")), R(c.ps[pv][0:64, 0:192].rearrange("p (a b) -> p a b", a=3), ("ps", pv)))
                P.copy("act", R(U["KD"][:], uk("KD")), R(c.ps[pv][0:64, 192:384].rearrange("p (a b) -> p a b", a=3), ("ps", pv)))
                if not smp:
                    pr_ = c.next_ps()
                    for i, h in enumerate(heads):
                        P.mm(R(c.ps[pr_][0:64, i * 64:(i + 1) * 64], ("ps", pr_)), R(U["kbgT"][:, i, :], uk("kbgT")), R(S[:, h, :], ("S", h)))
                    P.tt("dve", R(U["Rr"][:], uk("Rr")), R(F(U["VB"][:]), uk("VB")),
                         R(c.ps[pr_][0:64, 0:192].rearrange("p (a b) -> p a b", a=3), ("ps", pr_)), ALU.subtract)
                    pn = c.next_ps()
                    for i in range(3):
                        P.mm(R(c.ps[pn][0:64, i * 64:(i + 1) * 64], ("ps", pn)), R(U[TinvT][:, i, :], uk(TinvT)), R(U["Rr"][:, i, :], uk("Rr")))
                    P.copy("act", R(U["VN"][:], uk("VN")), R(c.ps[pn][0:64, 0:192].rearrange("p (a b) -> p a b", a=3), ("ps", pn)))
                    po = c.next_ps()
                    for i, h in enumerate(heads):
                        o = R(c.ps[po][0:64, i * 64:(i + 1) * 64], ("ps", po))
                        P.mm(o, R(S[:, h, :], ("S", h)), R(U["qgT"][:, i, :], uk("qgT")), start=True, stop=False)
                        P.mm(o, R(U["VN"][:, i, :], uk("VN")), R(U["aT"][:, i, :], uk("aT")), start=False, stop=True)
                    P.copy("act", R(OT[:, hs, cc], "OT"), R(c.ps[po][0:64, 0:192].rearrange("p (a b) -> p a b", a=3), ("ps", po)))
                    pS = c.next_ps()
                    for i, h in enumerate(heads):
                        P.mm(R(c.ps[pS][0:64, i * 64:(i + 1) * 64], ("ps", pS)), R(U["KD"][:, i, :], uk("KD")), R(U["VN"][:, i, :], uk("VN")))
                    for i, h in enumerate(heads):
                        P.stt("dve", R(S[:, h, :], ("S", h)), R(F(S[:, h, :]), ("S", h)), R(EGLB[:, h, ch:ch + 1], "EGLB"),
                              R(c.ps[pS][0:64, i * 64:(i + 1) * 64], ("ps", pS)), ALU.mult, ALU.add)
                else:
                    for i, h in enumerate(heads):
                        P.dma("sp", SS[:], c.sgd[l][:, h].rearrange("s k v -> k s v"), writes=["SS"])
                        pr_ = c.next_ps()
                        for s_ in range(NSQ):
                            P.mm(R(c.ps[pr_][0:64, s_ * 4:(s_ + 1) * 4], ("ps", pr_)), R(SS[:, s_, :], "SS"),
                                 R(F(U["kbgT"][:, i, s_ * 4:(s_ + 1) * 4]), uk("kbgT")))
                        P.tt("dve", R(U["RT"][:, i, :], uk("RT")), R(F(U["vbT"][:, i, :]), uk("vbT")), R(c.ps[pr_][0:64, 0:64], ("ps", pr_)), ALU.subtract)
                        pq = c.next_ps()
                        P.transpose(R(c.ps[pq][0:64, 0:64], ("ps", pq)), R(U["RT"][:, i, :], uk("RT")), ident64)
                        P.copy("act", R(U["Rr"][:, i, :], uk("Rr")), R(c.ps[pq][0:64, 0:64], ("ps", pq)))
                        pn = c.next_ps()
                        P.mm(R(c.ps[pn][0:64, 0:64], ("ps", pn)), R(U[TinvT][:, i, :], uk(TinvT)), R(U["Rr"][:, i, :], uk("Rr")))
                        P.copy("act", R(U["VN"][:, i, :], uk("VN")), R(c.ps[pn][0:64, 0:64], ("ps", pn)))
                        po = c.next_ps()
                        P.mm(R(c.ps[po][0:64, 0:64], ("ps", po)), R(F(U["VN"][:, i, :]), uk("VN")), R(F(U["aT"][:, i, :]), uk("aT")), start=True, stop=False)
                        for s_ in range(NSQ):
                            P.mm(R(c.ps[po][0:64, s_ * 4:(s_ + 1) * 4], ("ps", po)), R(SS[:, s_, :], "SS"),
                                 R(F(U["qgT"][:, i, s_ * 4:(s_ + 1) * 4]), uk("qgT")), start=False, stop=(s_ == NSQ - 1))
                        P.copy("act", R(OT[:, h, cc], "OT"), R(c.ps[po][0:64, 0:64], ("ps", po)))
                        seqm = c.cstf[0:64, C_SEQM:C_SEQM + 16].unsqueeze(2).to_broadcast([64, NSQ, 64])
                        kdb = F(U["KD"][:, i, :]).unsqueeze(1).to_broadcast([64, NSQ, 64])
                        P.tt("dve", R(KDblk[:], "KDblk"), R(kdb, uk("KD")), R(seqm, "cstf"), ALU.mult)
                        eglb = EGLB[:, h, 0:NSQ].unsqueeze(2).to_broadcast([64, NSQ, 64])
                        P.tt("dve", R(SS[:], "SS"), R(SS[:], "SS"), R(eglb, "EGLB"), ALU.mult)
                        for half in range(2):
                            pS = c.next_ps()
                            for s8 in range(8):
                                s_ = half * 8 + s8
                                P.mm(R(c.ps[pS][0:64, s8 * 64:(s8 + 1) * 64], ("ps", pS)), R(KDblk[:, s_, :], "KDblk"), R(U["VN"][:, i, :], uk("VN")))
                            P.tt("dve", R(SS[:, half * 8:half * 8 + 8, :], "SS"), R(SS[:, half * 8:half * 8 + 8, :], "SS"),
                                 R(c.ps[pS][0:64, :].rearrange("p (s v) -> p s v", s=8), ("ps", pS)), ALU.add)
                        P.dma("sp", c.gd_s[l][:, h].rearrange("s k v -> k s v"), SS[:], reads=["SS"], final=True)
        P.tt("dve", R(sqo[:, :, 0:ncol], ("H", 1)), R(OT[:, :, 0:ncol], "OT"), R(OT[:, :, 0:ncol], "OT"), ALU.mult)
        for hb in range(2):
            pg = c.next_ps()
            for i in range(3):
                P.mm(R(c.ps[pg][0:64, i * 128:i * 128 + ncol], ("ps", pg)), R(c.cstb[0:64, C_ONES:C_ONES + 64], "cstb"), R(sqo[:, hb * 3 + i, 0:ncol], ("H", 1)))
            rsqrt(c, R(rso[:, hb * 3:hb * 3 + 3, 0:ncol], ("H", 0)),
                  R(c.ps[pg][0:64, 0:384].rearrange("p (a b) -> p a b", a=3)[:, :, 0:ncol], ("ps", pg)), 1.0 / 64, EPS)
        P.tt("dve", R(rso[:, :, 0:ncol], ("H", 0)), R(rso[:, :, 0:ncol], ("H", 0)), R(OT[:, :, 0:ncol], "OT"), ALU.mult)
        P.stt("dve", R(mixB[:, :, 0:ncol], "mixB"), R(rso[:, :, 0:ncol], ("H", 0)), R(c.vec[0:64, V_GNW + l:V_GNW + l + 1], "vec"),
              R(HZ[:, :, 0:ncol], "HZ"), ALU.mult, ALU.mult)
        outproj(c, l, g5, lambda mc: R(mixB[:, mc, 0:ncol], "mixB"), 6, 64, cols=cols)
    P.dma("sp", c.gd_p[l].rearrange("h k v -> k h v"), F(S[:]), reads=[("S", h) for h in range(6)], final=True)
    for b3 in range(3):
        for bb in range(3):
            blk = b3 * 3 + bb
            pi = c.next_ps()
            P.transpose(R(c.ps[pi][0:51, 0:128], ("ps", pi)), R(gatB[:, blk, :], ("gatB", blk)), c.ident)
            P.copy("dve", R(stg[0:51, bb * 128:(bb + 1) * 128], "stgB"), R(c.ps[pi][0:51, 0:128], ("ps", pi)))
        P.dma("sp", c.cb_p[l][:, b3 * 384:(b3 + 1) * 384], stg[0:3, :], reads=["stgB"], final=True)
        P.dma("sp", c.cb_s[l][:, b3 * 384:(b3 + 1) * 384], stg[3:51, :], reads=["stgB"], final=True)
    P.barrier()


CDIL = (1, 4, 16)
CWIN = (128, 512, 2048)


def _rope(c, buf3, rows, tt, rt, key):
    P = c.P
    nh = buf3.shape[1]
    x1 = buf3[:, :, 0:8]; x2 = buf3[:, :, 8:16]
    cos = c.ropet[0:rows, tt, 0:8].unsqueeze(1).to_broadcast([rows, nh, 8])
    sin = c.ropet[0:rows, tt, 8:16].unsqueeze(1).to_broadcast([rows, nh, 8])
    t = [r_[0:rows, 0:nh * 8].rearrange("p (h e) -> p h e", e=8) for r_ in rt]
    P.tt("dve", R(t[0], "ropet0"), R(x1, key), R(cos, "rope"), ALU.mult)
    P.tt("dve", R(t[1], "ropet1"), R(x2, key), R(sin, "rope"), ALU.mult)
    P.tt("dve", R(t[2], "ropet2"), R(x2, key), R(cos, "rope"), ALU.mult)
    P.tt("dve", R(t[3], "ropet3"), R(x1, key), R(sin, "rope"), ALU.mult)
    P.tt("dve", R(x1, key), R(t[0], "ropet0"), R(t[1], "ropet1"), ALU.subtract)
    P.tt("dve", R(x2, key), R(t[2], "ropet2"), R(t[3], "ropet3"), ALU.add)


def branch_c(c, l):
    P = c.P
    A_ = c.Arena(); f32 = A_.f32
    ksT = f32(384)[0:64].rearrange("p (h t) -> p h t", h=6)
    qsT = f32(384)[0:64].rearrange("p (h t) -> p h t", h=6)
    vtok = f32(384)[0:64].rearrange("p (g x) -> p g x", g=3)
    ZS = f32(768)[0:64].rearrange("p (h t) -> p h t", h=6)
    c.optmp = f32(NS).rearrange("p (s i) -> p s i", i=SQ)
    prefix = A_.off
    KT = A_.bf16(6 * T)[0:64].rearrange("p (h t) -> p h t", h=6)
    VS = A_.bf16(48 * 128).rearrange("p (n x) -> p n x", x=128)
    kv_raw = f32(768)
    kvst = kv_raw.rearrange("p (g x) -> p g x", g=3)
    qsb = f32(384)
    rt = [f32(48) for _ in range(4)]
    QTt = A_.bf16(6 * 128)[0:64].rearrange("p (h t) -> p h t", h=6)
    PT = [A_.bf16(512) for _ in range(3)]
    dtot = f32(256)[0:64].rearrange("p (h t) -> p h t", h=2)
    onorm = kv_raw[0:64].rearrange("p (h t) -> p h t", h=6)
    mixC = A_.bf16(768)[0:64].rearrange("p (h t) -> p h t", h=6)

    load_wo(c, l, 640, 6, 64)
    wk_t, wk_k = c.wload(c.w_in[l][:, 2956:3340], 384)
    wv_t, wv_k = c.wload(c.w_in[l][:, 3340:3724], 384)
    wq_t, wq_k = c.wload(c.w_in[l][:, 2572:2956], 384)
    wz_t, wz_k = c.wload(c.w_in[l][:, 3724:4108], 384)
    ident = c.ident

    def tokproj(pi, wt_, wk_, tt, rows, ncols=384, c0=0):
        col0 = tt * 128
        g5 = min(tt // 4, 4)
        for k in range(KC):
            P.mm(R(c.ps[pi][0:rows, 0:ncols], ("ps", pi)), R(c.hnT[:, k, col0:col0 + rows], hkey(g5)), R(wt_[:, k, c0:c0 + ncols], wk_),
                 start=(k == 0), stop=(k == KC - 1))

    for tt in range(17):
        rows = 128 if tt < 16 else NS
        pk = c.next_ps(); tokproj(pk, wk_t, wk_k, tt, rows)
        pv = c.next_ps(); tokproj(pv, wv_t, wv_k, tt, rows)
        kk = ("kvst",)
        P.copy("act", R(kvst[0:rows, :, 0:128], "kvst"), R(c.ps[pk][0:rows, 0:384].rearrange("p (g x) -> p g x", g=3), ("ps", pk)))
        P.copy("act", R(kvst[0:rows, :, 128:256], "kvst"), R(c.ps[pv][0:rows, 0:384].rearrange("p (g x) -> p g x", g=3), ("ps", pv)))
        for gi in range(3):
            _rope(c, kvst[0:rows, gi, 0:128].rearrange("p (h d) -> p h d", h=2), rows, tt, rt, "kvst")
        for gi in range(3):
            if tt < 16:
                lo = T - CWIN[gi]
                if tt * 128 >= lo:
                    P.dma("sp", c.kv_p[gi][l][tt * 128 - lo:tt * 128 - lo + 128, :], kvst[:, gi, :], reads=["kvst"], final=True)
            else:
                P.dma("sp", c.kv_s[gi][l], kvst[0:NS, gi, :], reads=["kvst"], final=True)
        for half in range(2):
            pt = c.next_ps()
            for i in range(3):
                hx = half * 3 + i
                gi, h = hx // 2, hx % 2
                P.transpose(R(c.ps[pt][0:64, i * 128:i * 128 + rows], ("ps", pt)), R(kvst[0:rows, gi, h * 64:(h + 1) * 64], "kvst"),
                            R(c.cstf[0:rows, 0:rows], "cstf"))
            src = c.ps[pt][0:64, 0:384].rearrange("p (h t) -> p h t", h=3)[:, :, 0:rows]
            if tt < 16:
                P.copy("act", R(KT[:, half * 3:half * 3 + 3, tt * 128:tt * 128 + 128], ("KT", tt)), R(src, ("ps", pt)))
            else:
                P.copy("act", R(ksT[:, half * 3:half * 3 + 3, :], "ksT"), R(src, ("ps", pt)))
        if tt == 16:
            P.copy("dve", R(vtok[:], "vtok"), R(kvst[0:NS, :, 128:256], "kvst"))
    for gi in range(3):
        dil = CDIL[gi]; nb = 16 // dil
        for q4 in range(4):
            pi = c.next_ps()
            for j in range(4):
                st_ = q4 * 4 + j
                r, n = st_ // nb, st_ % nb
                lo = r + dil * n * 128
                for k in range(KC):
                    P.mm(R(c.ps[pi][:, j * 128:(j + 1) * 128], ("ps", pi)), (c.hnT[:, k, lo:lo + dil * 127 + 1:dil], tuple(hkey(g) for g in range(4))),
                         R(wv_t[:, k, gi * 128:(gi + 1) * 128], wv_k), start=(k == 0), stop=(k == KC - 1))
            P.copy("act" if q4 % 2 == 0 else "dve", R(VS[:, gi * 16 + q4 * 4:gi * 16 + q4 * 4 + 4, :], ("VS", gi)),
                   R(c.ps[pi][:].rearrange("p (n x) -> p n x", x=128), ("ps", pi)))

    c.ps_n = 4; c.ps_i = 0
    psO = [4, 5]; psD = [6, 7]
    onesb64 = R(c.cstb[:, C_ONES:C_ONES + 64], "cstb")
    mcur = c.cstb[:, C_MCUR:C_MCUR + 128]; mprev = c.cstb[:, C_MPREV:C_MPREV + 128]
    allkt = tuple(("KT", t_) for t_ in range(16))

    def q_and_z(tt, rows, QT_dst, qkey, Z_dst):
        pq = c.next_ps(); tokproj(pq, wq_t, wq_k, tt, rows)
        P.copy("act", R(qsb[0:rows, :], "qsb"), R(c.ps[pq][0:rows, 0:384], ("ps", pq)))
        _rope(c, qsb[0:rows, :].rearrange("p (h d) -> p h d", h=6), rows, tt, rt, "qsb")
        for half in range(2):
            pt = c.next_ps()
            for i in range(3):
                hx = half * 3 + i
                P.transpose(R(c.ps[pt][0:64, i * 128:i * 128 + rows], ("ps", pt)), R(qsb[0:rows, hx * 64:(hx + 1) * 64], "qsb"),
                            R(c.cstf[0:rows, 0:rows], "cstf"))
            P.copy("dve", R(QT_dst[:, half * 3:half * 3 + 3, 0:rows], qkey),
                   R(c.ps[pt][0:64, 0:384].rearrange("p (h t) -> p h t", h=3)[:, :, 0:rows], ("ps", pt)))
        col0 = tt * 128; g5 = min(tt // 4, 4)
        for half in range(2):
            pz = c.next_ps()
            for i in range(3):
                hx = half * 3 + i
                for k in range(KC):
                    P.mm(R(c.ps[pz][0:64, i * 128:i * 128 + rows], ("ps", pz)), R(wz_t[:, k, hx * 64:(hx + 1) * 64], wz_k),
                         R(c.hnT[:, k, col0:col0 + rows], hkey(g5)), start=(k == 0), stop=(k == KC - 1))
            P.act(R(Z_dst[:, half * 3:half * 3 + 3, 0:rows], "ZS"),
                  R(c.ps[pz][0:64, 0:384].rearrange("p (h t) -> p h t", h=3)[:, :, 0:rows], ("ps", pz)), AF.Silu)

    for tt in range(16):
        q_and_z(tt, 128, QTt, "QTt", ZS)

        def pv_den(hx, ocols, vs_tile, h, pt_ap, ptkey, first, last):
            o = c.ps[psO[hx // 3]][0:64, (hx % 3) * 128 + ocols[0]:(hx % 3) * 128 + ocols[0] + ocols[1]]
            d = c.ps[psD[hx // 3]][0:64, (hx % 3) * 128 + ocols[0]:(hx % 3) * 128 + ocols[0] + ocols[1]]
            P.mm(R(o, ("ps", psO[hx // 3])), R(VS[:, vs_tile, h * 64:(h + 1) * 64], ("VS", vs_tile // 16)), R(pt_ap, ptkey), start=first, stop=last)
            P.mm(R(d, ("ps", psD[hx // 3])), onesb64, R(pt_ap, ptkey), start=first, stop=last)

        for gi in range(3):
            dil = CDIL[gi]; nb = 16 // dil; nq = 128 // dil
            n = tt // dil
            qo = (tt % dil) * nq
            kbl = [0] if n == 0 else [0, 1]
            pS = c.next_ps()
            ptile = PT[gi]; ptk = ("PT", gi)
            slots = {}
            for kbi in kbl:
                kb = n - kbi
                for h in range(2):
                    hx = gi * 2 + h
                    for r in range(dil):
                        col = ((kbi * 2 + h) * dil + r) * nq
                        slots[(kbi, h, r)] = col
                        klo = r + dil * kb * 128
                        o = R(c.ps[pS][:, col:col + nq], ("ps", pS))
                        P.mm(o, (KT[:, hx, klo:klo + dil * 127 + 1:dil], allkt), R(QTt[:, hx, r:128:dil], "QTt"), start=True, stop=False)
                        msk = (mcur if kbi == 0 else mprev)[:, qo:qo + nq]
                        P.mm(o, c.identb, R(msk, "cstb"), start=False, stop=True)
            used = len(kbl) * 2 * dil * nq
            P.act(R(ptile[:, 0:used], ptk), R(c.ps[pS][:, 0:used], ("ps", pS)), AF.Exp, scale=0.125)
            for h in range(2):
                hx = gi * 2 + h
                for r in range(dil):
                    for ki, kbi in enumerate(kbl):
                        kb = n - kbi
                        col = slots[(kbi, h, r)]
                        pv_den(hx, (r * nq, nq), gi * 16 + r * nb + kb, h, ptile[:, col:col + nq], ptk, ki == 0, ki == len(kbl) - 1)
        def nat(ap, dil):
            if dil == 1:
                return ap
            return ap.rearrange("p (r m) -> p m r", r=dil)
        for h in range(2):
            dv_ = dtot[:, h, :]
            P.copy("dve", R(dv_, "dtot"), R(c.ps[psD[0]][0:64, h * 128:(h + 1) * 128], ("ps", psD[0])))
            for gi in (1, 2):
                hx = gi * 2 + h
                dil = CDIL[gi]
                src = c.ps[psD[hx // 3]][0:64, (hx % 3) * 128:(hx % 3) * 128 + 128]
                dvw = dv_.rearrange("p (m r) -> p m r", r=dil)
                P.tt("dve", R(dvw, "dtot"), R(dvw, "dtot"), R(nat(src, dil), ("ps", psD[hx // 3])), ALU.add)
            P.op("dve", lambda e, dv_=dv_: e.reciprocal(dv_, dv_), reads=["dtot"], writes=["dtot"])
        for hx in range(6):
            gi, h = hx // 2, hx % 2
            dil = CDIL[gi]
            src = c.ps[psO[hx // 3]][0:64, (hx % 3) * 128:(hx % 3) * 128 + 128]
            if dil == 1:
                P.tt("dve", R(onorm[:, hx, :], "kvst"), R(src, ("ps", psO[hx // 3])), R(dtot[:, h, :], "dtot"), ALU.mult)
            else:
                P.tt("dve", R(onorm[:, hx, :].rearrange("p (m r) -> p m r", r=dil), "kvst"), R(nat(src, dil), ("ps", psO[hx // 3])),
                     R(dtot[:, h, :].rearrange("p (m r) -> p m r", r=dil), "dtot"), ALU.mult)
        P.tt("dve", R(mixC[:], "mixC"), R(onorm[:], "kvst"), R(ZS[:], "ZS"), ALU.mult)
        outproj(c, l, tt // 4, lambda mc: R(mixC[:, mc, :], "mixC"), 6, 64, cols=(tt * 128, 128))
    c.ps_n = 8; c.ps_i = 0

    ZSs = ZS
    q_and_z(16, NS, qsT, "qsT", ZSs)
    P.barrier()
    B_ = c.Arena()
    b32 = B_.f32
    _skip = b32(prefix)
    CK = [b32(9 * 256).rearrange("p (b x) -> p b x", b=9) for _ in range(2)]
    KcT = b32(18 * 128)[0:64].rearrange("p (b t) -> p b t", b=18)
    PTn = b32(384)[0:64].rearrange("p (h t) -> p h t", h=6)
    PTc = b32(24)
    dts = b32(128)[0:64].rearrange("p (h t) -> p h t", h=2)
    ons = b32(384)[0:64].rearrange("p (h t) -> p h t", h=6)
    mixS = B_.bf16(384)[0:64].rearrange("p (h t) -> p h t", h=6)
    ones64f = R(c.cstf[0:64, C_ONES:C_ONES + 64], "cstf")
    ones128f = R(c.cstf[:, C_ONES:C_ONES + 64], "cstf")
    pO = 6; pD = 7
    c.ps_n = 6
    pn_ = c.next_ps()
    for hx in range(6):
        gi = hx // 2
        o = R(c.ps[pn_][0:64, hx * 64:(hx + 1) * 64], ("ps", pn_))
        P.mm(o, R(ksT[:, hx, :], "ksT"), R(qsT[:, hx, :], "qsT"), start=True, stop=False)
        msk = c.cstb[0:64, 704:768] if gi == 0 else c.cstb[0:64, 768:832]
        P.mm(o, R(c.cstb[0:64, 0:64], "cstb"), R(msk, "cstb"), start=False, stop=True)
    P.act(R(PTn[:].rearrange("p h t -> p (h t)"), "PTn"), R(c.ps[pn_][0:64, 0:384], ("ps", pn_)), AF.Exp, scale=0.125)
    for hx in range(6):
        gi, h = hx // 2, hx % 2
        P.mm(R(c.ps[pO][0:64, hx * 64:(hx + 1) * 64], ("ps", pO)), R(vtok[:, gi, h * 64:(h + 1) * 64], "vtok"), R(PTn[:, hx, :], "PTn"), start=True, stop=False)
        P.mm(R(c.ps[pD][0:64, hx * 64:(hx + 1) * 64], ("ps", pD)), ones64f, R(PTn[:, hx, :], "PTn"), start=True, stop=False)
    for s_ in range(NSQ):
        ck = CK[s_ % 2]; ckk = ("CK", s_ % 2)
        P.dma("sp", ck[:, 0, :], c.ck128[l][s_], writes=[ckk])
        P.dma("sp", ck[:, 1:5, :], c.ck512[l][s_].rearrange("(m i) x -> m i x", i=4), writes=[ckk])
        P.dma("sp", ck[:, 5:9, :], c.ck2048[l][s_].rearrange("(m i) x -> m i x", i=16)[:, 0:4, :], writes=[ckk])
        for q5 in range(5):
            pt = c.next_ps()
            nn = 4 if q5 < 4 else 2
            for j in range(nn):
                bh = q5 * 4 + j
                blk, h = bh // 2, bh % 2
                P.transpose(R(c.ps[pt][0:64, j * 128:(j + 1) * 128], ("ps", pt)), R(ck[:, blk, h * 64:(h + 1) * 64], ckk), ident)
            P.copy("act" if q5 % 2 == 0 else "dve", R(KcT[:, q5 * 4:q5 * 4 + nn, :], "KcT"),
                   R(c.ps[pt][0:64, 0:nn * 128].rearrange("p (b t) -> p b t", b=nn), ("ps", pt)))
        psc = c.next_ps()
        for h in range(2):
            o = R(c.ps[psc][:, h * 4:h * 4 + 4], ("ps", psc))
            P.mm(o, R(KcT[:, h, :], "KcT"), R(qsT[:, h, s_ * 4:s_ * 4 + 4], "qsT"), start=True, stop=False)
            P.mm(o, c.identb, R(c.cstb[:, 832:836], "cstb"), start=False, stop=True)
            for gi in (1, 2):
                for i in range(SQ):
                    blk = (1 if gi == 1 else 5) + i
                    col = gi * 8 + h * 4 + i
                    P.mm(R(c.ps[psc][:, col:col + 1], ("ps", psc)), R(KcT[:, blk * 2 + h, :], "KcT"),
                         R(qsT[:, gi * 2 + h, s_ * 4 + i:s_ * 4 + i + 1], "qsT"))
        P.act(R(PTc[:, 0:24], "PTc"), R(c.ps[psc][:, 0:24], ("ps", psc)), AF.Exp, scale=0.125)
        for h in range(2):
            for gi in range(3):
                hx = gi * 2 + h
                if gi == 0:
                    items = [(0, s_ * 4, 4, h * 4)]
                else:
                    items = [((1 if gi == 1 else 5) + i, s_ * 4 + i, 1, gi * 8 + h * 4 + i) for i in range(SQ)]
                for (blk, ocol, w_, pcol) in items:
                    P.mm(R(c.ps[pO][0:64, hx * 64 + ocol:hx * 64 + ocol + w_], ("ps", pO)), R(ck[:, blk, 128 + h * 64:128 + (h + 1) * 64], ckk),
                         R(PTc[:, pcol:pcol + w_], "PTc"), start=False, stop=True)
                    P.mm(R(c.ps[pD][0:64, hx * 64 + ocol:hx * 64 + ocol + w_], ("ps", pD)), ones128f,
                         R(PTc[:, pcol:pcol + w_], "PTc"), start=False, stop=True)
    for h in range(2):
        P.copy("dve", R(dts[:, h, :], "dts"), R(c.ps[pD][0:64, h * 64:(h + 1) * 64], ("ps", pD)))
        for gi in (1, 2):
            hx = gi * 2 + h
            P.tt("dve", R(dts[:, h, :], "dts"), R(dts[:, h, :], "dts"), R(c.ps[pD][0:64, hx * 64:(hx + 1) * 64], ("ps", pD)), ALU.add)
    dall = dts[:]
    P.op("dve", lambda e: e.reciprocal(dall, dall), reads=["dts"], writes=["dts"])
    for hx in range(6):
        P.tt("dve", R(ons[:, hx, :], "ons"), R(c.ps[pO][0:64, hx * 64:(hx + 1) * 64], ("ps", pO)), R(dts[:, hx % 2, :], "dts"), ALU.mult)
    P.tt("dve", R(mixS[:], "mixS"), R(ons[:], "ons"), R(ZSs[:, :, 0:NS], "ZS"), ALU.mult)
    c.ps_n = 8; c.ps_i = 0
    outproj(c, l, 4, lambda mc: R(mixS[:, mc, :], "mixS"), 6, 64, cols=(T, NS))
    P.barrier()
```

```python
import contextlib
import numpy as np
import concourse.bass as bass
import concourse.mybir as mybir
from concourse.bass_utils import run_bass_kernel_spmd

F32 = mybir.dt.float32
F32R = mybir.dt.float32r
BF16 = mybir.dt.bfloat16
AF = mybir.ActivationFunctionType
ALU = mybir.AluOpType
AX = mybir.AxisListType


class Prog:
    COMPUTE = ("pe", "act", "dve", "pool")
    ALL = ("pe", "act", "dve", "pool", "sp")

    def __init__(self, nc, stack, n_dma_sems=24):
        self.nc = nc
        self.stack = stack
        self.streams = {e: [] for e in self.ALL}
        self.esem = {e: stack.enter_context(nc.semaphore("prog_" + e)) for e in self.COMPUTE}
        self.ecount = {e: 0 for e in self.COMPUTE}
        self.known = {e: {} for e in self.ALL}
        self.res = {}
        self.dsem = {}
        for q in ("sp", "act", "pool"):
            self.dsem[q] = [[stack.enter_context(nc.semaphore("dq_%s_%d" % (q, i))), 0] for i in range(n_dma_sems)]
        self.dnext = {q: 0 for q in self.dsem}
        self.final_tokens = []

    def _deps(self, eng, reads, writes, same_engine_sem=None):
        toks = []
        for k in reads:
            r = self.res.get(k)
            if r and r["w"] is not None:
                toks.append(r["w"])
        for k in writes:
            r = self.res.get(k)
            if r:
                if r["w"] is not None:
                    toks.append(r["w"])
                toks.extend(r["r"])
        return toks

    def _record(self, tok, reads, writes):
        for k in reads:
            r = self.res.setdefault(k, {"w": None, "r": []})
            r["r"].append(tok)
        for k in writes:
            self.res[k] = {"w": tok, "r": []}

    def _waits(self, eng, toks, skip_sem=None):
        need = {}
        for (sem, val) in toks:
            if skip_sem is not None and sem is skip_sem:
                continue
            sid = id(sem)
            if self.known[eng].get(sid, 0) >= val:
                continue
            if sid not in need or need[sid][1] < val:
                need[sid] = (sem, val)
        for sid, (sem, val) in need.items():
            self.known[eng][sid] = val
        return list(need.values())

    def op(self, eng, fn, reads=(), writes=()):
        toks = self._deps(eng, reads, writes)
        own = self.esem[eng]
        if eng == "pe":
            waits = self._waits(eng, toks, skip_sem=own)
        else:
            waits = self._waits(eng, toks)
        self.ecount[eng] += 1
        tok = (own, self.ecount[eng])
        self.streams[eng].append((waits, fn, (own, 1)))
        self._record(tok, reads, writes)
        return tok

    def dma(self, q, out, in_, reads=(), writes=(), final=False, **kw):
        pool = self.dsem[q]
        i = self.dnext[q]
        self.dnext[q] = (i + 1) % len(pool)
        sem, val = pool[i]
        toks = self._deps(q, reads, writes)
        if val > 0:
            toks = toks + [(sem, val)]
        eng = {"sp": "sp", "act": "act", "pool": "pool"}[q]
        waits = self._waits(eng, toks)
        pool[i][1] = val + 16
        tok = (sem, val + 16)

        def fn(e, out=out, in_=in_, kw=kw):
            return e.dma_start(out, in_, **kw)
        self.streams[eng].append((waits, fn, (sem, 16)))
        self._record(tok, reads, writes)
        if final:
            self.final_tokens.append(tok)
        return tok


    @staticmethod
    def _k(*xs):
        out = []
        for x in xs:
            if isinstance(x, tuple):
                out.extend(x[1])
        return out

    @staticmethod
    def _a(x):
        return x[0] if isinstance(x, tuple) else x

    def mm(self, out, lhsT, rhs, start=True, stop=True):
        o, l, r = out[0], lhsT[0], rhs[0]
        return self.op("pe", lambda e: e.matmul(o, l, r, start=start, stop=stop),
                       reads=self._k(lhsT, rhs), writes=self._k(out))

    def transpose(self, out, in_, ident):
        o, i, d = out[0], in_[0], ident[0]
        return self.op("pe", lambda e: e.transpose(o, i, d), reads=self._k(in_, ident), writes=self._k(out))

    def act(self, out, in_, func, bias=0.0, scale=1.0, eng="act"):
        o, i, b, sc = out[0], in_[0], self._a(bias), self._a(scale)
        return self.op(eng, lambda e: e.activation(o, i, func, bias=b, scale=sc),
                       reads=self._k(in_, bias, scale), writes=self._k(out))

    def copy(self, eng, out, in_):
        o, i = out[0], in_[0]
        if eng == "act":
            return self.op(eng, lambda e: e.copy(o, i), reads=self._k(in_), writes=self._k(out))
        return self.op(eng, lambda e: e.tensor_copy(o, i), reads=self._k(in_), writes=self._k(out))

    def tt(self, eng, out, in0, in1, op):
        o, a, b = out[0], in0[0], in1[0]
        return self.op(eng, lambda e: e.tensor_tensor(o, a, b, op), reads=self._k(in0, in1), writes=self._k(out))

    def ts(self, eng, out, in0, s1, s2, op0, op1=None):
        o, a, x1, x2 = out[0], in0[0], self._a(s1), self._a(s2)
        if op1 is None:
            return self.op(eng, lambda e: e.tensor_scalar(o, a, x1, None, op0), reads=self._k(in0, s1), writes=self._k(out))
        return self.op(eng, lambda e: e.tensor_scalar(o, a, x1, x2, op0, op1), reads=self._k(in0, s1, s2), writes=self._k(out))

    def stt(self, eng, out, in0, scalar, in1, op0, op1):
        o, a, sc, b = out[0], in0[0], self._a(scalar), in1[0]
        return self.op(eng, lambda e: e.scalar_tensor_tensor(o, a, sc, b, op0, op1),
                       reads=self._k(in0, scalar, in1), writes=self._k(out))

    def memset(self, eng, out, val):
        o = out[0]
        return self.op(eng, lambda e: e.memset(o, val), writes=self._k(out))

    def barrier(self):
        toks = [(self.esem[e], self.ecount[e]) for e in self.COMPUTE if self.ecount[e] > 0]
        for q in self.dsem:
            for sem, val in self.dsem[q]:
                if val > 0:
                    toks.append((sem, val))
        for e in self.ALL:
            waits = self._waits(e, toks, skip_sem=self.esem.get(e))
            if waits:
                self.streams[e].append((waits, None, None))
        self.res = {}

    def new_epoch(self):
        self.epoch = getattr(self, "epoch", 0) + 1
        for e in self.COMPUTE:
            self.esem[e] = self.stack.enter_context(self.nc.semaphore("prog_%s_%d" % (e, self.epoch)))
            self.ecount[e] = 0

    def emit(self):
        nc = self.nc
        fin = self._waits("sp", self.final_tokens)
        streams = self.streams

        def run(ename, e):
            for (waits, fn, inc) in streams[ename]:
                for (sem, val) in waits:
                    e.wait_ge(sem, val)
                if fn is None:
                    continue
                ins = fn(e)
                if inc is not None:
                    ins.then_inc(inc[0], inc[1])
            if ename == "sp":
                for (sem, val) in fin:
                    e.wait_ge(sem, val)

        with nc.Block() as block:
            @block.sync
            def _(e):
                run("sp", e)

            @block.tensor
            def _(e):
                run("pe", e)

            @block.scalar
            def _(e):
                run("act", e)

            @block.vector
            def _(e):
                run("dve", e)

            @block.gpsimd
            def _(e):
                run("pool", e)


D = 1024
KC = 8
T = 2048
NSQ = 16
SQ = 4
NS = NSQ * SQ
NTOK = T + NS
DEPTH = 4
INW = 4108
NCORE = 8
GROUPS = [(0, 512), (512, 512), (1024, 512), (1536, 512), (2048, 64)]
WSLOT = 384
EPS = 1e-6

V_BADA = 0
V_NORMW = 96
V_FNORM = 128
V_BADAF = 136
V_CAW = 152
V_CBW = 176
V_GNW = 320
V_ALOG = 324
V_DTB = 328
NV = 332

C_IDENT = 0
C_SHIFT = 128
C_ONES = 192
C_MPREV = 320
C_MCUR = 448
C_BLK = 576
C_EH = 704
C_OFFD = 1088
C_MSEQ = 1152
C_SEQM = 1216
C_SCAN64 = 1232
C_SCAN4 = 1360
C_MDIAG = 1424
C_MS128 = 1488
NCST = 1492
NEGM = -240000.0


def R(ap, *keys):
    return (ap, keys)


class Ctx:
    pass


def build_program(phases=("A",), depth=DEPTH):
    nc = bass.Bass("TRN2", target_bir_lowering=False)
    st = contextlib.ExitStack()
    with st:
        P = Prog(nc, st)
        c = Ctx()
        c.nc, c.P, c.st = nc, P, st

        def din(name, shape):
            return nc.dram_tensor(name, list(shape), F32, kind="ExternalInput").ap()

        def dout(name, shape):
            return nc.dram_tensor(name, list(shape), F32, kind="ExternalOutput").ap()

        c.xp = din("xp", (T, D)); c.xs = din("xs", (NS, D))
        c.cT = din("cT", (128, KC, 17))
        c.sca = din("sca", (DEPTH, NSQ * 2, 256)); c.scb = din("scb", (DEPTH, NSQ * 3, 1152))
        c.sgd = din("sgd", (DEPTH, NSQ, 6, 64, 64))
        c.ck128 = din("ck128", (DEPTH, NSQ, 128, 256)); c.ck512 = din("ck512", (DEPTH, NSQ, 512, 256))
        c.ck2048 = din("ck2048", (DEPTH, NSQ, 2048, 256))
        c.w_in = din("w_in", (DEPTH, D, INW)); c.w_out = din("w_out", (DEPTH, D, D))
        c.w_ada = din("w_ada", (DEPTH, D, 3 * D)); c.w_adaf = din("w_adaf", (D, 2 * D))
        c.vecT = din("vecT", (128, NV)); c.cst = din("cst", (128, NCST))
        c.rope = din("rope", (128, 17, 16))
        c.yp = dout("yp", (T, D)); c.ys = dout("ys", (NS, D))
        c.ca_p = dout("ca_p", (DEPTH, 2, 256)); c.ca_s = dout("ca_s", (DEPTH, NSQ * 2, 256))
        c.cb_p = dout("cb_p", (DEPTH, 3, 1152)); c.cb_s = dout("cb_s", (DEPTH, NSQ * 3, 1152))
        c.gd_p = dout("gd_p", (DEPTH, 6, 64, 64)); c.gd_s = dout("gd_s", (DEPTH, NSQ, 6, 64, 64))
        c.kv_p = [dout("kv128_p", (DEPTH, 128, 256)), dout("kv512_p", (DEPTH, 512, 256)), dout("kv2048_p", (DEPTH, 2048, 256))]
        c.kv_s = [dout("kv128_s", (DEPTH, NS, 256)), dout("kv512_s", (DEPTH, NS, 256)), dout("kv2048_s", (DEPTH, NS, 256))]

        def sb(name, shape, dt):
            return st.enter_context(nc.sbuf_tensor(name, list(shape), dt))

        c.xT = sb("xT", (128, KC, NTOK), F32)
        c.hnT = sb("hnT", (128, KC, NTOK), BF16)
        c.ring = [sb("ring%d" % i, (128, KC, WSLOT), BF16) for i in range(4)]
        c.wab = sb("wab", (128, KC, 12), BF16)
        c.wo = sb("wo", (128, 6, D), BF16)
        c.vec = sb("vec", (128, NV), F32)
        c.cstf = sb("cstf", (128, NCST), F32)
        c.cstb = sb("cstb", (128, 836), BF16)
        c.ropet = sb("ropet", (128, 17, 16), F32)
        c.cTf = sb("cTf", (128, KC, 17), F32)
        c.cTb = sb("cTb", (128, KC, 17), BF16)
        c.ada = sb("ada", (128, 24, 17), F32)
        c.m1 = sb("m1", (128, KC, 17), F32)
        c.g1 = sb("g1", (128, KC, 17), F32)
        ARENA_F32 = 14592
        c.arena = sb("arena", (128, ARENA_F32), F32)
        c.ps = [st.enter_context(nc.psum_tensor("ps%d" % i, [128, 512], F32)) for i in range(8)]
        c.ps_i = 0
        c.ring_i = 0
        c.phases = phases

        c.ps_n = 8

        def next_ps():
            i = c.ps_i % c.ps_n
            c.ps_i = (i + 1) % c.ps_n
            return i
        c.next_ps = next_ps

        class Arena:
            def __init__(self):
                self.off = 0

            def f32(self, n):
                o = self.off
                self.off += n
                c.arena_max = max(getattr(c, "arena_max", 0), self.off)
                assert self.off <= ARENA_F32, ("arena overflow", self.off)
                return c.arena[:, o:o + n]

            def bf16(self, n):
                assert n % 2 == 0
                return self.f32(n // 2).bitcast(BF16)

            def f32r(self, n):
                return self.f32(n).bitcast(F32R)
        c.Arena = Arena

        def wload(src, ncols):
            i = c.ring_i
            c.ring_i = (i + 1) % 4
            key = ("ring", i)
            P.dma("pool", c.ring[i][:, :, 0:ncols], src.rearrange("(k p) c -> p k c", p=128), writes=[key])
            return c.ring[i], key
        c.wload = wload

        P.dma("sp", c.vec[:], c.vecT, writes=["vec"])
        P.dma("sp", c.cstf[:], c.cst, writes=["cstf"])
        P.dma("sp", c.ropet[:], c.rope, writes=["rope"])
        P.dma("sp", c.cTf[:], c.cT, writes=["cTf"])
        P.copy("dve", R(c.cstb[:, 0:704], "cstb"), R(c.cstf[:, 0:704], "cstf"))
        P.copy("dve", R(c.cstb[:, 704:768], "cstb"), R(c.cstf[:, C_MSEQ:C_MSEQ + 64], "cstf"))
        P.copy("dve", R(c.cstb[:, 768:836], "cstb"), R(c.cstf[:, C_MDIAG:C_MDIAG + 68], "cstf"))
        P.copy("dve", R(c.cTb[:], "cTb"), R(c.cTf[:], "cTf"))
        for i in range(8):
            P.memset("dve", R(c.ps[i][:], ("ps", i)), 0.0)
        c.zero = sb("zero", (128, 384), F32)
        P.memset("dve", R(c.zero[:], "zero"), 0.0)
        c.ident = R(c.cstf[:, C_IDENT:C_IDENT + 128], "cstf")
        c.identb = R(c.cstb[:, C_IDENT:C_IDENT + 128], "cstb")
        c.onesb = R(c.cstb[:, C_ONES:C_ONES + 128], "cstb")

        load_x(c)
        for l in range(depth):
            layer(c, l)
        final(c)
        P.emit()
    return nc


def xkey(g):
    return ("x", g)


def hkey(g):
    return ("hn", g)


def load_x(c):
    P = c.P
    A = c.Arena()
    stg = [A.f32(D) for _ in range(2)]
    for tt in range(17):
        rows = 128 if tt < 16 else NS
        g = min(tt // 4, 4)
        s = stg[tt % 2]
        skey = ("xstg", tt % 2)
        src = c.xp[tt * 128:(tt + 1) * 128, :] if tt < 16 else c.xs
        P.dma("sp", s[0:rows, :], src, writes=[skey])
        for half in range(2):
            pi = c.next_ps()
            for kk in range(4):
                k = half * 4 + kk
                P.transpose(R(c.ps[pi][:, kk * 128:kk * 128 + rows], ("ps", pi)),
                            R(s[0:rows, k * 128:(k + 1) * 128], skey), R(c.cstf[0:rows, C_IDENT:C_IDENT + rows], "cstf"))
            col0 = tt * 128
            dst = c.xT[:, half * 4:half * 4 + 4, col0:col0 + rows]
            srcp = c.ps[pi][:].rearrange("p (k t) -> p k t", k=4)[:, :, 0:rows]
            P.copy("act" if half == 0 else "dve", R(dst, xkey(g)), R(srcp, ("ps", pi)))
    P.barrier()


def ada_vectors(c, l):
    P = c.P
    final_ = (l == DEPTH)
    ncol = 2 * D if final_ else 3 * D
    nj = ncol // 128
    pi = c.next_ps()
    pst = c.ps[pi][:, 0:24 * 17].rearrange("p (j s) -> p j s", s=17)
    for t0 in range(0, ncol, WSLOT):
        nc_ = min(WSLOT, ncol - t0)
        src = (c.w_adaf if final_ else c.w_ada[l])[:, t0:t0 + nc_]
        wt, wk = c.wload(src, nc_)
        for jj in range(nc_ // 128):
            j = t0 // 128 + jj
            for k in range(KC):
                P.mm(R(pst[:, j, :], ("ps", pi)), R(wt[:, k, jj * 128:(jj + 1) * 128], wk), R(c.cTb[:, k, :], "cTb"),
                     start=(k == 0), stop=(k == KC - 1))
    vb = V_BADAF if final_ else V_BADA + l * 24
    bias = c.vec[:, vb:vb + nj].unsqueeze(2).to_broadcast([128, nj, 17])
    P.tt("dve", R(c.ada[:, 0:nj, :], "ada"), R(pst[:, 0:nj, :], ("ps", pi)), R(bias, "vec"), ALU.add)
    nw0 = V_FNORM if final_ else V_NORMW + l * 8
    nw = c.vec[:, nw0:nw0 + KC].unsqueeze(2).to_broadcast([128, KC, 17])
    P.stt("dve", R(c.m1[:], "m1"), R(c.ada[:, 8:16, :], "ada"), 1.0, R(nw, "vec"), ALU.add, ALU.mult)
    if not final_:
        P.ts("dve", R(c.g1[:], "g1"), R(c.ada[:, 16:24, :], "ada"), 1.0, None, ALU.add)


def rsqrt(c, out, in_, scale, bias):
    P = c.P
    P.act(out, in_, AF.Ln, bias=bias, scale=scale)
    P.act(out, out, AF.Exp, scale=-0.5)


def sigmoid_(c, out, in_):
    P = c.P
    P.act(out, in_, AF.Exp, scale=-1.0)
    P.act(out, out, AF.Ln, bias=1.0)
    P.act(out, out, AF.Exp, scale=-1.0)


def silu_(c, out, in_):
    sigmoid_(c, out, in_)
    c.P.tt("dve", out, in_, out, ALU.mult)


def rms_stats(c, A, g, tag):
    P = c.P
    col0, ncol = GROUPS[g]
    sq = A["sq"]
    P.act(R(sq[:, :, 0:ncol], "sq"), R(c.xT[:, :, col0:col0 + ncol], xkey(g)), AF.Square)
    pi = c.next_ps()
    for k in range(KC):
        P.mm(R(c.ps[pi][:, 0:ncol], ("ps", pi)), c.onesb, R(sq[:, k, 0:ncol], "sq"), start=(k == 0), stop=(k == KC - 1))
    rstd = A["rstd"]
    rsqrt(c, R(rstd[:, 0:ncol], "rstd"), R(c.ps[pi][:, 0:ncol], ("ps", pi)), 1.0 / D, EPS)
    return rstd


def norm_phase(c, l, out_fn):
    P = c.P
    A_ = c.Arena()
    A = {"sq": A_.bf16(KC * 512).rearrange("p (k t) -> p k t", k=KC), "rstd": A_.f32(512),
         "tmp": [A_.f32(512) for _ in range(2)], "tmps": A_.f32(KC * NS).rearrange("p (k t) -> p k t", k=KC)}
    for g in range(5):
        col0, ncol = GROUPS[g]
        rstd = rms_stats(c, A, g, "n")
        if g < 4:
            for k in range(KC):
                tmp = A["tmp"][k % 2]
                tk = ("ntmp", k % 2)
                P.tt("dve", R(tmp[:, 0:ncol], tk), R(c.xT[:, k, col0:col0 + ncol], xkey(g)), R(rstd[:, 0:ncol], "rstd"), ALU.mult)
                P.act(out_fn(g, k, ncol), R(tmp[:, 0:ncol], tk), AF.Identity,
                      bias=R(c.ada[:, k, 0:1], "ada"), scale=R(c.m1[:, k, 0:1], "m1"))
        else:
            ts_ = A["tmps"]
            P.tt("dve", R(ts_[:], "ntmps"), R(c.xT[:, :, col0:col0 + ncol], xkey(g)),
                 R(rstd[:, 0:ncol].unsqueeze(1).to_broadcast([128, KC, NS]), "rstd"), ALU.mult)
            v4 = ts_[:].rearrange("p k (s i) -> p k s i", i=SQ)
            m1b = c.m1[:, :, 1:17].unsqueeze(3).to_broadcast([128, KC, NSQ, SQ])
            shb = c.ada[:, 0:8, 1:17].unsqueeze(3).to_broadcast([128, KC, NSQ, SQ])
            P.tt("dve", R(v4, "ntmps"), R(v4, "ntmps"), R(m1b, "m1"), ALU.mult)
            for k in range(KC):
                o = out_fn(g, k, ncol)
                P.tt("dve", (o[0].rearrange("p (s i) -> p s i", i=SQ), o[1]), R(v4[:, k], "ntmps"), R(shb[:, k], "ada"), ALU.add)
    P.barrier()


def proj(c, pi, wt, wk, wc0, m, g, ncol_override=None, cols=None):
    P = c.P
    col0, ncol = GROUPS[g] if cols is None else cols
    for k in range(KC):
        P.mm(R(c.ps[pi][0:m, 0:ncol], ("ps", pi)), R(wt[:, k, wc0:wc0 + m], wk), R(c.hnT[:, k, col0:col0 + ncol], hkey(g)),
             start=(k == 0), stop=(k == KC - 1))


def outproj(c, l, g, mix_fn, nchunk, kpart, cols=None):
    P = c.P
    col0, ncol = GROUPS[g] if cols is None else cols
    for dc in range(KC):
        pi = c.next_ps()
        for mc in range(nchunk):
            P.mm(R(c.ps[pi][:, 0:ncol], ("ps", pi)), R(c.wo[0:kpart, mc, dc * 128:(dc + 1) * 128], "wo"), mix_fn(mc),
                 start=(mc == 0), stop=(mc == nchunk - 1))
        xs = c.xT[:, dc, col0:col0 + ncol]
        if g < 4:
            P.stt("dve", R(xs, xkey(g)), R(c.ps[pi][:, 0:ncol], ("ps", pi)), R(c.g1[:, dc, 0:1], "g1"), R(xs, xkey(g)), ALU.mult, ALU.add)
        else:
            x3 = xs.rearrange("p (s i) -> p s i", i=SQ)
            p3 = c.ps[pi][:, 0:ncol].rearrange("p (s i) -> p s i", i=SQ)
            g1b = c.g1[:, dc, 1:17].unsqueeze(2).to_broadcast([128, NSQ, SQ])
            tmp = c.optmp
            P.tt("dve", R(tmp, "optmp"), R(p3, ("ps", pi)), R(g1b, "g1"), ALU.mult)
            P.tt("dve", R(x3, xkey(g)), R(x3, xkey(g)), R(tmp, "optmp"), ALU.add)


def load_wo(c, l, r0, nchunk, kpart):
    P = c.P
    src = c.w_out[l][r0:r0 + nchunk * kpart, :].rearrange("(j p) d -> p j d", p=kpart)
    P.dma("pool", c.wo[0:kpart, 0:nchunk, :], src, writes=["wo"])


def layer(c, l):
    P = c.P
    ada_vectors(c, l)
    norm_phase(c, l, lambda g, k, ncol: R(c.hnT[:, k, GROUPS[g][0]:GROUPS[g][0] + ncol], hkey(g)))
    if "A" in c.phases:
        branch_a(c, l)
    if "B" in c.phases:
        branch_b(c, l)
    if "C" in c.phases:
        branch_c(c, l)


def branch_a(c, l):
    P = c.P
    A_ = c.Arena()
    CI = A_.f32(2 * (2 + T)).rearrange("p (j t) -> p j t", j=2)
    CIs = A_.f32(2 * NSQ * 6).rearrange("p (j s t) -> p j s t", j=2, s=NSQ)
    tmpx = A_.f32(512); sz = A_.f32(512); acc = A_.f32(512); tz = A_.f32(512)
    mixA = A_.bf16(2 * 512).rearrange("p (j t) -> p j t", j=2)
    c.optmp = A_.f32(NS).rearrange("p (s i) -> p s i", i=SQ)
    sin_ = A_.f32(256)
    gat = A_.f32(2 * 34).rearrange("p (j t) -> p j t", j=2)
    outa = A_.f32(256)
    load_wo(c, l, 0, 2, 128)
    tiles = []
    for t0 in (0, 384, 768):
        ncols = min(384, 1024 - t0)
        tiles.append(c.wload(c.w_in[l][:, t0:t0 + ncols], ncols))

    def wsel(col):
        ti = col // 384
        return tiles[ti][0], tiles[ti][1], col - ti * 384

    P.memset("dve", R(CI[:, :, 0:2], "CIh"), 0.0)
    P.dma("sp", sin_[0:NSQ * 2, :], c.sca[l], writes=["sin"])
    for j in range(2):
        pi = c.next_ps()
        P.transpose(R(c.ps[pi][:, 0:32], ("ps", pi)), R(sin_[0:32, j * 128:(j + 1) * 128], "sin"), R(c.cstf[0:32, 0:32], "cstf"))
        P.copy("act", R(CIs[:, j, :, 0:2], ("CIs", j)), R(c.ps[pi][:, 0:32].rearrange("p (s r) -> p s r", r=2), ("ps", pi)))
    for g in range(5):
        col0, ncol = GROUPS[g]
        for j in range(2):
            p0 = c.next_ps(); wt, wk, wc = wsel(j * 128); proj(c, p0, wt, wk, wc, 128, g)
            p1 = c.next_ps(); wt, wk, wc = wsel(256 + j * 128); proj(c, p1, wt, wk, wc, 128, g)
            P.copy("act", R(tmpx[:, 0:ncol], "tmpx"), R(c.ps[p0][:, 0:ncol], ("ps", p0)))
            vb = V_CAW + (l * 3) * 2 + j
            w0 = R(c.vec[:, vb:vb + 1], "vec"); w1 = R(c.vec[:, vb + 2:vb + 3], "vec"); w2 = R(c.vec[:, vb + 4:vb + 5], "vec")
            if g < 4:
                ck = ("CI", j, g)
                P.tt("dve", R(CI[:, j, 2 + col0:2 + col0 + ncol], ck), R(tmpx[:, 0:ncol], "tmpx"), R(c.ps[p1][:, 0:ncol], ("ps", p1)), ALU.mult)
                rd = [ck, ("CI", j, g - 1), "CIh"]
                P.ts("dve", R(acc[:, 0:ncol], "acc"), (CI[:, j, col0 + 2:col0 + 2 + ncol], rd), w2, None, ALU.mult)
                P.stt("dve", R(acc[:, 0:ncol], "acc"), (CI[:, j, col0 + 1:col0 + 1 + ncol], rd), w1, R(acc[:, 0:ncol], "acc"), ALU.mult, ALU.add)
                P.stt("dve", R(acc[:, 0:ncol], "acc"), (CI[:, j, col0:col0 + ncol], rd), w0, R(acc[:, 0:ncol], "acc"), ALU.mult, ALU.add)
                accv = acc[:, 0:ncol]
            else:
                ck = ("CIs", j)
                P.tt("dve", R(CIs[:, j, :, 2:6], ck), R(tmpx[:, 0:ncol].rearrange("p (s i) -> p s i", i=SQ), "tmpx"),
                     R(c.ps[p1][:, 0:ncol].rearrange("p (s i) -> p s i", i=SQ), ("ps", p1)), ALU.mult)
                a3 = acc[:, 0:ncol].rearrange("p (s i) -> p s i", i=SQ)
                P.ts("dve", R(a3, "acc"), R(CIs[:, j, :, 2:6], ck), w2, None, ALU.mult)
                P.stt("dve", R(a3, "acc"), R(CIs[:, j, :, 1:5], ck), w1, R(a3, "acc"), ALU.mult, ALU.add)
                P.stt("dve", R(a3, "acc"), R(CIs[:, j, :, 0:4], ck), w0, R(a3, "acc"), ALU.mult, ALU.add)
                accv = acc[:, 0:ncol]
            p2 = c.next_ps(); wt, wk, wc = wsel(512 + j * 128); proj(c, p2, wt, wk, wc, 128, g)
            p3 = c.next_ps(); wt, wk, wc = wsel(768 + j * 128); proj(c, p3, wt, wk, wc, 128, g)
            silu_(c, R(sz[:, 0:ncol], "sz"), R(c.ps[p3][:, 0:ncol], ("ps", p3)))
            P.tt("dve", R(tz[:, 0:ncol], "tz"), R(accv, "acc"), R(sz[:, 0:ncol], "sz"), ALU.mult)
            P.tt("dve", R(mixA[:, j, 0:ncol], ("mixA", j)), R(tz[:, 0:ncol], "tz"), R(c.ps[p2][:, 0:ncol], ("ps", p2)), ALU.mult)
        outproj(c, l, g, lambda mc: R(mixA[:, mc, 0:GROUPS[g][1]], ("mixA", mc)), 2, 128)
    for j in range(2):
        P.copy("act", R(gat[:, j, 0:2], ("gat", j)), R(CI[:, j, T:T + 2], ("CI", j, 3)))
        P.copy("act", R(gat[:, j, 2:34].rearrange("p (s r) -> p s r", r=2), ("gat", j)), R(CIs[:, j, :, 4:6], ("CIs", j)))
        pi = c.next_ps()
        P.transpose(R(c.ps[pi][0:34, 0:128], ("ps", pi)), R(gat[:, j, :], ("gat", j)), c.ident)
        P.copy("dve", R(outa[0:34, j * 128:(j + 1) * 128], "outa"), R(c.ps[pi][0:34, 0:128], ("ps", pi)))
    P.dma("sp", c.ca_p[l], outa[0:2, :], reads=["outa"], final=True)
    P.dma("sp", c.ca_s[l], outa[2:34, :], reads=["outa"], final=True)
    P.barrier()


def final(c):
    P = c.P
    ada_vectors(c, DEPTH)
    A_ = c.Arena()
    yT = A_.f32(KC * 512).rearrange("p (k t) -> p k t", k=KC)
    A = {"sq": A_.bf16(KC * 512).rearrange("p (k t) -> p k t", k=KC), "rstd": A_.f32(512),
         "tmp": [A_.f32(512) for _ in range(2)], "tmps": A_.f32(KC * NS).rearrange("p (k t) -> p k t", k=KC)}
    ystg = [A_.f32(D) for _ in range(2)]
    si = 0
    for g in range(5):
        col0, ncol = GROUPS[g]
        rstd = rms_stats(c, A, g, "f")
        yk = ("yT",)
        if g < 4:
            for k in range(KC):
                tmp = A["tmp"][k % 2]; tk = ("ntmp", k % 2)
                P.tt("dve", R(tmp[:, 0:ncol], tk), R(c.xT[:, k, col0:col0 + ncol], xkey(g)), R(rstd[:, 0:ncol], "rstd"), ALU.mult)
                P.act(R(yT[:, k, 0:ncol], "yT"), R(tmp[:, 0:ncol], tk), AF.Identity,
                      bias=R(c.ada[:, k, 0:1], "ada"), scale=R(c.m1[:, k, 0:1], "m1"))
        else:
            ts_ = A["tmps"]
            P.tt("dve", R(ts_[:], "ntmps"), R(c.xT[:, :, col0:col0 + ncol], xkey(g)),
                 R(rstd[:, 0:ncol].unsqueeze(1).to_broadcast([128, KC, NS]), "rstd"), ALU.mult)
            v4 = ts_[:].rearrange("p k (s i) -> p k s i", i=SQ)
            m1b = c.m1[:, :, 1:17].unsqueeze(3).to_broadcast([128, KC, NSQ, SQ])
            shb = c.ada[:, 0:8, 1:17].unsqueeze(3).to_broadcast([128, KC, NSQ, SQ])
            P.tt("dve", R(v4, "ntmps"), R(v4, "ntmps"), R(m1b, "m1"), ALU.mult)
            P.tt("dve", R(yT[:, :, 0:NS].rearrange("p k (s i) -> p k s i", i=SQ), "yT"), R(v4, "ntmps"), R(shb, "ada"), ALU.add)
        for tt in range((ncol + 127) // 128):
            rows = min(128, ncol - tt * 128)
            stg = ystg[si % 2]; sk = ("ystg", si % 2); si += 1
            for half in range(2):
                pi = c.next_ps()
                for kk in range(4):
                    k = half * 4 + kk
                    P.transpose(R(c.ps[pi][0:rows, kk * 128:(kk + 1) * 128], ("ps", pi)),
                                R(yT[:, k, tt * 128:tt * 128 + rows], "yT"), c.ident)
                P.copy("act" if half == 0 else "dve", R(stg[0:rows, half * 512:(half + 1) * 512], sk), R(c.ps[pi][0:rows, :], ("ps", pi)))
            if g < 4:
                dst = c.yp[col0 + tt * 128:col0 + tt * 128 + rows, :]
            else:
                dst = c.ys
            P.dma("sp", dst, stg[0:rows, :], reads=[sk], final=True)


_PHASES = ("A", "B", "C")
_NC_CACHE = {}


def _host_consts():
    cst = np.zeros((128, NCST), np.float32)
    cst[:, C_IDENT:C_IDENT + 128] = np.eye(128, dtype=np.float32)
    for m in range(64):
        cst[64 + m, C_SHIFT + m] = 1.0
    cst[:, C_ONES:C_ONES + 128] = 1.0
    k = np.arange(128)[:, None]
    q = np.arange(128)[None, :]
    cst[:, C_MPREV:C_MPREV + 128] = np.where(k >= q, 0.0, NEGM)
    cst[:, C_MCUR:C_MCUR + 128] = np.where(k <= q, 0.0, NEGM)
    cst[0:64, C_BLK:C_BLK + 64] = 1.0
    cst[64:128, C_BLK + 64:C_BLK + 128] = 1.0
    for h in range(6):
        cst[h, C_EH + h * 64:C_EH + (h + 1) * 64] = 1.0
    cst[0:64, C_OFFD:C_OFFD + 64] = 1.0 - np.eye(64, dtype=np.float32)
    j64 = np.arange(64)[:, None]
    i64 = np.arange(64)[None, :]
    cst[0:64, C_MSEQ:C_MSEQ + 64] = np.where((j64 // 4 == i64 // 4) & (j64 <= i64), 0.0, NEGM)
    cst[0:64, C_SEQM:C_SEQM + 16] = (j64 // 4 == np.arange(16)[None, :]).astype(np.float32)
    cst[:, C_SCAN64:C_SCAN64 + 128] = (np.arange(128) % 64 != 0).astype(np.float32)[None, :]
    cst[:, C_SCAN4:C_SCAN4 + 64] = (np.arange(64) % 4 != 0).astype(np.float32)[None, :]
    cst[0:64, C_MDIAG:C_MDIAG + 64] = np.where(j64 == i64, 0.0, NEGM)
    cst[:, C_MS128:C_MS128 + 4] = np.where(np.arange(128)[:, None] >= np.arange(4)[None, :], 0.0, NEGM)
    half = 8
    inv_freq = (500000.0 ** (-np.arange(half, dtype=np.float32) * np.float32(2.0 / 16))).astype(np.float32)
    rope = np.zeros((128, 17, 16), np.float32)
    for tt in range(17):
        if tt < 16:
            pos = (tt * 128 + np.arange(128)).astype(np.float32)
        else:
            pos = (T + (np.arange(128) % SQ)).astype(np.float32)
        ang = pos[:, None] * inv_freq[None, :]
        rope[:, tt, 0:8] = np.cos(ang)
        rope[:, tt, 8:16] = np.sin(ang)
    return cst, rope


def _fm(v):
    v = np.asarray(v, np.float32)
    return np.ascontiguousarray(v.reshape(-1, 128).T)


def _host_vecT(b_ada, norm_w, final_norm_w, b_ada_final, conv_a_w, conv_b_w, gdn_norm_w, a_log, dt_bias):
    vt = np.zeros((128, NV), np.float32)
    for l in range(DEPTH):
        vt[:, V_BADA + l * 24:V_BADA + (l + 1) * 24] = _fm(b_ada[l])
        vt[:, V_NORMW + l * 8:V_NORMW + (l + 1) * 8] = _fm(norm_w[l])
        for tap in range(3):
            vt[:, V_CAW + (l * 3 + tap) * 2:V_CAW + (l * 3 + tap) * 2 + 2] = _fm(conv_a_w[l, tap])
        for tap in range(4):
            vt[:, V_CBW + (l * 4 + tap) * 9:V_CBW + (l * 4 + tap) * 9 + 9] = _fm(conv_b_w[l, tap])
        vt[:, V_GNW + l] = np.tile(np.asarray(gdn_norm_w[l], np.float32), 2)
        vt[0:6, V_ALOG + l] = a_log[l]
        vt[0:6, V_DTB + l] = dt_bias[l]
    vt[:, V_FNORM:V_FNORM + 8] = _fm(final_norm_w)
    vt[:, V_BADAF:V_BADAF + 16] = _fm(b_ada_final)
    return vt


def kernel(x_prompt, x_sample, state_conv_a, state_conv_b, state_gdn, cache_kv_w128, cache_kv_w512,
           cache_kv_w2048, c_prompt, c_sample, w_in, w_out, w_ada, b_ada, norm_w, conv_a_w, conv_b_w,
           a_log, dt_bias, gdn_norm_w, final_norm_w, w_ada_final, b_ada_final, _phases=None, _depth=DEPTH):
    phases = tuple(_phases) if _phases is not None else _PHASES
    f = lambda a: np.ascontiguousarray(np.asarray(a, dtype=np.float32))
    key = (phases, _depth)
    if key not in _NC_CACHE:
        _NC_CACHE[key] = build_program(phases, _depth)
    nc = _NC_CACHE[key]
    cst, rope = _host_consts()
    vt = _host_vecT(f(b_ada), f(norm_w), f(final_norm_w), f(b_ada_final), f(conv_a_w), f(conv_b_w), f(gdn_norm_w),
                    f(a_log), f(dt_bias))
    w_in, w_out, w_ada, w_adaf = f(w_in), f(w_out), f(w_ada), f(w_ada_final)
    x_prompt, x_sample = f(x_prompt), f(x_sample)
    c_prompt, c_sample = f(c_prompt), f(c_sample)
    sca, scb, sgd = f(state_conv_a), f(state_conv_b), f(state_gdn)
    k128, k512, k2048 = f(cache_kv_w128), f(cache_kv_w512), f(cache_kv_w2048)
    in_maps = []
    for i in range(NCORE):
        ss = slice(i * NSQ, (i + 1) * NSQ)
        call = np.concatenate([c_prompt[i:i + 1], c_sample[ss]], axis=0)
        cT = np.ascontiguousarray(call.reshape(17, KC, 128).transpose(2, 1, 0))
        in_maps.append({
            "xp": x_prompt[i], "xs": np.ascontiguousarray(x_sample[ss].reshape(NS, D)), "cT": cT,
            "sca": np.ascontiguousarray(sca[:, ss].reshape(DEPTH, NSQ * 2, 256)),
            "scb": np.ascontiguousarray(scb[:, ss].reshape(DEPTH, NSQ * 3, 1152)),
            "sgd": np.ascontiguousarray(sgd[:, ss]),
            "ck128": np.ascontiguousarray(k128[:, ss].reshape(DEPTH, NSQ, 128, 256)),
            "ck512": np.ascontiguousarray(k512[:, ss].reshape(DEPTH, NSQ, 512, 256)),
            "ck2048": np.ascontiguousarray(k2048[:, ss].reshape(DEPTH, NSQ, 2048, 256)),
            "w_in": w_in, "w_out": w_out, "w_ada": w_ada, "w_adaf": w_adaf,
            "vecT": vt, "cst": cst, "rope": rope,
        })
    res = run_bass_kernel_spmd(nc, in_maps, core_ids=list(range(NCORE)))
    rs = res.results
    cat = lambda name: np.stack([r[name] for r in rs], axis=0)
    y_p = cat("yp")
    y_s = cat("ys").reshape(NCORE * NSQ, SQ, D)
    ca_p = cat("ca_p").transpose(1, 0, 2, 3)
    ca_s = cat("ca_s").reshape(NCORE, DEPTH, NSQ, 2, 256).transpose(1, 0, 2, 3, 4).reshape(DEPTH, NCORE * NSQ, 2, 256)
    cb_p = cat("cb_p").transpose(1, 0, 2, 3)
    cb_s = cat("cb_s").reshape(NCORE, DEPTH, NSQ, 3, 1152).transpose(1, 0, 2, 3, 4).reshape(DEPTH, NCORE * NSQ, 3, 1152)
    gd_p = cat("gd_p").transpose(1, 0, 2, 3, 4)
    gd_s = cat("gd_s").transpose(1, 0, 2, 3, 4, 5).reshape(DEPTH, NCORE * NSQ, 6, 64, 64)
    outs = [y_p, y_s, ca_p, ca_s, cb_p, cb_s, gd_p, gd_s]
    for gi, win in enumerate((128, 512, 2048)):
        name = "kv%d" % win
        kp = cat(name + "_p").transpose(1, 0, 2, 3).reshape(DEPTH, NCORE, win, 2, 2, 64)
        ks = cat(name + "_s").reshape(NCORE, DEPTH, NSQ, SQ, 256).transpose(1, 0, 2, 3, 4).reshape(DEPTH, NCORE * NSQ, SQ, 2, 2, 64)
        outs += [kp, ks]
    return tuple(np.ascontiguousarray(o.astype(np.float32)) for o in outs)


def branch_b(c, l):
    P = c.P
    A_ = c.Arena()
    f32 = A_.f32

    def t3(n_mid, n_in, dt=F32):
        a = f32(n_mid * n_in)[0:64]
        return a.rearrange("p (a b) -> p a b", a=n_mid)

    pre = [f32(131) for _ in range(2)]
    pres = f32(NSQ * 7).rearrange("p (s t) -> p s t", t=7)
    halo = f32(27).rearrange("p (b t) -> p b t", t=3)
    acc = f32(128); act_ = f32(128); rinv = f32(128)
    sqb = A_.bf16(128)
    nrm = f32(128)
    hq_raw = f32(768)[0:64]; hk_raw = f32(768)[0:64]
    HQ = hq_raw.rearrange("p (a b) -> p a b", a=6)
    HK = hk_raw.rearrange("p (a b) -> p a b", a=6)
    HV = t3(6, 128, F32R)
    HZ = t3(6, 128)
    szt = f32(128)
    G = f32(128); BETA = f32(128); GC = f32(128); EG = f32(128); DL = f32(128); NGC = f32(128); tmpd = f32(128)
    EGL = f32(16); nA = f32(1)
    EGLB = t3(6, 16)
    OT = t3(6, 128)
    mixB = A_.bf16(6 * 128)[0:64].rearrange("p (h t) -> p h t", h=6)
    sqo = hk_raw[:, 0:384].bitcast(BF16).rearrange("p (h t) -> p h t", h=6)
    rso = hq_raw.rearrange("p (a b) -> p a b", a=6)
    S = t3(6, 64, F32R)
    SS = t3(NSQ, 64)
    KDblk = t3(NSQ, 64, F32R)
    U = {}
    for nm in ("decT", "LT0", "RT"):
        U[nm] = t3(3, 64)
    for nm in ("kbT", "kbgT", "qgT", "kdT", "vbT", "LT", "aT", "L", "P0", "P1", "PT0", "PT1", "X0", "X1", "VB", "KD", "Rr", "VN"):
        U[nm] = t3(3, 64, F32R)
    stg = f32(384)
    gatB = f32(9 * 51).rearrange("p (b t) -> p b t", b=9)
    c.optmp = f32(NS).rearrange("p (s i) -> p s i", i=SQ)

    def F(ap):
        return ap

    identr = R(c.cstf[0:64, 0:64], "cstf")
    shiftr = R(c.cstf[:, C_SHIFT:C_SHIFT + 64], "cstf")
    ident64 = R(c.cstf[0:64, 0:64], "cstf")
    identb64 = R(c.cstb[0:64, 0:64], "cstb")

    def EH(h):
        return R(c.cstf[0:6, C_EH + h * 64:C_EH + (h + 1) * 64], "cstf")

    load_wo(c, l, 256, 6, 64)
    wt = [c.wload(c.w_in[l][:, 1024 + i * 384:1024 + (i + 1) * 384], 384) for i in range(4)]
    P.dma("pool", c.wab[:], c.w_in[l][:, 2560:2572].rearrange("(k p) c -> p k c", p=128), writes=["wab"])

    P.copy("dve", R(S[:], "S"), R(c.zero[0:64, 0:384].rearrange("p (h t) -> p h t", h=6), "zero"))
    P.memset("dve", R(halo[:], "halo"), 0.0)
    P.act(R(nA[0:6, :], "nA"), R(c.vec[0:6, V_ALOG + l:V_ALOG + l + 1], "vec"), AF.Exp)
    P.ts("dve", R(nA[0:6, :], "nA"), R(nA[0:6, :], "nA"), -1.0, None, ALU.mult)
    for b3 in range(3):
        P.dma("sp", stg[0:NSQ * 3, :], c.scb[l][:, b3 * 384:(b3 + 1) * 384], writes=["stgB"])
        for bb in range(3):
            blk = b3 * 3 + bb
            pi = c.next_ps()
            P.transpose(R(c.ps[pi][:, 0:48], ("ps", pi)), R(stg[0:48, bb * 128:(bb + 1) * 128], "stgB"), R(c.cstf[0:48, 0:48], "cstf"))
            P.copy("act", R(gatB[:, blk, 0:48], ("gatB", blk)), R(c.ps[pi][:, 0:48], ("ps", pi)))

    groups = [(gb * 128, 128, gb // 4) for gb in range(16)] + [(T, NS, 4)]
    for gi, (col0, ncol, g5) in enumerate(groups):
        smp = (g5 == 4)
        nch = 1 if smp else 2
        cols = (col0, ncol)
        for blk in range(9):
            typ, sub = blk // 3, blk % 3
            wtile, wkey = wt[typ]
            pi = c.next_ps()
            proj(c, pi, wtile, wkey, sub * 128, 128, g5, cols=cols)
            vb = V_CBW + (l * 4) * 9 + blk
            wtap = [R(c.vec[:, vb + 9 * tap:vb + 9 * tap + 1], "vec") for tap in range(4)]
            if not smp:
                pr = pre[blk % 2]; pk = ("pre", blk % 2)
                P.copy("dve", R(pr[:, 0:3], pk), R(halo[:, blk, :], ("halo", blk)))
                P.copy("act", R(pr[:, 3:3 + ncol], pk), R(c.ps[pi][:, 0:ncol], ("ps", pi)))
                P.copy("dve", R(halo[:, blk, :], ("halo", blk)), R(pr[:, ncol:ncol + 3], pk))
                P.ts("dve", R(acc[:, 0:ncol], "accB"), R(pr[:, 3:3 + ncol], pk), wtap[3], None, ALU.mult)
                for tap in range(3):
                    P.stt("dve", R(acc[:, 0:ncol], "accB"), R(pr[:, tap:tap + ncol], pk), wtap[tap], R(acc[:, 0:ncol], "accB"), ALU.mult, ALU.add)
            else:
                pk = ("pres",)
                P.copy("dve", R(pres[:, :, 0:3], pk), R(gatB[:, blk, 0:48].rearrange("p (s r) -> p s r", r=3), ("gatB", blk)))
                P.copy("act", R(pres[:, :, 3:7], pk), R(c.ps[pi][:, 0:ncol].rearrange("p (s i) -> p s i", i=SQ), ("ps", pi)))
                a3 = acc[:, 0:ncol].rearrange("p (s i) -> p s i", i=SQ)
                P.ts("dve", R(a3, "accB"), R(pres[:, :, 3:7], pk), wtap[3], None, ALU.mult)
                for tap in range(3):
                    P.stt("dve", R(a3, "accB"), R(pres[:, :, tap:tap + 4], pk), wtap[tap], R(a3, "accB"), ALU.mult, ALU.add)
                P.copy("act", R(gatB[:, blk, 3:51].rearrange("p (s r) -> p s r", r=3), ("gatB", blk)), R(pres[:, :, 4:7], pk))
                P.copy("act", R(gatB[:, blk, 0:3], ("gatB", blk)), R(halo[:, blk, :], ("halo", blk)))
            silu_(c, R(act_[:, 0:ncol], "actB"), R(acc[:, 0:ncol], "accB"))
            if typ < 2:
                P.tt("dve", R(sqb[:, 0:ncol], "sqb"), R(act_[:, 0:ncol], "actB"), R(act_[:, 0:ncol], "actB"), ALU.mult)
                p2 = c.next_ps()
                P.mm(R(c.ps[p2][:, 0:ncol], ("ps", p2)), R(c.cstb[:, C_BLK:C_BLK + 128], "cstb"), R(sqb[:, 0:ncol], "sqb"))
                rsqrt(c, R(rinv[:, 0:ncol], "rinvB"), R(c.ps[p2][:, 0:ncol], ("ps", p2)), 1.0, 1e-6)
                P.stt("dve", R(nrm[:, 0:ncol], "nrm"), R(act_[:, 0:ncol], "actB"), 0.125 if typ == 0 else 1.0,
                      R(rinv[:, 0:ncol], "rinvB"), ALU.mult, ALU.mult)
            else:
                P.copy("dve", R(nrm[:, 0:ncol], "nrm"), R(act_[:, 0:ncol], "actB"))
            H = (HQ, HK, HV)[typ]
            hk = ("H", typ)
            P.copy("act", R(H[:, 2 * sub, 0:ncol], hk), R(nrm[0:64, 0:ncol], "nrm"))
            p3 = c.next_ps()
            P.mm(R(c.ps[p3][0:64, 0:ncol], ("ps", p3)), shiftr, R(nrm[:, 0:ncol], "nrm"))
            P.copy("act", R(H[:, 2 * sub + 1, 0:ncol], hk), R(c.ps[p3][0:64, 0:ncol], ("ps", p3)))
        for sub in range(3):
            pi = c.next_ps()
            proj(c, pi, wt[3][0], wt[3][1], sub * 128, 128, g5, cols=cols)
            silu_(c, R(szt[:, 0:ncol], "szt"), R(c.ps[pi][:, 0:ncol], ("ps", pi)))
            P.copy("dve", R(HZ[:, 2 * sub, 0:ncol], "HZ"), R(F(szt[0:64, 0:ncol]), "szt"))
            p3 = c.next_ps()
            P.mm(R(c.ps[p3][0:64, 0:ncol], ("ps", p3)), shiftr, R(szt[:, 0:ncol], "szt"))
            P.copy("act", R(HZ[:, 2 * sub + 1, 0:ncol], "HZ"), R(c.ps[p3][0:64, 0:ncol], ("ps", p3)))
        pa = c.next_ps(); pb = c.next_ps()
        for k in range(KC):
            P.mm(R(c.ps[pa][0:6, 0:ncol], ("ps", pa)), R(c.wab[:, k, 0:6], "wab"), R(c.hnT[:, k, col0:col0 + ncol], hkey(g5)),
                 start=(k == 0), stop=(k == KC - 1))
        for k in range(KC):
            P.mm(R(c.ps[pb][0:6, 0:ncol], ("ps", pb)), R(c.wab[:, k, 6:12], "wab"), R(c.hnT[:, k, col0:col0 + ncol], hkey(g5)),
                 start=(k == 0), stop=(k == KC - 1))
        rk = "rowsB"
        P.act(R(G[0:6, 0:ncol], rk), R(c.ps[pa][0:6, 0:ncol], ("ps", pa)), AF.Exp, bias=R(c.vec[0:6, V_DTB + l:V_DTB + l + 1], "vec"))
        P.act(R(G[0:6, 0:ncol], rk), R(G[0:6, 0:ncol], rk), AF.Ln, bias=1.0)
        P.ts("dve", R(G[0:6, 0:ncol], rk), R(G[0:6, 0:ncol], rk), R(nA[0:6, 0:1], "nA"), None, ALU.mult)
        sigmoid_(c, R(BETA[0:6, 0:ncol], rk), R(c.ps[pb][0:6, 0:ncol], ("ps", pb)))
        scm = c.cstf[0:6, C_SCAN4:C_SCAN4 + 64] if smp else c.cstf[0:6, C_SCAN64:C_SCAN64 + 128]
        gco, go, sco = GC[0:6, 0:ncol], G[0:6, 0:ncol], scm
        P.op("dve", lambda e, gco=gco, go=go, sco=sco: e.tensor_tensor_scan(gco, sco, go, 0.0, ALU.mult, ALU.add), reads=[rk, "cstf"], writes=[rk])
        P.act(R(EG[0:6, 0:ncol], rk), R(GC[0:6, 0:ncol], rk), AF.Exp)
        P.ts("dve", R(NGC[0:6, 0:ncol], rk), R(GC[0:6, 0:ncol], rk), -1.0, None, ALU.mult)
        clen = SQ if smp else 64
        nseg = ncol // clen
        gc3 = GC[0:6, 0:ncol].rearrange("p (n t) -> p n t", t=clen)
        glb = gc3[:, :, clen - 1:clen].to_broadcast([6, nseg, clen])
        P.tt("dve", R(tmpd[0:6, 0:ncol].rearrange("p (n t) -> p n t", t=clen), rk), R(glb, rk), R(gc3, rk), ALU.subtract)
        P.act(R(DL[0:6, 0:ncol], rk), R(tmpd[0:6, 0:ncol], rk), AF.Exp)
        P.act(R(EGL[0:6, 0:nseg], rk), R(gc3[:, :, clen - 1], rk), AF.Exp)
        pe_ = c.next_ps()
        for h in range(6):
            P.mm(R(c.ps[pe_][0:64, h * 16:h * 16 + nseg], ("ps", pe_)), EH(h), R(EGL[0:6, 0:nseg], rk))
        P.copy("dve", R(EGLB[:, :, 0:nseg], "EGLB"), R(c.ps[pe_][0:64, 0:96].rearrange("p (h n) -> p h n", h=6)[:, :, 0:nseg], ("ps", pe_)))

        for ch in range(nch):
            cc = slice(ch * 64, ch * 64 + 64)
            for hb in range(2):
                heads = [hb * 3 + i for i in range(3)]
                uk = lambda nm: ("U", nm)
                pd = c.next_ps()
                maskap = c.cstb[0:64, 704:768] if smp else c.cstb[0:64, C_MCUR:C_MCUR + 64]
                for i, h in enumerate(heads):
                    o = R(c.ps[pd][0:64, i * 64:(i + 1) * 64], ("ps", pd))
                    P.mm(o, identb64, R(maskap, "cstb"), start=True, stop=False)
                    P.mm(o, EH(h), R(GC[0:6, cc], rk), start=False, stop=False)
                    P.mm(o, R(NGC[0:6, cc], rk), EH(h), start=False, stop=True)
                P.act(R(U["decT"][:].rearrange("p a b -> p (a b)"), uk("decT")), R(c.ps[pd][0:64, 0:192], ("ps", pd)), AF.Exp)
                pbb = c.next_ps(); pb2 = c.next_ps()
                for qi, row in enumerate((BETA, EG, DL)):
                    pq_ = pbb if qi < 2 else pb2
                    for i, h in enumerate(heads):
                        P.mm(R(c.ps[pq_][0:64, ((qi % 2) * 3 + i) * 64:((qi % 2) * 3 + i + 1) * 64], ("ps", pq_)), EH(h), R(row[0:6, cc], rk))

                def bview(qi, pbb=pbb, pb2=pb2):
                    pq_ = pbb if qi < 2 else pb2
                    return R(c.ps[pq_][0:64, (qi % 2) * 192:(qi % 2 + 1) * 192].rearrange("p (a b) -> p a b", a=3), ("ps", pq_))
                hs = slice(hb * 3, hb * 3 + 3)
                P.tt("dve", R(U["kbT"][:], uk("kbT")), R(F(HK[:, hs, cc]), ("H", 1)), bview(0), ALU.mult)
                P.tt("dve", R(U["vbT"][:], uk("vbT")), R(F(HV[:, hs, cc]), ("H", 2)), bview(0), ALU.mult)
                P.tt("dve", R(U["kbgT"][:], uk("kbgT")), R(F(U["kbT"][:]), uk("kbT")), bview(1), ALU.mult)
                P.tt("dve", R(U["qgT"][:], uk("qgT")), R(F(HQ[:, hs, cc]), ("H", 0)), bview(1), ALU.mult)
                P.tt("dve", R(U["kdT"][:], uk("kdT")), R(F(HK[:, hs, cc]), ("H", 1)), bview(2), ALU.mult)
                pk_ = c.next_ps()
                for i, h in enumerate(heads):
                    P.mm(R(c.ps[pk_][0:64, i * 64:(i + 1) * 64], ("ps", pk_)), R(HK[:, h, cc], ("H", 1)), R(U["kbT"][:, i, :], uk("kbT")))
                    P.mm(R(c.ps[pk_][0:64, 192 + i * 64:192 + (i + 1) * 64], ("ps", pk_)), R(HK[:, h, cc], ("H", 1)), R(HQ[:, h, cc], ("H", 0)))
                kk3 = R(c.ps[pk_][0:64, 0:192].rearrange("p (a b) -> p a b", a=3), ("ps", pk_))
                qk3 = R(c.ps[pk_][0:64, 192:384].rearrange("p (a b) -> p a b", a=3), ("ps", pk_))
                P.tt("dve", R(U["LT0"][:], uk("LT0")), kk3, R(U["decT"][:], uk("decT")), ALU.mult)
                offd = c.cstf[0:64, C_OFFD:C_OFFD + 64].unsqueeze(1).to_broadcast([64, 3, 64])
                P.tt("dve", R(U["LT"][:], uk("LT")), R(U["LT0"][:], uk("LT0")), R(offd, "cstf"), ALU.mult)
                P.tt("dve", R(U["aT"][:], uk("aT")), qk3, R(U["decT"][:], uk("decT")), ALU.mult)
                pl = c.next_ps()
                for i in range(3):
                    P.transpose(R(c.ps[pl][0:64, i * 64:(i + 1) * 64], ("ps", pl)), R(U["LT0"][:, i, :], uk("LT0")), ident64)
                P.tt("dve", R(U["L"][:], uk("L")), R(c.ps[pl][0:64, 0:192].rearrange("p (a b) -> p a b", a=3), ("ps", pl)), R(offd, "cstf"), ALU.mult)
                idb = c.cstf[0:64, 0:64].unsqueeze(1).to_broadcast([64, 3, 64])
                P.stt("dve", R(U["X0"][:], uk("X0")), R(F(U["LT"][:]), uk("LT")), -1.0, R(idb, "cstf"), ALU.mult, ALU.add)
                Pc, PTc, Xc = "L", "LT", "X0"
                nlev = 1 if smp else 5
                for lev in range(nlev):
                    Pn = "P%d" % (lev % 2); PTn = "PT%d" % (lev % 2); Xn = "X%d" % ((lev + 1) % 2)
                    pp = c.next_ps()
                    for i in range(3):
                        P.mm(R(c.ps[pp][0:64, i * 64:(i + 1) * 64], ("ps", pp)), R(U[PTc][:, i, :], uk(PTc)), R(U[Pc][:, i, :], uk(Pc)))
                    P.copy("act", R(U[Pn][:], uk(Pn)), R(c.ps[pp][0:64, 0:192].rearrange("p (a b) -> p a b", a=3), ("ps", pp)))
                    if lev < nlev - 1:
                        pt_ = c.next_ps()
                        for i in range(3):
                            P.mm(R(c.ps[pt_][0:64, i * 64:(i + 1) * 64], ("ps", pt_)), R(U[Pc][:, i, :], uk(Pc)), R(U[PTc][:, i, :], uk(PTc)))
                        P.copy("act", R(U[PTn][:], uk(PTn)), R(c.ps[pt_][0:64, 0:192].rearrange("p (a b) -> p a b", a=3), ("ps", pt_)))
                    px = c.next_ps()
                    for i in range(3):
                        o = R(c.ps[px][0:64, i * 64:(i + 1) * 64], ("ps", px))
                        P.mm(o, R(U[Pn][:, i, :], uk(Pn)), R(U[Xc][:, i, :], uk(Xc)))
                    P.tt("dve", R(U[Xn][:], uk(Xn)), R(U[Xc][:], uk(Xc)),
                         R(c.ps[px][0:64, 0:192].rearrange("p (a b) -> p a b", a=3), ("ps", px)), ALU.add)
                    Pc, PTc, Xc = Pn, PTn, Xn
                TinvT = Xc
                pv = c.next_ps()
                for i in range(3):
                    P.transpose(R(c.ps[pv][0:64, i * 64:(i + 1) * 64], ("ps", pv)), R(F(U["vbT"][:, i, :]), uk("vbT")), ident64)
                    P.transpose(R(c.ps[pv][0:64, 192 + i * 64:192 + (i + 1) * 64], ("ps", pv)), R(F(U["kdT"][:, i, :]), uk("kdT")), ident64)
                P.copy("act", R(U["VB"][:], uk("VB")), R(c.ps[pv][0:64, 0:192].rearrange("p (a b) -> p a b", a=3), ("ps", pv)))
                P.copy("act", R(U["KD"][:], uk("KD")), R(c.ps[pv][0:64, 192:384].rearrange("p (a b) -> p a b", a=3), ("ps", pv)))
                if not smp:
                    pr_ = c.next_ps()
                    for i, h in enumerate(heads):
                        P.mm(R(c.ps[pr_][0:64, i * 64:(i + 1) * 64], ("ps", pr_)), R(U["kbgT"][:, i, :], uk("kbgT")), R(S[:, h, :], ("S", h)))
                    P.tt("dve", R(U["Rr"][:], uk("Rr")), R(F(U["VB"][:]), uk("VB")),
                         R(c.ps[pr_][0:64, 0:192].rearrange("p (a b) -> p a b", a=3), ("ps", pr_)), ALU.subtract)
                    pn = c.next_ps()
                    for i in range(3):
                        P.mm(R(c.ps[pn][0:64, i * 64:(i + 1) * 64], ("ps", pn)), R(U[TinvT][:, i, :], uk(TinvT)), R(U["Rr"][:, i, :], uk("Rr")))
                    P.copy("act", R(U["VN"][:], uk("VN")), R(c.ps[pn][0:64, 0:192].rearrange("p (a b) -> p a b", a=3), ("ps", pn)))
                    po = c.next_ps()
                    for i, h in enumerate(heads):
                        o = R(c.ps[po][0:64, i * 64:(i + 1) * 64], ("ps", po))
                        P.mm(o, R(S[:, h, :], ("S", h)), R(U["qgT"][:, i, :], uk("qgT")), start=True, stop=False)
                        P.mm(o, R(U["VN"][:, i, :], uk("VN")), R(U["aT"][:, i, :], uk("aT")), start=False, stop=True)
                    P.copy("act", R(OT[:, hs, cc], "OT"), R(c.ps[po][0:64, 0:192].rearrange("p (a b) -> p a b", a=3), ("ps", po)))
                    pS = c.next_ps()
                    for i, h in enumerate(heads):
                        P.mm(R(c.ps[pS][0:64, i * 64:(i + 1) * 64], ("ps", pS)), R(U["KD"][:, i, :], uk("KD")), R(U["VN"][:, i, :], uk("VN")))
                    for i, h in enumerate(heads):
                        P.stt("dve", R(S[:, h, :], ("S", h)), R(F(S[:, h, :]), ("S", h)), R(EGLB[:, h, ch:ch + 1], "EGLB"),
                              R(c.ps[pS][0:64, i * 64:(i + 1) * 64], ("ps", pS)), ALU.mult, ALU.add)
                else:
                    for i, h in enumerate(heads):
                        P.dma("sp", SS[:], c.sgd[l][:, h].rearrange("s k v -> k s v"), writes=["SS"])
                        pr_ = c.next_ps()
                        for s_ in range(NSQ):
                            P.mm(R(c.ps[pr_][0:64, s_ * 4:(s_ + 1) * 4], ("ps", pr_)), R(SS[:, s_, :], "SS"),
                                 R(F(U["kbgT"][:, i, s_ * 4:(s_ + 1) * 4]), uk("kbgT")))
                        P.tt("dve", R(U["RT"][:, i, :], uk("RT")), R(F(U["vbT"][:, i, :]), uk("vbT")), R(c.ps[pr_][0:64, 0:64], ("ps", pr_)), ALU.subtract)
                        pq = c.next_ps()
                        P.transpose(R(c.ps[pq][0:64, 0:64], ("ps", pq)), R(U["RT"][:, i, :], uk("RT")), ident64)
                        P.copy("act", R(U["Rr"][:, i, :], uk("Rr")), R(c.ps[pq][0:64, 0:64], ("ps", pq)))
                        pn = c.next_ps()
                        P.mm(R(c.ps[pn][0:64, 0:64], ("ps", pn)), R(U[TinvT][:, i, :], uk(TinvT)), R(U["Rr"][:, i, :], uk("Rr")))
                        P.copy("act", R(U["VN"][:, i, :], uk("VN")), R(c.ps[pn][0:64, 0:64], ("ps", pn)))
                        po = c.next_ps()
                        P.mm(R(c.ps[po][0:64, 0:64], ("ps", po)), R(F(U["VN"][:, i, :]), uk("VN")), R(F(U["aT"][:, i, :]), uk("aT")), start=True, stop=False)
                        for s_ in range(NSQ):
                            P.mm(R(c.ps[po][0:64, s_ * 4:(s_ + 1) * 4], ("ps", po)), R(SS[:, s_, :], "SS"),
                                 R(F(U["qgT"][:, i, s_ * 4:(s_ + 1) * 4]), uk("qgT")), start=False, stop=(s_ == NSQ - 1))
                        P.copy("act", R(OT[:, h, cc], "OT"), R(c.ps[po][0:64, 0:64], ("ps", po)))
                        seqm = c.cstf[0:64, C_SEQM:C_SEQM + 16].unsqueeze(2).to_broadcast([64, NSQ, 64])
                        kdb = F(U["KD"][:, i, :]).unsqueeze(1).to_broadcast([64, NSQ, 64])
                        P.tt("dve", R(KDblk[:], "KDblk"), R(kdb, uk("KD")), R(seqm, "cstf"), ALU.mult)
                        eglb = EGLB[:, h, 0:NSQ].unsqueeze(2).to_broadcast([64, NSQ, 64])
                        P.tt("dve", R(SS[:], "SS"), R(SS[:], "SS"), R(eglb, "EGLB"), ALU.mult)
                        for half in range(2):
                            pS = c.next_ps()
                            for s8 in range(8):
                                s_ = half * 8 + s8
                                P.mm(R(c.ps[pS][0:64, s8 * 64:(s8 + 1) * 64], ("ps", pS)), R(KDblk[:, s_, :], "KDblk"), R(U["VN"][:, i, :], uk("VN")))
                            P.tt("dve", R(SS[:, half * 8:half * 8 + 8, :], "SS"), R(SS[:, half * 8:half * 8 + 8, :], "SS"),
                                 R(c.ps[pS][0:64, :].rearrange("p (s v) -> p s v", s=8), ("ps", pS)), ALU.add)
                        P.dma("sp", c.gd_s[l][:, h].rearrange("s k v -> k s v"), SS[:], reads=["SS"], final=True)
        P.tt("dve", R(sqo[:, :, 0:ncol], ("H", 1)), R(OT[:, :, 0:ncol], "OT"), R(OT[:, :, 0:ncol], "OT"), ALU.mult)
        for hb in range(2):
            pg = c.next_ps()
            for i in range(3):
                P.mm(R(c.ps[pg][0:64, i * 128:i * 128 + ncol], ("ps", pg)), R(c.cstb[0:64, C_ONES:C_ONES + 64], "cstb"), R(sqo[:, hb * 3 + i, 0:ncol], ("H", 1)))
            rsqrt(c, R(rso[:, hb * 3:hb * 3 + 3, 0:ncol], ("H", 0)),
                  R(c.ps[pg][0:64, 0:384].rearrange("p (a b) -> p a b", a=3)[:, :, 0:ncol], ("ps", pg)), 1.0 / 64, EPS)
        P.tt("dve", R(rso[:, :, 0:ncol], ("H", 0)), R(rso[:, :, 0:ncol], ("H", 0)), R(OT[:, :, 0:ncol], "OT"), ALU.mult)
        P.stt("dve", R(mixB[:, :, 0:ncol], "mixB"), R(rso[:, :, 0:ncol], ("H", 0)), R(c.vec[0:64, V_GNW + l:V_GNW + l + 1], "vec"),
              R(HZ[:, :, 0:ncol], "HZ"), ALU.mult, ALU.mult)
        outproj(c, l, g5, lambda mc: R(mixB[:, mc, 0:ncol], "mixB"), 6, 64, cols=cols)
    P.dma("sp", c.gd_p[l].rearrange("h k v -> k h v"), F(S[:]), reads=[("S", h) for h in range(6)], final=True)
    for b3 in range(3):
        for bb in range(3):
            blk = b3 * 3 + bb
            pi = c.next_ps()
            P.transpose(R(c.ps[pi][0:51, 0:128], ("ps", pi)), R(gatB[:, blk, :], ("gatB", blk)), c.ident)
            P.copy("dve", R(stg[0:51, bb * 128:(bb + 1) * 128], "stgB"), R(c.ps[pi][0:51, 0:128], ("ps", pi)))
        P.dma("sp", c.cb_p[l][:, b3 * 384:(b3 + 1) * 384], stg[0:3, :], reads=["stgB"], final=True)
        P.dma("sp", c.cb_s[l][:, b3 * 384:(b3 + 1) * 384], stg[3:51, :], reads=["stgB"], final=True)
    P.barrier()


CDIL = (1, 4, 16)
CWIN = (128, 512, 2048)


def _rope(c, buf3, rows, tt, rt, key):
    P = c.P
    nh = buf3.shape[1]
    x1 = buf3[:, :, 0:8]; x2 = buf3[:, :, 8:16]
    cos = c.ropet[0:rows, tt, 0:8].unsqueeze(1).to_broadcast([rows, nh, 8])
    sin = c.ropet[0:rows, tt, 8:16].unsqueeze(1).to_broadcast([rows, nh, 8])
    t = [r_[0:rows, 0:nh * 8].rearrange("p (h e) -> p h e", e=8) for r_ in rt]
    P.tt("dve", R(t[0], "ropet0"), R(x1, key), R(cos, "rope"), ALU.mult)
    P.tt("dve", R(t[1], "ropet1"), R(x2, key), R(sin, "rope"), ALU.mult)
    P.tt("dve", R(t[2], "ropet2"), R(x2, key), R(cos, "rope"), ALU.mult)
    P.tt("dve", R(t[3], "ropet3"), R(x1, key), R(sin, "rope"), ALU.mult)
    P.tt("dve", R(x1, key), R(t[0], "ropet0"), R(t[1], "ropet1"), ALU.subtract)
    P.tt("dve", R(x2, key), R(t[2], "ropet2"), R(t[3], "ropet3"), ALU.add)


def branch_c(c, l):
    P = c.P
    A_ = c.Arena(); f32 = A_.f32
    ksT = f32(384)[0:64].rearrange("p (h t) -> p h t", h=6)
    qsT = f32(384)[0:64].rearrange("p (h t) -> p h t", h=6)
    vtok = f32(384)[0:64].rearrange("p (g x) -> p g x", g=3)
    ZS = f32(768)[0:64].rearrange("p (h t) -> p h t", h=6)
    c.optmp = f32(NS).rearrange("p (s i) -> p s i", i=SQ)
    prefix = A_.off
    KT = A_.bf16(6 * T)[0:64].rearrange("p (h t) -> p h t", h=6)
    VS = A_.bf16(48 * 128).rearrange("p (n x) -> p n x", x=128)
    kv_raw = f32(768)
    kvst = kv_raw.rearrange("p (g x) -> p g x", g=3)
    qsb = f32(384)
    rt = [f32(48) for _ in range(4)]
    QTt = A_.bf16(6 * 128)[0:64].rearrange("p (h t) -> p h t", h=6)
    PT = [A_.bf16(512) for _ in range(3)]
    dtot = f32(256)[0:64].rearrange("p (h t) -> p h t", h=2)
    onorm = kv_raw[0:64].rearrange("p (h t) -> p h t", h=6)
    mixC = A_.bf16(768)[0:64].rearrange("p (h t) -> p h t", h=6)

    load_wo(c, l, 640, 6, 64)
    wk_t, wk_k = c.wload(c.w_in[l][:, 2956:3340], 384)
    wv_t, wv_k = c.wload(c.w_in[l][:, 3340:3724], 384)
    wq_t, wq_k = c.wload(c.w_in[l][:, 2572:2956], 384)
    wz_t, wz_k = c.wload(c.w_in[l][:, 3724:4108], 384)
    ident = c.ident

    def tokproj(pi, wt_, wk_, tt, rows, ncols=384, c0=0):
        col0 = tt * 128
        g5 = min(tt // 4, 4)
        for k in range(KC):
            P.mm(R(c.ps[pi][0:rows, 0:ncols], ("ps", pi)), R(c.hnT[:, k, col0:col0 + rows], hkey(g5)), R(wt_[:, k, c0:c0 + ncols], wk_),
                 start=(k == 0), stop=(k == KC - 1))

    for tt in range(17):
        rows = 128 if tt < 16 else NS
        pk = c.next_ps(); tokproj(pk, wk_t, wk_k, tt, rows)
        pv = c.next_ps(); tokproj(pv, wv_t, wv_k, tt, rows)
        kk = ("kvst",)
        P.copy("act", R(kvst[0:rows, :, 0:128], "kvst"), R(c.ps[pk][0:rows, 0:384].rearrange("p (g x) -> p g x", g=3), ("ps", pk)))
        P.copy("act", R(kvst[0:rows, :, 128:256], "kvst"), R(c.ps[pv][0:rows, 0:384].rearrange("p (g x) -> p g x", g=3), ("ps", pv)))
        for gi in range(3):
            _rope(c, kvst[0:rows, gi, 0:128].rearrange("p (h d) -> p h d", h=2), rows, tt, rt, "kvst")
        for gi in range(3):
            if tt < 16:
                lo = T - CWIN[gi]
                if tt * 128 >= lo:
                    P.dma("sp", c.kv_p[gi][l][tt * 128 - lo:tt * 128 - lo + 128, :], kvst[:, gi, :], reads=["kvst"], final=True)
            else:
                P.dma("sp", c.kv_s[gi][l], kvst[0:NS, gi, :], reads=["kvst"], final=True)
        for half in range(2):
            pt = c.next_ps()
            for i in range(3):
                hx = half * 3 + i
                gi, h = hx // 2, hx % 2
                P.transpose(R(c.ps[pt][0:64, i * 128:i * 128 + rows], ("ps", pt)), R(kvst[0:rows, gi, h * 64:(h + 1) * 64], "kvst"),
                            R(c.cstf[0:rows, 0:rows], "cstf"))
            src = c.ps[pt][0:64, 0:384].rearrange("p (h t) -> p h t", h=3)[:, :, 0:rows]
            if tt < 16:
                P.copy("act", R(KT[:, half * 3:half * 3 + 3, tt * 128:tt * 128 + 128], ("KT", tt)), R(src, ("ps", pt)))
            else:
                P.copy("act", R(ksT[:, half * 3:half * 3 + 3, :], "ksT"), R(src, ("ps", pt)))
        if tt == 16:
            P.copy("dve", R(vtok[:], "vtok"), R(kvst[0:NS, :, 128:256], "kvst"))
    for gi in range(3):
        dil = CDIL[gi]; nb = 16 // dil
        for q4 in range(4):
            pi = c.next_ps()
            for j in range(4):
                st_ = q4 * 4 + j
                r, n = st_ // nb, st_ % nb
                lo = r + dil * n * 128
                for k in range(KC):
                    P.mm(R(c.ps[pi][:, j * 128:(j + 1) * 128], ("ps", pi)), (c.hnT[:, k, lo:lo + dil * 127 + 1:dil], tuple(hkey(g) for g in range(4))),
                         R(wv_t[:, k, gi * 128:(gi + 1) * 128], wv_k), start=(k == 0), stop=(k == KC - 1))
            P.copy("act" if q4 % 2 == 0 else "dve", R(VS[:, gi * 16 + q4 * 4:gi * 16 + q4 * 4 + 4, :], ("VS", gi)),
                   R(c.ps[pi][:].rearrange("p (n x) -> p n x", x=128), ("ps", pi)))

    c.ps_n = 4; c.ps_i = 0
    psO = [4, 5]; psD = [6, 7]
    onesb64 = R(c.cstb[:, C_ONES:C_ONES + 64], "cstb")
    mcur = c.cstb[:, C_MCUR:C_MCUR + 128]; mprev = c.cstb[:, C_MPREV:C_MPREV + 128]
    allkt = tuple(("KT", t_) for t_ in range(16))

    def q_and_z(tt, rows, QT_dst, qkey, Z_dst):
        pq = c.next_ps(); tokproj(pq, wq_t, wq_k, tt, rows)
        P.copy("act", R(qsb[0:rows, :], "qsb"), R(c.ps[pq][0:rows, 0:384], ("ps", pq)))
        _rope(c, qsb[0:rows, :].rearrange("p (h d) -> p h d", h=6), rows, tt, rt, "qsb")
        for half in range(2):
            pt = c.next_ps()
            for i in range(3):
                hx = half * 3 + i
                P.transpose(R(c.ps[pt][0:64, i * 128:i * 128 + rows], ("ps", pt)), R(qsb[0:rows, hx * 64:(hx + 1) * 64], "qsb"),
                            R(c.cstf[0:rows, 0:rows], "cstf"))
            P.copy("dve", R(QT_dst[:, half * 3:half * 3 + 3, 0:rows], qkey),
                   R(c.ps[pt][0:64, 0:384].rearrange("p (h t) -> p h t", h=3)[:, :, 0:rows], ("ps", pt)))
        col0 = tt * 128; g5 = min(tt // 4, 4)
        for half in range(2):
            pz = c.next_ps()
            for i in range(3):
                hx = half * 3 + i
                for k in range(KC):
                    P.mm(R(c.ps[pz][0:64, i * 128:i * 128 + rows], ("ps", pz)), R(wz_t[:, k, hx * 64:(hx + 1) * 64], wz_k),
                         R(c.hnT[:, k, col0:col0 + rows], hkey(g5)), start=(k == 0), stop=(k == KC - 1))
            silu_(c, R(Z_dst[:, half * 3:half * 3 + 3, 0:rows], "ZS"),
                  R(c.ps[pz][0:64, 0:384].rearrange("p (h t) -> p h t", h=3)[:, :, 0:rows], ("ps", pz)))

    for tt in range(16):
        q_and_z(tt, 128, QTt, "QTt", ZS)

        def pv_den(hx, ocols, vs_tile, h, pt_ap, ptkey, first, last):
            o = c.ps[psO[hx // 3]][0:64, (hx % 3) * 128 + ocols[0]:(hx % 3) * 128 + ocols[0] + ocols[1]]
            d = c.ps[psD[hx // 3]][0:64, (hx % 3) * 128 + ocols[0]:(hx % 3) * 128 + ocols[0] + ocols[1]]
            P.mm(R(o, ("ps", psO[hx // 3])), R(VS[:, vs_tile, h * 64:(h + 1) * 64], ("VS", vs_tile // 16)), R(pt_ap, ptkey), start=first, stop=last)
            P.mm(R(d, ("ps", psD[hx // 3])), onesb64, R(pt_ap, ptkey), start=first, stop=last)

        for gi in range(3):
            dil = CDIL[gi]; nb = 16 // dil; nq = 128 // dil
            n = tt // dil
            qo = (tt % dil) * nq
            kbl = [0] if n == 0 else [0, 1]
            pS = c.next_ps()
            ptile = PT[gi]; ptk = ("PT", gi)
            slots = {}
            for kbi in kbl:
                kb = n - kbi
                for h in range(2):
                    hx = gi * 2 + h
                    for r in range(dil):
                        col = ((kbi * 2 + h) * dil + r) * nq
                        slots[(kbi, h, r)] = col
                        klo = r + dil * kb * 128
                        o = R(c.ps[pS][:, col:col + nq], ("ps", pS))
                        P.mm(o, (KT[:, hx, klo:klo + dil * 127 + 1:dil], allkt), R(QTt[:, hx, r:128:dil], "QTt"), start=True, stop=False)
                        msk = (mcur if kbi == 0 else mprev)[:, qo:qo + nq]
                        P.mm(o, c.identb, R(msk, "cstb"), start=False, stop=True)
            used = len(kbl) * 2 * dil * nq
            P.act(R(ptile[:, 0:used], ptk), R(c.ps[pS][:, 0:used], ("ps", pS)), AF.Exp, scale=0.125)
            for h in range(2):
                hx = gi * 2 + h
                for r in range(dil):
                    for ki, kbi in enumerate(kbl):
                        kb = n - kbi
                        col = slots[(kbi, h, r)]
                        pv_den(hx, (r * nq, nq), gi * 16 + r * nb + kb, h, ptile[:, col:col + nq], ptk, ki == 0, ki == len(kbl) - 1)
        def nat(ap, dil):
            if dil == 1:
                return ap
            return ap.rearrange("p (r m) -> p m r", r=dil)
        for h in range(2):
            dv_ = dtot[:, h, :]
            P.copy("dve", R(dv_, "dtot"), R(c.ps[psD[0]][0:64, h * 128:(h + 1) * 128], ("ps", psD[0])))
            for gi in (1, 2):
                hx = gi * 2 + h
                dil = CDIL[gi]
                src = c.ps[psD[hx // 3]][0:64, (hx % 3) * 128:(hx % 3) * 128 + 128]
                dvw = dv_.rearrange("p (m r) -> p m r", r=dil)
                P.tt("dve", R(dvw, "dtot"), R(dvw, "dtot"), R(nat(src, dil), ("ps", psD[hx // 3])), ALU.add)
            P.op("dve", lambda e, dv_=dv_: e.reciprocal(dv_, dv_), reads=["dtot"], writes=["dtot"])
        for hx in range(6):
            gi, h = hx // 2, hx % 2
            dil = CDIL[gi]
            src = c.ps[psO[hx // 3]][0:64, (hx % 3) * 128:(hx % 3) * 128 + 128]
            if dil == 1:
                P.tt("dve", R(onorm[:, hx, :], "kvst"), R(src, ("ps", psO[hx // 3])), R(dtot[:, h, :], "dtot"), ALU.mult)
            else:
                P.tt("dve", R(onorm[:, hx, :].rearrange("p (m r) -> p m r", r=dil), "kvst"), R(nat(src, dil), ("ps", psO[hx // 3])),
                     R(dtot[:, h, :].rearrange("p (m r) -> p m r", r=dil), "dtot"), ALU.mult)
        P.tt("dve", R(mixC[:], "mixC"), R(onorm[:], "kvst"), R(ZS[:], "ZS"), ALU.mult)
        outproj(c, l, tt // 4, lambda mc: R(mixC[:, mc, :], "mixC"), 6, 64, cols=(tt * 128, 128))
    c.ps_n = 8; c.ps_i = 0

    ZSs = ZS
    q_and_z(16, NS, qsT, "qsT", ZSs)
    P.barrier()
    B_ = c.Arena()
    b32 = B_.f32
    _skip = b32(prefix)
    CK = [b32(9 * 256).rearrange("p (b x) -> p b x", b=9) for _ in range(2)]
    KcT = b32(18 * 128)[0:64].rearrange("p (b t) -> p b t", b=18)
    PTn = b32(384)[0:64].rearrange("p (h t) -> p h t", h=6)
    PTc = b32(24)
    dts = b32(128)[0:64].rearrange("p (h t) -> p h t", h=2)
    ons = b32(384)[0:64].rearrange("p (h t) -> p h t", h=6)
    mixS = B_.bf16(384)[0:64].rearrange("p (h t) -> p h t", h=6)
    ones64f = R(c.cstf[0:64, C_ONES:C_ONES + 64], "cstf")
    ones128f = R(c.cstf[:, C_ONES:C_ONES + 64], "cstf")
    pO = 6; pD = 7
    c.ps_n = 6
    pn_ = c.next_ps()
    for hx in range(6):
        gi = hx // 2
        o = R(c.ps[pn_][0:64, hx * 64:(hx + 1) * 64], ("ps", pn_))
        P.mm(o, R(ksT[:, hx, :], "ksT"), R(qsT[:, hx, :], "qsT"), start=True, stop=False)
        msk = c.cstb[0:64, 704:768] if gi == 0 else c.cstb[0:64, 768:832]
        P.mm(o, R(c.cstb[0:64, 0:64], "cstb"), R(msk, "cstb"), start=False, stop=True)
    P.act(R(PTn[:].rearrange("p h t -> p (h t)"), "PTn"), R(c.ps[pn_][0:64, 0:384], ("ps", pn_)), AF.Exp, scale=0.125)
    for hx in range(6):
        gi, h = hx // 2, hx % 2
        P.mm(R(c.ps[pO][0:64, hx * 64:(hx + 1) * 64], ("ps", pO)), R(vtok[:, gi, h * 64:(h + 1) * 64], "vtok"), R(PTn[:, hx, :], "PTn"), start=True, stop=False)
        P.mm(R(c.ps[pD][0:64, hx * 64:(hx + 1) * 64], ("ps", pD)), ones64f, R(PTn[:, hx, :], "PTn"), start=True, stop=False)
    for s_ in range(NSQ):
        ck = CK[s_ % 2]; ckk = ("CK", s_ % 2)
        P.dma("sp", ck[:, 0, :], c.ck128[l][s_], writes=[ckk])
        P.dma("sp", ck[:, 1:5, :], c.ck512[l][s_].rearrange("(m i) x -> m i x", i=4), writes=[ckk])
        P.dma("sp", ck[:, 5:9, :], c.ck2048[l][s_].rearrange("(m i) x -> m i x", i=16)[:, 0:4, :], writes=[ckk])
        for q5 in range(5):
            pt = c.next_ps()
            nn = 4 if q5 < 4 else 2
            for j in range(nn):
                bh = q5 * 4 + j
                blk, h = bh // 2, bh % 2
                P.transpose(R(c.ps[pt][0:64, j * 128:(j + 1) * 128], ("ps", pt)), R(ck[:, blk, h * 64:(h + 1) * 64], ckk), ident)
            P.copy("act" if q5 % 2 == 0 else "dve", R(KcT[:, q5 * 4:q5 * 4 + nn, :], "KcT"),
                   R(c.ps[pt][0:64, 0:nn * 128].rearrange("p (b t) -> p b t", b=nn), ("ps", pt)))
        psc = c.next_ps()
        for h in range(2):
            o = R(c.ps[psc][:, h * 4:h * 4 + 4], ("ps", psc))
            P.mm(o, R(KcT[:, h, :], "KcT"), R(qsT[:, h, s_ * 4:s_ * 4 + 4], "qsT"), start=True, stop=False)
            P.mm(o, c.identb, R(c.cstb[:, 832:836], "cstb"), start=False, stop=True)
            for gi in (1, 2):
                for i in range(SQ):
                    blk = (1 if gi == 1 else 5) + i
                    col = gi * 8 + h * 4 + i
                    P.mm(R(c.ps[psc][:, col:col + 1], ("ps", psc)), R(KcT[:, blk * 2 + h, :], "KcT"),
                         R(qsT[:, gi * 2 + h, s_ * 4 + i:s_ * 4 + i + 1], "qsT"))
        P.act(R(PTc[:, 0:24], "PTc"), R(c.ps[psc][:, 0:24], ("ps", psc)), AF.Exp, scale=0.125)
        for h in range(2):
            for gi in range(3):
                hx = gi * 2 + h
                if gi == 0:
                    items = [(0, s_ * 4, 4, h * 4)]
                else:
                    items = [((1 if gi == 1 else 5) + i, s_ * 4 + i, 1, gi * 8 + h * 4 + i) for i in range(SQ)]
                for (blk, ocol, w_, pcol) in items:
                    P.mm(R(c.ps[pO][0:64, hx * 64 + ocol:hx * 64 + ocol + w_], ("ps", pO)), R(ck[:, blk, 128 + h * 64:128 + (h + 1) * 64], ckk),
                         R(PTc[:, pcol:pcol + w_], "PTc"), start=False, stop=True)
                    P.mm(R(c.ps[pD][0:64, hx * 64 + ocol:hx * 64 + ocol + w_], ("ps", pD)), ones128f,
                         R(PTc[:, pcol:pcol + w_], "PTc"), start=False, stop=True)
    for h in range(2):
        P.copy("dve", R(dts[:, h, :], "dts"), R(c.ps[pD][0:64, h * 64:(h + 1) * 64], ("ps", pD)))
        for gi in (1, 2):
            hx = gi * 2 + h
            P.tt("dve", R(dts[:, h, :], "dts"), R(dts[:, h, :], "dts"), R(c.ps[pD][0:64, hx * 64:(hx + 1) * 64], ("ps", pD)), ALU.add)
    dall = dts[:]
    P.op("dve", lambda e: e.reciprocal(dall, dall), reads=["dts"], writes=["dts"])
    for hx in range(6):
        P.tt("dve", R(ons[:, hx, :], "ons"), R(c.ps[pO][0:64, hx * 64:(hx + 1) * 64], ("ps", pO)), R(dts[:, hx % 2, :], "dts"), ALU.mult)
    P.tt("dve", R(mixS[:], "mixS"), R(ons[:], "ons"), R(ZSs[:, :, 0:NS], "ZS"), ALU.mult)
    c.ps_n = 8; c.ps_i = 0
    outproj(c, l, 4, lambda mc: R(mixS[:, mc, :], "mixS"), 6, 64, cols=(T, NS))
    P.barrier()
```

```python
import contextlib
import numpy as np
import concourse.bass as bass
import concourse.mybir as mybir
from concourse.bass_utils import run_bass_kernel_spmd

F32 = mybir.dt.float32
F32R = mybir.dt.float32r
BF16 = mybir.dt.bfloat16
AF = mybir.ActivationFunctionType
ALU = mybir.AluOpType
AX = mybir.AxisListType


class Prog:
    COMPUTE = ("pe", "act", "dve", "pool")
    ALL = ("pe", "act", "dve", "pool", "sp")

    def __init__(self, nc, stack, n_dma_sems=24):
        self.nc = nc
        self.stack = stack
        self.streams = {e: [] for e in self.ALL}
        self.esem = {e: stack.enter_context(nc.semaphore("prog_" + e)) for e in self.COMPUTE}
        self.ecount = {e: 0 for e in self.COMPUTE}
        self.known = {e: {} for e in self.ALL}
        self.res = {}
        self.dsem = {}
        for q in ("sp", "act", "pool"):
            self.dsem[q] = [[stack.enter_context(nc.semaphore("dq_%s_%d" % (q, i))), 0] for i in range(n_dma_sems)]
        self.dnext = {q: 0 for q in self.dsem}
        self.final_tokens = []

    def _deps(self, eng, reads, writes, same_engine_sem=None):
        toks = []
        for k in reads:
            r = self.res.get(k)
            if r and r["w"] is not None:
                toks.append(r["w"])
        for k in writes:
            r = self.res.get(k)
            if r:
                if r["w"] is not None:
                    toks.append(r["w"])
                toks.extend(r["r"])
        return toks

    def _record(self, tok, reads, writes):
        for k in reads:
            r = self.res.setdefault(k, {"w": None, "r": []})
            r["r"].append(tok)
        for k in writes:
            self.res[k] = {"w": tok, "r": []}

    def _waits(self, eng, toks, skip_sem=None):
        need = {}
        for (sem, val) in toks:
            if skip_sem is not None and sem is skip_sem:
                continue
            sid = id(sem)
            if self.known[eng].get(sid, 0) >= val:
                continue
            if sid not in need or need[sid][1] < val:
                need[sid] = (sem, val)
        for sid, (sem, val) in need.items():
            self.known[eng][sid] = val
        return list(need.values())

    def op(self, eng, fn, reads=(), writes=()):
        toks = self._deps(eng, reads, writes)
        own = self.esem[eng]
        if eng == "pe":
            waits = self._waits(eng, toks, skip_sem=own)
        else:
            waits = self._waits(eng, toks)
        self.ecount[eng] += 1
        tok = (own, self.ecount[eng])
        self.streams[eng].append((waits, fn, (own, 1)))
        self._record(tok, reads, writes)
        return tok

    def dma(self, q, out, in_, reads=(), writes=(), final=False, **kw):
        pool = self.dsem[q]
        i = self.dnext[q]
        self.dnext[q] = (i + 1) % len(pool)
        sem, val = pool[i]
        toks = self._deps(q, reads, writes)
        if val > 0:
            toks = toks + [(sem, val)]
        eng = {"sp": "sp", "act": "act", "pool": "pool"}[q]
        waits = self._waits(eng, toks)
        pool[i][1] = val + 16
        tok = (sem, val + 16)

        def fn(e, out=out, in_=in_, kw=kw):
            return e.dma_start(out, in_, **kw)
        self.streams[eng].append((waits, fn, (sem, 16)))
        self._record(tok, reads, writes)
        if final:
            self.final_tokens.append(tok)
        return tok


    @staticmethod
    def _k(*xs):
        out = []
        for x in xs:
            if isinstance(x, tuple):
                out.extend(x[1])
        return out

    @staticmethod
    def _a(x):
        return x[0] if isinstance(x, tuple) else x

    def mm(self, out, lhsT, rhs, start=True, stop=True):
        o, l, r = out[0], lhsT[0], rhs[0]
        return self.op("pe", lambda e: e.matmul(o, l, r, start=start, stop=stop),
                       reads=self._k(lhsT, rhs), writes=self._k(out))

    def transpose(self, out, in_, ident):
        o, i, d = out[0], in_[0], ident[0]
        return self.op("pe", lambda e: e.transpose(o, i, d), reads=self._k(in_, ident), writes=self._k(out))

    def act(self, out, in_, func, bias=0.0, scale=1.0, eng="act"):
        o, i, b, sc = out[0], in_[0], self._a(bias), self._a(scale)
        return self.op(eng, lambda e: e.activation(o, i, func, bias=b, scale=sc),
                       reads=self._k(in_, bias, scale), writes=self._k(out))

    def copy(self, eng, out, in_):
        o, i = out[0], in_[0]
        if eng == "act":
            return self.op(eng, lambda e: e.copy(o, i), reads=self._k(in_), writes=self._k(out))
        return self.op(eng, lambda e: e.tensor_copy(o, i), reads=self._k(in_), writes=self._k(out))

    def tt(self, eng, out, in0, in1, op):
        o, a, b = out[0], in0[0], in1[0]
        return self.op(eng, lambda e: e.tensor_tensor(o, a, b, op), reads=self._k(in0, in1), writes=self._k(out))

    def ts(self, eng, out, in0, s1, s2, op0, op1=None):
        o, a, x1, x2 = out[0], in0[0], self._a(s1), self._a(s2)
        if op1 is None:
            return self.op(eng, lambda e: e.tensor_scalar(o, a, x1, None, op0), reads=self._k(in0, s1), writes=self._k(out))
        return self.op(eng, lambda e: e.tensor_scalar(o, a, x1, x2, op0, op1), reads=self._k(in0, s1, s2), writes=self._k(out))

    def stt(self, eng, out, in0, scalar, in1, op0, op1):
        o, a, sc, b = out[0], in0[0], self._a(scalar), in1[0]
        return self.op(eng, lambda e: e.scalar_tensor_tensor(o, a, sc, b, op0, op1),
                       reads=self._k(in0, scalar, in1), writes=self._k(out))

    def memset(self, eng, out, val):
        o = out[0]
        return self.op(eng, lambda e: e.memset(o, val), writes=self._k(out))

    def barrier(self):
        toks = [(self.esem[e], self.ecount[e]) for e in self.COMPUTE if self.ecount[e] > 0]
        for q in self.dsem:
            for sem, val in self.dsem[q]:
                if val > 0:
                    toks.append((sem, val))
        for e in self.ALL:
            waits = self._waits(e, toks, skip_sem=self.esem.get(e))
            if waits:
                self.streams[e].append((waits, None, None))
        self.res = {}

    def new_epoch(self):
        self.epoch = getattr(self, "epoch", 0) + 1
        for e in self.COMPUTE:
            self.esem[e] = self.stack.enter_context(self.nc.semaphore("prog_%s_%d" % (e, self.epoch)))
            self.ecount[e] = 0

    def emit(self):
        nc = self.nc
        fin = self._waits("sp", self.final_tokens)
        streams = self.streams

        def run(ename, e):
            for (waits, fn, inc) in streams[ename]:
                for (sem, val) in waits:
                    e.wait_ge(sem, val)
                if fn is None:
                    continue
                ins = fn(e)
                if inc is not None:
                    ins.then_inc(inc[0], inc[1])
            if ename == "sp":
                for (sem, val) in fin:
                    e.wait_ge(sem, val)

        with nc.Block() as block:
            @block.sync
            def _(e):
                run("sp", e)

            @block.tensor
            def _(e):
                run("pe", e)

            @block.scalar
            def _(e):
                run("act", e)

            @block.vector
            def _(e):
                run("dve", e)

            @block.gpsimd
            def _(e):
                run("pool", e)


D = 1024
KC = 8
T = 2048
NSQ = 16
SQ = 4
NS = NSQ * SQ
NTOK = T + NS
DEPTH = 4
INW = 4108
NCORE = 8
GROUPS = [(0, 512), (512, 512), (1024, 512), (1536, 512), (2048, 64)]
WSLOT = 384
EPS = 1e-6

V_BADA = 0
V_NORMW = 96
V_FNORM = 128
V_BADAF = 136
V_CAW = 152
V_CBW = 176
V_GNW = 320
V_ALOG = 324
V_DTB = 328
NV = 332

C_IDENT = 0
C_SHIFT = 128
C_ONES = 192
C_MPREV = 320
C_MCUR = 448
C_BLK = 576
C_EH = 704
C_OFFD = 1088
C_MSEQ = 1152
C_SEQM = 1216
C_SCAN64 = 1232
C_SCAN4 = 1360
C_MDIAG = 1424
C_MS128 = 1488
NCST = 1492
NEGM = -240000.0


def R(ap, *keys):
    return (ap, keys)


class Ctx:
    pass


def build_program(phases=("A",), depth=DEPTH):
    nc = bass.Bass("TRN2", target_bir_lowering=False)
    st = contextlib.ExitStack()
    with st:
        P = Prog(nc, st)
        c = Ctx()
        c.nc, c.P, c.st = nc, P, st

        def din(name, shape):
            return nc.dram_tensor(name, list(shape), F32, kind="ExternalInput").ap()

        def dout(name, shape):
            return nc.dram_tensor(name, list(shape), F32, kind="ExternalOutput").ap()

        c.xp = din("xp", (T, D)); c.xs = din("xs", (NS, D))
        c.cT = din("cT", (128, KC, 17))
        c.sca = din("sca", (DEPTH, NSQ * 2, 256)); c.scb = din("scb", (DEPTH, NSQ * 3, 1152))
        c.sgd = din("sgd", (DEPTH, NSQ, 6, 64, 64))
        c.ck128 = din("ck128", (DEPTH, NSQ, 128, 256)); c.ck512 = din("ck512", (DEPTH, NSQ, 512, 256))
        c.ck2048 = din("ck2048", (DEPTH, NSQ, 2048, 256))
        c.w_in = din("w_in", (DEPTH, D, INW)); c.w_out = din("w_out", (DEPTH, D, D))
        c.w_ada = din("w_ada", (DEPTH, D, 3 * D)); c.w_adaf = din("w_adaf", (D, 2 * D))
        c.vecT = din("vecT", (128, NV)); c.cst = din("cst", (128, NCST))
        c.rope = din("rope", (128, 17, 16))
        c.yp = dout("yp", (T, D)); c.ys = dout("ys", (NS, D))
        c.ca_p = dout("ca_p", (DEPTH, 2, 256)); c.ca_s = dout("ca_s", (DEPTH, NSQ * 2, 256))
        c.cb_p = dout("cb_p", (DEPTH, 3, 1152)); c.cb_s = dout("cb_s", (DEPTH, NSQ * 3, 1152))
        c.gd_p = dout("gd_p", (DEPTH, 6, 64, 64)); c.gd_s = dout("gd_s", (DEPTH, NSQ, 6, 64, 64))
        c.kv_p = [dout("kv128_p", (DEPTH, 128, 256)), dout("kv512_p", (DEPTH, 512, 256)), dout("kv2048_p", (DEPTH, 2048, 256))]
        c.kv_s = [dout("kv128_s", (DEPTH, NS, 256)), dout("kv512_s", (DEPTH, NS, 256)), dout("kv2048_s", (DEPTH, NS, 256))]

        def sb(name, shape, dt):
            return st.enter_context(nc.sbuf_tensor(name, list(shape), dt))

        c.xT = sb("xT", (128, KC, NTOK), F32)
        c.hnT = sb("hnT", (128, KC, NTOK), BF16)
        c.ring = [sb("ring%d" % i, (128, KC, WSLOT), BF16) for i in range(4)]
        c.wab = sb("wab", (128, KC, 12), BF16)
        c.wo = sb("wo", (128, 6, D), BF16)
        c.vec = sb("vec", (128, NV), F32)
        c.cstf = sb("cstf", (128, NCST), F32)
        c.cstb = sb("cstb", (128, 836), BF16)
        c.ropet = sb("ropet", (128, 17, 16), F32)
        c.cTf = sb("cTf", (128, KC, 17), F32)
        c.cTb = sb("cTb", (128, KC, 17), BF16)
        c.ada = sb("ada", (128, 24, 17), F32)
        c.m1 = sb("m1", (128, KC, 17), F32)
        c.g1 = sb("g1", (128, KC, 17), F32)
        ARENA_F32 = 14592
        c.arena = sb("arena", (128, ARENA_F32), F32)
        c.ps = [st.enter_context(nc.psum_tensor("ps%d" % i, [128, 512], F32)) for i in range(8)]
        c.ps_i = 0
        c.ring_i = 0
        c.phases = phases

        c.ps_n = 8

        def next_ps():
            i = c.ps_i % c.ps_n
            c.ps_i = (i + 1) % c.ps_n
            return i
        c.next_ps = next_ps

        class Arena:
            def __init__(self):
                self.off = 0

            def f32(self, n):
                o = self.off
                self.off += n
                c.arena_max = max(getattr(c, "arena_max", 0), self.off)
                assert self.off <= ARENA_F32, ("arena overflow", self.off)
                return c.arena[:, o:o + n]

            def bf16(self, n):
                assert n % 2 == 0
                return self.f32(n // 2).bitcast(BF16)

            def f32r(self, n):
                return self.f32(n).bitcast(F32R)
        c.Arena = Arena

        def wload(src, ncols):
            i = c.ring_i
            c.ring_i = (i + 1) % 4
            key = ("ring", i)
            P.dma("pool", c.ring[i][:, :, 0:ncols], src.rearrange("(k p) c -> p k c", p=128), writes=[key])
            return c.ring[i], key
        c.wload = wload

        P.dma("sp", c.vec[:], c.vecT, writes=["vec"])
        P.dma("sp", c.cstf[:], c.cst, writes=["cstf"])
        P.dma("sp", c.ropet[:], c.rope, writes=["rope"])
        P.dma("sp", c.cTf[:], c.cT, writes=["cTf"])
        P.copy("dve", R(c.cstb[:, 0:704], "cstb"), R(c.cstf[:, 0:704], "cstf"))
        P.copy("dve", R(c.cstb[:, 704:768], "cstb"), R(c.cstf[:, C_MSEQ:C_MSEQ + 64], "cstf"))
        P.copy("dve", R(c.cstb[:, 768:836], "cstb"), R(c.cstf[:, C_MDIAG:C_MDIAG + 68], "cstf"))
        P.copy("dve", R(c.cTb[:], "cTb"), R(c.cTf[:], "cTf"))
        for i in range(8):
            P.memset("dve", R(c.ps[i][:], ("ps", i)), 0.0)
        c.zero = sb("zero", (128, 384), F32)
        P.memset("dve", R(c.zero[:], "zero"), 0.0)
        c.ident = R(c.cstf[:, C_IDENT:C_IDENT + 128], "cstf")
        c.identb = R(c.cstb[:, C_IDENT:C_IDENT + 128], "cstb")
        c.onesb = R(c.cstb[:, C_ONES:C_ONES + 128], "cstb")

        load_x(c)
        for l in range(depth):
            layer(c, l)
        final(c)
        P.emit()
    return nc


def xkey(g):
    return ("x", g)


def hkey(g):
    return ("hn", g)


def load_x(c):
    P = c.P
    A = c.Arena()
    stg = [A.f32(D) for _ in range(2)]
    for tt in range(17):
        rows = 128 if tt < 16 else NS
        g = min(tt // 4, 4)
        s = stg[tt % 2]
        skey = ("xstg", tt % 2)
        src = c.xp[tt * 128:(tt + 1) * 128, :] if tt < 16 else c.xs
        P.dma("sp", s[0:rows, :], src, writes=[skey])
        for half in range(2):
            pi = c.next_ps()
            for kk in range(4):
                k = half * 4 + kk
                P.transpose(R(c.ps[pi][:, kk * 128:kk * 128 + rows], ("ps", pi)),
                            R(s[0:rows, k * 128:(k + 1) * 128], skey), R(c.cstf[0:rows, C_IDENT:C_IDENT + rows], "cstf"))
            col0 = tt * 128
            dst = c.xT[:, half * 4:half * 4 + 4, col0:col0 + rows]
            srcp = c.ps[pi][:].rearrange("p (k t) -> p k t", k=4)[:, :, 0:rows]
            P.copy("act" if half == 0 else "dve", R(dst, xkey(g)), R(srcp, ("ps", pi)))
    P.barrier()


def ada_vectors(c, l):
    P = c.P
    final_ = (l == DEPTH)
    ncol = 2 * D if final_ else 3 * D
    nj = ncol // 128
    pi = c.next_ps()
    pst = c.ps[pi][:, 0:24 * 17].rearrange("p (j s) -> p j s", s=17)
    for t0 in range(0, ncol, WSLOT):
        nc_ = min(WSLOT, ncol - t0)
        src = (c.w_adaf if final_ else c.w_ada[l])[:, t0:t0 + nc_]
        wt, wk = c.wload(src, nc_)
        for jj in range(nc_ // 128):
            j = t0 // 128 + jj
            for k in range(KC):
                P.mm(R(pst[:, j, :], ("ps", pi)), R(wt[:, k, jj * 128:(jj + 1) * 128], wk), R(c.cTb[:, k, :], "cTb"),
                     start=(k == 0), stop=(k == KC - 1))
    vb = V_BADAF if final_ else V_BADA + l * 24
    bias = c.vec[:, vb:vb + nj].unsqueeze(2).to_broadcast([128, nj, 17])
    P.tt("dve", R(c.ada[:, 0:nj, :], "ada"), R(pst[:, 0:nj, :], ("ps", pi)), R(bias, "vec"), ALU.add)
    nw0 = V_FNORM if final_ else V_NORMW + l * 8
    nw = c.vec[:, nw0:nw0 + KC].unsqueeze(2).to_broadcast([128, KC, 17])
    P.stt("dve", R(c.m1[:], "m1"), R(c.ada[:, 8:16, :], "ada"), 1.0, R(nw, "vec"), ALU.add, ALU.mult)
    if not final_:
        P.ts("dve", R(c.g1[:], "g1"), R(c.ada[:, 16:24, :], "ada"), 1.0, None, ALU.add)


def rsqrt(c, out, in_, scale, bias):
    P = c.P
    P.act(out, in_, AF.Ln, bias=bias, scale=scale)
    P.act(out, out, AF.Exp, scale=-0.5)


def sigmoid_(c, out, in_):
    P = c.P
    P.act(out, in_, AF.Exp, scale=-1.0)
    P.act(out, out, AF.Ln, bias=1.0)
    P.act(out, out, AF.Exp, scale=-1.0)


def silu_(c, out, in_):
    sigmoid_(c, out, in_)
    c.P.tt("dve", out, in_, out, ALU.mult)


def rms_stats(c, A, g, tag):
    P = c.P
    col0, ncol = GROUPS[g]
    sq = A["sq"]
    P.act(R(sq[:, :, 0:ncol], "sq"), R(c.xT[:, :, col0:col0 + ncol], xkey(g)), AF.Square)
    pi = c.next_ps()
    for k in range(KC):
        P.mm(R(c.ps[pi][:, 0:ncol], ("ps", pi)), c.onesb, R(sq[:, k, 0:ncol], "sq"), start=(k == 0), stop=(k == KC - 1))
    rstd = A["rstd"]
    rsqrt(c, R(rstd[:, 0:ncol], "rstd"), R(c.ps[pi][:, 0:ncol], ("ps", pi)), 1.0 / D, EPS)
    return rstd


def norm_phase(c, l, out_fn):
    P = c.P
    A_ = c.Arena()
    A = {"sq": A_.bf16(KC * 512).rearrange("p (k t) -> p k t", k=KC), "rstd": A_.f32(512),
         "tmp": [A_.f32(512) for _ in range(2)], "tmps": A_.f32(KC * NS).rearrange("p (k t) -> p k t", k=KC)}
    for g in range(5):
        col0, ncol = GROUPS[g]
        rstd = rms_stats(c, A, g, "n")
        if g < 4:
            for k in range(KC):
                tmp = A["tmp"][k % 2]
                tk = ("ntmp", k % 2)
                P.tt("dve", R(tmp[:, 0:ncol], tk), R(c.xT[:, k, col0:col0 + ncol], xkey(g)), R(rstd[:, 0:ncol], "rstd"), ALU.mult)
                P.act(out_fn(g, k, ncol), R(tmp[:, 0:ncol], tk), AF.Identity,
                      bias=R(c.ada[:, k, 0:1], "ada"), scale=R(c.m1[:, k, 0:1], "m1"))
        else:
            ts_ = A["tmps"]
            P.tt("dve", R(ts_[:], "ntmps"), R(c.xT[:, :, col0:col0 + ncol], xkey(g)),
                 R(rstd[:, 0:ncol].unsqueeze(1).to_broadcast([128, KC, NS]), "rstd"), ALU.mult)
            v4 = ts_[:].rearrange("p k (s i) -> p k s i", i=SQ)
            m1b = c.m1[:, :, 1:17].unsqueeze(3).to_broadcast([128, KC, NSQ, SQ])
            shb = c.ada[:, 0:8, 1:17].unsqueeze(3).to_broadcast([128, KC, NSQ, SQ])
            P.tt("dve", R(v4, "ntmps"), R(v4, "ntmps"), R(m1b, "m1"), ALU.mult)
            for k in range(KC):
                o = out_fn(g, k, ncol)
                P.tt("dve", (o[0].rearrange("p (s i) -> p s i", i=SQ), o[1]), R(v4[:, k], "ntmps"), R(shb[:, k], "ada"), ALU.add)
    P.barrier()


def proj(c, pi, wt, wk, wc0, m, g, ncol_override=None, cols=None):
    P = c.P
    col0, ncol = GROUPS[g] if cols is None else cols
    for k in range(KC):
        P.mm(R(c.ps[pi][0:m, 0:ncol], ("ps", pi)), R(wt[:, k, wc0:wc0 + m], wk), R(c.hnT[:, k, col0:col0 + ncol], hkey(g)),
             start=(k == 0), stop=(k == KC - 1))


def outproj(c, l, g, mix_fn, nchunk, kpart, cols=None):
    P = c.P
    col0, ncol = GROUPS[g] if cols is None else cols
    for dc in range(KC):
        pi = c.next_ps()
        for mc in range(nchunk):
            P.mm(R(c.ps[pi][:, 0:ncol], ("ps", pi)), R(c.wo[0:kpart, mc, dc * 128:(dc + 1) * 128], "wo"), mix_fn(mc),
                 start=(mc == 0), stop=(mc == nchunk - 1))
        xs = c.xT[:, dc, col0:col0 + ncol]
        if g < 4:
            P.stt("dve", R(xs, xkey(g)), R(c.ps[pi][:, 0:ncol], ("ps", pi)), R(c.g1[:, dc, 0:1], "g1"), R(xs, xkey(g)), ALU.mult, ALU.add)
        else:
            x3 = xs.rearrange("p (s i) -> p s i", i=SQ)
            p3 = c.ps[pi][:, 0:ncol].rearrange("p (s i) -> p s i", i=SQ)
            g1b = c.g1[:, dc, 1:17].unsqueeze(2).to_broadcast([128, NSQ, SQ])
            tmp = c.optmp
            P.tt("dve", R(tmp, "optmp"), R(p3, ("ps", pi)), R(g1b, "g1"), ALU.mult)
            P.tt("dve", R(x3, xkey(g)), R(x3, xkey(g)), R(tmp, "optmp"), ALU.add)


def load_wo(c, l, r0, nchunk, kpart):
    P = c.P
    src = c.w_out[l][r0:r0 + nchunk * kpart, :].rearrange("(j p) d -> p j d", p=kpart)
    P.dma("pool", c.wo[0:kpart, 0:nchunk, :], src, writes=["wo"])


def layer(c, l):
    P = c.P
    ada_vectors(c, l)
    norm_phase(c, l, lambda g, k, ncol: R(c.hnT[:, k, GROUPS[g][0]:GROUPS[g][0] + ncol], hkey(g)))
    if "A" in c.phases:
        branch_a(c, l)
    if "B" in c.phases:
        branch_b(c, l)
    if "C" in c.phases:
        branch_c(c, l)


def branch_a(c, l):
    P = c.P
    A_ = c.Arena()
    CI = A_.f32(2 * (2 + T)).rearrange("p (j t) -> p j t", j=2)
    CIs = A_.f32(2 * NSQ * 6).rearrange("p (j s t) -> p j s t", j=2, s=NSQ)
    tmpx = A_.f32(512); sz = A_.f32(512); acc = A_.f32(512); tz = A_.f32(512)
    mixA = A_.bf16(2 * 512).rearrange("p (j t) -> p j t", j=2)
    c.optmp = A_.f32(NS).rearrange("p (s i) -> p s i", i=SQ)
    sin_ = A_.f32(256)
    gat = A_.f32(2 * 34).rearrange("p (j t) -> p j t", j=2)
    outa = A_.f32(256)
    load_wo(c, l, 0, 2, 128)
    tiles = []
    for t0 in (0, 384, 768):
        ncols = min(384, 1024 - t0)
        tiles.append(c.wload(c.w_in[l][:, t0:t0 + ncols], ncols))

    def wsel(col):
        ti = col // 384
        return tiles[ti][0], tiles[ti][1], col - ti * 384

    P.memset("dve", R(CI[:, :, 0:2], "CIh"), 0.0)
    P.dma("sp", sin_[0:NSQ * 2, :], c.sca[l], writes=["sin"])
    for j in range(2):
        pi = c.next_ps()
        P.transpose(R(c.ps[pi][:, 0:32], ("ps", pi)), R(sin_[0:32, j * 128:(j + 1) * 128], "sin"), R(c.cstf[0:32, 0:32], "cstf"))
        P.copy("act", R(CIs[:, j, :, 0:2], ("CIs", j)), R(c.ps[pi][:, 0:32].rearrange("p (s r) -> p s r", r=2), ("ps", pi)))
    for g in range(5):
        col0, ncol = GROUPS[g]
        for j in range(2):
            p0 = c.next_ps(); wt, wk, wc = wsel(j * 128); proj(c, p0, wt, wk, wc, 128, g)
            p1 = c.next_ps(); wt, wk, wc = wsel(256 + j * 128); proj(c, p1, wt, wk, wc, 128, g)
            P.copy("act", R(tmpx[:, 0:ncol], "tmpx"), R(c.ps[p0][:, 0:ncol], ("ps", p0)))
            vb = V_CAW + (l * 3) * 2 + j
            w0 = R(c.vec[:, vb:vb + 1], "vec"); w1 = R(c.vec[:, vb + 2:vb + 3], "vec"); w2 = R(c.vec[:, vb + 4:vb + 5], "vec")
            if g < 4:
                ck = ("CI", j, g)
                P.tt("dve", R(CI[:, j, 2 + col0:2 + col0 + ncol], ck), R(tmpx[:, 0:ncol], "tmpx"), R(c.ps[p1][:, 0:ncol], ("ps", p1)), ALU.mult)
                rd = [ck, ("CI", j, g - 1), "CIh"]
                P.ts("dve", R(acc[:, 0:ncol], "acc"), (CI[:, j, col0 + 2:col0 + 2 + ncol], rd), w2, None, ALU.mult)
                P.stt("dve", R(acc[:, 0:ncol], "acc"), (CI[:, j, col0 + 1:col0 + 1 + ncol], rd), w1, R(acc[:, 0:ncol], "acc"), ALU.mult, ALU.add)
                P.stt("dve", R(acc[:, 0:ncol], "acc"), (CI[:, j, col0:col0 + ncol], rd), w0, R(acc[:, 0:ncol], "acc"), ALU.mult, ALU.add)
                accv = acc[:, 0:ncol]
            else:
                ck = ("CIs", j)
                P.tt("dve", R(CIs[:, j, :, 2:6], ck), R(tmpx[:, 0:ncol].rearrange("p (s i) -> p s i", i=SQ), "tmpx"),
                     R(c.ps[p1][:, 0:ncol].rearrange("p (s i) -> p s i", i=SQ), ("ps", p1)), ALU.mult)
                a3 = acc[:, 0:ncol].rearrange("p (s i) -> p s i", i=SQ)
                P.ts("dve", R(a3, "acc"), R(CIs[:, j, :, 2:6], ck), w2, None, ALU.mult)
                P.stt("dve", R(a3, "acc"), R(CIs[:, j, :, 1:5], ck), w1, R(a3, "acc"), ALU.mult, ALU.add)
                P.stt("dve", R(a3, "acc"), R(CIs[:, j, :, 0:4], ck), w0, R(a3, "acc"), ALU.mult, ALU.add)
                accv = acc[:, 0:ncol]
            p2 = c.next_ps(); wt, wk, wc = wsel(512 + j * 128); proj(c, p2, wt, wk, wc, 128, g)
            p3 = c.next_ps(); wt, wk, wc = wsel(768 + j * 128); proj(c, p3, wt, wk, wc, 128, g)
            silu_(c, R(sz[:, 0:ncol], "sz"), R(c.ps[p3][:, 0:ncol], ("ps", p3)))
            P.tt("dve", R(tz[:, 0:ncol], "tz"), R(accv, "acc"), R(sz[:, 0:ncol], "sz"), ALU.mult)
            P.tt("dve", R(mixA[:, j, 0:ncol], ("mixA", j)), R(tz[:, 0:ncol], "tz"), R(c.ps[p2][:, 0:ncol], ("ps", p2)), ALU.mult)
        outproj(c, l, g, lambda mc: R(mixA[:, mc, 0:GROUPS[g][1]], ("mixA", mc)), 2, 128)
    for j in range(2):
        P.copy("act", R(gat[:, j, 0:2], ("gat", j)), R(CI[:, j, T:T + 2], ("CI", j, 3)))
        P.copy("act", R(gat[:, j, 2:34].rearrange("p (s r) -> p s r", r=2), ("gat", j)), R(CIs[:, j, :, 4:6], ("CIs", j)))
        pi = c.next_ps()
        P.transpose(R(c.ps[pi][0:34, 0:128], ("ps", pi)), R(gat[:, j, :], ("gat", j)), c.ident)
        P.copy("dve", R(outa[0:34, j * 128:(j + 1) * 128], "outa"), R(c.ps[pi][0:34, 0:128], ("ps", pi)))
    P.dma("sp", c.ca_p[l], outa[0:2, :], reads=["outa"], final=True)
    P.dma("sp", c.ca_s[l], outa[2:34, :], reads=["outa"], final=True)
    P.barrier()


def final(c):
    P = c.P
    ada_vectors(c, DEPTH)
    A_ = c.Arena()
    yT = A_.f32(KC * 512).rearrange("p (k t) -> p k t", k=KC)
    A = {"sq": A_.bf16(KC * 512).rearrange("p (k t) -> p k t", k=KC), "rstd": A_.f32(512),
         "tmp": [A_.f32(512) for _ in range(2)], "tmps": A_.f32(KC * NS).rearrange("p (k t) -> p k t", k=KC)}
    ystg = [A_.f32(D) for _ in range(2)]
    si = 0
    for g in range(5):
        col0, ncol = GROUPS[g]
        rstd = rms_stats(c, A, g, "f")
        yk = ("yT",)
        if g < 4:
            for k in range(KC):
                tmp = A["tmp"][k % 2]; tk = ("ntmp", k % 2)
                P.tt("dve", R(tmp[:, 0:ncol], tk), R(c.xT[:, k, col0:col0 + ncol], xkey(g)), R(rstd[:, 0:ncol], "rstd"), ALU.mult)
                P.act(R(yT[:, k, 0:ncol], "yT"), R(tmp[:, 0:ncol], tk), AF.Identity,
                      bias=R(c.ada[:, k, 0:1], "ada"), scale=R(c.m1[:, k, 0:1], "m1"))
        else:
            ts_ = A["tmps"]
            P.tt("dve", R(ts_[:], "ntmps"), R(c.xT[:, :, col0:col0 + ncol], xkey(g)),
                 R(rstd[:, 0:ncol].unsqueeze(1).to_broadcast([128, KC, NS]), "rstd"), ALU.mult)
            v4 = ts_[:].rearrange("p k (s i) -> p k s i", i=SQ)
            m1b = c.m1[:, :, 1:17].unsqueeze(3).to_broadcast([128, KC, NSQ, SQ])
            shb = c.ada[:, 0:8, 1:17].unsqueeze(3).to_broadcast([128, KC, NSQ, SQ])
            P.tt("dve", R(v4, "ntmps"), R(v4, "ntmps"), R(m1b, "m1"), ALU.mult)
            P.tt("dve", R(yT[:, :, 0:NS].rearrange("p k (s i) -> p k s i", i=SQ), "yT"), R(v4, "ntmps"), R(shb, "ada"), ALU.add)
        for tt in range((ncol + 127) // 128):
            rows = min(128, ncol - tt * 128)
            stg = ystg[si % 2]; sk = ("ystg", si % 2); si += 1
            for half in range(2):
                pi = c.next_ps()
                for kk in range(4):
                    k = half * 4 + kk
                    P.transpose(R(c.ps[pi][0:rows, kk * 128:(kk + 1) * 128], ("ps", pi)),
                                R(yT[:, k, tt * 128:tt * 128 + rows], "yT"), c.ident)
                P.copy("act" if half == 0 else "dve", R(stg[0:rows, half * 512:(half + 1) * 512], sk), R(c.ps[pi][0:rows, :], ("ps", pi)))
            if g < 4:
                dst = c.yp[col0 + tt * 128:col0 + tt * 128 + rows, :]
            else:
                dst = c.ys
            P.dma("sp", dst, stg[0:rows, :], reads=[sk], final=True)


_PHASES = ("A", "B", "C")
_NC_CACHE = {}


def _host_consts():
    cst = np.zeros((128, NCST), np.float32)
    cst[:, C_IDENT:C_IDENT + 128] = np.eye(128, dtype=np.float32)
    for m in range(64):
        cst[64 + m, C_SHIFT + m] = 1.0
    cst[:, C_ONES:C_ONES + 128] = 1.0
    k = np.arange(128)[:, None]
    q = np.arange(128)[None, :]
    cst[:, C_MPREV:C_MPREV + 128] = np.where(k >= q, 0.0, NEGM)
    cst[:, C_MCUR:C_MCUR + 128] = np.where(k <= q, 0.0, NEGM)
    cst[0:64, C_BLK:C_BLK + 64] = 1.0
    cst[64:128, C_BLK + 64:C_BLK + 128] = 1.0
    for h in range(6):
        cst[h, C_EH + h * 64:C_EH + (h + 1) * 64] = 1.0
    cst[0:64, C_OFFD:C_OFFD + 64] = 1.0 - np.eye(64, dtype=np.float32)
    j64 = np.arange(64)[:, None]
    i64 = np.arange(64)[None, :]
    cst[0:64, C_MSEQ:C_MSEQ + 64] = np.where((j64 // 4 == i64 // 4) & (j64 <= i64), 0.0, NEGM)
    cst[0:64, C_SEQM:C_SEQM + 16] = (j64 // 4 == np.arange(16)[None, :]).astype(np.float32)
    cst[:, C_SCAN64:C_SCAN64 + 128] = (np.arange(128) % 64 != 0).astype(np.float32)[None, :]
    cst[:, C_SCAN4:C_SCAN4 + 64] = (np.arange(64) % 4 != 0).astype(np.float32)[None, :]
    cst[0:64, C_MDIAG:C_MDIAG + 64] = np.where(j64 == i64, 0.0, NEGM)
    cst[:, C_MS128:C_MS128 + 4] = np.where(np.arange(128)[:, None] >= np.arange(4)[None, :], 0.0, NEGM)
    half = 8
    inv_freq = (500000.0 ** (-np.arange(half, dtype=np.float32) * np.float32(2.0 / 16))).astype(np.float32)
    rope = np.zeros((128, 17, 16), np.float32)
    for tt in range(17):
        if tt < 16:
            pos = (tt * 128 + np.arange(128)).astype(np.float32)
        else:
            pos = (T + (np.arange(128) % SQ)).astype(np.float32)
        ang = pos[:, None] * inv_freq[None, :]
        rope[:, tt, 0:8] = np.cos(ang)
        rope[:, tt, 8:16] = np.sin(ang)
    return cst, rope


def _fm(v):
    v = np.asarray(v, np.float32)
    return np.ascontiguousarray(v.reshape(-1, 128).T)


def _host_vecT(b_ada, norm_w, final_norm_w, b_ada_final, conv_a_w, conv_b_w, gdn_norm_w, a_log, dt_bias):
    vt = np.zeros((128, NV), np.float32)
    for l in range(DEPTH):
        vt[:, V_BADA + l * 24:V_BADA + (l + 1) * 24] = _fm(b_ada[l])
        vt[:, V_NORMW + l * 8:V_NORMW + (l + 1) * 8] = _fm(norm_w[l])
        for tap in range(3):
            vt[:, V_CAW + (l * 3 + tap) * 2:V_CAW + (l * 3 + tap) * 2 + 2] = _fm(conv_a_w[l, tap])
        for tap in range(4):
            vt[:, V_CBW + (l * 4 + tap) * 9:V_CBW + (l * 4 + tap) * 9 + 9] = _fm(conv_b_w[l, tap])
        vt[:, V_GNW + l] = np.tile(np.asarray(gdn_norm_w[l], np.float32), 2)
        vt[0:6, V_ALOG + l] = a_log[l]
        vt[0:6, V_DTB + l] = dt_bias[l]
    vt[:, V_FNORM:V_FNORM + 8] = _fm(final_norm_w)
    vt[:, V_BADAF:V_BADAF + 16] = _fm(b_ada_final)
    return vt


def kernel(x_prompt, x_sample, state_conv_a, state_conv_b, state_gdn, cache_kv_w128, cache_kv_w512,
           cache_kv_w2048, c_prompt, c_sample, w_in, w_out, w_ada, b_ada, norm_w, conv_a_w, conv_b_w,
           a_log, dt_bias, gdn_norm_w, final_norm_w, w_ada_final, b_ada_final, _phases=None, _depth=DEPTH):
    phases = tuple(_phases) if _phases is not None else _PHASES
    f = lambda a: np.ascontiguousarray(np.asarray(a, dtype=np.float32))
    key = (phases, _depth)
    if key not in _NC_CACHE:
        _NC_CACHE[key] = build_program(phases, _depth)
    nc = _NC_CACHE[key]
    cst, rope = _host_consts()
    vt = _host_vecT(f(b_ada), f(norm_w), f(final_norm_w), f(b_ada_final), f(conv_a_w), f(conv_b_w), f(gdn_norm_w),
                    f(a_log), f(dt_bias))
    w_in, w_out, w_ada, w_adaf = f(w_in), f(w_out), f(w_ada), f(w_ada_final)
    x_prompt, x_sample = f(x_prompt), f(x_sample)
    c_prompt, c_sample = f(c_prompt), f(c_sample)
    sca, scb, sgd = f(state_conv_a), f(state_conv_b), f(state_gdn)
    k128, k512, k2048 = f(cache_kv_w128), f(cache_kv_w512), f(cache_kv_w2048)
    in_maps = []
    for i in range(NCORE):
        ss = slice(i * NSQ, (i + 1) * NSQ)
        call = np.concatenate([c_prompt[i:i + 1], c_sample[ss]], axis=0)
        cT = np.ascontiguousarray(call.reshape(17, KC, 128).transpose(2, 1, 0))
        in_maps.append({
            "xp": x_prompt[i], "xs": np.ascontiguousarray(x_sample[ss].reshape(NS, D)), "cT": cT,
            "sca": np.ascontiguousarray(sca[:, ss].reshape(DEPTH, NSQ * 2, 256)),
            "scb": np.ascontiguousarray(scb[:, ss].reshape(DEPTH, NSQ * 3, 1152)),
            "sgd": np.ascontiguousarray(sgd[:, ss]),
            "ck128": np.ascontiguousarray(k128[:, ss].reshape(DEPTH, NSQ, 128, 256)),
            "ck512": np.ascontiguousarray(k512[:, ss].reshape(DEPTH, NSQ, 512, 256)),
            "ck2048": np.ascontiguousarray(k2048[:, ss].reshape(DEPTH, NSQ, 2048, 256)),
            "w_in": w_in, "w_out": w_out, "w_ada": w_ada, "w_adaf": w_adaf,
            "vecT": vt, "cst": cst, "rope": rope,
        })
    res = run_bass_kernel_spmd(nc, in_maps, core_ids=list(range(NCORE)))
    rs = res.results
    cat = lambda name: np.stack([r[name] for r in rs], axis=0)
    y_p = cat("yp")
    y_s = cat("ys").reshape(NCORE * NSQ, SQ, D)
    ca_p = cat("ca_p").transpose(1, 0, 2, 3)
    ca_s = cat("ca_s").reshape(NCORE, DEPTH, NSQ, 2, 256).transpose(1, 0, 2, 3, 4).reshape(DEPTH, NCORE * NSQ, 2, 256)
    cb_p = cat("cb_p").transpose(1, 0, 2, 3)
    cb_s = cat("cb_s").reshape(NCORE, DEPTH, NSQ, 3, 1152).transpose(1, 0, 2, 3, 4).reshape(DEPTH, NCORE * NSQ, 3, 1152)
    gd_p = cat("gd_p").transpose(1, 0, 2, 3, 4)
    gd_s = cat("gd_s").transpose(1, 0, 2, 3, 4, 5).reshape(DEPTH, NCORE * NSQ, 6, 64, 64)
    outs = [y_p, y_s, ca_p, ca_s, cb_p, cb_s, gd_p, gd_s]
    for gi, win in enumerate((128, 512, 2048)):
        name = "kv%d" % win
        kp = cat(name + "_p").transpose(1, 0, 2, 3).reshape(DEPTH, NCORE, win, 2, 2, 64)
        ks = cat(name + "_s").reshape(NCORE, DEPTH, NSQ, SQ, 256).transpose(1, 0, 2, 3, 4).reshape(DEPTH, NCORE * NSQ, SQ, 2, 2, 64)
        outs += [kp, ks]
    return tuple(np.ascontiguousarray(o.astype(np.float32)) for o in outs)


def branch_b(c, l):
    P = c.P
    A_ = c.Arena()
    f32 = A_.f32

    def t3(n_mid, n_in, dt=F32):
        a = f32(n_mid * n_in)[0:64]
        return a.rearrange("p (a b) -> p a b", a=n_mid)

    _pre = f32(131)
    pre = [_pre, _pre]
    pres = f32(NSQ * 7).rearrange("p (s t) -> p s t", t=7)
    halo = f32(27).rearrange("p (b t) -> p b t", t=3)
    acc = f32(128); act_ = f32(128); rinv = f32(128)
    sqb = A_.bf16(128)
    nrm = act_
    hq_raw = f32(768)[0:64]; hk_raw = f32(768)[0:64]
    HQ = hq_raw.rearrange("p (a b) -> p a b", a=6)
    HK = hk_raw.rearrange("p (a b) -> p a b", a=6)
    HV = t3(6, 128, F32R)
    HZ = t3(6, 128)
    szt = f32(128)
    G = f32(128); BETA = f32(128); GC = f32(128); EG = f32(128); DL = f32(128); NGC = f32(128); tmpd = G
    EGL = f32(16); nA = f32(1)
    EGLB = t3(6, 16)
    OT = t3(6, 128)
    mixB = A_.bf16(6 * 128)[0:64].rearrange("p (h t) -> p h t", h=6)
    sqo = hk_raw[:, 0:384].bitcast(BF16).rearrange("p (h t) -> p h t", h=6)
    rso = hq_raw.rearrange("p (a b) -> p a b", a=6)
    S = t3(6, 64, F32R)
    SS = t3(NSQ, 64)
    KDblk = t3(NSQ, 64, F32R)
    U = {}
    for nm in ("decT", "LT0", "kbT", "kdT", "vbT", "LT", "L", "P0", "P1", "PT0", "PT1", "X0", "X1", "Rr", "VN"):
        U[nm] = t3(3, 64)
    U["RT"] = U["LT0"]
    SC = [{nm: t3(3, 64) for nm in ("kbgT", "qgT", "aT", "TinvT", "VB", "KD")} for _ in range(2)]
    stg = f32(384)
    gatB = f32(9 * 51).rearrange("p (b t) -> p b t", b=9)
    c.optmp = f32(NS).rearrange("p (s i) -> p s i", i=SQ)

    def F(ap):
        return ap

    identr = R(c.cstf[0:64, 0:64], "cstf")
    shiftr = R(c.cstf[:, C_SHIFT:C_SHIFT + 64], "cstf")
    ident64 = R(c.cstf[0:64, 0:64], "cstf")
    identb64 = R(c.cstb[0:64, 0:64], "cstb")

    def EH(h):
        return R(c.cstf[0:6, C_EH + h * 64:C_EH + (h + 1) * 64], "cstf")

    load_wo(c, l, 256, 6, 64)
    wt = [c.wload(c.w_in[l][:, 1024 + i * 384:1024 + (i + 1) * 384], 384) for i in range(4)]
    P.dma("pool", c.wab[:], c.w_in[l][:, 2560:2572].rearrange("(k p) c -> p k c", p=128), writes=["wab"])

    P.copy("dve", R(S[:], "S"), R(c.zero[0:64, 0:384].rearrange("p (h t) -> p h t", h=6), "zero"))
    P.memset("dve", R(halo[:], "halo"), 0.0)
    P.act(R(nA[0:6, :], "nA"), R(c.vec[0:6, V_ALOG + l:V_ALOG + l + 1], "vec"), AF.Exp)
    P.ts("dve", R(nA[0:6, :], "nA"), R(nA[0:6, :], "nA"), -1.0, None, ALU.mult)
    for b3 in range(3):
        P.dma("sp", stg[0:NSQ * 3, :], c.scb[l][:, b3 * 384:(b3 + 1) * 384], writes=["stgB"])
        for bb in range(3):
            blk = b3 * 3 + bb
            pi = c.next_ps()
            P.transpose(R(c.ps[pi][:, 0:48], ("ps", pi)), R(stg[0:48, bb * 128:(bb + 1) * 128], "stgB"), R(c.cstf[0:48, 0:48], "cstf"))
            P.copy("act", R(gatB[:, blk, 0:48], ("gatB", blk)), R(c.ps[pi][:, 0:48], ("ps", pi)))

    groups = [(gb * 128, 128, gb // 4) for gb in range(16)] + [(T, NS, 4)]
    for gi, (col0, ncol, g5) in enumerate(groups):
        smp = (g5 == 4)
        nch = 1 if smp else 2
        cols = (col0, ncol)
        for blk in range(9):
            typ, sub = blk // 3, blk % 3
            wtile, wkey = wt[typ]
            pi = c.next_ps()
            proj(c, pi, wtile, wkey, sub * 128, 128, g5, cols=cols)
            vb = V_CBW + (l * 4) * 9 + blk
            wtap = [R(c.vec[:, vb + 9 * tap:vb + 9 * tap + 1], "vec") for tap in range(4)]
            if not smp:
                pr = pre[0]; pk = ("pre", 0)
                P.copy("dve", R(pr[:, 0:3], pk), R(halo[:, blk, :], ("halo", blk)))
                P.copy("act", R(pr[:, 3:3 + ncol], pk), R(c.ps[pi][:, 0:ncol], ("ps", pi)))
                P.copy("dve", R(halo[:, blk, :], ("halo", blk)), R(pr[:, ncol:ncol + 3], pk))
                P.ts("dve", R(acc[:, 0:ncol], "accB"), R(pr[:, 3:3 + ncol], pk), wtap[3], None, ALU.mult)
                for tap in range(3):
                    P.stt("dve", R(acc[:, 0:ncol], "accB"), R(pr[:, tap:tap + ncol], pk), wtap[tap], R(acc[:, 0:ncol], "accB"), ALU.mult, ALU.add)
            else:
                pk = ("pres",)
                P.copy("dve", R(pres[:, :, 0:3], pk), R(gatB[:, blk, 0:48].rearrange("p (s r) -> p s r", r=3), ("gatB", blk)))
                P.copy("act", R(pres[:, :, 3:7], pk), R(c.ps[pi][:, 0:ncol].rearrange("p (s i) -> p s i", i=SQ), ("ps", pi)))
                a3 = acc[:, 0:ncol].rearrange("p (s i) -> p s i", i=SQ)
                P.ts("dve", R(a3, "accB"), R(pres[:, :, 3:7], pk), wtap[3], None, ALU.mult)
                for tap in range(3):
                    P.stt("dve", R(a3, "accB"), R(pres[:, :, tap:tap + 4], pk), wtap[tap], R(a3, "accB"), ALU.mult, ALU.add)
                P.copy("act", R(gatB[:, blk, 3:51].rearrange("p (s r) -> p s r", r=3), ("gatB", blk)), R(pres[:, :, 4:7], pk))
                P.copy("act", R(gatB[:, blk, 0:3], ("gatB", blk)), R(halo[:, blk, :], ("halo", blk)))
            silu_(c, R(act_[:, 0:ncol], "actB"), R(acc[:, 0:ncol], "accB"))
            if typ < 2:
                P.tt("dve", R(sqb[:, 0:ncol], "sqb"), R(act_[:, 0:ncol], "actB"), R(act_[:, 0:ncol], "actB"), ALU.mult)
                p2 = c.next_ps()
                P.mm(R(c.ps[p2][:, 0:ncol], ("ps", p2)), R(c.cstb[:, C_BLK:C_BLK + 128], "cstb"), R(sqb[:, 0:ncol], "sqb"))
                rsqrt(c, R(rinv[:, 0:ncol], "rinvB"), R(c.ps[p2][:, 0:ncol], ("ps", p2)), 1.0, 1e-6)
                P.stt("dve", R(nrm[:, 0:ncol], "actB"), R(act_[:, 0:ncol], "actB"), 0.125 if typ == 0 else 1.0,
                      R(rinv[:, 0:ncol], "rinvB"), ALU.mult, ALU.mult)
            H = (HQ, HK, HV)[typ]
            hk = ("H", typ)
            P.copy("act", R(H[:, 2 * sub, 0:ncol], hk), R(nrm[0:64, 0:ncol], "actB"))
            p3 = c.next_ps()
            P.mm(R(c.ps[p3][0:64, 0:ncol], ("ps", p3)), shiftr, R(nrm[:, 0:ncol], "actB"))
            P.copy("act", R(H[:, 2 * sub + 1, 0:ncol], hk), R(c.ps[p3][0:64, 0:ncol], ("ps", p3)))
        for sub in range(3):
            pi = c.next_ps()
            proj(c, pi, wt[3][0], wt[3][1], sub * 128, 128, g5, cols=cols)
            silu_(c, R(szt[:, 0:ncol], "szt"), R(c.ps[pi][:, 0:ncol], ("ps", pi)))
            P.copy("dve", R(HZ[:, 2 * sub, 0:ncol], "HZ"), R(F(szt[0:64, 0:ncol]), "szt"))
            p3 = c.next_ps()
            P.mm(R(c.ps[p3][0:64, 0:ncol], ("ps", p3)), shiftr, R(szt[:, 0:ncol], "szt"))
            P.copy("act", R(HZ[:, 2 * sub + 1, 0:ncol], "HZ"), R(c.ps[p3][0:64, 0:ncol], ("ps", p3)))
        pa = c.next_ps(); pb = c.next_ps()
        for k in range(KC):
            P.mm(R(c.ps[pa][0:6, 0:ncol], ("ps", pa)), R(c.wab[:, k, 0:6], "wab"), R(c.hnT[:, k, col0:col0 + ncol], hkey(g5)),
                 start=(k == 0), stop=(k == KC - 1))
        for k in range(KC):
            P.mm(R(c.ps[pb][0:6, 0:ncol], ("ps", pb)), R(c.wab[:, k, 6:12], "wab"), R(c.hnT[:, k, col0:col0 + ncol], hkey(g5)),
                 start=(k == 0), stop=(k == KC - 1))
        rk = "rowsB"
        P.act(R(G[0:6, 0:ncol], rk), R(c.ps[pa][0:6, 0:ncol], ("ps", pa)), AF.Exp, bias=R(c.vec[0:6, V_DTB + l:V_DTB + l + 1], "vec"))
        P.act(R(G[0:6, 0:ncol], rk), R(G[0:6, 0:ncol], rk), AF.Ln, bias=1.0)
        P.ts("dve", R(G[0:6, 0:ncol], rk), R(G[0:6, 0:ncol], rk), R(nA[0:6, 0:1], "nA"), None, ALU.mult)
        sigmoid_(c, R(BETA[0:6, 0:ncol], rk), R(c.ps[pb][0:6, 0:ncol], ("ps", pb)))
        scm = c.cstf[0:6, C_SCAN4:C_SCAN4 + 64] if smp else c.cstf[0:6, C_SCAN64:C_SCAN64 + 128]
        gco, go, sco = GC[0:6, 0:ncol], G[0:6, 0:ncol], scm
        P.op("dve", lambda e, gco=gco, go=go, sco=sco: e.tensor_tensor_scan(gco, sco, go, 0.0, ALU.mult, ALU.add), reads=[rk, "cstf"], writes=[rk])
        P.act(R(EG[0:6, 0:ncol], rk), R(GC[0:6, 0:ncol], rk), AF.Exp)
        P.ts("dve", R(NGC[0:6, 0:ncol], rk), R(GC[0:6, 0:ncol], rk), -1.0, None, ALU.mult)
        clen = SQ if smp else 64
        nseg = ncol // clen
        gc3 = GC[0:6, 0:ncol].rearrange("p (n t) -> p n t", t=clen)
        glb = gc3[:, :, clen - 1:clen].to_broadcast([6, nseg, clen])
        P.tt("dve", R(tmpd[0:6, 0:ncol].rearrange("p (n t) -> p n t", t=clen), rk), R(glb, rk), R(gc3, rk), ALU.subtract)
        P.act(R(DL[0:6, 0:ncol], rk), R(tmpd[0:6, 0:ncol], rk), AF.Exp)
        P.act(R(EGL[0:6, 0:nseg], rk), R(gc3[:, :, clen - 1], rk), AF.Exp)
        pe_ = c.next_ps()
        for h in range(6):
            P.mm(R(c.ps[pe_][0:64, h * 16:h * 16 + nseg], ("ps", pe_)), EH(h), R(EGL[0:6, 0:nseg], rk))
        P.copy("dve", R(EGLB[:, :, 0:nseg], "EGLB"), R(c.ps[pe_][0:64, 0:96].rearrange("p (h n) -> p h n", h=6)[:, :, 0:nseg], ("ps", pe_)))

        uk = lambda nm: ("U", nm)
        maskap = c.cstb[0:64, 704:768] if smp else c.cstb[0:64, C_MCUR:C_MCUR + 64]
        offd = c.cstf[0:64, C_OFFD:C_OFFD + 64].unsqueeze(1).to_broadcast([64, 3, 64])
        idb = c.cstf[0:64, 0:64].unsqueeze(1).to_broadcast([64, 3, 64])
        nlev = 1 if smp else 5
        v3 = lambda pi_, lo=0: R(c.ps[pi_][0:64, lo:lo + 192].rearrange("p (a b) -> p a b", a=3), ("ps", pi_))

        def pre1(ch, hb, st_):
            cc = slice(ch * 64, ch * 64 + 64)
            heads = [hb * 3 + i for i in range(3)]
            hs = slice(hb * 3, hb * 3 + 3)
            sck = lambda nm: ("SC", st_, nm)
            SCt = SC[st_]
            pd = c.next_ps()
            for i, h in enumerate(heads):
                o = R(c.ps[pd][0:64, i * 64:(i + 1) * 64], ("ps", pd))
                P.mm(o, identb64, R(maskap, "cstb"), start=True, stop=False)
                P.mm(o, EH(h), R(GC[0:6, cc], rk), start=False, stop=False)
                P.mm(o, R(NGC[0:6, cc], rk), EH(h), start=False, stop=True)
            P.act(R(U["decT"][:].rearrange("p a b -> p (a b)"), uk("decT")), R(c.ps[pd][0:64, 0:192], ("ps", pd)), AF.Exp)
            pbb = c.next_ps(); pb2 = c.next_ps()
            for qi, row in enumerate((BETA, EG, DL)):
                pq_ = pbb if qi < 2 else pb2
                for i, h in enumerate(heads):
                    P.mm(R(c.ps[pq_][0:64, ((qi % 2) * 3 + i) * 64:((qi % 2) * 3 + i + 1) * 64], ("ps", pq_)), EH(h), R(row[0:6, cc], rk))

            def bview(qi):
                pq_ = pbb if qi < 2 else pb2
                return v3(pq_, (qi % 2) * 192)
            P.tt("dve", R(U["kbT"][:], uk("kbT")), R(HK[:, hs, cc], ("H", 1)), bview(0), ALU.mult)
            P.tt("dve", R(U["vbT"][:], uk("vbT")), R(HV[:, hs, cc], ("H", 2)), bview(0), ALU.mult)
            P.tt("dve", R(SCt["kbgT"][:], sck("kbgT")), R(U["kbT"][:], uk("kbT")), bview(1), ALU.mult)
            P.tt("dve", R(SCt["qgT"][:], sck("qgT")), R(HQ[:, hs, cc], ("H", 0)), bview(1), ALU.mult)
            P.tt("dve", R(U["kdT"][:], uk("kdT")), R(HK[:, hs, cc], ("H", 1)), bview(2), ALU.mult)
            pk_ = c.next_ps()
            for i, h in enumerate(heads):
                P.mm(R(c.ps[pk_][0:64, i * 64:(i + 1) * 64], ("ps", pk_)), R(HK[:, h, cc], ("H", 1)), R(U["kbT"][:, i, :], uk("kbT")))
                P.mm(R(c.ps[pk_][0:64, 192 + i * 64:192 + (i + 1) * 64], ("ps", pk_)), R(HK[:, h, cc], ("H", 1)), R(HQ[:, h, cc], ("H", 0)))
            P.tt("dve", R(U["LT0"][:], uk("LT0")), v3(pk_, 0), R(U["decT"][:], uk("decT")), ALU.mult)
            P.tt("dve", R(U["LT"][:], uk("LT")), R(U["LT0"][:], uk("LT0")), R(offd, "cstf"), ALU.mult)
            P.tt("dve", R(SCt["aT"][:], sck("aT")), v3(pk_, 192), R(U["decT"][:], uk("decT")), ALU.mult)
            pl = c.next_ps()
            for i in range(3):
                P.transpose(R(c.ps[pl][0:64, i * 64:(i + 1) * 64], ("ps", pl)), R(U["LT0"][:, i, :], uk("LT0")), ident64)
            P.tt("dve", R(U["L"][:], uk("L")), v3(pl), R(offd, "cstf"), ALU.mult)
            P.stt("dve", R(U["X0"][:], uk("X0")), R(U["LT"][:], uk("LT")), -1.0, R(idb, "cstf"), ALU.mult, ALU.add)
            pv = c.next_ps()
            for i in range(3):
                P.transpose(R(c.ps[pv][0:64, i * 64:(i + 1) * 64], ("ps", pv)), R(U["vbT"][:, i, :], uk("vbT")), ident64)
                P.transpose(R(c.ps[pv][0:64, 192 + i * 64:192 + (i + 1) * 64], ("ps", pv)), R(U["kdT"][:, i, :], uk("kdT")), ident64)
            P.copy("act", R(SCt["VB"][:], sck("VB")), v3(pv, 0))
            P.copy("act", R(SCt["KD"][:], sck("KD")), v3(pv, 192))
            return {"P": "L", "PT": "LT", "X": "X0"}

        def neumann(st_, state, lev):
            sck = lambda nm: ("SC", st_, nm)
            Pc, PTc, Xc = state["P"], state["PT"], state["X"]
            Pn = "P%d" % (lev % 2); PTn = "PT%d" % (lev % 2); Xn = "X%d" % ((lev + 1) % 2)
            last = (lev == nlev - 1)
            pp = c.next_ps()
            for i in range(3):
                P.mm(R(c.ps[pp][0:64, i * 64:(i + 1) * 64], ("ps", pp)), R(U[PTc][:, i, :], uk(PTc)), R(U[Pc][:, i, :], uk(Pc)))
            P.copy("act", R(U[Pn][:], uk(Pn)), v3(pp))
            if not last:
                pt_ = c.next_ps()
                for i in range(3):
                    P.mm(R(c.ps[pt_][0:64, i * 64:(i + 1) * 64], ("ps", pt_)), R(U[Pc][:, i, :], uk(Pc)), R(U[PTc][:, i, :], uk(PTc)))
                P.copy("act", R(U[PTn][:], uk(PTn)), v3(pt_))
            px = c.next_ps()
            for i in range(3):
                P.mm(R(c.ps[px][0:64, i * 64:(i + 1) * 64], ("ps", px)), R(U[Pn][:, i, :], uk(Pn)), R(U[Xc][:, i, :], uk(Xc)))
            dst = R(SC[st_]["TinvT"][:], sck("TinvT")) if last else R(U[Xn][:], uk(Xn))
            P.tt("dve", dst, R(U[Xc][:], uk(Xc)), v3(px), ALU.add)
            state["P"], state["PT"], state["X"] = Pn, PTn, Xn

        def scan_steps(ch, hb, st_):
            cc = slice(ch * 64, ch * 64 + 64)
            heads = [hb * 3 + i for i in range(3)]
            hs = slice(hb * 3, hb * 3 + 3)
            sck = lambda nm: ("SC", st_, nm)
            SCt = SC[st_]

            def s_g():
                pr_ = c.next_ps()
                for i, h in enumerate(heads):
                    P.mm(R(c.ps[pr_][0:64, i * 64:(i + 1) * 64], ("ps", pr_)), R(SCt["kbgT"][:, i, :], sck("kbgT")), R(S[:, h, :], ("S", h)))
                P.tt("dve", R(U["Rr"][:], uk("Rr")), R(SCt["VB"][:], sck("VB")), v3(pr_), ALU.subtract)

            def s_h():
                pn = c.next_ps()
                for i in range(3):
                    P.mm(R(c.ps[pn][0:64, i * 64:(i + 1) * 64], ("ps", pn)), R(SCt["TinvT"][:, i, :], sck("TinvT")), R(U["Rr"][:, i, :], uk("Rr")))
                P.copy("act", R(U["VN"][:], uk("VN")), v3(pn))

            def s_i():
                po = c.next_ps()
                for i, h in enumerate(heads):
                    o = R(c.ps[po][0:64, i * 64:(i + 1) * 64], ("ps", po))
                    P.mm(o, R(S[:, h, :], ("S", h)), R(SCt["qgT"][:, i, :], sck("qgT")), start=True, stop=False)
                    P.mm(o, R(U["VN"][:, i, :], uk("VN")), R(SCt["aT"][:, i, :], sck("aT")), start=False, stop=True)
                P.copy("act", R(OT[:, hs, cc], "OT"), v3(po))

            def s_j():
                pS = c.next_ps()
                for i, h in enumerate(heads):
                    P.mm(R(c.ps[pS][0:64, i * 64:(i + 1) * 64], ("ps", pS)), R(SCt["KD"][:, i, :], sck("KD")), R(U["VN"][:, i, :], uk("VN")))
                for i, h in enumerate(heads):
                    P.stt("dve", R(S[:, h, :], ("S", h)), R(S[:, h, :], ("S", h)), R(EGLB[:, h, ch:ch + 1], "EGLB"),
                          R(c.ps[pS][0:64, i * 64:(i + 1) * 64], ("ps", pS)), ALU.mult, ALU.add)
            return [s_g, s_h, s_i, s_j]

        if not smp:
            units = [(ch, hb) for ch in range(nch) for hb in range(2)]
            st0 = pre1(units[0][0], units[0][1], 0)
            for lev in range(nlev):
                neumann(0, st0, lev)
            for ui in range(1, len(units)):
                st_ = ui % 2
                state = pre1(units[ui][0], units[ui][1], st_)
                steps = scan_steps(units[ui - 1][0], units[ui - 1][1], 1 - st_)
                for lev in range(nlev):
                    neumann(st_, state, lev)
                    if lev < len(steps):
                        steps[lev]()
                for fn in steps[nlev:]:
                    fn()
            for fn in scan_steps(units[-1][0], units[-1][1], (len(units) - 1) % 2):
                fn()
        else:
            ch = 0
            cc = slice(0, 64)
            for hb in range(2):
                heads = [hb * 3 + i for i in range(3)]
                st_ = 0
                sck = lambda nm: ("SC", 0, nm)
                SCt = SC[0]
                state = pre1(ch, hb, 0)
                for lev in range(nlev):
                    neumann(0, state, lev)
                for i, h in enumerate(heads):
                    P.dma("sp", SS[:], c.sgd[l][:, h].rearrange("s k v -> k s v"), writes=["SS"])
                    pr_ = c.next_ps()
                    for s_ in range(NSQ):
                        P.mm(R(c.ps[pr_][0:64, s_ * 4:(s_ + 1) * 4], ("ps", pr_)), R(SS[:, s_, :], "SS"),
                             R(SCt["kbgT"][:, i, s_ * 4:(s_ + 1) * 4], sck("kbgT")))
                    P.tt("dve", R(U["RT"][:, i, :], uk("LT0")), R(U["vbT"][:, i, :], uk("vbT")), R(c.ps[pr_][0:64, 0:64], ("ps", pr_)), ALU.subtract)
                    pq = c.next_ps()
                    P.transpose(R(c.ps[pq][0:64, 0:64], ("ps", pq)), R(U["RT"][:, i, :], uk("LT0")), ident64)
                    P.copy("act", R(U["Rr"][:, i, :], uk("Rr")), R(c.ps[pq][0:64, 0:64], ("ps", pq)))
                    pn = c.next_ps()
                    P.mm(R(c.ps[pn][0:64, 0:64], ("ps", pn)), R(SCt["TinvT"][:, i, :], sck("TinvT")), R(U["Rr"][:, i, :], uk("Rr")))
                    P.copy("act", R(U["VN"][:, i, :], uk("VN")), R(c.ps[pn][0:64, 0:64], ("ps", pn)))
                    po = c.next_ps()
                    P.mm(R(c.ps[po][0:64, 0:64], ("ps", po)), R(U["VN"][:, i, :], uk("VN")), R(SCt["aT"][:, i, :], sck("aT")), start=True, stop=False)
                    for s_ in range(NSQ):
                        P.mm(R(c.ps[po][0:64, s_ * 4:(s_ + 1) * 4], ("ps", po)), R(SS[:, s_, :], "SS"),
                             R(SCt["qgT"][:, i, s_ * 4:(s_ + 1) * 4], sck("qgT")), start=False, stop=(s_ == NSQ - 1))
                    P.copy("act", R(OT[:, h, cc], "OT"), R(c.ps[po][0:64, 0:64], ("ps", po)))
                    seqm = c.cstf[0:64, C_SEQM:C_SEQM + 16].unsqueeze(2).to_broadcast([64, NSQ, 64])
                    kdb = SCt["KD"][:, i, :].unsqueeze(1).to_broadcast([64, NSQ, 64])
                    P.tt("dve", R(KDblk[:], "KDblk"), R(kdb, sck("KD")), R(seqm, "cstf"), ALU.mult)
                    eglb = EGLB[:, h, 0:NSQ].unsqueeze(2).to_broadcast([64, NSQ, 64])
                    P.tt("dve", R(SS[:], "SS"), R(SS[:], "SS"), R(eglb, "EGLB"), ALU.mult)
                    for half in range(2):
                        pS = c.next_ps()
                        for s8 in range(8):
                            s_ = half * 8 + s8
                            P.mm(R(c.ps[pS][0:64, s8 * 64:(s8 + 1) * 64], ("ps", pS)), R(KDblk[:, s_, :], "KDblk"), R(U["VN"][:, i, :], uk("VN")))
                        P.tt("dve", R(SS[:, half * 8:half * 8 + 8, :], "SS"), R(SS[:, half * 8:half * 8 + 8, :], "SS"),
                             R(c.ps[pS][0:64, :].rearrange("p (s v) -> p s v", s=8), ("ps", pS)), ALU.add)
                    P.dma("sp", c.gd_s[l][:, h].rearrange("s k v -> k s v"), SS[:], reads=["SS"], final=True)
        P.tt("dve", R(sqo[:, :, 0:ncol], ("H", 1)), R(OT[:, :, 0:ncol], "OT"), R(OT[:, :, 0:ncol], "OT"), ALU.mult)
        for hb in range(2):
            pg = c.next_ps()
            for i in range(3):
                P.mm(R(c.ps[pg][0:64, i * 128:i * 128 + ncol], ("ps", pg)), R(c.cstb[0:64, C_ONES:C_ONES + 64], "cstb"), R(sqo[:, hb * 3 + i, 0:ncol], ("H", 1)))
            rsqrt(c, R(rso[:, hb * 3:hb * 3 + 3, 0:ncol], ("H", 0)),
                  R(c.ps[pg][0:64, 0:384].rearrange("p (a b) -> p a b", a=3)[:, :, 0:ncol], ("ps", pg)), 1.0 / 64, EPS)
        P.tt("dve", R(rso[:, :, 0:ncol], ("H", 0)), R(rso[:, :, 0:ncol], ("H", 0)), R(OT[:, :, 0:ncol], "OT"), ALU.mult)
        P.stt("dve", R(mixB[:, :, 0:ncol], "mixB"), R(rso[:, :, 0:ncol], ("H", 0)), R(c.vec[0:64, V_GNW + l:V_GNW + l + 1], "vec"),
              R(HZ[:, :, 0:ncol], "HZ"), ALU.mult, ALU.mult)
        outproj(c, l, g5, lambda mc: R(mixB[:, mc, 0:ncol], "mixB"), 6, 64, cols=cols)
    P.dma("sp", c.gd_p[l].rearrange("h k v -> k h v"), F(S[:]), reads=[("S", h) for h in range(6)], final=True)
    for b3 in range(3):
        for bb in range(3):
            blk = b3 * 3 + bb
            pi = c.next_ps()
            P.transpose(R(c.ps[pi][0:51, 0:128], ("ps", pi)), R(gatB[:, blk, :], ("gatB", blk)), c.ident)
            P.copy("dve", R(stg[0:51, bb * 128:(bb + 1) * 128], "stgB"), R(c.ps[pi][0:51, 0:128], ("ps", pi)))
        P.dma("sp", c.cb_p[l][:, b3 * 384:(b3 + 1) * 384], stg[0:3, :], reads=["stgB"], final=True)
        P.dma("sp", c.cb_s[l][:, b3 * 384:(b3 + 1) * 384], stg[3:51, :], reads=["stgB"], final=True)
    P.barrier()


CDIL = (1, 4, 16)
CWIN = (128, 512, 2048)


def _rope(c, buf3, rows, tt, rt, key):
    P = c.P
    nh = buf3.shape[1]
    x1 = buf3[:, :, 0:8]; x2 = buf3[:, :, 8:16]
    cos = c.ropet[0:rows, tt, 0:8].unsqueeze(1).to_broadcast([rows, nh, 8])
    sin = c.ropet[0:rows, tt, 8:16].unsqueeze(1).to_broadcast([rows, nh, 8])
    t = [r_[0:rows, 0:nh * 8].rearrange("p (h e) -> p h e", e=8) for r_ in rt]
    P.tt("dve", R(t[0], "ropet0"), R(x1, key), R(cos, "rope"), ALU.mult)
    P.tt("dve", R(t[1], "ropet1"), R(x2, key), R(sin, "rope"), ALU.mult)
    P.tt("dve", R(t[2], "ropet2"), R(x2, key), R(cos, "rope"), ALU.mult)
    P.tt("dve", R(t[3], "ropet3"), R(x1, key), R(sin, "rope"), ALU.mult)
    P.tt("dve", R(x1, key), R(t[0], "ropet0"), R(t[1], "ropet1"), ALU.subtract)
    P.tt("dve", R(x2, key), R(t[2], "ropet2"), R(t[3], "ropet3"), ALU.add)


def branch_c(c, l):
    P = c.P
    A_ = c.Arena(); f32 = A_.f32
    ksT = f32(384)[0:64].rearrange("p (h t) -> p h t", h=6)
    qsT = f32(384)[0:64].rearrange("p (h t) -> p h t", h=6)
    vtok = f32(384)[0:64].rearrange("p (g x) -> p g x", g=3)
    ZS = f32(768)[0:64].rearrange("p (h t) -> p h t", h=6)
    c.optmp = f32(NS).rearrange("p (s i) -> p s i", i=SQ)
    prefix = A_.off
    KT = A_.bf16(6 * T)[0:64].rearrange("p (h t) -> p h t", h=6)
    VS = A_.bf16(48 * 128).rearrange("p (n x) -> p n x", x=128)
    kv_raw = f32(768)
    kvst = kv_raw.rearrange("p (g x) -> p g x", g=3)
    qsb = f32(384)
    rt = [f32(48) for _ in range(4)]
    QTt = A_.bf16(6 * 128)[0:64].rearrange("p (h t) -> p h t", h=6)
    PT = [A_.bf16(512) for _ in range(3)]
    dtot = f32(256)[0:64].rearrange("p (h t) -> p h t", h=2)
    onorm = kv_raw[0:64].rearrange("p (h t) -> p h t", h=6)
    mixC = A_.bf16(768)[0:64].rearrange("p (h t) -> p h t", h=6)

    load_wo(c, l, 640, 6, 64)
    wk_t, wk_k = c.wload(c.w_in[l][:, 2956:3340], 384)
    wv_t, wv_k = c.wload(c.w_in[l][:, 3340:3724], 384)
    wq_t, wq_k = c.wload(c.w_in[l][:, 2572:2956], 384)
    wz_t, wz_k = c.wload(c.w_in[l][:, 3724:4108], 384)
    ident = c.ident

    def tokproj(pi, wt_, wk_, tt, rows, ncols=384, c0=0):
        col0 = tt * 128
        g5 = min(tt // 4, 4)
        for k in range(KC):
            P.mm(R(c.ps[pi][0:rows, 0:ncols], ("ps", pi)), R(c.hnT[:, k, col0:col0 + rows], hkey(g5)), R(wt_[:, k, c0:c0 + ncols], wk_),
                 start=(k == 0), stop=(k == KC - 1))

    for tt in range(17):
        rows = 128 if tt < 16 else NS
        pk = c.next_ps(); tokproj(pk, wk_t, wk_k, tt, rows)
        pv = c.next_ps(); tokproj(pv, wv_t, wv_k, tt, rows)
        kk = ("kvst",)
        P.copy("act", R(kvst[0:rows, :, 0:128], "kvst"), R(c.ps[pk][0:rows, 0:384].rearrange("p (g x) -> p g x", g=3), ("ps", pk)))
        P.copy("act", R(kvst[0:rows, :, 128:256], "kvst"), R(c.ps[pv][0:rows, 0:384].rearrange("p (g x) -> p g x", g=3), ("ps", pv)))
        for gi in range(3):
            _rope(c, kvst[0:rows, gi, 0:128].rearrange("p (h d) -> p h d", h=2), rows, tt, rt, "kvst")
        for gi in range(3):
            if tt < 16:
                lo = T - CWIN[gi]
                if tt * 128 >= lo:
                    P.dma("sp", c.kv_p[gi][l][tt * 128 - lo:tt * 128 - lo + 128, :], kvst[:, gi, :], reads=["kvst"], final=True)
            else:
                P.dma("sp", c.kv_s[gi][l], kvst[0:NS, gi, :], reads=["kvst"], final=True)
        for half in range(2):
            pt = c.next_ps()
            for i in range(3):
                hx = half * 3 + i
                gi, h = hx // 2, hx % 2
                P.transpose(R(c.ps[pt][0:64, i * 128:i * 128 + rows], ("ps", pt)), R(kvst[0:rows, gi, h * 64:(h + 1) * 64], "kvst"),
                            R(c.cstf[0:rows, 0:rows], "cstf"))
            src = c.ps[pt][0:64, 0:384].rearrange("p (h t) -> p h t", h=3)[:, :, 0:rows]
            if tt < 16:
                P.copy("act", R(KT[:, half * 3:half * 3 + 3, tt * 128:tt * 128 + 128], ("KT", tt)), R(src, ("ps", pt)))
            else:
                P.copy("act", R(ksT[:, half * 3:half * 3 + 3, :], "ksT"), R(src, ("ps", pt)))
        if tt == 16:
            P.copy("dve", R(vtok[:], "vtok"), R(kvst[0:NS, :, 128:256], "kvst"))
    for gi in range(3):
        dil = CDIL[gi]; nb = 16 // dil
        for q4 in range(4):
            pi = c.next_ps()
            for j in range(4):
                st_ = q4 * 4 + j
                r, n = st_ // nb, st_ % nb
                lo = r + dil * n * 128
                for k in range(KC):
                    P.mm(R(c.ps[pi][:, j * 128:(j + 1) * 128], ("ps", pi)), (c.hnT[:, k, lo:lo + dil * 127 + 1:dil], tuple(hkey(g) for g in range(4))),
                         R(wv_t[:, k, gi * 128:(gi + 1) * 128], wv_k), start=(k == 0), stop=(k == KC - 1))
            P.copy("act" if q4 % 2 == 0 else "dve", R(VS[:, gi * 16 + q4 * 4:gi * 16 + q4 * 4 + 4, :], ("VS", gi)),
                   R(c.ps[pi][:].rearrange("p (n x) -> p n x", x=128), ("ps", pi)))

    c.ps_n = 4; c.ps_i = 0
    psO = [4, 5]; psD = [6, 7]
    onesb64 = R(c.cstb[:, C_ONES:C_ONES + 64], "cstb")
    mcur = c.cstb[:, C_MCUR:C_MCUR + 128]; mprev = c.cstb[:, C_MPREV:C_MPREV + 128]
    allkt = tuple(("KT", t_) for t_ in range(16))

    def q_and_z(tt, rows, QT_dst, qkey, Z_dst):
        pq = c.next_ps(); tokproj(pq, wq_t, wq_k, tt, rows)
        P.copy("act", R(qsb[0:rows, :], "qsb"), R(c.ps[pq][0:rows, 0:384], ("ps", pq)))
        _rope(c, qsb[0:rows, :].rearrange("p (h d) -> p h d", h=6), rows, tt, rt, "qsb")
        for half in range(2):
            pt = c.next_ps()
            for i in range(3):
                hx = half * 3 + i
                P.transpose(R(c.ps[pt][0:64, i * 128:i * 128 + rows], ("ps", pt)), R(qsb[0:rows, hx * 64:(hx + 1) * 64], "qsb"),
                            R(c.cstf[0:rows, 0:rows], "cstf"))
            P.copy("dve", R(QT_dst[:, half * 3:half * 3 + 3, 0:rows], qkey),
                   R(c.ps[pt][0:64, 0:384].rearrange("p (h t) -> p h t", h=3)[:, :, 0:rows], ("ps", pt)))
        col0 = tt * 128; g5 = min(tt // 4, 4)
        for half in range(2):
            pz = c.next_ps()
            for i in range(3):
                hx = half * 3 + i
                for k in range(KC):
                    P.mm(R(c.ps[pz][0:64, i * 128:i * 128 + rows], ("ps", pz)), R(wz_t[:, k, hx * 64:(hx + 1) * 64], wz_k),
                         R(c.hnT[:, k, col0:col0 + rows], hkey(g5)), start=(k == 0), stop=(k == KC - 1))
            silu_(c, R(Z_dst[:, half * 3:half * 3 + 3, 0:rows], "ZS"),
                  R(c.ps[pz][0:64, 0:384].rearrange("p (h t) -> p h t", h=3)[:, :, 0:rows], ("ps", pz)))

    for tt in range(16):
        q_and_z(tt, 128, QTt, "QTt", ZS)

        def pv_den(hx, ocols, vs_tile, h, pt_ap, ptkey, first, last):
            o = c.ps[psO[hx // 3]][0:64, (hx % 3) * 128 + ocols[0]:(hx % 3) * 128 + ocols[0] + ocols[1]]
            d = c.ps[psD[hx // 3]][0:64, (hx % 3) * 128 + ocols[0]:(hx % 3) * 128 + ocols[0] + ocols[1]]
            P.mm(R(o, ("ps", psO[hx // 3])), R(VS[:, vs_tile, h * 64:(h + 1) * 64], ("VS", vs_tile // 16)), R(pt_ap, ptkey), start=first, stop=last)
            P.mm(R(d, ("ps", psD[hx // 3])), onesb64, R(pt_ap, ptkey), start=first, stop=last)

        for gi in range(3):
            dil = CDIL[gi]; nb = 16 // dil; nq = 128 // dil
            n = tt // dil
            qo = (tt % dil) * nq
            kbl = [0] if n == 0 else [0, 1]
            pS = c.next_ps()
            ptile = PT[gi]; ptk = ("PT", gi)
            slots = {}
            for kbi in kbl:
                kb = n - kbi
                for h in range(2):
                    hx = gi * 2 + h
                    for r in range(dil):
                        col = ((kbi * 2 + h) * dil + r) * nq
                        slots[(kbi, h, r)] = col
                        klo = r + dil * kb * 128
                        o = R(c.ps[pS][:, col:col + nq], ("ps", pS))
                        P.mm(o, (KT[:, hx, klo:klo + dil * 127 + 1:dil], allkt), R(QTt[:, hx, r:128:dil], "QTt"), start=True, stop=False)
                        msk = (mcur if kbi == 0 else mprev)[:, qo:qo + nq]
                        P.mm(o, c.identb, R(msk, "cstb"), start=False, stop=True)
            used = len(kbl) * 2 * dil * nq
            P.act(R(ptile[:, 0:used], ptk), R(c.ps[pS][:, 0:used], ("ps", pS)), AF.Exp, scale=0.125)
            for h in range(2):
                hx = gi * 2 + h
                for r in range(dil):
                    for ki, kbi in enumerate(kbl):
                        kb = n - kbi
                        col = slots[(kbi, h, r)]
                        pv_den(hx, (r * nq, nq), gi * 16 + r * nb + kb, h, ptile[:, col:col + nq], ptk, ki == 0, ki == len(kbl) - 1)
        def nat(ap, dil):
            if dil == 1:
                return ap
            return ap.rearrange("p (r m) -> p m r", r=dil)
        for h in range(2):
            dv_ = dtot[:, h, :]
            P.copy("dve", R(dv_, "dtot"), R(c.ps[psD[0]][0:64, h * 128:(h + 1) * 128], ("ps", psD[0])))
            for gi in (1, 2):
                hx = gi * 2 + h
                dil = CDIL[gi]
                src = c.ps[psD[hx // 3]][0:64, (hx % 3) * 128:(hx % 3) * 128 + 128]
                dvw = dv_.rearrange("p (m r) -> p m r", r=dil)
                P.tt("dve", R(dvw, "dtot"), R(dvw, "dtot"), R(nat(src, dil), ("ps", psD[hx // 3])), ALU.add)
            P.op("dve", lambda e, dv_=dv_: e.reciprocal(dv_, dv_), reads=["dtot"], writes=["dtot"])
        for hx in range(6):
            gi, h = hx // 2, hx % 2
            dil = CDIL[gi]
            src = c.ps[psO[hx // 3]][0:64, (hx % 3) * 128:(hx % 3) * 128 + 128]
            if dil == 1:
                P.tt("dve", R(onorm[:, hx, :], "kvst"), R(src, ("ps", psO[hx // 3])), R(dtot[:, h, :], "dtot"), ALU.mult)
            else:
                P.tt("dve", R(onorm[:, hx, :].rearrange("p (m r) -> p m r", r=dil), "kvst"), R(nat(src, dil), ("ps", psO[hx // 3])),
                     R(dtot[:, h, :].rearrange("p (m r) -> p m r", r=dil), "dtot"), ALU.mult)
        P.tt("dve", R(mixC[:], "mixC"), R(onorm[:], "kvst"), R(ZS[:], "ZS"), ALU.mult)
        outproj(c, l, tt // 4, lambda mc: R(mixC[:, mc, :], "mixC"), 6, 64, cols=(tt * 128, 128))
    c.ps_n = 8; c.ps_i = 0

    ZSs = ZS
    q_and_z(16, NS, qsT, "qsT", ZSs)
    P.barrier()
    B_ = c.Arena()
    b32 = B_.f32
    _skip = b32(prefix)
    CK = [b32(9 * 256).rearrange("p (b x) -> p b x", b=9) for _ in range(2)]
    KcT = b32(18 * 128)[0:64].rearrange("p (b t) -> p b t", b=18)
    PTn = b32(384)[0:64].rearrange("p (h t) -> p h t", h=6)
    PTc = b32(24)
    dts = b32(128)[0:64].rearrange("p (h t) -> p h t", h=2)
    ons = b32(384)[0:64].rearrange("p (h t) -> p h t", h=6)
    mixS = B_.bf16(384)[0:64].rearrange("p (h t) -> p h t", h=6)
    ones64f = R(c.cstf[0:64, C_ONES:C_ONES + 64], "cstf")
    ones128f = R(c.cstf[:, C_ONES:C_ONES + 64], "cstf")
    pO = 6; pD = 7
    c.ps_n = 6
    pn_ = c.next_ps()
    for hx in range(6):
        gi = hx // 2
        o = R(c.ps[pn_][0:64, hx * 64:(hx + 1) * 64], ("ps", pn_))
        P.mm(o, R(ksT[:, hx, :], "ksT"), R(qsT[:, hx, :], "qsT"), start=True, stop=False)
        msk = c.cstb[0:64, 704:768] if gi == 0 else c.cstb[0:64, 768:832]
        P.mm(o, R(c.cstb[0:64, 0:64], "cstb"), R(msk, "cstb"), start=False, stop=True)
    P.act(R(PTn[:].rearrange("p h t -> p (h t)"), "PTn"), R(c.ps[pn_][0:64, 0:384], ("ps", pn_)), AF.Exp, scale=0.125)
    for hx in range(6):
        gi, h = hx // 2, hx % 2
        P.mm(R(c.ps[pO][0:64, hx * 64:(hx + 1) * 64], ("ps", pO)), R(vtok[:, gi, h * 64:(h + 1) * 64], "vtok"), R(PTn[:, hx, :], "PTn"), start=True, stop=False)
        P.mm(R(c.ps[pD][0:64, hx * 64:(hx + 1) * 64], ("ps", pD)), ones64f, R(PTn[:, hx, :], "PTn"), start=True, stop=False)
    for s_ in range(NSQ):
        ck = CK[s_ % 2]; ckk = ("CK", s_ % 2)
        P.dma("sp", ck[:, 0, :], c.ck128[l][s_], writes=[ckk])
        P.dma("sp", ck[:, 1:5, :], c.ck512[l][s_].rearrange("(m i) x -> m i x", i=4), writes=[ckk])
        P.dma("sp", ck[:, 5:9, :], c.ck2048[l][s_].rearrange("(m i) x -> m i x", i=16)[:, 0:4, :], writes=[ckk])
        for q5 in range(5):
            pt = c.next_ps()
            nn = 4 if q5 < 4 else 2
            for j in range(nn):
                bh = q5 * 4 + j
                blk, h = bh // 2, bh % 2
                P.transpose(R(c.ps[pt][0:64, j * 128:(j + 1) * 128], ("ps", pt)), R(ck[:, blk, h * 64:(h + 1) * 64], ckk), ident)
            P.copy("act" if q5 % 2 == 0 else "dve", R(KcT[:, q5 * 4:q5 * 4 + nn, :], "KcT"),
                   R(c.ps[pt][0:64, 0:nn * 128].rearrange("p (b t) -> p b t", b=nn), ("ps", pt)))
        psc = c.next_ps()
        for h in range(2):
            o = R(c.ps[psc][:, h * 4:h * 4 + 4], ("ps", psc))
            P.mm(o, R(KcT[:, h, :], "KcT"), R(qsT[:, h, s_ * 4:s_ * 4 + 4], "qsT"), start=True, stop=False)
            P.mm(o, c.identb, R(c.cstb[:, 832:836], "cstb"), start=False, stop=True)
            for gi in (1, 2):
                for i in range(SQ):
                    blk = (1 if gi == 1 else 5) + i
                    col = gi * 8 + h * 4 + i
                    P.mm(R(c.ps[psc][:, col:col + 1], ("ps", psc)), R(KcT[:, blk * 2 + h, :], "KcT"),
                         R(qsT[:, gi * 2 + h, s_ * 4 + i:s_ * 4 + i + 1], "qsT"))
        P.act(R(PTc[:, 0:24], "PTc"), R(c.ps[psc][:, 0:24], ("ps", psc)), AF.Exp, scale=0.125)
        for h in range(2):
            for gi in range(3):
                hx = gi * 2 + h
                if gi == 0:
                    items = [(0, s_ * 4, 4, h * 4)]
                else:
                    items = [((1 if gi == 1 else 5) + i, s_ * 4 + i, 1, gi * 8 + h * 4 + i) for i in range(SQ)]
                for (blk, ocol, w_, pcol) in items:
                    P.mm(R(c.ps[pO][0:64, hx * 64 + ocol:hx * 64 + ocol + w_], ("ps", pO)), R(ck[:, blk, 128 + h * 64:128 + (h + 1) * 64], ckk),
                         R(PTc[:, pcol:pcol + w_], "PTc"), start=False, stop=True)
                    P.mm(R(c.ps[pD][0:64, hx * 64 + ocol:hx * 64 + ocol + w_], ("ps", pD)), ones128f,
                         R(PTc[:, pcol:pcol + w_], "PTc"), start=False, stop=True)
    for h in range(2):
        P.copy("dve", R(dts[:, h, :], "dts"), R(c.ps[pD][0:64, h * 64:(h + 1) * 64], ("ps", pD)))
        for gi in (1, 2):
            hx = gi * 2 + h
            P.tt("dve", R(dts[:, h, :], "dts"), R(dts[:, h, :], "dts"), R(c.ps[pD][0:64, hx * 64:(hx + 1) * 64], ("ps", pD)), ALU.add)
    dall = dts[:]
    P.op("dve", lambda e: e.reciprocal(dall, dall), reads=["dts"], writes=["dts"])
    for hx in range(6):
        P.tt("dve", R(ons[:, hx, :], "ons"), R(c.ps[pO][0:64, hx * 64:(hx + 1) * 64], ("ps", pO)), R(dts[:, hx % 2, :], "dts"), ALU.mult)
    P.tt("dve", R(mixS[:], "mixS"), R(ons[:], "ons"), R(ZSs[:, :, 0:NS], "ZS"), ALU.mult)
    c.ps_n = 8; c.ps_i = 0
    outproj(c, l, 4, lambda mc: R(mixS[:, mc, :], "mixS"), 6, 64, cols=(T, NS))
    P.barrier()
```

```python
import contextlib
import numpy as np
import concourse.bass as bass
import concourse.mybir as mybir
from concourse.bass_utils import run_bass_kernel_spmd

F32 = mybir.dt.float32
F32R = mybir.dt.float32r
BF16 = mybir.dt.bfloat16
AF = mybir.ActivationFunctionType
ALU = mybir.AluOpType
AX = mybir.AxisListType


class Prog:
    COMPUTE = ("pe", "act", "dve", "pool")
    ALL = ("pe", "act", "dve", "pool", "sp")

    def __init__(self, nc, stack, n_dma_sems=24):
        self.nc = nc
        self.stack = stack
        self.streams = {e: [] for e in self.ALL}
        self.esem = {e: stack.enter_context(nc.semaphore("prog_" + e)) for e in self.COMPUTE}
        self.ecount = {e: 0 for e in self.COMPUTE}
        self.known = {e: {} for e in self.ALL}
        self.res = {}
        self.dsem = {}
        for q in ("sp", "act", "pool"):
            self.dsem[q] = [[stack.enter_context(nc.semaphore("dq_%s_%d" % (q, i))), 0] for i in range(n_dma_sems)]
        self.dnext = {q: 0 for q in self.dsem}
        self.final_tokens = []

    def _deps(self, eng, reads, writes, same_engine_sem=None):
        toks = []
        for k in reads:
            r = self.res.get(k)
            if r and r["w"] is not None:
                toks.append(r["w"])
        for k in writes:
            r = self.res.get(k)
            if r:
                if r["w"] is not None:
                    toks.append(r["w"])
                toks.extend(r["r"])
        return toks

    def _record(self, tok, reads, writes):
        for k in reads:
            r = self.res.setdefault(k, {"w": None, "r": []})
            r["r"].append(tok)
        for k in writes:
            self.res[k] = {"w": tok, "r": []}

    def _waits(self, eng, toks, skip_sem=None):
        need = {}
        for (sem, val) in toks:
            if skip_sem is not None and sem is skip_sem:
                continue
            sid = id(sem)
            if self.known[eng].get(sid, 0) >= val:
                continue
            if sid not in need or need[sid][1] < val:
                need[sid] = (sem, val)
        for sid, (sem, val) in need.items():
            self.known[eng][sid] = val
        return list(need.values())

    def op(self, eng, fn, reads=(), writes=()):
        toks = self._deps(eng, reads, writes)
        own = self.esem[eng]
        if eng == "pe":
            waits = self._waits(eng, toks, skip_sem=own)
        else:
            waits = self._waits(eng, toks)
        if eng == "pe" and getattr(self, "_last_emit", None) == "pe":
            w_, f_, _ = self.streams["pe"][self._pe_last_idx]
            self.streams["pe"][self._pe_last_idx] = (w_, f_, None)
        else:
            self.ecount[eng] += 1
        tok = (own, self.ecount[eng])
        self.streams[eng].append((waits, fn, (own, 1)))
        if eng == "pe":
            self._pe_last_idx = len(self.streams["pe"]) - 1
        self._last_emit = eng
        self._record(tok, reads, writes)
        return tok

    def dma(self, q, out, in_, reads=(), writes=(), final=False, **kw):
        pool = self.dsem[q]
        i = self.dnext[q]
        self.dnext[q] = (i + 1) % len(pool)
        sem, val = pool[i]
        toks = self._deps(q, reads, writes)
        if val > 0:
            toks = toks + [(sem, val)]
        eng = {"sp": "sp", "act": "act", "pool": "pool"}[q]
        waits = self._waits(eng, toks)
        pool[i][1] = val + 16
        tok = (sem, val + 16)

        def fn(e, out=out, in_=in_, kw=kw):
            return e.dma_start(out, in_, **kw)
        self.streams[eng].append((waits, fn, (sem, 16)))
        self._last_emit = "dma"
        self._record(tok, reads, writes)
        if final:
            self.final_tokens.append(tok)
        return tok


    @staticmethod
    def _k(*xs):
        out = []
        for x in xs:
            if isinstance(x, tuple):
                out.extend(x[1])
        return out

    @staticmethod
    def _a(x):
        return x[0] if isinstance(x, tuple) else x

    def mm(self, out, lhsT, rhs, start=True, stop=True):
        o, l, r = out[0], lhsT[0], rhs[0]
        return self.op("pe", lambda e: e.matmul(o, l, r, start=start, stop=stop),
                       reads=self._k(lhsT, rhs), writes=self._k(out))

    def transpose(self, out, in_, ident):
        o, i, d = out[0], in_[0], ident[0]
        return self.op("pe", lambda e: e.transpose(o, i, d), reads=self._k(in_, ident), writes=self._k(out))

    def act(self, out, in_, func, bias=0.0, scale=1.0, eng="act"):
        o, i, b, sc = out[0], in_[0], self._a(bias), self._a(scale)
        return self.op(eng, lambda e: e.activation(o, i, func, bias=b, scale=sc),
                       reads=self._k(in_, bias, scale), writes=self._k(out))

    def copy(self, eng, out, in_):
        o, i = out[0], in_[0]
        if eng == "act":
            return self.op(eng, lambda e: e.copy(o, i), reads=self._k(in_), writes=self._k(out))
        return self.op(eng, lambda e: e.tensor_copy(o, i), reads=self._k(in_), writes=self._k(out))

    def tt(self, eng, out, in0, in1, op):
        o, a, b = out[0], in0[0], in1[0]
        return self.op(eng, lambda e: e.tensor_tensor(o, a, b, op), reads=self._k(in0, in1), writes=self._k(out))

    def ts(self, eng, out, in0, s1, s2, op0, op1=None):
        o, a, x1, x2 = out[0], in0[0], self._a(s1), self._a(s2)
        if op1 is None:
            return self.op(eng, lambda e: e.tensor_scalar(o, a, x1, None, op0), reads=self._k(in0, s1), writes=self._k(out))
        return self.op(eng, lambda e: e.tensor_scalar(o, a, x1, x2, op0, op1), reads=self._k(in0, s1, s2), writes=self._k(out))

    def stt(self, eng, out, in0, scalar, in1, op0, op1):
        o, a, sc, b = out[0], in0[0], self._a(scalar), in1[0]
        return self.op(eng, lambda e: e.scalar_tensor_tensor(o, a, sc, b, op0, op1),
                       reads=self._k(in0, scalar, in1), writes=self._k(out))

    def memset(self, eng, out, val):
        o = out[0]
        return self.op(eng, lambda e: e.memset(o, val), writes=self._k(out))

    def barrier(self):
        self._last_emit = "barrier"
        toks = [(self.esem[e], self.ecount[e]) for e in self.COMPUTE if self.ecount[e] > 0]
        for q in self.dsem:
            for sem, val in self.dsem[q]:
                if val > 0:
                    toks.append((sem, val))
        for e in self.ALL:
            waits = self._waits(e, toks, skip_sem=self.esem.get(e))
            if waits:
                self.streams[e].append((waits, None, None))
        self.res = {}

    def new_epoch(self):
        self.epoch = getattr(self, "epoch", 0) + 1
        for e in self.COMPUTE:
            self.esem[e] = self.stack.enter_context(self.nc.semaphore("prog_%s_%d" % (e, self.epoch)))
            self.ecount[e] = 0

    def emit(self):
        nc = self.nc
        fin = self._waits("sp", self.final_tokens)
        streams = self.streams

        def run(ename, e):
            for (waits, fn, inc) in streams[ename]:
                for (sem, val) in waits:
                    e.wait_ge(sem, val)
                if fn is None:
                    continue
                ins = fn(e)
                if inc is not None:
                    ins.then_inc(inc[0], inc[1])
            if ename == "sp":
                for (sem, val) in fin:
                    e.wait_ge(sem, val)

        with nc.Block() as block:
            @block.sync
            def _(e):
                run("sp", e)

            @block.tensor
            def _(e):
                run("pe", e)

            @block.scalar
            def _(e):
                run("act", e)

            @block.vector
            def _(e):
                run("dve", e)

            @block.gpsimd
            def _(e):
                run("pool", e)


D = 1024
KC = 8
T = 2048
NSQ = 16
SQ = 4
NS = NSQ * SQ
NTOK = T + NS
DEPTH = 4
INW = 4108
NCORE = 8
GROUPS = [(0, 512), (512, 512), (1024, 512), (1536, 512), (2048, 64)]
WSLOT = 384
EPS = 1e-6

V_BADA = 0
V_NORMW = 96
V_FNORM = 128
V_BADAF = 136
V_CAW = 152
V_CBW = 176
V_GNW = 320
V_ALOG = 324
V_DTB = 328
NV = 332

C_IDENT = 0
C_SHIFT = 128
C_ONES = 192
C_MPREV = 320
C_MCUR = 448
C_BLK = 576
C_EH = 704
C_OFFD = 1088
C_MSEQ = 1152
C_SEQM = 1216
C_SCAN64 = 1232
C_SCAN4 = 1360
C_MDIAG = 1424
C_MS128 = 1488
NCST = 1492
NEGM = -240000.0


def R(ap, *keys):
    return (ap, keys)


class Ctx:
    pass


def build_program(phases=("A",), depth=DEPTH):
    nc = bass.Bass("TRN2", target_bir_lowering=False)
    st = contextlib.ExitStack()
    with st:
        P = Prog(nc, st)
        c = Ctx()
        c.nc, c.P, c.st = nc, P, st

        def din(name, shape):
            return nc.dram_tensor(name, list(shape), F32, kind="ExternalInput").ap()

        def dout(name, shape):
            return nc.dram_tensor(name, list(shape), F32, kind="ExternalOutput").ap()

        c.xp = din("xp", (T, D)); c.xs = din("xs", (NS, D))
        c.cT = din("cT", (128, KC, 17))
        c.sca = din("sca", (DEPTH, NSQ * 2, 256)); c.scb = din("scb", (DEPTH, NSQ * 3, 1152))
        c.sgd = din("sgd", (DEPTH, NSQ, 6, 64, 64))
        c.ck128 = din("ck128", (DEPTH, NSQ, 128, 256)); c.ck512 = din("ck512", (DEPTH, NSQ, 512, 256))
        c.ck2048 = din("ck2048", (DEPTH, NSQ, 2048, 256))
        c.w_in = din("w_in", (DEPTH, D, INW)); c.w_out = din("w_out", (DEPTH, D, D))
        c.w_ada = din("w_ada", (DEPTH, D, 3 * D)); c.w_adaf = din("w_adaf", (D, 2 * D))
        c.vecT = din("vecT", (128, NV)); c.cst = din("cst", (128, NCST))
        c.rope = din("rope", (128, 17, 16))
        c.yp = dout("yp", (T, D)); c.ys = dout("ys", (NS, D))
        c.ca_p = dout("ca_p", (DEPTH, 2, 256)); c.ca_s = dout("ca_s", (DEPTH, NSQ * 2, 256))
        c.cb_p = dout("cb_p", (DEPTH, 3, 1152)); c.cb_s = dout("cb_s", (DEPTH, NSQ * 3, 1152))
        c.gd_p = dout("gd_p", (DEPTH, 6, 64, 64)); c.gd_s = dout("gd_s", (DEPTH, NSQ, 6, 64, 64))
        c.kv_p = [dout("kv128_p", (DEPTH, 128, 256)), dout("kv512_p", (DEPTH, 512, 256)), dout("kv2048_p", (DEPTH, 2048, 256))]
        c.kv_s = [dout("kv128_s", (DEPTH, NS, 256)), dout("kv512_s", (DEPTH, NS, 256)), dout("kv2048_s", (DEPTH, NS, 256))]

        def sb(name, shape, dt):
            return st.enter_context(nc.sbuf_tensor(name, list(shape), dt))

        c.xT = sb("xT", (128, KC, NTOK), F32)
        c.hnT = sb("hnT", (128, KC, NTOK), BF16)
        c.ring = [sb("ring%d" % i, (128, KC, WSLOT), BF16) for i in range(4)]
        c.wab = sb("wab", (128, KC, 12), BF16)
        c.wo = sb("wo", (128, 6, D), BF16)
        c.vec = sb("vec", (128, NV), F32)
        c.cstf = sb("cstf", (128, NCST), F32)
        c.cstb = sb("cstb", (128, 836), BF16)
        c.ropet = sb("ropet", (128, 17, 16), F32)
        c.cTf = sb("cTf", (128, KC, 17), F32)
        c.cTb = sb("cTb", (128, KC, 17), BF16)
        c.ada = sb("ada", (128, 24, 17), F32)
        c.m1 = sb("m1", (128, KC, 17), F32)
        c.g1 = sb("g1", (128, KC, 17), F32)
        ARENA_F32 = 14592 - 2112
        c.arena = sb("arena", (128, ARENA_F32), F32)
        c.nmr = sb("nmr", (128, 2112), F32R)
        c.ps = [st.enter_context(nc.psum_tensor("ps%d" % i, [128, 512], F32)) for i in range(8)]
        c.ps_i = 0
        c.ring_i = 0
        c.phases = phases

        c.ps_n = 8

        def next_ps():
            i = c.ps_i % c.ps_n
            c.ps_i = (i + 1) % c.ps_n
            return i
        c.next_ps = next_ps

        class Arena:
            def __init__(self):
                self.off = 0

            def f32(self, n):
                o = self.off
                self.off += n
                c.arena_max = max(getattr(c, "arena_max", 0), self.off)
                assert self.off <= ARENA_F32, ("arena overflow", self.off)
                return c.arena[:, o:o + n]

            def bf16(self, n):
                assert n % 2 == 0
                return self.f32(n // 2).bitcast(BF16)

            def f32r(self, n):
                return self.f32(n).bitcast(F32R)
        c.Arena = Arena

        def wload(src, ncols):
            i = c.ring_i
            c.ring_i = (i + 1) % 4
            key = ("ring", i)
            P.dma("pool", c.ring[i][:, :, 0:ncols], src.rearrange("(k p) c -> p k c", p=128), writes=[key])
            return c.ring[i], key
        c.wload = wload

        P.dma("sp", c.vec[:], c.vecT, writes=["vec"])
        P.dma("sp", c.cstf[:], c.cst, writes=["cstf"])
        P.dma("sp", c.ropet[:], c.rope, writes=["rope"])
        P.dma("sp", c.cTf[:], c.cT, writes=["cTf"])
        P.copy("dve", R(c.cstb[:, 0:704], "cstb"), R(c.cstf[:, 0:704], "cstf"))
        P.copy("dve", R(c.cstb[:, 704:768], "cstb"), R(c.cstf[:, C_MSEQ:C_MSEQ + 64], "cstf"))
        P.copy("dve", R(c.cstb[:, 768:836], "cstb"), R(c.cstf[:, C_MDIAG:C_MDIAG + 68], "cstf"))
        P.copy("dve", R(c.cTb[:], "cTb"), R(c.cTf[:], "cTf"))
        for i in range(8):
            P.memset("dve", R(c.ps[i][:], ("ps", i)), 0.0)
        c.zero = sb("zero", (128, 384), F32)
        P.memset("dve", R(c.zero[:], "zero"), 0.0)
        c.ident = R(c.cstf[:, C_IDENT:C_IDENT + 128], "cstf")
        c.identb = R(c.cstb[:, C_IDENT:C_IDENT + 128], "cstb")
        c.onesb = R(c.cstb[:, C_ONES:C_ONES + 128], "cstb")

        load_x(c)
        for l in range(depth):
            layer(c, l)
        final(c)
        P.emit()
    return nc


def xkey(g):
    return ("x", g)


def hkey(g):
    return ("hn", g)


def load_x(c):
    P = c.P
    A = c.Arena()
    stg = [A.f32(D) for _ in range(2)]
    for tt in range(17):
        rows = 128 if tt < 16 else NS
        g = min(tt // 4, 4)
        s = stg[tt % 2]
        skey = ("xstg", tt % 2)
        src = c.xp[tt * 128:(tt + 1) * 128, :] if tt < 16 else c.xs
        P.dma("sp", s[0:rows, :], src, writes=[skey])
        for half in range(2):
            pi = c.next_ps()
            for kk in range(4):
                k = half * 4 + kk
                P.transpose(R(c.ps[pi][:, kk * 128:kk * 128 + rows], ("ps", pi)),
                            R(s[0:rows, k * 128:(k + 1) * 128], skey), R(c.cstf[0:rows, C_IDENT:C_IDENT + rows], "cstf"))
            col0 = tt * 128
            dst = c.xT[:, half * 4:half * 4 + 4, col0:col0 + rows]
            srcp = c.ps[pi][:].rearrange("p (k t) -> p k t", k=4)[:, :, 0:rows]
            P.copy("act" if half == 0 else "dve", R(dst, xkey(g)), R(srcp, ("ps", pi)))
    P.barrier()


def ada_vectors(c, l):
    P = c.P
    final_ = (l == DEPTH)
    ncol = 2 * D if final_ else 3 * D
    nj = ncol // 128
    pi = c.next_ps()
    pst = c.ps[pi][:, 0:24 * 17].rearrange("p (j s) -> p j s", s=17)
    for t0 in range(0, ncol, WSLOT):
        nc_ = min(WSLOT, ncol - t0)
        src = (c.w_adaf if final_ else c.w_ada[l])[:, t0:t0 + nc_]
        wt, wk = c.wload(src, nc_)
        for jj in range(nc_ // 128):
            j = t0 // 128 + jj
            for k in range(KC):
                P.mm(R(pst[:, j, :], ("ps", pi)), R(wt[:, k, jj * 128:(jj + 1) * 128], wk), R(c.cTb[:, k, :], "cTb"),
                     start=(k == 0), stop=(k == KC - 1))
    vb = V_BADAF if final_ else V_BADA + l * 24
    bias = c.vec[:, vb:vb + nj].unsqueeze(2).to_broadcast([128, nj, 17])
    P.tt("dve", R(c.ada[:, 0:nj, :], "ada"), R(pst[:, 0:nj, :], ("ps", pi)), R(bias, "vec"), ALU.add)
    nw0 = V_FNORM if final_ else V_NORMW + l * 8
    nw = c.vec[:, nw0:nw0 + KC].unsqueeze(2).to_broadcast([128, KC, 17])
    P.stt("dve", R(c.m1[:], "m1"), R(c.ada[:, 8:16, :], "ada"), 1.0, R(nw, "vec"), ALU.add, ALU.mult)
    if not final_:
        P.ts("dve", R(c.g1[:], "g1"), R(c.ada[:, 16:24, :], "ada"), 1.0, None, ALU.add)


def rsqrt(c, out, in_, scale, bias):
    P = c.P
    P.act(out, in_, AF.Ln, bias=bias, scale=scale)
    P.act(out, out, AF.Exp, scale=-0.5)


def sigmoid_(c, out, in_):
    P = c.P
    P.act(out, in_, AF.Exp, scale=-1.0)
    P.act(out, out, AF.Ln, bias=1.0)
    P.act(out, out, AF.Exp, scale=-1.0)


def silu_(c, out, in_):
    sigmoid_(c, out, in_)
    c.P.tt("dve", out, in_, out, ALU.mult)


def rms_stats(c, A, g, tag):
    P = c.P
    col0, ncol = GROUPS[g]
    sq = A["sq"]
    P.act(R(sq[:, :, 0:ncol], "sq"), R(c.xT[:, :, col0:col0 + ncol], xkey(g)), AF.Square)
    pi = c.next_ps()
    for k in range(KC):
        P.mm(R(c.ps[pi][:, 0:ncol], ("ps", pi)), c.onesb, R(sq[:, k, 0:ncol], "sq"), start=(k == 0), stop=(k == KC - 1))
    rstd = A["rstd"]
    rsqrt(c, R(rstd[:, 0:ncol], "rstd"), R(c.ps[pi][:, 0:ncol], ("ps", pi)), 1.0 / D, EPS)
    return rstd


def norm_phase(c, l, out_fn):
    P = c.P
    A_ = c.Arena()
    A = {"sq": A_.bf16(KC * 512).rearrange("p (k t) -> p k t", k=KC), "rstd": A_.f32(512),
         "tmp": [A_.f32(512) for _ in range(2)], "tmps": A_.f32(KC * NS).rearrange("p (k t) -> p k t", k=KC)}
    for g in range(5):
        col0, ncol = GROUPS[g]
        rstd = rms_stats(c, A, g, "n")
        if g < 4:
            for k in range(KC):
                tmp = A["tmp"][k % 2]
                tk = ("ntmp", k % 2)
                P.tt("dve", R(tmp[:, 0:ncol], tk), R(c.xT[:, k, col0:col0 + ncol], xkey(g)), R(rstd[:, 0:ncol], "rstd"), ALU.mult)
                P.act(out_fn(g, k, ncol), R(tmp[:, 0:ncol], tk), AF.Identity,
                      bias=R(c.ada[:, k, 0:1], "ada"), scale=R(c.m1[:, k, 0:1], "m1"))
        else:
            ts_ = A["tmps"]
            P.tt("dve", R(ts_[:], "ntmps"), R(c.xT[:, :, col0:col0 + ncol], xkey(g)),
                 R(rstd[:, 0:ncol].unsqueeze(1).to_broadcast([128, KC, NS]), "rstd"), ALU.mult)
            v4 = ts_[:].rearrange("p k (s i) -> p k s i", i=SQ)
            m1b = c.m1[:, :, 1:17].unsqueeze(3).to_broadcast([128, KC, NSQ, SQ])
            shb = c.ada[:, 0:8, 1:17].unsqueeze(3).to_broadcast([128, KC, NSQ, SQ])
            P.tt("dve", R(v4, "ntmps"), R(v4, "ntmps"), R(m1b, "m1"), ALU.mult)
            for k in range(KC):
                o = out_fn(g, k, ncol)
                P.tt("dve", (o[0].rearrange("p (s i) -> p s i", i=SQ), o[1]), R(v4[:, k], "ntmps"), R(shb[:, k], "ada"), ALU.add)
    P.barrier()


def proj(c, pi, wt, wk, wc0, m, g, ncol_override=None, cols=None):
    P = c.P
    col0, ncol = GROUPS[g] if cols is None else cols
    for k in range(KC):
        P.mm(R(c.ps[pi][0:m, 0:ncol], ("ps", pi)), R(wt[:, k, wc0:wc0 + m], wk), R(c.hnT[:, k, col0:col0 + ncol], hkey(g)),
             start=(k == 0), stop=(k == KC - 1))


def outproj(c, l, g, mix_fn, nchunk, kpart, cols=None):
    P = c.P
    col0, ncol = GROUPS[g] if cols is None else cols
    for dc in range(KC):
        pi = c.next_ps()
        for mc in range(nchunk):
            P.mm(R(c.ps[pi][:, 0:ncol], ("ps", pi)), R(c.wo[0:kpart, mc, dc * 128:(dc + 1) * 128], "wo"), mix_fn(mc),
                 start=(mc == 0), stop=(mc == nchunk - 1))
        xs = c.xT[:, dc, col0:col0 + ncol]
        if g < 4:
            P.stt("dve", R(xs, xkey(g)), R(c.ps[pi][:, 0:ncol], ("ps", pi)), R(c.g1[:, dc, 0:1], "g1"), R(xs, xkey(g)), ALU.mult, ALU.add)
        else:
            x3 = xs.rearrange("p (s i) -> p s i", i=SQ)
            p3 = c.ps[pi][:, 0:ncol].rearrange("p (s i) -> p s i", i=SQ)
            g1b = c.g1[:, dc, 1:17].unsqueeze(2).to_broadcast([128, NSQ, SQ])
            tmp = c.optmp
            P.tt("dve", R(tmp, "optmp"), R(p3, ("ps", pi)), R(g1b, "g1"), ALU.mult)
            P.tt("dve", R(x3, xkey(g)), R(x3, xkey(g)), R(tmp, "optmp"), ALU.add)


def load_wo(c, l, r0, nchunk, kpart):
    P = c.P
    src = c.w_out[l][r0:r0 + nchunk * kpart, :].rearrange("(j p) d -> p j d", p=kpart)
    P.dma("pool", c.wo[0:kpart, 0:nchunk, :], src, writes=["wo"])


def layer(c, l):
    P = c.P
    ada_vectors(c, l)
    norm_phase(c, l, lambda g, k, ncol: R(c.hnT[:, k, GROUPS[g][0]:GROUPS[g][0] + ncol], hkey(g)))
    if "A" in c.phases:
        branch_a(c, l)
    if "B" in c.phases:
        branch_b(c, l)
    if "C" in c.phases:
        branch_c(c, l)


def branch_a(c, l):
    P = c.P
    A_ = c.Arena()
    CI = A_.f32(2 * (2 + T)).rearrange("p (j t) -> p j t", j=2)
    CIs = A_.f32(2 * NSQ * 6).rearrange("p (j s t) -> p j s t", j=2, s=NSQ)
    tmpx = A_.f32(512); sz = A_.f32(512); acc = A_.f32(512); tz = A_.f32(512)
    mixA = A_.bf16(2 * 512).rearrange("p (j t) -> p j t", j=2)
    c.optmp = A_.f32(NS).rearrange("p (s i) -> p s i", i=SQ)
    sin_ = A_.f32(256)
    gat = A_.f32(2 * 34).rearrange("p (j t) -> p j t", j=2)
    outa = A_.f32(256)
    load_wo(c, l, 0, 2, 128)
    tiles = []
    for t0 in (0, 384, 768):
        ncols = min(384, 1024 - t0)
        tiles.append(c.wload(c.w_in[l][:, t0:t0 + ncols], ncols))

    def wsel(col):
        ti = col // 384
        return tiles[ti][0], tiles[ti][1], col - ti * 384

    P.memset("dve", R(CI[:, :, 0:2], "CIh"), 0.0)
    P.dma("sp", sin_[0:NSQ * 2, :], c.sca[l], writes=["sin"])
    for j in range(2):
        pi = c.next_ps()
        P.transpose(R(c.ps[pi][:, 0:32], ("ps", pi)), R(sin_[0:32, j * 128:(j + 1) * 128], "sin"), R(c.cstf[0:32, 0:32], "cstf"))
        P.copy("act", R(CIs[:, j, :, 0:2], ("CIs", j)), R(c.ps[pi][:, 0:32].rearrange("p (s r) -> p s r", r=2), ("ps", pi)))
    for g in range(5):
        col0, ncol = GROUPS[g]
        for j in range(2):
            p0 = c.next_ps(); wt, wk, wc = wsel(j * 128); proj(c, p0, wt, wk, wc, 128, g)
            p1 = c.next_ps(); wt, wk, wc = wsel(256 + j * 128); proj(c, p1, wt, wk, wc, 128, g)
            P.copy("act", R(tmpx[:, 0:ncol], "tmpx"), R(c.ps[p0][:, 0:ncol], ("ps", p0)))
            vb = V_CAW + (l * 3) * 2 + j
            w0 = R(c.vec[:, vb:vb + 1], "vec"); w1 = R(c.vec[:, vb + 2:vb + 3], "vec"); w2 = R(c.vec[:, vb + 4:vb + 5], "vec")
            if g < 4:
                ck = ("CI", j, g)
                P.tt("dve", R(CI[:, j, 2 + col0:2 + col0 + ncol], ck), R(tmpx[:, 0:ncol], "tmpx"), R(c.ps[p1][:, 0:ncol], ("ps", p1)), ALU.mult)
                rd = [ck, ("CI", j, g - 1), "CIh"]
                P.ts("dve", R(acc[:, 0:ncol], "acc"), (CI[:, j, col0 + 2:col0 + 2 + ncol], rd), w2, None, ALU.mult)
                P.stt("dve", R(acc[:, 0:ncol], "acc"), (CI[:, j, col0 + 1:col0 + 1 + ncol], rd), w1, R(acc[:, 0:ncol], "acc"), ALU.mult, ALU.add)
                P.stt("dve", R(acc[:, 0:ncol], "acc"), (CI[:, j, col0:col0 + ncol], rd), w0, R(acc[:, 0:ncol], "acc"), ALU.mult, ALU.add)
                accv = acc[:, 0:ncol]
            else:
                ck = ("CIs", j)
                P.tt("dve", R(CIs[:, j, :, 2:6], ck), R(tmpx[:, 0:ncol].rearrange("p (s i) -> p s i", i=SQ), "tmpx"),
                     R(c.ps[p1][:, 0:ncol].rearrange("p (s i) -> p s i", i=SQ), ("ps", p1)), ALU.mult)
                a3 = acc[:, 0:ncol].rearrange("p (s i) -> p s i", i=SQ)
                P.ts("dve", R(a3, "acc"), R(CIs[:, j, :, 2:6], ck), w2, None, ALU.mult)
                P.stt("dve", R(a3, "acc"), R(CIs[:, j, :, 1:5], ck), w1, R(a3, "acc"), ALU.mult, ALU.add)
                P.stt("dve", R(a3, "acc"), R(CIs[:, j, :, 0:4], ck), w0, R(a3, "acc"), ALU.mult, ALU.add)
                accv = acc[:, 0:ncol]
            p2 = c.next_ps(); wt, wk, wc = wsel(512 + j * 128); proj(c, p2, wt, wk, wc, 128, g)
            p3 = c.next_ps(); wt, wk, wc = wsel(768 + j * 128); proj(c, p3, wt, wk, wc, 128, g)
            silu_(c, R(sz[:, 0:ncol], "sz"), R(c.ps[p3][:, 0:ncol], ("ps", p3)))
            P.tt("dve", R(tz[:, 0:ncol], "tz"), R(accv, "acc"), R(sz[:, 0:ncol], "sz"), ALU.mult)
            P.tt("dve", R(mixA[:, j, 0:ncol], ("mixA", j)), R(tz[:, 0:ncol], "tz"), R(c.ps[p2][:, 0:ncol], ("ps", p2)), ALU.mult)
        outproj(c, l, g, lambda mc: R(mixA[:, mc, 0:GROUPS[g][1]], ("mixA", mc)), 2, 128)
    for j in range(2):
        P.copy("act", R(gat[:, j, 0:2], ("gat", j)), R(CI[:, j, T:T + 2], ("CI", j, 3)))
        P.copy("act", R(gat[:, j, 2:34].rearrange("p (s r) -> p s r", r=2), ("gat", j)), R(CIs[:, j, :, 4:6], ("CIs", j)))
        pi = c.next_ps()
        P.transpose(R(c.ps[pi][0:34, 0:128], ("ps", pi)), R(gat[:, j, :], ("gat", j)), c.ident)
        P.copy("dve", R(outa[0:34, j * 128:(j + 1) * 128], "outa"), R(c.ps[pi][0:34, 0:128], ("ps", pi)))
    P.dma("sp", c.ca_p[l], outa[0:2, :], reads=["outa"], final=True)
    P.dma("sp", c.ca_s[l], outa[2:34, :], reads=["outa"], final=True)
    P.barrier()


def final(c):
    P = c.P
    ada_vectors(c, DEPTH)
    A_ = c.Arena()
    yT = A_.f32(KC * 512).rearrange("p (k t) -> p k t", k=KC)
    A = {"sq": A_.bf16(KC * 512).rearrange("p (k t) -> p k t", k=KC), "rstd": A_.f32(512),
         "tmp": [A_.f32(512) for _ in range(2)], "tmps": A_.f32(KC * NS).rearrange("p (k t) -> p k t", k=KC)}
    ystg = [A_.f32(D) for _ in range(2)]
    si = 0
    for g in range(5):
        col0, ncol = GROUPS[g]
        rstd = rms_stats(c, A, g, "f")
        yk = ("yT",)
        if g < 4:
            for k in range(KC):
                tmp = A["tmp"][k % 2]; tk = ("ntmp", k % 2)
                P.tt("dve", R(tmp[:, 0:ncol], tk), R(c.xT[:, k, col0:col0 + ncol], xkey(g)), R(rstd[:, 0:ncol], "rstd"), ALU.mult)
                P.act(R(yT[:, k, 0:ncol], "yT"), R(tmp[:, 0:ncol], tk), AF.Identity,
                      bias=R(c.ada[:, k, 0:1], "ada"), scale=R(c.m1[:, k, 0:1], "m1"))
        else:
            ts_ = A["tmps"]
            P.tt("dve", R(ts_[:], "ntmps"), R(c.xT[:, :, col0:col0 + ncol], xkey(g)),
                 R(rstd[:, 0:ncol].unsqueeze(1).to_broadcast([128, KC, NS]), "rstd"), ALU.mult)
            v4 = ts_[:].rearrange("p k (s i) -> p k s i", i=SQ)
            m1b = c.m1[:, :, 1:17].unsqueeze(3).to_broadcast([128, KC, NSQ, SQ])
            shb = c.ada[:, 0:8, 1:17].unsqueeze(3).to_broadcast([128, KC, NSQ, SQ])
            P.tt("dve", R(v4, "ntmps"), R(v4, "ntmps"), R(m1b, "m1"), ALU.mult)
            P.tt("dve", R(yT[:, :, 0:NS].rearrange("p k (s i) -> p k s i", i=SQ), "yT"), R(v4, "ntmps"), R(shb, "ada"), ALU.add)
        for tt in range((ncol + 127) // 128):
            rows = min(128, ncol - tt * 128)
            stg = ystg[si % 2]; sk = ("ystg", si % 2); si += 1
            for half in range(2):
                pi = c.next_ps()
                for kk in range(4):
                    k = half * 4 + kk
                    P.transpose(R(c.ps[pi][0:rows, kk * 128:(kk + 1) * 128], ("ps", pi)),
                                R(yT[:, k, tt * 128:tt * 128 + rows], "yT"), c.ident)
                P.copy("act" if half == 0 else "dve", R(stg[0:rows, half * 512:(half + 1) * 512], sk), R(c.ps[pi][0:rows, :], ("ps", pi)))
            if g < 4:
                dst = c.yp[col0 + tt * 128:col0 + tt * 128 + rows, :]
            else:
                dst = c.ys
            P.dma("sp", dst, stg[0:rows, :], reads=[sk], final=True)


_PHASES = ("A", "B", "C")
_NC_CACHE = {}


def _host_consts():
    cst = np.zeros((128, NCST), np.float32)
    cst[:, C_IDENT:C_IDENT + 128] = np.eye(128, dtype=np.float32)
    for m in range(64):
        cst[64 + m, C_SHIFT + m] = 1.0
    cst[:, C_ONES:C_ONES + 128] = 1.0
    k = np.arange(128)[:, None]
    q = np.arange(128)[None, :]
    cst[:, C_MPREV:C_MPREV + 128] = np.where(k >= q, 0.0, NEGM)
    cst[:, C_MCUR:C_MCUR + 128] = np.where(k <= q, 0.0, NEGM)
    cst[0:64, C_BLK:C_BLK + 64] = 1.0
    cst[64:128, C_BLK + 64:C_BLK + 128] = 1.0
    for h in range(6):
        cst[h, C_EH + h * 64:C_EH + (h + 1) * 64] = 1.0
    cst[0:64, C_OFFD:C_OFFD + 64] = 1.0 - np.eye(64, dtype=np.float32)
    j64 = np.arange(64)[:, None]
    i64 = np.arange(64)[None, :]
    cst[0:64, C_MSEQ:C_MSEQ + 64] = np.where((j64 // 4 == i64 // 4) & (j64 <= i64), 0.0, NEGM)
    cst[0:64, C_SEQM:C_SEQM + 16] = (j64 // 4 == np.arange(16)[None, :]).astype(np.float32)
    cst[:, C_SCAN64:C_SCAN64 + 128] = (np.arange(128) % 64 != 0).astype(np.float32)[None, :]
    cst[:, C_SCAN4:C_SCAN4 + 64] = (np.arange(64) % 4 != 0).astype(np.float32)[None, :]
    cst[0:64, C_MDIAG:C_MDIAG + 64] = np.where(j64 == i64, 0.0, NEGM)
    cst[:, C_MS128:C_MS128 + 4] = np.where(np.arange(128)[:, None] >= np.arange(4)[None, :], 0.0, NEGM)
    half = 8
    inv_freq = (500000.0 ** (-np.arange(half, dtype=np.float32) * np.float32(2.0 / 16))).astype(np.float32)
    rope = np.zeros((128, 17, 16), np.float32)
    for tt in range(17):
        if tt < 16:
            pos = (tt * 128 + np.arange(128)).astype(np.float32)
        else:
            pos = (T + (np.arange(128) % SQ)).astype(np.float32)
        ang = pos[:, None] * inv_freq[None, :]
        rope[:, tt, 0:8] = np.cos(ang)
        rope[:, tt, 8:16] = np.sin(ang)
    return cst, rope


def _fm(v):
    v = np.asarray(v, np.float32)
    return np.ascontiguousarray(v.reshape(-1, 128).T)


def _host_vecT(b_ada, norm_w, final_norm_w, b_ada_final, conv_a_w, conv_b_w, gdn_norm_w, a_log, dt_bias):
    vt = np.zeros((128, NV), np.float32)
    for l in range(DEPTH):
        vt[:, V_BADA + l * 24:V_BADA + (l + 1) * 24] = _fm(b_ada[l])
        vt[:, V_NORMW + l * 8:V_NORMW + (l + 1) * 8] = _fm(norm_w[l])
        for tap in range(3):
            vt[:, V_CAW + (l * 3 + tap) * 2:V_CAW + (l * 3 + tap) * 2 + 2] = _fm(conv_a_w[l, tap])
        for tap in range(4):
            vt[:, V_CBW + (l * 4 + tap) * 9:V_CBW + (l * 4 + tap) * 9 + 9] = _fm(conv_b_w[l, tap])
        vt[:, V_GNW + l] = np.tile(np.asarray(gdn_norm_w[l], np.float32), 2)
        vt[0:6, V_ALOG + l] = a_log[l]
        vt[0:6, V_DTB + l] = dt_bias[l]
    vt[:, V_FNORM:V_FNORM + 8] = _fm(final_norm_w)
    vt[:, V_BADAF:V_BADAF + 16] = _fm(b_ada_final)
    return vt


def kernel(x_prompt, x_sample, state_conv_a, state_conv_b, state_gdn, cache_kv_w128, cache_kv_w512,
           cache_kv_w2048, c_prompt, c_sample, w_in, w_out, w_ada, b_ada, norm_w, conv_a_w, conv_b_w,
           a_log, dt_bias, gdn_norm_w, final_norm_w, w_ada_final, b_ada_final, _phases=None, _depth=DEPTH):
    phases = tuple(_phases) if _phases is not None else _PHASES
    f = lambda a: np.ascontiguousarray(np.asarray(a, dtype=np.float32))
    key = (phases, _depth)
    if key not in _NC_CACHE:
        _NC_CACHE[key] = build_program(phases, _depth)
    nc = _NC_CACHE[key]
    cst, rope = _host_consts()
    vt = _host_vecT(f(b_ada), f(norm_w), f(final_norm_w), f(b_ada_final), f(conv_a_w), f(conv_b_w), f(gdn_norm_w),
                    f(a_log), f(dt_bias))
    w_in, w_out, w_ada, w_adaf = f(w_in), f(w_out), f(w_ada), f(w_ada_final)
    x_prompt, x_sample = f(x_prompt), f(x_sample)
    c_prompt, c_sample = f(c_prompt), f(c_sample)
    sca, scb, sgd = f(state_conv_a), f(state_conv_b), f(state_gdn)
    k128, k512, k2048 = f(cache_kv_w128), f(cache_kv_w512), f(cache_kv_w2048)
    in_maps = []
    for i in range(NCORE):
        ss = slice(i * NSQ, (i + 1) * NSQ)
        call = np.concatenate([c_prompt[i:i + 1], c_sample[ss]], axis=0)
        cT = np.ascontiguousarray(call.reshape(17, KC, 128).transpose(2, 1, 0))
        in_maps.append({
            "xp": x_prompt[i], "xs": np.ascontiguousarray(x_sample[ss].reshape(NS, D)), "cT": cT,
            "sca": np.ascontiguousarray(sca[:, ss].reshape(DEPTH, NSQ * 2, 256)),
            "scb": np.ascontiguousarray(scb[:, ss].reshape(DEPTH, NSQ * 3, 1152)),
            "sgd": np.ascontiguousarray(sgd[:, ss]),
            "ck128": np.ascontiguousarray(k128[:, ss].reshape(DEPTH, NSQ, 128, 256)),
            "ck512": np.ascontiguousarray(k512[:, ss].reshape(DEPTH, NSQ, 512, 256)),
            "ck2048": np.ascontiguousarray(k2048[:, ss].reshape(DEPTH, NSQ, 2048, 256)),
            "w_in": w_in, "w_out": w_out, "w_ada": w_ada, "w_adaf": w_adaf,
            "vecT": vt, "cst": cst, "rope": rope,
        })
    res = run_bass_kernel_spmd(nc, in_maps, core_ids=list(range(NCORE)))
    rs = res.results
    cat = lambda name: np.stack([r[name] for r in rs], axis=0)
    y_p = cat("yp")
    y_s = cat("ys").reshape(NCORE * NSQ, SQ, D)
    ca_p = cat("ca_p").transpose(1, 0, 2, 3)
    ca_s = cat("ca_s").reshape(NCORE, DEPTH, NSQ, 2, 256).transpose(1, 0, 2, 3, 4).reshape(DEPTH, NCORE * NSQ, 2, 256)
    cb_p = cat("cb_p").transpose(1, 0, 2, 3)
    cb_s = cat("cb_s").reshape(NCORE, DEPTH, NSQ, 3, 1152).transpose(1, 0, 2, 3, 4).reshape(DEPTH, NCORE * NSQ, 3, 1152)
    gd_p = cat("gd_p").transpose(1, 0, 2, 3, 4)
    gd_s = cat("gd_s").transpose(1, 0, 2, 3, 4, 5).reshape(DEPTH, NCORE * NSQ, 6, 64, 64)
    outs = [y_p, y_s, ca_p, ca_s, cb_p, cb_s, gd_p, gd_s]
    for gi, win in enumerate((128, 512, 2048)):
        name = "kv%d" % win
        kp = cat(name + "_p").transpose(1, 0, 2, 3).reshape(DEPTH, NCORE, win, 2, 2, 64)
        ks = cat(name + "_s").reshape(NCORE, DEPTH, NSQ, SQ, 256).transpose(1, 0, 2, 3, 4).reshape(DEPTH, NCORE * NSQ, SQ, 2, 2, 64)
        outs += [kp, ks]
    return tuple(np.ascontiguousarray(o.astype(np.float32)) for o in outs)


def branch_b(c, l):
    P = c.P
    A_ = c.Arena()
    f32 = A_.f32

    def t3(n_mid, n_in, dt=F32):
        a = f32(n_mid * n_in)[0:64]
        return a.rearrange("p (a b) -> p a b", a=n_mid)

    _pre = f32(131)
    pre = [_pre, _pre]
    pres = f32(NSQ * 7).rearrange("p (s t) -> p s t", t=7)
    halo = f32(27).rearrange("p (b t) -> p b t", t=3)
    acc = f32(128); act_ = f32(128); rinv = f32(128)
    sqb = A_.bf16(128)
    nrm = act_
    hq_raw = f32(768)[0:64]; hk_raw = f32(768)[0:64]
    HQ = hq_raw.rearrange("p (a b) -> p a b", a=6)
    HK = hk_raw.rearrange("p (a b) -> p a b", a=6)
    HV = t3(6, 128, F32R)
    HZ = t3(6, 128)
    szt = f32(128)
    G = f32(128); BETA = f32(128); GC = f32(128); EG = f32(128); DL = f32(128); NGC = f32(128); tmpd = G
    EGL = f32(16); nA = f32(1)
    EGLB = t3(6, 16)
    OT = t3(6, 128)
    mixB = A_.bf16(6 * 128)[0:64].rearrange("p (h t) -> p h t", h=6)
    sqo = hk_raw[:, 0:384].bitcast(BF16).rearrange("p (h t) -> p h t", h=6)
    rso = hq_raw.rearrange("p (a b) -> p a b", a=6)
    S = t3(6, 64, F32R)
    SS = t3(NSQ, 64)
    KDblk = t3(NSQ, 64, F32R)
    U = {}
    for nm in ("decT", "LT0", "kbT", "kdT", "vbT", "VN"):
        U[nm] = t3(3, 64)
    U["RT"] = U["LT0"]
    _nm_i = [0]

    def rt3():
        o = _nm_i[0]; _nm_i[0] += 192
        return c.nmr[0:64, o:o + 192].rearrange("p (a b) -> p a b", a=3)
    for nm in ("LT", "L", "P0", "P1", "PT0", "PT1", "X0", "X1", "Rr"):
        U[nm] = rt3()
    SC = [{nm: t3(3, 64) for nm in ("kbgT", "qgT", "aT", "VB", "KD")} for _ in range(2)]
    for sc_ in SC:
        sc_["TinvT"] = rt3()

    def Fv(ap):
        return ap.bitcast(F32)
    stg = f32(384)
    gatB = f32(9 * 51).rearrange("p (b t) -> p b t", b=9)
    c.optmp = f32(NS).rearrange("p (s i) -> p s i", i=SQ)

    def F(ap):
        return ap

    identr = R(c.cstf[0:64, 0:64], "cstf")
    shiftr = R(c.cstf[:, C_SHIFT:C_SHIFT + 64], "cstf")
    ident64 = R(c.cstf[0:64, 0:64], "cstf")
    identb64 = R(c.cstb[0:64, 0:64], "cstb")

    def EH(h):
        return R(c.cstf[0:6, C_EH + h * 64:C_EH + (h + 1) * 64], "cstf")

    load_wo(c, l, 256, 6, 64)
    wt = [c.wload(c.w_in[l][:, 1024 + i * 384:1024 + (i + 1) * 384], 384) for i in range(4)]
    P.dma("pool", c.wab[:], c.w_in[l][:, 2560:2572].rearrange("(k p) c -> p k c", p=128), writes=["wab"])

    P.copy("dve", R(S[:], "S"), R(c.zero[0:64, 0:384].rearrange("p (h t) -> p h t", h=6), "zero"))
    P.memset("dve", R(halo[:], "halo"), 0.0)
    P.act(R(nA[0:6, :], "nA"), R(c.vec[0:6, V_ALOG + l:V_ALOG + l + 1], "vec"), AF.Exp)
    P.ts("dve", R(nA[0:6, :], "nA"), R(nA[0:6, :], "nA"), -1.0, None, ALU.mult)
    for b3 in range(3):
        P.dma("sp", stg[0:NSQ * 3, :], c.scb[l][:, b3 * 384:(b3 + 1) * 384], writes=["stgB"])
        for bb in range(3):
            blk = b3 * 3 + bb
            pi = c.next_ps()
            P.transpose(R(c.ps[pi][:, 0:48], ("ps", pi)), R(stg[0:48, bb * 128:(bb + 1) * 128], "stgB"), R(c.cstf[0:48, 0:48], "cstf"))
            P.copy("act", R(gatB[:, blk, 0:48], ("gatB", blk)), R(c.ps[pi][:, 0:48], ("ps", pi)))

    groups = [(gb * 128, 128, gb // 4) for gb in range(16)] + [(T, NS, 4)]
    for gi, (col0, ncol, g5) in enumerate(groups):
        smp = (g5 == 4)
        nch = 1 if smp else 2
        cols = (col0, ncol)
        for blk in range(9):
            typ, sub = blk // 3, blk % 3
            wtile, wkey = wt[typ]
            pi = c.next_ps()
            proj(c, pi, wtile, wkey, sub * 128, 128, g5, cols=cols)
            vb = V_CBW + (l * 4) * 9 + blk
            wtap = [R(c.vec[:, vb + 9 * tap:vb + 9 * tap + 1], "vec") for tap in range(4)]
            if not smp:
                pr = pre[0]; pk = ("pre", 0)
                P.copy("dve", R(pr[:, 0:3], pk), R(halo[:, blk, :], ("halo", blk)))
                P.copy("act", R(pr[:, 3:3 + ncol], pk), R(c.ps[pi][:, 0:ncol], ("ps", pi)))
                P.copy("dve", R(halo[:, blk, :], ("halo", blk)), R(pr[:, ncol:ncol + 3], pk))
                P.ts("dve", R(acc[:, 0:ncol], "accB"), R(pr[:, 3:3 + ncol], pk), wtap[3], None, ALU.mult)
                for tap in range(3):
                    P.stt("dve", R(acc[:, 0:ncol], "accB"), R(pr[:, tap:tap + ncol], pk), wtap[tap], R(acc[:, 0:ncol], "accB"), ALU.mult, ALU.add)
            else:
                pk = ("pres",)
                P.copy("dve", R(pres[:, :, 0:3], pk), R(gatB[:, blk, 0:48].rearrange("p (s r) -> p s r", r=3), ("gatB", blk)))
                P.copy("act", R(pres[:, :, 3:7], pk), R(c.ps[pi][:, 0:ncol].rearrange("p (s i) -> p s i", i=SQ), ("ps", pi)))
                a3 = acc[:, 0:ncol].rearrange("p (s i) -> p s i", i=SQ)
                P.ts("dve", R(a3, "accB"), R(pres[:, :, 3:7], pk), wtap[3], None, ALU.mult)
                for tap in range(3):
                    P.stt("dve", R(a3, "accB"), R(pres[:, :, tap:tap + 4], pk), wtap[tap], R(a3, "accB"), ALU.mult, ALU.add)
                P.copy("act", R(gatB[:, blk, 3:51].rearrange("p (s r) -> p s r", r=3), ("gatB", blk)), R(pres[:, :, 4:7], pk))
                P.copy("act", R(gatB[:, blk, 0:3], ("gatB", blk)), R(halo[:, blk, :], ("halo", blk)))
            silu_(c, R(act_[:, 0:ncol], "actB"), R(acc[:, 0:ncol], "accB"))
            if typ < 2:
                P.tt("dve", R(sqb[:, 0:ncol], "sqb"), R(act_[:, 0:ncol], "actB"), R(act_[:, 0:ncol], "actB"), ALU.mult)
                p2 = c.next_ps()
                P.mm(R(c.ps[p2][:, 0:ncol], ("ps", p2)), R(c.cstb[:, C_BLK:C_BLK + 128], "cstb"), R(sqb[:, 0:ncol], "sqb"))
                rsqrt(c, R(rinv[:, 0:ncol], "rinvB"), R(c.ps[p2][:, 0:ncol], ("ps", p2)), 1.0, 1e-6)
                P.stt("dve", R(nrm[:, 0:ncol], "actB"), R(act_[:, 0:ncol], "actB"), 0.125 if typ == 0 else 1.0,
                      R(rinv[:, 0:ncol], "rinvB"), ALU.mult, ALU.mult)
            H = (HQ, HK, HV)[typ]
            hk = ("H", typ)
            P.copy("act", R(H[:, 2 * sub, 0:ncol], hk), R(nrm[0:64, 0:ncol], "actB"))
            p3 = c.next_ps()
            P.mm(R(c.ps[p3][0:64, 0:ncol], ("ps", p3)), shiftr, R(nrm[:, 0:ncol], "actB"))
            P.copy("act", R(H[:, 2 * sub + 1, 0:ncol], hk), R(c.ps[p3][0:64, 0:ncol], ("ps", p3)))
        for sub in range(3):
            pi = c.next_ps()
            proj(c, pi, wt[3][0], wt[3][1], sub * 128, 128, g5, cols=cols)
            silu_(c, R(szt[:, 0:ncol], "szt"), R(c.ps[pi][:, 0:ncol], ("ps", pi)))
            P.copy("dve", R(HZ[:, 2 * sub, 0:ncol], "HZ"), R(F(szt[0:64, 0:ncol]), "szt"))
            p3 = c.next_ps()
            P.mm(R(c.ps[p3][0:64, 0:ncol], ("ps", p3)), shiftr, R(szt[:, 0:ncol], "szt"))
            P.copy("act", R(HZ[:, 2 * sub + 1, 0:ncol], "HZ"), R(c.ps[p3][0:64, 0:ncol], ("ps", p3)))
        pa = c.next_ps(); pb = c.next_ps()
        for k in range(KC):
            P.mm(R(c.ps[pa][0:6, 0:ncol], ("ps", pa)), R(c.wab[:, k, 0:6], "wab"), R(c.hnT[:, k, col0:col0 + ncol], hkey(g5)),
                 start=(k == 0), stop=(k == KC - 1))
        for k in range(KC):
            P.mm(R(c.ps[pb][0:6, 0:ncol], ("ps", pb)), R(c.wab[:, k, 6:12], "wab"), R(c.hnT[:, k, col0:col0 + ncol], hkey(g5)),
                 start=(k == 0), stop=(k == KC - 1))
        rk = "rowsB"
        P.act(R(G[0:6, 0:ncol], rk), R(c.ps[pa][0:6, 0:ncol], ("ps", pa)), AF.Exp, bias=R(c.vec[0:6, V_DTB + l:V_DTB + l + 1], "vec"))
        P.act(R(G[0:6, 0:ncol], rk), R(G[0:6, 0:ncol], rk), AF.Ln, bias=1.0)
        P.ts("dve", R(G[0:6, 0:ncol], rk), R(G[0:6, 0:ncol], rk), R(nA[0:6, 0:1], "nA"), None, ALU.mult)
        sigmoid_(c, R(BETA[0:6, 0:ncol], rk), R(c.ps[pb][0:6, 0:ncol], ("ps", pb)))
        scm = c.cstf[0:6, C_SCAN4:C_SCAN4 + 64] if smp else c.cstf[0:6, C_SCAN64:C_SCAN64 + 128]
        gco, go, sco = GC[0:6, 0:ncol], G[0:6, 0:ncol], scm
        P.op("dve", lambda e, gco=gco, go=go, sco=sco: e.tensor_tensor_scan(gco, sco, go, 0.0, ALU.mult, ALU.add), reads=[rk, "cstf"], writes=[rk])
        P.act(R(EG[0:6, 0:ncol], rk), R(GC[0:6, 0:ncol], rk), AF.Exp)
        P.ts("dve", R(NGC[0:6, 0:ncol], rk), R(GC[0:6, 0:ncol], rk), -1.0, None, ALU.mult)
        clen = SQ if smp else 64
        nseg = ncol // clen
        gc3 = GC[0:6, 0:ncol].rearrange("p (n t) -> p n t", t=clen)
        glb = gc3[:, :, clen - 1:clen].to_broadcast([6, nseg, clen])
        P.tt("dve", R(tmpd[0:6, 0:ncol].rearrange("p (n t) -> p n t", t=clen), rk), R(glb, rk), R(gc3, rk), ALU.subtract)
        P.act(R(DL[0:6, 0:ncol], rk), R(tmpd[0:6, 0:ncol], rk), AF.Exp)
        P.act(R(EGL[0:6, 0:nseg], rk), R(gc3[:, :, clen - 1], rk), AF.Exp)
        pe_ = c.next_ps()
        for h in range(6):
            P.mm(R(c.ps[pe_][0:64, h * 16:h * 16 + nseg], ("ps", pe_)), EH(h), R(EGL[0:6, 0:nseg], rk))
        P.copy("dve", R(EGLB[:, :, 0:nseg], "EGLB"), R(c.ps[pe_][0:64, 0:96].rearrange("p (h n) -> p h n", h=6)[:, :, 0:nseg], ("ps", pe_)))

        uk = lambda nm: ("U", nm)
        maskap = c.cstb[0:64, 704:768] if smp else c.cstb[0:64, C_MCUR:C_MCUR + 64]
        offd = c.cstf[0:64, C_OFFD:C_OFFD + 64].unsqueeze(1).to_broadcast([64, 3, 64])
        idb = c.cstf[0:64, 0:64].unsqueeze(1).to_broadcast([64, 3, 64])
        nlev = 1 if smp else 5
        v3 = lambda pi_, lo=0: R(c.ps[pi_][0:64, lo:lo + 192].rearrange("p (a b) -> p a b", a=3), ("ps", pi_))

        def pre1(ch, hb, st_):
            cc = slice(ch * 64, ch * 64 + 64)
            heads = [hb * 3 + i for i in range(3)]
            hs = slice(hb * 3, hb * 3 + 3)
            sck = lambda nm: ("SC", st_, nm)
            SCt = SC[st_]
            pd = c.next_ps()
            for i, h in enumerate(heads):
                o = R(c.ps[pd][0:64, i * 64:(i + 1) * 64], ("ps", pd))
                P.mm(o, identb64, R(maskap, "cstb"), start=True, stop=False)
                P.mm(o, EH(h), R(GC[0:6, cc], rk), start=False, stop=False)
                P.mm(o, R(NGC[0:6, cc], rk), EH(h), start=False, stop=True)
            P.act(R(U["decT"][:].rearrange("p a b -> p (a b)"), uk("decT")), R(c.ps[pd][0:64, 0:192], ("ps", pd)), AF.Exp)
            pbb = c.next_ps(); pb2 = c.next_ps()
            for qi, row in enumerate((BETA, EG, DL)):
                pq_ = pbb if qi < 2 else pb2
                for i, h in enumerate(heads):
                    P.mm(R(c.ps[pq_][0:64, ((qi % 2) * 3 + i) * 64:((qi % 2) * 3 + i + 1) * 64], ("ps", pq_)), EH(h), R(row[0:6, cc], rk))

            def bview(qi):
                pq_ = pbb if qi < 2 else pb2
                return v3(pq_, (qi % 2) * 192)
            P.tt("dve", R(U["kbT"][:], uk("kbT")), R(HK[:, hs, cc], ("H", 1)), bview(0), ALU.mult)
            P.tt("dve", R(U["vbT"][:], uk("vbT")), R(HV[:, hs, cc], ("H", 2)), bview(0), ALU.mult)
            P.tt("dve", R(SCt["kbgT"][:], sck("kbgT")), R(U["kbT"][:], uk("kbT")), bview(1), ALU.mult)
            P.tt("dve", R(SCt["qgT"][:], sck("qgT")), R(HQ[:, hs, cc], ("H", 0)), bview(1), ALU.mult)
            P.tt("dve", R(U["kdT"][:], uk("kdT")), R(HK[:, hs, cc], ("H", 1)), bview(2), ALU.mult)
            pk_ = c.next_ps()
            for i, h in enumerate(heads):
                P.mm(R(c.ps[pk_][0:64, i * 64:(i + 1) * 64], ("ps", pk_)), R(HK[:, h, cc], ("H", 1)), R(U["kbT"][:, i, :], uk("kbT")))
                P.mm(R(c.ps[pk_][0:64, 192 + i * 64:192 + (i + 1) * 64], ("ps", pk_)), R(HK[:, h, cc], ("H", 1)), R(HQ[:, h, cc], ("H", 0)))
            P.tt("dve", R(U["LT0"][:], uk("LT0")), v3(pk_, 0), R(U["decT"][:], uk("decT")), ALU.mult)
            P.tt("dve", R(U["LT"][:], uk("LT")), R(U["LT0"][:], uk("LT0")), R(offd, "cstf"), ALU.mult)
            P.tt("dve", R(SCt["aT"][:], sck("aT")), v3(pk_, 192), R(U["decT"][:], uk("decT")), ALU.mult)
            pl = c.next_ps()
            for i in range(3):
                P.transpose(R(c.ps[pl][0:64, i * 64:(i + 1) * 64], ("ps", pl)), R(U["LT0"][:, i, :], uk("LT0")), ident64)
            P.tt("dve", R(U["L"][:], uk("L")), v3(pl), R(offd, "cstf"), ALU.mult)
            P.stt("dve", R(U["X0"][:], uk("X0")), R(Fv(U["LT"][:]), uk("LT")), -1.0, R(idb, "cstf"), ALU.mult, ALU.add)
            pv = c.next_ps()
            for i in range(3):
                P.transpose(R(c.ps[pv][0:64, i * 64:(i + 1) * 64], ("ps", pv)), R(U["vbT"][:, i, :], uk("vbT")), ident64)
                P.transpose(R(c.ps[pv][0:64, 192 + i * 64:192 + (i + 1) * 64], ("ps", pv)), R(U["kdT"][:, i, :], uk("kdT")), ident64)
            P.copy("act", R(SCt["VB"][:], sck("VB")), v3(pv, 0))
            P.copy("act", R(SCt["KD"][:], sck("KD")), v3(pv, 192))
            return {"P": "L", "PT": "LT", "X": "X0"}

        def neumann(st_, state, lev):
            sck = lambda nm: ("SC", st_, nm)
            Pc, PTc, Xc = state["P"], state["PT"], state["X"]
            Pn = "P%d" % (lev % 2); PTn = "PT%d" % (lev % 2); Xn = "X%d" % ((lev + 1) % 2)
            last = (lev == nlev - 1)
            pp = c.next_ps()
            for i in range(3):
                P.mm(R(c.ps[pp][0:64, i * 64:(i + 1) * 64], ("ps", pp)), R(U[PTc][:, i, :], uk(PTc)), R(U[Pc][:, i, :], uk(Pc)))
            P.copy("act", R(U[Pn][:], uk(Pn)), v3(pp))
            if not last:
                pt_ = c.next_ps()
                for i in range(3):
                    P.mm(R(c.ps[pt_][0:64, i * 64:(i + 1) * 64], ("ps", pt_)), R(U[Pc][:, i, :], uk(Pc)), R(U[PTc][:, i, :], uk(PTc)))
                P.copy("act", R(U[PTn][:], uk(PTn)), v3(pt_))
            px = c.next_ps()
            for i in range(3):
                P.mm(R(c.ps[px][0:64, i * 64:(i + 1) * 64], ("ps", px)), R(U[Pn][:, i, :], uk(Pn)), R(U[Xc][:, i, :], uk(Xc)))
            dst = R(SC[st_]["TinvT"][:], sck("TinvT")) if last else R(U[Xn][:], uk(Xn))
            P.tt("dve", dst, R(Fv(U[Xc][:]), uk(Xc)), v3(px), ALU.add)
            state["P"], state["PT"], state["X"] = Pn, PTn, Xn

        def scan_steps(ch, hb, st_):
            cc = slice(ch * 64, ch * 64 + 64)
            heads = [hb * 3 + i for i in range(3)]
            hs = slice(hb * 3, hb * 3 + 3)
            sck = lambda nm: ("SC", st_, nm)
            SCt = SC[st_]

            def s_g():
                pr_ = c.next_ps()
                for i, h in enumerate(heads):
                    P.mm(R(c.ps[pr_][0:64, i * 64:(i + 1) * 64], ("ps", pr_)), R(SCt["kbgT"][:, i, :], sck("kbgT")), R(S[:, h, :], ("S", h)))
                P.tt("dve", R(U["Rr"][:], uk("Rr")), R(SCt["VB"][:], sck("VB")), v3(pr_), ALU.subtract)

            def s_h():
                pn = c.next_ps()
                for i in range(3):
                    P.mm(R(c.ps[pn][0:64, i * 64:(i + 1) * 64], ("ps", pn)), R(SCt["TinvT"][:, i, :], sck("TinvT")), R(U["Rr"][:, i, :], uk("Rr")))
                P.copy("act", R(U["VN"][:], uk("VN")), v3(pn))

            def s_i():
                po = c.next_ps()
                for i, h in enumerate(heads):
                    o = R(c.ps[po][0:64, i * 64:(i + 1) * 64], ("ps", po))
                    P.mm(o, R(S[:, h, :], ("S", h)), R(SCt["qgT"][:, i, :], sck("qgT")), start=True, stop=False)
                    P.mm(o, R(U["VN"][:, i, :], uk("VN")), R(SCt["aT"][:, i, :], sck("aT")), start=False, stop=True)
                P.copy("act", R(OT[:, hs, cc], "OT"), v3(po))

            def s_j():
                pS = c.next_ps()
                for i, h in enumerate(heads):
                    P.mm(R(c.ps[pS][0:64, i * 64:(i + 1) * 64], ("ps", pS)), R(SCt["KD"][:, i, :], sck("KD")), R(U["VN"][:, i, :], uk("VN")))
                for i, h in enumerate(heads):
                    P.stt("dve", R(S[:, h, :], ("S", h)), R(S[:, h, :], ("S", h)), R(EGLB[:, h, ch:ch + 1], "EGLB"),
                          R(c.ps[pS][0:64, i * 64:(i + 1) * 64], ("ps", pS)), ALU.mult, ALU.add)
            return [s_g, s_h, s_i, s_j]

        if not smp:
            units = [(ch, hb) for ch in range(nch) for hb in range(2)]
            st0 = pre1(units[0][0], units[0][1], 0)
            for lev in range(nlev):
                neumann(0, st0, lev)
            for ui in range(1, len(units)):
                st_ = ui % 2
                state = pre1(units[ui][0], units[ui][1], st_)
                steps = scan_steps(units[ui - 1][0], units[ui - 1][1], 1 - st_)
                for lev in range(nlev):
                    neumann(st_, state, lev)
                    if lev < len(steps):
                        steps[lev]()
                for fn in steps[nlev:]:
                    fn()
            for fn in scan_steps(units[-1][0], units[-1][1], (len(units) - 1) % 2):
                fn()
        else:
            ch = 0
            cc = slice(0, 64)
            for hb in range(2):
                heads = [hb * 3 + i for i in range(3)]
                st_ = 0
                sck = lambda nm: ("SC", 0, nm)
                SCt = SC[0]
                state = pre1(ch, hb, 0)
                for lev in range(nlev):
                    neumann(0, state, lev)
                for i, h in enumerate(heads):
                    P.dma("sp", SS[:], c.sgd[l][:, h].rearrange("s k v -> k s v"), writes=["SS"])
                    pr_ = c.next_ps()
                    for s_ in range(NSQ):
                        P.mm(R(c.ps[pr_][0:64, s_ * 4:(s_ + 1) * 4], ("ps", pr_)), R(SS[:, s_, :], "SS"),
                             R(SCt["kbgT"][:, i, s_ * 4:(s_ + 1) * 4], sck("kbgT")))
                    P.tt("dve", R(U["RT"][:, i, :], uk("LT0")), R(U["vbT"][:, i, :], uk("vbT")), R(c.ps[pr_][0:64, 0:64], ("ps", pr_)), ALU.subtract)
                    pq = c.next_ps()
                    P.transpose(R(c.ps[pq][0:64, 0:64], ("ps", pq)), R(U["RT"][:, i, :], uk("LT0")), ident64)
                    P.copy("act", R(U["Rr"][:, i, :], uk("Rr")), R(c.ps[pq][0:64, 0:64], ("ps", pq)))
                    pn = c.next_ps()
                    P.mm(R(c.ps[pn][0:64, 0:64], ("ps", pn)), R(SCt["TinvT"][:, i, :], sck("TinvT")), R(U["Rr"][:, i, :], uk("Rr")))
                    P.copy("act", R(U["VN"][:, i, :], uk("VN")), R(c.ps[pn][0:64, 0:64], ("ps", pn)))
                    po = c.next_ps()
                    P.mm(R(c.ps[po][0:64, 0:64], ("ps", po)), R(U["VN"][:, i, :], uk("VN")), R(SCt["aT"][:, i, :], sck("aT")), start=True, stop=False)
                    for s_ in range(NSQ):
                        P.mm(R(c.ps[po][0:64, s_ * 4:(s_ + 1) * 4], ("ps", po)), R(SS[:, s_, :], "SS"),
                             R(SCt["qgT"][:, i, s_ * 4:(s_ + 1) * 4], sck("qgT")), start=False, stop=(s_ == NSQ - 1))
                    P.copy("act", R(OT[:, h, cc], "OT"), R(c.ps[po][0:64, 0:64], ("ps", po)))
                    seqm = c.cstf[0:64, C_SEQM:C_SEQM + 16].unsqueeze(2).to_broadcast([64, NSQ, 64])
                    kdb = SCt["KD"][:, i, :].unsqueeze(1).to_broadcast([64, NSQ, 64])
                    P.tt("dve", R(KDblk[:], "KDblk"), R(kdb, sck("KD")), R(seqm, "cstf"), ALU.mult)
                    eglb = EGLB[:, h, 0:NSQ].unsqueeze(2).to_broadcast([64, NSQ, 64])
                    P.tt("dve", R(SS[:], "SS"), R(SS[:], "SS"), R(eglb, "EGLB"), ALU.mult)
                    for half in range(2):
                        pS = c.next_ps()
                        for s8 in range(8):
                            s_ = half * 8 + s8
                            P.mm(R(c.ps[pS][0:64, s8 * 64:(s8 + 1) * 64], ("ps", pS)), R(KDblk[:, s_, :], "KDblk"), R(U["VN"][:, i, :], uk("VN")))
                        P.tt("dve", R(SS[:, half * 8:half * 8 + 8, :], "SS"), R(SS[:, half * 8:half * 8 + 8, :], "SS"),
                             R(c.ps[pS][0:64, :].rearrange("p (s v) -> p s v", s=8), ("ps", pS)), ALU.add)
                    P.dma("sp", c.gd_s[l][:, h].rearrange("s k v -> k s v"), SS[:], reads=["SS"], final=True)
        P.tt("dve", R(sqo[:, :, 0:ncol], ("H", 1)), R(OT[:, :, 0:ncol], "OT"), R(OT[:, :, 0:ncol], "OT"), ALU.mult)
        for hb in range(2):
            pg = c.next_ps()
            for i in range(3):
                P.mm(R(c.ps[pg][0:64, i * 128:i * 128 + ncol], ("ps", pg)), R(c.cstb[0:64, C_ONES:C_ONES + 64], "cstb"), R(sqo[:, hb * 3 + i, 0:ncol], ("H", 1)))
            rsqrt(c, R(rso[:, hb * 3:hb * 3 + 3, 0:ncol], ("H", 0)),
                  R(c.ps[pg][0:64, 0:384].rearrange("p (a b) -> p a b", a=3)[:, :, 0:ncol], ("ps", pg)), 1.0 / 64, EPS)
        P.tt("dve", R(rso[:, :, 0:ncol], ("H", 0)), R(rso[:, :, 0:ncol], ("H", 0)), R(OT[:, :, 0:ncol], "OT"), ALU.mult)
        P.stt("dve", R(mixB[:, :, 0:ncol], "mixB"), R(rso[:, :, 0:ncol], ("H", 0)), R(c.vec[0:64, V_GNW + l:V_GNW + l + 1], "vec"),
              R(HZ[:, :, 0:ncol], "HZ"), ALU.mult, ALU.mult)
        outproj(c, l, g5, lambda mc: R(mixB[:, mc, 0:ncol], "mixB"), 6, 64, cols=cols)
    P.dma("sp", c.gd_p[l].rearrange("h k v -> k h v"), F(S[:]), reads=[("S", h) for h in range(6)], final=True)
    for b3 in range(3):
        for bb in range(3):
            blk = b3 * 3 + bb
            pi = c.next_ps()
            P.transpose(R(c.ps[pi][0:51, 0:128], ("ps", pi)), R(gatB[:, blk, :], ("gatB", blk)), c.ident)
            P.copy("dve", R(stg[0:51, bb * 128:(bb + 1) * 128], "stgB"), R(c.ps[pi][0:51, 0:128], ("ps", pi)))
        P.dma("sp", c.cb_p[l][:, b3 * 384:(b3 + 1) * 384], stg[0:3, :], reads=["stgB"], final=True)
        P.dma("sp", c.cb_s[l][:, b3 * 384:(b3 + 1) * 384], stg[3:51, :], reads=["stgB"], final=True)
    P.barrier()


CDIL = (1, 4, 16)
CWIN = (128, 512, 2048)


def _rope(c, buf3, rows, tt, rt, key):
    P = c.P
    nh = buf3.shape[1]
    x1 = buf3[:, :, 0:8]; x2 = buf3[:, :, 8:16]
    cos = c.ropet[0:rows, tt, 0:8].unsqueeze(1).to_broadcast([rows, nh, 8])
    sin = c.ropet[0:rows, tt, 8:16].unsqueeze(1).to_broadcast([rows, nh, 8])
    t = [r_[0:rows, 0:nh * 8].rearrange("p (h e) -> p h e", e=8) for r_ in rt]
    P.tt("dve", R(t[0], "ropet0"), R(x1, key), R(cos, "rope"), ALU.mult)
    P.tt("dve", R(t[1], "ropet1"), R(x2, key), R(sin, "rope"), ALU.mult)
    P.tt("dve", R(t[2], "ropet2"), R(x2, key), R(cos, "rope"), ALU.mult)
    P.tt("dve", R(t[3], "ropet3"), R(x1, key), R(sin, "rope"), ALU.mult)
    P.tt("dve", R(x1, key), R(t[0], "ropet0"), R(t[1], "ropet1"), ALU.subtract)
    P.tt("dve", R(x2, key), R(t[2], "ropet2"), R(t[3], "ropet3"), ALU.add)


def branch_c(c, l):
    P = c.P
    A_ = c.Arena(); f32 = A_.f32
    ZS = f32(768)[0:64].rearrange("p (h t) -> p h t", h=6)
    KT = A_.bf16(6 * T)[0:64].rearrange("p (h t) -> p h t", h=6)
    VS = A_.bf16(48 * 128).rearrange("p (n x) -> p n x", x=128)
    kv_raw = f32(768)
    kvst = kv_raw.rearrange("p (g x) -> p g x", g=3)
    qsb = kv_raw[:, 0:384]
    rt = [f32(48) for _ in range(4)]
    QTt = A_.bf16(6 * 128)[0:64].rearrange("p (h t) -> p h t", h=6)
    PT = [A_.bf16(512) for _ in range(2)]
    dtot = f32(256)[0:64].rearrange("p (h t) -> p h t", h=2)
    onorm = kv_raw[0:64].rearrange("p (h t) -> p h t", h=6)
    mixC = A_.bf16(768)[0:64].rearrange("p (h t) -> p h t", h=6)

    load_wo(c, l, 640, 6, 64)
    wk_t, wk_k = c.wload(c.w_in[l][:, 2956:3340], 384)
    wv_t, wv_k = c.wload(c.w_in[l][:, 3340:3724], 384)
    wq_t, wq_k = c.wload(c.w_in[l][:, 2572:2956], 384)
    wz_t, wz_k = c.wload(c.w_in[l][:, 3724:4108], 384)
    ident = c.ident

    def tokproj(pi, wt_, wk_, tt, rows, ncols=384, c0=0):
        col0 = tt * 128
        g5 = min(tt // 4, 4)
        for k in range(KC):
            P.mm(R(c.ps[pi][0:rows, 0:ncols], ("ps", pi)), R(c.hnT[:, k, col0:col0 + rows], hkey(g5)), R(wt_[:, k, c0:c0 + ncols], wk_),
                 start=(k == 0), stop=(k == KC - 1))

    def kv_tile(tt, kvst_, rt_, ksT_=None, vtok_=None):
        rows = 128 if tt < 16 else NS
        pk = c.next_ps(); tokproj(pk, wk_t, wk_k, tt, rows)
        pv = c.next_ps(); tokproj(pv, wv_t, wv_k, tt, rows)
        P.copy("act", R(kvst_[0:rows, :, 0:128], "kvst"), R(c.ps[pk][0:rows, 0:384].rearrange("p (g x) -> p g x", g=3), ("ps", pk)))
        P.copy("act", R(kvst_[0:rows, :, 128:256], "kvst"), R(c.ps[pv][0:rows, 0:384].rearrange("p (g x) -> p g x", g=3), ("ps", pv)))
        for gi in range(3):
            _rope(c, kvst_[0:rows, gi, 0:128].rearrange("p (h d) -> p h d", h=2), rows, tt, rt_, "kvst")
        for gi in range(3):
            if tt < 16:
                lo = T - CWIN[gi]
                if tt * 128 >= lo:
                    P.dma("sp", c.kv_p[gi][l][tt * 128 - lo:tt * 128 - lo + 128, :], kvst_[:, gi, :], reads=["kvst"], final=True)
            else:
                P.dma("sp", c.kv_s[gi][l], kvst_[0:NS, gi, :], reads=["kvst"], final=True)
        for half in range(2):
            pt = c.next_ps()
            for i in range(3):
                hx = half * 3 + i
                gi, h = hx // 2, hx % 2
                P.transpose(R(c.ps[pt][0:64, i * 128:i * 128 + rows], ("ps", pt)), R(kvst_[0:rows, gi, h * 64:(h + 1) * 64], "kvst"),
                            R(c.cstf[0:rows, 0:rows], "cstf"))
            src = c.ps[pt][0:64, 0:384].rearrange("p (h t) -> p h t", h=3)[:, :, 0:rows]
            if tt < 16:
                P.copy("act", R(KT[:, half * 3:half * 3 + 3, tt * 128:tt * 128 + 128], ("KT", tt)), R(src, ("ps", pt)))
            else:
                P.copy("act", R(ksT_[:, half * 3:half * 3 + 3, :], "ksT"), R(src, ("ps", pt)))
        if tt == 16:
            P.copy("dve", R(vtok_[:], "vtok"), R(kvst_[0:NS, :, 128:256], "kvst"))

    for tt in range(16):
        kv_tile(tt, kvst, rt)
    for gi in range(3):
        dil = CDIL[gi]; nb = 16 // dil
        for q4 in range(4):
            pi = c.next_ps()
            for j in range(4):
                st_ = q4 * 4 + j
                r, n = st_ // nb, st_ % nb
                lo = r + dil * n * 128
                for k in range(KC):
                    P.mm(R(c.ps[pi][:, j * 128:(j + 1) * 128], ("ps", pi)), (c.hnT[:, k, lo:lo + dil * 127 + 1:dil], tuple(hkey(g) for g in range(4))),
                         R(wv_t[:, k, gi * 128:(gi + 1) * 128], wv_k), start=(k == 0), stop=(k == KC - 1))
            P.copy("act" if q4 % 2 == 0 else "dve", R(VS[:, gi * 16 + q4 * 4:gi * 16 + q4 * 4 + 4, :], ("VS", gi)),
                   R(c.ps[pi][:].rearrange("p (n x) -> p n x", x=128), ("ps", pi)))

    c.ps_n = 4; c.ps_i = 0
    psO = [4, 5]; psD = [6, 7]
    onesb64 = R(c.cstb[:, C_ONES:C_ONES + 64], "cstb")
    mcur = c.cstb[:, C_MCUR:C_MCUR + 128]; mprev = c.cstb[:, C_MPREV:C_MPREV + 128]
    allkt = tuple(("KT", t_) for t_ in range(16))

    def q_and_z(tt, rows, QT_dst, qkey, Z_dst, qsb=qsb, rt=rt):
        pq = c.next_ps(); tokproj(pq, wq_t, wq_k, tt, rows)
        P.copy("act", R(qsb[0:rows, :], "kvst"), R(c.ps[pq][0:rows, 0:384], ("ps", pq)))
        _rope(c, qsb[0:rows, :].rearrange("p (h d) -> p h d", h=6), rows, tt, rt, "kvst")
        for half in range(2):
            pt = c.next_ps()
            for i in range(3):
                hx = half * 3 + i
                P.transpose(R(c.ps[pt][0:64, i * 128:i * 128 + rows], ("ps", pt)), R(qsb[0:rows, hx * 64:(hx + 1) * 64], "kvst"),
                            R(c.cstf[0:rows, 0:rows], "cstf"))
            P.copy("dve", R(QT_dst[:, half * 3:half * 3 + 3, 0:rows], qkey),
                   R(c.ps[pt][0:64, 0:384].rearrange("p (h t) -> p h t", h=3)[:, :, 0:rows], ("ps", pt)))
        col0 = tt * 128; g5 = min(tt // 4, 4)
        for half in range(2):
            pz = c.next_ps()
            for i in range(3):
                hx = half * 3 + i
                for k in range(KC):
                    P.mm(R(c.ps[pz][0:64, i * 128:i * 128 + rows], ("ps", pz)), R(wz_t[:, k, hx * 64:(hx + 1) * 64], wz_k),
                         R(c.hnT[:, k, col0:col0 + rows], hkey(g5)), start=(k == 0), stop=(k == KC - 1))
            silu_(c, R(Z_dst[:, half * 3:half * 3 + 3, 0:rows], "ZS"),
                  R(c.ps[pz][0:64, 0:384].rearrange("p (h t) -> p h t", h=3)[:, :, 0:rows], ("ps", pz)))

    for tt in range(16):
        q_and_z(tt, 128, QTt, "QTt", ZS)

        def pv_den(hx, ocols, vs_tile, h, pt_ap, ptkey, first, last):
            o = c.ps[psO[hx // 3]][0:64, (hx % 3) * 128 + ocols[0]:(hx % 3) * 128 + ocols[0] + ocols[1]]
            d = c.ps[psD[hx // 3]][0:64, (hx % 3) * 128 + ocols[0]:(hx % 3) * 128 + ocols[0] + ocols[1]]
            P.mm(R(o, ("ps", psO[hx // 3])), R(VS[:, vs_tile, h * 64:(h + 1) * 64], ("VS", vs_tile // 16)), R(pt_ap, ptkey), start=first, stop=last)
            P.mm(R(d, ("ps", psD[hx // 3])), onesb64, R(pt_ap, ptkey), start=first, stop=last)

        for gi in range(3):
            dil = CDIL[gi]; nb = 16 // dil; nq = 128 // dil
            n = tt // dil
            qo = (tt % dil) * nq
            kbl = [0] if n == 0 else [0, 1]
            pS = c.next_ps()
            ptile = PT[gi % 2]; ptk = ("PT", gi % 2)
            slots = {}
            for kbi in kbl:
                kb = n - kbi
                for h in range(2):
                    hx = gi * 2 + h
                    for r in range(dil):
                        col = ((kbi * 2 + h) * dil + r) * nq
                        slots[(kbi, h, r)] = col
                        klo = r + dil * kb * 128
                        o = R(c.ps[pS][:, col:col + nq], ("ps", pS))
                        P.mm(o, (KT[:, hx, klo:klo + dil * 127 + 1:dil], allkt), R(QTt[:, hx, r:128:dil], "QTt"), start=True, stop=False)
                        msk = (mcur if kbi == 0 else mprev)[:, qo:qo + nq]
                        P.mm(o, c.identb, R(msk, "cstb"), start=False, stop=True)
            used = len(kbl) * 2 * dil * nq
            P.act(R(ptile[:, 0:used], ptk), R(c.ps[pS][:, 0:used], ("ps", pS)), AF.Exp, scale=0.125)
            for h in range(2):
                hx = gi * 2 + h
                for r in range(dil):
                    for ki, kbi in enumerate(kbl):
                        kb = n - kbi
                        col = slots[(kbi, h, r)]
                        pv_den(hx, (r * nq, nq), gi * 16 + r * nb + kb, h, ptile[:, col:col + nq], ptk, ki == 0, ki == len(kbl) - 1)
        def nat(ap, dil):
            if dil == 1:
                return ap
            return ap.rearrange("p (r m) -> p m r", r=dil)
        for h in range(2):
            dv_ = dtot[:, h, :]
            P.copy("dve", R(dv_, "dtot"), R(c.ps[psD[0]][0:64, h * 128:(h + 1) * 128], ("ps", psD[0])))
            for gi in (1, 2):
                hx = gi * 2 + h
                dil = CDIL[gi]
                src = c.ps[psD[hx // 3]][0:64, (hx % 3) * 128:(hx % 3) * 128 + 128]
                dvw = dv_.rearrange("p (m r) -> p m r", r=dil)
                P.tt("dve", R(dvw, "dtot"), R(dvw, "dtot"), R(nat(src, dil), ("ps", psD[hx // 3])), ALU.add)
            P.op("dve", lambda e, dv_=dv_: e.reciprocal(dv_, dv_), reads=["dtot"], writes=["dtot"])
        for hx in range(6):
            gi, h = hx // 2, hx % 2
            dil = CDIL[gi]
            src = c.ps[psO[hx // 3]][0:64, (hx % 3) * 128:(hx % 3) * 128 + 128]
            if dil == 1:
                P.tt("dve", R(onorm[:, hx, :], "kvst"), R(src, ("ps", psO[hx // 3])), R(dtot[:, h, :], "dtot"), ALU.mult)
            else:
                P.tt("dve", R(onorm[:, hx, :].rearrange("p (m r) -> p m r", r=dil), "kvst"), R(nat(src, dil), ("ps", psO[hx // 3])),
                     R(dtot[:, h, :].rearrange("p (m r) -> p m r", r=dil), "dtot"), ALU.mult)
        P.tt("dve", R(mixC[:], "mixC"), R(onorm[:], "kvst"), R(ZS[:], "ZS"), ALU.mult)
        outproj(c, l, tt // 4, lambda mc: R(mixC[:, mc, :], "mixC"), 6, 64, cols=(tt * 128, 128))
    c.ps_n = 8; c.ps_i = 0

    P.barrier()
    B_ = c.Arena()
    b32 = B_.f32
    kvst_s = b32(768).rearrange("p (g x) -> p g x", g=3)
    qsb_s = b32(384)
    rt_s = [b32(48) for _ in range(4)]
    ksT = b32(384)[0:64].rearrange("p (h t) -> p h t", h=6)
    qsT = b32(384)[0:64].rearrange("p (h t) -> p h t", h=6)
    vtok = b32(384)[0:64].rearrange("p (g x) -> p g x", g=3)
    ZSs = b32(384)[0:64].rearrange("p (h t) -> p h t", h=6)
    c.optmp = b32(NS).rearrange("p (s i) -> p s i", i=SQ)
    kv_tile(16, kvst_s, rt_s, ksT, vtok)
    q_and_z(16, NS, qsT, "qsT", ZSs, qsb=qsb_s, rt=rt_s)
    CK = [b32(9 * 256).rearrange("p (b x) -> p b x", b=9) for _ in range(2)]
    KcT = b32(18 * 128)[0:64].rearrange("p (b t) -> p b t", b=18)
    PTn = b32(384)[0:64].rearrange("p (h t) -> p h t", h=6)
    PTc = b32(24)
    dts = b32(128)[0:64].rearrange("p (h t) -> p h t", h=2)
    ons = b32(384)[0:64].rearrange("p (h t) -> p h t", h=6)
    mixS = B_.bf16(384)[0:64].rearrange("p (h t) -> p h t", h=6)
    ones64f = R(c.cstf[0:64, C_ONES:C_ONES + 64], "cstf")
    ones128f = R(c.cstf[:, C_ONES:C_ONES + 64], "cstf")
    pO = 6; pD = 7
    c.ps_n = 6
    pn_ = c.next_ps()
    for hx in range(6):
        gi = hx // 2
        o = R(c.ps[pn_][0:64, hx * 64:(hx + 1) * 64], ("ps", pn_))
        P.mm(o, R(ksT[:, hx, :], "ksT"), R(qsT[:, hx, :], "qsT"), start=True, stop=False)
        msk = c.cstb[0:64, 704:768] if gi == 0 else c.cstb[0:64, 768:832]
        P.mm(o, R(c.cstb[0:64, 0:64], "cstb"), R(msk, "cstb"), start=False, stop=True)
    P.act(R(PTn[:].rearrange("p h t -> p (h t)"), "PTn"), R(c.ps[pn_][0:64, 0:384], ("ps", pn_)), AF.Exp, scale=0.125)
    for hx in range(6):
        gi, h = hx // 2, hx % 2
        P.mm(R(c.ps[pO][0:64, hx * 64:(hx + 1) * 64], ("ps", pO)), R(vtok[:, gi, h * 64:(h + 1) * 64], "vtok"), R(PTn[:, hx, :], "PTn"), start=True, stop=False)
        P.mm(R(c.ps[pD][0:64, hx * 64:(hx + 1) * 64], ("ps", pD)), ones64f, R(PTn[:, hx, :], "PTn"), start=True, stop=False)
    for s_ in range(NSQ):
        ck = CK[s_ % 2]; ckk = ("CK", s_ % 2)
        P.dma("sp", ck[:, 0, :], c.ck128[l][s_], writes=[ckk])
        P.dma("sp", ck[:, 1:5, :], c.ck512[l][s_].rearrange("(m i) x -> m i x", i=4), writes=[ckk])
        P.dma("sp", ck[:, 5:9, :], c.ck2048[l][s_].rearrange("(m i) x -> m i x", i=16)[:, 0:4, :], writes=[ckk])
        for q5 in range(5):
            pt = c.next_ps()
            nn = 4 if q5 < 4 else 2
            for j in range(nn):
                bh = q5 * 4 + j
                blk, h = bh // 2, bh % 2
                P.transpose(R(c.ps[pt][0:64, j * 128:(j + 1) * 128], ("ps", pt)), R(ck[:, blk, h * 64:(h + 1) * 64], ckk), ident)
            P.copy("act" if q5 % 2 == 0 else "dve", R(KcT[:, q5 * 4:q5 * 4 + nn, :], "KcT"),
                   R(c.ps[pt][0:64, 0:nn * 128].rearrange("p (b t) -> p b t", b=nn), ("ps", pt)))
        psc = c.next_ps()
        for h in range(2):
            o = R(c.ps[psc][:, h * 4:h * 4 + 4], ("ps", psc))
            P.mm(o, R(KcT[:, h, :], "KcT"), R(qsT[:, h, s_ * 4:s_ * 4 + 4], "qsT"), start=True, stop=False)
            P.mm(o, c.identb, R(c.cstb[:, 832:836], "cstb"), start=False, stop=True)
            for gi in (1, 2):
                for i in range(SQ):
                    blk = (1 if gi == 1 else 5) + i
                    col = gi * 8 + h * 4 + i
                    P.mm(R(c.ps[psc][:, col:col + 1], ("ps", psc)), R(KcT[:, blk * 2 + h, :], "KcT"),
                         R(qsT[:, gi * 2 + h, s_ * 4 + i:s_ * 4 + i + 1], "qsT"))
        P.act(R(PTc[:, 0:24], "PTc"), R(c.ps[psc][:, 0:24], ("ps", psc)), AF.Exp, scale=0.125)
        for h in range(2):
            for gi in range(3):
                hx = gi * 2 + h
                if gi == 0:
                    items = [(0, s_ * 4, 4, h * 4)]
                else:
                    items = [((1 if gi == 1 else 5) + i, s_ * 4 + i, 1, gi * 8 + h * 4 + i) for i in range(SQ)]
                for (blk, ocol, w_, pcol) in items:
                    P.mm(R(c.ps[pO][0:64, hx * 64 + ocol:hx * 64 + ocol + w_], ("ps", pO)), R(ck[:, blk, 128 + h * 64:128 + (h + 1) * 64], ckk),
                         R(PTc[:, pcol:pcol + w_], "PTc"), start=False, stop=True)
                    P.mm(R(c.ps[pD][0:64, hx * 64 + ocol:hx * 64 + ocol + w_], ("ps", pD)), ones128f,
                         R(PTc[:, pcol:pcol + w_], "PTc"), start=False, stop=True)
    for h in range(2):
        P.copy("dve", R(dts[:, h, :], "dts"), R(c.ps[pD][0:64, h * 64:(h + 1) * 64], ("ps", pD)))
        for gi in (1, 2):
            hx = gi * 2 + h
            P.tt("dve", R(dts[:, h, :], "dts"), R(dts[:, h, :], "dts"), R(c.ps[pD][0:64, hx * 64:(hx + 1) * 64], ("ps", pD)), ALU.add)
    dall = dts[:]
    P.op("dve", lambda e: e.reciprocal(dall, dall), reads=["dts"], writes=["dts"])
    for hx in range(6):
        P.tt("dve", R(ons[:, hx, :], "ons"), R(c.ps[pO][0:64, hx * 64:(hx + 1) * 64], ("ps", pO)), R(dts[:, hx % 2, :], "dts"), ALU.mult)
    P.tt("dve", R(mixS[:], "mixS"), R(ons[:], "ons"), R(ZSs[:, :, 0:NS], "ZS"), ALU.mult)
    c.ps_n = 8; c.ps_i = 0
    outproj(c, l, 4, lambda mc: R(mixS[:, mc, :], "mixS"), 6, 64, cols=(T, NS))
    P.barrier()
```

```python
import contextlib
import numpy as np
import concourse.bass as bass
import concourse.mybir as mybir
from concourse.bass_utils import run_bass_kernel_spmd

F32 = mybir.dt.float32
F32R = mybir.dt.float32r
BF16 = mybir.dt.bfloat16
AF = mybir.ActivationFunctionType
ALU = mybir.AluOpType
AX = mybir.AxisListType


class Prog:
    COMPUTE = ("pe", "act", "dve", "pool")
    ALL = ("pe", "act", "dve", "pool", "sp")

    def __init__(self, nc, stack, n_dma_sems=24):
        self.nc = nc
        self.stack = stack
        self.streams = {e: [] for e in self.ALL}
        self.esem = {e: stack.enter_context(nc.semaphore("prog_" + e)) for e in self.COMPUTE}
        self.ecount = {e: 0 for e in self.COMPUTE}
        self.known = {e: {} for e in self.ALL}
        self.res = {}
        self.dsem = {}
        for q in ("sp", "act", "pool"):
            self.dsem[q] = [[stack.enter_context(nc.semaphore("dq_%s_%d" % (q, i))), 0] for i in range(n_dma_sems)]
        self.dnext = {q: 0 for q in self.dsem}
        self.final_tokens = []

    def _deps(self, eng, reads, writes, same_engine_sem=None):
        toks = []
        for k in reads:
            r = self.res.get(k)
            if r and r["w"] is not None:
                toks.append(r["w"])
        for k in writes:
            r = self.res.get(k)
            if r:
                if r["w"] is not None:
                    toks.append(r["w"])
                toks.extend(r["r"])
        return toks

    def _record(self, tok, reads, writes):
        for k in reads:
            r = self.res.setdefault(k, {"w": None, "r": []})
            r["r"].append(tok)
        for k in writes:
            self.res[k] = {"w": tok, "r": []}

    def _waits(self, eng, toks, skip_sem=None):
        need = {}
        for (sem, val) in toks:
            if skip_sem is not None and sem is skip_sem:
                continue
            sid = id(sem)
            if self.known[eng].get(sid, 0) >= val:
                continue
            if sid not in need or need[sid][1] < val:
                need[sid] = (sem, val)
        for sid, (sem, val) in need.items():
            self.known[eng][sid] = val
        return list(need.values())

    def op(self, eng, fn, reads=(), writes=()):
        toks = self._deps(eng, reads, writes)
        own = self.esem[eng]
        if eng == "pe":
            waits = self._waits(eng, toks, skip_sem=own)
        else:
            waits = self._waits(eng, toks)
        self.ecount[eng] += 1
        tok = (own, self.ecount[eng])
        self.streams[eng].append((waits, fn, (own, 1)))
        self._record(tok, reads, writes)
        return tok

    def dma(self, q, out, in_, reads=(), writes=(), final=False, **kw):
        pool = self.dsem[q]
        i = self.dnext[q]
        self.dnext[q] = (i + 1) % len(pool)
        sem, val = pool[i]
        toks = self._deps(q, reads, writes)
        if val > 0:
            toks = toks + [(sem, val)]
        eng = {"sp": "sp", "act": "act", "pool": "pool"}[q]
        waits = self._waits(eng, toks)
        pool[i][1] = val + 16
        tok = (sem, val + 16)

        def fn(e, out=out, in_=in_, kw=kw):
            return e.dma_start(out, in_, **kw)
        self.streams[eng].append((waits, fn, (sem, 16)))
        self._record(tok, reads, writes)
        if final:
            self.final_tokens.append(tok)
        return tok


    @staticmethod
    def _k(*xs):
        out = []
        for x in xs:
            if isinstance(x, tuple):
                out.extend(x[1])
        return out

    @staticmethod
    def _a(x):
        return x[0] if isinstance(x, tuple) else x

    def mm(self, out, lhsT, rhs, start=True, stop=True):
        o, l, r = out[0], lhsT[0], rhs[0]
        return self.op("pe", lambda e: e.matmul(o, l, r, start=start, stop=stop),
                       reads=self._k(lhsT, rhs), writes=self._k(out))

    def transpose(self, out, in_, ident):
        o, i, d = out[0], in_[0], ident[0]
        return self.op("pe", lambda e: e.transpose(o, i, d), reads=self._k(in_, ident), writes=self._k(out))

    def act(self, out, in_, func, bias=0.0, scale=1.0, eng="act"):
        o, i, b, sc = out[0], in_[0], self._a(bias), self._a(scale)
        return self.op(eng, lambda e: e.activation(o, i, func, bias=b, scale=sc),
                       reads=self._k(in_, bias, scale), writes=self._k(out))

    def copy(self, eng, out, in_):
        o, i = out[0], in_[0]
        if eng == "act":
            return self.op(eng, lambda e: e.copy(o, i), reads=self._k(in_), writes=self._k(out))
        return self.op(eng, lambda e: e.tensor_copy(o, i), reads=self._k(in_), writes=self._k(out))

    def tt(self, eng, out, in0, in1, op):
        o, a, b = out[0], in0[0], in1[0]
        return self.op(eng, lambda e: e.tensor_tensor(o, a, b, op), reads=self._k(in0, in1), writes=self._k(out))

    def ts(self, eng, out, in0, s1, s2, op0, op1=None):
        o, a, x1, x2 = out[0], in0[0], self._a(s1), self._a(s2)
        if op1 is None:
            return self.op(eng, lambda e: e.tensor_scalar(o, a, x1, None, op0), reads=self._k(in0, s1), writes=self._k(out))
        return self.op(eng, lambda e: e.tensor_scalar(o, a, x1, x2, op0, op1), reads=self._k(in0, s1, s2), writes=self._k(out))

    def stt(self, eng, out, in0, scalar, in1, op0, op1):
        o, a, sc, b = out[0], in0[0], self._a(scalar), in1[0]
        return self.op(eng, lambda e: e.scalar_tensor_tensor(o, a, sc, b, op0, op1),
                       reads=self._k(in0, scalar, in1), writes=self._k(out))

    def memset(self, eng, out, val):
        o = out[0]
        return self.op(eng, lambda e: e.memset(o, val), writes=self._k(out))

    def barrier(self):
        toks = [(self.esem[e], self.ecount[e]) for e in self.COMPUTE if self.ecount[e] > 0]
        for q in self.dsem:
            for sem, val in self.dsem[q]:
                if val > 0:
                    toks.append((sem, val))
        for e in self.ALL:
            if e == "pool":
                continue
            waits = self._waits(e, toks, skip_sem=self.esem.get(e))
            if waits:
                self.streams[e].append((waits, None, None))
        self.res = {k: v for k, v in self.res.items()
                    if k in ("wo", "wab") or (isinstance(k, tuple) and len(k) == 2 and k[0] == "ring")}

    def new_epoch(self):
        self.epoch = getattr(self, "epoch", 0) + 1
        for e in self.COMPUTE:
            self.esem[e] = self.stack.enter_context(self.nc.semaphore("prog_%s_%d" % (e, self.epoch)))
            self.ecount[e] = 0

    def emit(self):
        nc = self.nc
        fin = self._waits("sp", self.final_tokens)
        streams = self.streams

        def run(ename, e):
            for (waits, fn, inc) in streams[ename]:
                for (sem, val) in waits:
                    e.wait_ge(sem, val)
                if fn is None:
                    continue
                ins = fn(e)
                if inc is not None:
                    ins.then_inc(inc[0], inc[1])
            if ename == "sp":
                for (sem, val) in fin:
                    e.wait_ge(sem, val)

        with nc.Block() as block:
            @block.sync
            def _(e):
                run("sp", e)

            @block.tensor
            def _(e):
                run("pe", e)

            @block.scalar
            def _(e):
                run("act", e)

            @block.vector
            def _(e):
                run("dve", e)

            @block.gpsimd
            def _(e):
                run("pool", e)


D = 1024
KC = 8
T = 2048
NSQ = 16
SQ = 4
NS = NSQ * SQ
NTOK = T + NS
DEPTH = 4
INW = 4108
NCORE = 8
GROUPS = [(0, 512), (512, 512), (1024, 512), (1536, 512), (2048, 64)]
WSLOT = 384
EPS = 1e-6

V_BADA = 0
V_NORMW = 96
V_FNORM = 128
V_BADAF = 136
V_CAW = 152
V_CBW = 176
V_GNW = 320
V_ALOG = 324
V_DTB = 328
NV = 332

C_IDENT = 0
C_SHIFT = 128
C_ONES = 192
C_MPREV = 320
C_MCUR = 448
C_BLK = 576
C_EH = 704
C_OFFD = 1088
C_MSEQ = 1152
C_SEQM = 1216
C_SCAN64 = 1232
C_SCAN4 = 1360
C_MDIAG = 1424
C_MS128 = 1488
NCST = 1492
NEGM = -240000.0


def R(ap, *keys):
    return (ap, keys)


class Ctx:
    pass


def build_program(phases=("A",), depth=DEPTH):
    nc = bass.Bass("TRN2", target_bir_lowering=False)
    st = contextlib.ExitStack()
    with st:
        P = Prog(nc, st)
        c = Ctx()
        c.nc, c.P, c.st = nc, P, st

        def din(name, shape):
            return nc.dram_tensor(name, list(shape), F32, kind="ExternalInput").ap()

        def dout(name, shape):
            return nc.dram_tensor(name, list(shape), F32, kind="ExternalOutput").ap()

        c.xp = din("xp", (T, D)); c.xs = din("xs", (NS, D))
        c.cT = din("cT", (128, KC, 17))
        c.sca = din("sca", (DEPTH, NSQ * 2, 256)); c.scb = din("scb", (DEPTH, NSQ * 3, 1152))
        c.sgd = din("sgd", (DEPTH, NSQ, 6, 64, 64))
        c.ck128 = din("ck128", (DEPTH, NSQ, 128, 256)); c.ck512 = din("ck512", (DEPTH, NSQ, 512, 256))
        c.ck2048 = din("ck2048", (DEPTH, NSQ, 2048, 256))
        c.w_in = din("w_in", (DEPTH, D, INW)); c.w_out = din("w_out", (DEPTH, D, D))
        c.w_ada = din("w_ada", (DEPTH, D, 3 * D)); c.w_adaf = din("w_adaf", (D, 2 * D))
        c.vecT = din("vecT", (128, NV)); c.cst = din("cst", (128, NCST))
        c.rope = din("rope", (128, 17, 16))
        c.yp = dout("yp", (T, D)); c.ys = dout("ys", (NS, D))
        c.ca_p = dout("ca_p", (DEPTH, 2, 256)); c.ca_s = dout("ca_s", (DEPTH, NSQ * 2, 256))
        c.cb_p = dout("cb_p", (DEPTH, 3, 1152)); c.cb_s = dout("cb_s", (DEPTH, NSQ * 3, 1152))
        c.gd_p = dout("gd_p", (DEPTH, 6, 64, 64)); c.gd_s = dout("gd_s", (DEPTH, NSQ, 6, 64, 64))
        c.kv_p = [dout("kv128_p", (DEPTH, 128, 256)), dout("kv512_p", (DEPTH, 512, 256)), dout("kv2048_p", (DEPTH, 2048, 256))]
        c.kv_s = [dout("kv128_s", (DEPTH, NS, 256)), dout("kv512_s", (DEPTH, NS, 256)), dout("kv2048_s", (DEPTH, NS, 256))]

        def sb(name, shape, dt):
            return st.enter_context(nc.sbuf_tensor(name, list(shape), dt))

        c.xT = sb("xT", (128, KC, NTOK), F32)
        c.hnT = sb("hnT", (128, KC, NTOK), BF16)
        c.ring = [sb("ring%d" % i, (128, KC, WSLOT), BF16) for i in range(4)]
        c.wab = sb("wab", (128, KC, 12), BF16)
        c.wo = sb("wo", (128, 6, D), BF16)
        c.vec = sb("vec", (128, NV), F32)
        c.cstf = sb("cstf", (128, NCST), F32)
        c.cstb = sb("cstb", (128, 836), BF16)
        c.ropet = sb("ropet", (128, 17, 16), F32)
        c.cTf = sb("cTf", (128, KC, 17), F32)
        c.cTb = sb("cTb", (128, KC, 17), BF16)
        c.ada = sb("ada", (128, 24, 17), F32)
        c.m1 = sb("m1", (128, KC, 17), F32)
        c.g1 = sb("g1", (128, KC, 17), F32)
        ARENA_F32 = 14592 - 2112
        c.arena = sb("arena", (128, ARENA_F32), F32)
        c.nmr = sb("nmr", (128, 2112), F32R)
        c.ps = [st.enter_context(nc.psum_tensor("ps%d" % i, [128, 512], F32)) for i in range(8)]
        c.ps_i = 0
        c.ring_i = 0
        c.phases = phases

        c.ps_n = 8

        def next_ps():
            i = c.ps_i % c.ps_n
            c.ps_i = (i + 1) % c.ps_n
            return i
        c.next_ps = next_ps

        class Arena:
            def __init__(self):
                self.off = 0

            def f32(self, n):
                o = self.off
                self.off += n
                c.arena_max = max(getattr(c, "arena_max", 0), self.off)
                assert self.off <= ARENA_F32, ("arena overflow", self.off)
                return c.arena[:, o:o + n]

            def bf16(self, n):
                assert n % 2 == 0
                return self.f32(n // 2).bitcast(BF16)

            def f32r(self, n):
                return self.f32(n).bitcast(F32R)
        c.Arena = Arena

        def wload(src, ncols):
            i = c.ring_i
            c.ring_i = (i + 1) % 4
            key = ("ring", i)
            P.dma("pool", c.ring[i][:, :, 0:ncols], src.rearrange("(k p) c -> p k c", p=128), writes=[key])
            return c.ring[i], key
        c.wload = wload

        P.dma("sp", c.vec[:], c.vecT, writes=["vec"])
        P.dma("sp", c.cstf[:], c.cst, writes=["cstf"])
        P.dma("sp", c.ropet[:], c.rope, writes=["rope"])
        P.dma("sp", c.cTf[:], c.cT, writes=["cTf"])
        P.copy("dve", R(c.cstb[:, 0:704], "cstb"), R(c.cstf[:, 0:704], "cstf"))
        P.copy("dve", R(c.cstb[:, 704:768], "cstb"), R(c.cstf[:, C_MSEQ:C_MSEQ + 64], "cstf"))
        P.copy("dve", R(c.cstb[:, 768:836], "cstb"), R(c.cstf[:, C_MDIAG:C_MDIAG + 68], "cstf"))
        P.copy("dve", R(c.cTb[:], "cTb"), R(c.cTf[:], "cTf"))
        for i in range(8):
            P.memset("dve", R(c.ps[i][:], ("ps", i)), 0.0)
        c.zero = sb("zero", (128, 384), F32)
        P.memset("dve", R(c.zero[:], "zero"), 0.0)
        c.ident = R(c.cstf[:, C_IDENT:C_IDENT + 128], "cstf")
        c.identb = R(c.cstb[:, C_IDENT:C_IDENT + 128], "cstb")
        c.onesb = R(c.cstb[:, C_ONES:C_ONES + 128], "cstb")

        load_x(c)
        for l in range(depth):
            layer(c, l)
        final(c)
        P.emit()
    return nc


def xkey(g):
    return ("x", g)


def hkey(g):
    return ("hn", g)


def load_x(c):
    P = c.P
    A = c.Arena()
    stg = [A.f32(D) for _ in range(2)]
    for tt in range(17):
        rows = 128 if tt < 16 else NS
        g = min(tt // 4, 4)
        s = stg[tt % 2]
        skey = ("xstg", tt % 2)
        src = c.xp[tt * 128:(tt + 1) * 128, :] if tt < 16 else c.xs
        P.dma("sp", s[0:rows, :], src, writes=[skey])
        for half in range(2):
            pi = c.next_ps()
            for kk in range(4):
                k = half * 4 + kk
                P.transpose(R(c.ps[pi][:, kk * 128:kk * 128 + rows], ("ps", pi)),
                            R(s[0:rows, k * 128:(k + 1) * 128], skey), R(c.cstf[0:rows, C_IDENT:C_IDENT + rows], "cstf"))
            col0 = tt * 128
            dst = c.xT[:, half * 4:half * 4 + 4, col0:col0 + rows]
            srcp = c.ps[pi][:].rearrange("p (k t) -> p k t", k=4)[:, :, 0:rows]
            P.copy("act" if half == 0 else "dve", R(dst, xkey(g)), R(srcp, ("ps", pi)))
    P.barrier()


def ada_vectors(c, l):
    P = c.P
    final_ = (l == DEPTH)
    ncol = 2 * D if final_ else 3 * D
    nj = ncol // 128
    pi = c.next_ps()
    pst = c.ps[pi][:, 0:24 * 17].rearrange("p (j s) -> p j s", s=17)
    for t0 in range(0, ncol, WSLOT):
        nc_ = min(WSLOT, ncol - t0)
        src = (c.w_adaf if final_ else c.w_ada[l])[:, t0:t0 + nc_]
        wt, wk = c.wload(src, nc_)
        for jj in range(nc_ // 128):
            j = t0 // 128 + jj
            for k in range(KC):
                P.mm(R(pst[:, j, :], ("ps", pi)), R(wt[:, k, jj * 128:(jj + 1) * 128], wk), R(c.cTb[:, k, :], "cTb"),
                     start=(k == 0), stop=(k == KC - 1))
    vb = V_BADAF if final_ else V_BADA + l * 24
    bias = c.vec[:, vb:vb + nj].unsqueeze(2).to_broadcast([128, nj, 17])
    P.tt("dve", R(c.ada[:, 0:nj, :], "ada"), R(pst[:, 0:nj, :], ("ps", pi)), R(bias, "vec"), ALU.add)
    nw0 = V_FNORM if final_ else V_NORMW + l * 8
    nw = c.vec[:, nw0:nw0 + KC].unsqueeze(2).to_broadcast([128, KC, 17])
    P.stt("dve", R(c.m1[:], "m1"), R(c.ada[:, 8:16, :], "ada"), 1.0, R(nw, "vec"), ALU.add, ALU.mult)
    if not final_:
        P.ts("dve", R(c.g1[:], "g1"), R(c.ada[:, 16:24, :], "ada"), 1.0, None, ALU.add)


def rsqrt(c, out, in_, scale, bias):
    P = c.P
    P.act(out, in_, AF.Ln, bias=bias, scale=scale)
    P.act(out, out, AF.Exp, scale=-0.5)


def sigmoid_(c, out, in_):
    P = c.P
    P.act(out, in_, AF.Exp, scale=-1.0)
    P.act(out, out, AF.Ln, bias=1.0)
    P.act(out, out, AF.Exp, scale=-1.0)


def silu_(c, out, in_):
    sigmoid_(c, out, in_)
    c.P.tt("dve", out, in_, out, ALU.mult)


def rms_stats(c, A, g, tag):
    P = c.P
    col0, ncol = GROUPS[g]
    sq = A["sq"]
    P.act(R(sq[:, :, 0:ncol], "sq"), R(c.xT[:, :, col0:col0 + ncol], xkey(g)), AF.Square)
    pi = c.next_ps()
    for k in range(KC):
        P.mm(R(c.ps[pi][:, 0:ncol], ("ps", pi)), c.onesb, R(sq[:, k, 0:ncol], "sq"), start=(k == 0), stop=(k == KC - 1))
    rstd = A["rstd"]
    rsqrt(c, R(rstd[:, 0:ncol], "rstd"), R(c.ps[pi][:, 0:ncol], ("ps", pi)), 1.0 / D, EPS)
    return rstd


def norm_phase(c, l, out_fn):
    P = c.P
    A_ = c.Arena()
    A = {"sq": A_.bf16(KC * 512).rearrange("p (k t) -> p k t", k=KC), "rstd": A_.f32(512),
         "tmp": [A_.f32(512) for _ in range(2)], "tmps": A_.f32(KC * NS).rearrange("p (k t) -> p k t", k=KC)}
    for g in range(5):
        col0, ncol = GROUPS[g]
        rstd = rms_stats(c, A, g, "n")
        if g < 4:
            for k in range(KC):
                tmp = A["tmp"][k % 2]
                tk = ("ntmp", k % 2)
                P.tt("dve", R(tmp[:, 0:ncol], tk), R(c.xT[:, k, col0:col0 + ncol], xkey(g)), R(rstd[:, 0:ncol], "rstd"), ALU.mult)
                P.act(out_fn(g, k, ncol), R(tmp[:, 0:ncol], tk), AF.Identity,
                      bias=R(c.ada[:, k, 0:1], "ada"), scale=R(c.m1[:, k, 0:1], "m1"))
        else:
            ts_ = A["tmps"]
            P.tt("dve", R(ts_[:], "ntmps"), R(c.xT[:, :, col0:col0 + ncol], xkey(g)),
                 R(rstd[:, 0:ncol].unsqueeze(1).to_broadcast([128, KC, NS]), "rstd"), ALU.mult)
            v4 = ts_[:].rearrange("p k (s i) -> p k s i", i=SQ)
            m1b = c.m1[:, :, 1:17].unsqueeze(3).to_broadcast([128, KC, NSQ, SQ])
            shb = c.ada[:, 0:8, 1:17].unsqueeze(3).to_broadcast([128, KC, NSQ, SQ])
            P.tt("dve", R(v4, "ntmps"), R(v4, "ntmps"), R(m1b, "m1"), ALU.mult)
            for k in range(KC):
                o = out_fn(g, k, ncol)
                P.tt("dve", (o[0].rearrange("p (s i) -> p s i", i=SQ), o[1]), R(v4[:, k], "ntmps"), R(shb[:, k], "ada"), ALU.add)
    P.barrier()


def proj(c, pi, wt, wk, wc0, m, g, ncol_override=None, cols=None):
    P = c.P
    col0, ncol = GROUPS[g] if cols is None else cols
    for k in range(KC):
        P.mm(R(c.ps[pi][0:m, 0:ncol], ("ps", pi)), R(wt[:, k, wc0:wc0 + m], wk), R(c.hnT[:, k, col0:col0 + ncol], hkey(g)),
             start=(k == 0), stop=(k == KC - 1))


def outproj(c, l, g, mix_fn, nchunk, kpart, cols=None):
    P = c.P
    col0, ncol = GROUPS[g] if cols is None else cols
    for dc in range(KC):
        pi = c.next_ps()
        for mc in range(nchunk):
            P.mm(R(c.ps[pi][:, 0:ncol], ("ps", pi)), R(c.wo[0:kpart, mc, dc * 128:(dc + 1) * 128], "wo"), mix_fn(mc),
                 start=(mc == 0), stop=(mc == nchunk - 1))
        xs = c.xT[:, dc, col0:col0 + ncol]
        if g < 4:
            P.stt("dve", R(xs, xkey(g)), R(c.ps[pi][:, 0:ncol], ("ps", pi)), R(c.g1[:, dc, 0:1], "g1"), R(xs, xkey(g)), ALU.mult, ALU.add)
        else:
            x3 = xs.rearrange("p (s i) -> p s i", i=SQ)
            p3 = c.ps[pi][:, 0:ncol].rearrange("p (s i) -> p s i", i=SQ)
            g1b = c.g1[:, dc, 1:17].unsqueeze(2).to_broadcast([128, NSQ, SQ])
            tmp = c.optmp
            P.tt("dve", R(tmp, "optmp"), R(p3, ("ps", pi)), R(g1b, "g1"), ALU.mult)
            P.tt("dve", R(x3, xkey(g)), R(x3, xkey(g)), R(tmp, "optmp"), ALU.add)


def load_wo(c, l, r0, nchunk, kpart):
    P = c.P
    src = c.w_out[l][r0:r0 + nchunk * kpart, :].rearrange("(j p) d -> p j d", p=kpart)
    P.dma("pool", c.wo[0:kpart, 0:nchunk, :], src, writes=["wo"])


def layer(c, l):
    P = c.P
    ada_vectors(c, l)
    norm_phase(c, l, lambda g, k, ncol: R(c.hnT[:, k, GROUPS[g][0]:GROUPS[g][0] + ncol], hkey(g)))
    if "A" in c.phases:
        branch_a(c, l)
    if "B" in c.phases:
        branch_b(c, l)
    if "C" in c.phases:
        branch_c(c, l)


def branch_a(c, l):
    P = c.P
    A_ = c.Arena()
    CI = A_.f32(2 * (2 + T)).rearrange("p (j t) -> p j t", j=2)
    CIs = A_.f32(2 * NSQ * 6).rearrange("p (j s t) -> p j s t", j=2, s=NSQ)
    tmpx = A_.f32(512); sz = A_.f32(512); acc = A_.f32(512); tz = A_.f32(512)
    mixA = A_.bf16(2 * 512).rearrange("p (j t) -> p j t", j=2)
    c.optmp = A_.f32(NS).rearrange("p (s i) -> p s i", i=SQ)
    sin_ = A_.f32(256)
    gat = A_.f32(2 * 34).rearrange("p (j t) -> p j t", j=2)
    outa = A_.f32(256)
    load_wo(c, l, 0, 2, 128)
    tiles = []
    for t0 in (0, 384, 768):
        ncols = min(384, 1024 - t0)
        tiles.append(c.wload(c.w_in[l][:, t0:t0 + ncols], ncols))

    def wsel(col):
        ti = col // 384
        return tiles[ti][0], tiles[ti][1], col - ti * 384

    P.memset("dve", R(CI[:, :, 0:2], "CIh"), 0.0)
    P.dma("sp", sin_[0:NSQ * 2, :], c.sca[l], writes=["sin"])
    for j in range(2):
        pi = c.next_ps()
        P.transpose(R(c.ps[pi][:, 0:32], ("ps", pi)), R(sin_[0:32, j * 128:(j + 1) * 128], "sin"), R(c.cstf[0:32, 0:32], "cstf"))
        P.copy("act", R(CIs[:, j, :, 0:2], ("CIs", j)), R(c.ps[pi][:, 0:32].rearrange("p (s r) -> p s r", r=2), ("ps", pi)))
    for g in range(5):
        col0, ncol = GROUPS[g]
        for j in range(2):
            p0 = c.next_ps(); wt, wk, wc = wsel(j * 128); proj(c, p0, wt, wk, wc, 128, g)
            p1 = c.next_ps(); wt, wk, wc = wsel(256 + j * 128); proj(c, p1, wt, wk, wc, 128, g)
            P.copy("act", R(tmpx[:, 0:ncol], "tmpx"), R(c.ps[p0][:, 0:ncol], ("ps", p0)))
            vb = V_CAW + (l * 3) * 2 + j
            w0 = R(c.vec[:, vb:vb + 1], "vec"); w1 = R(c.vec[:, vb + 2:vb + 3], "vec"); w2 = R(c.vec[:, vb + 4:vb + 5], "vec")
            if g < 4:
                ck = ("CI", j, g)
                P.tt("dve", R(CI[:, j, 2 + col0:2 + col0 + ncol], ck), R(tmpx[:, 0:ncol], "tmpx"), R(c.ps[p1][:, 0:ncol], ("ps", p1)), ALU.mult)
                rd = [ck, ("CI", j, g - 1), "CIh"]
                P.ts("dve", R(acc[:, 0:ncol], "acc"), (CI[:, j, col0 + 2:col0 + 2 + ncol], rd), w2, None, ALU.mult)
                P.stt("dve", R(acc[:, 0:ncol], "acc"), (CI[:, j, col0 + 1:col0 + 1 + ncol], rd), w1, R(acc[:, 0:ncol], "acc"), ALU.mult, ALU.add)
                P.stt("dve", R(acc[:, 0:ncol], "acc"), (CI[:, j, col0:col0 + ncol], rd), w0, R(acc[:, 0:ncol], "acc"), ALU.mult, ALU.add)
                accv = acc[:, 0:ncol]
            else:
                ck = ("CIs", j)
                P.tt("dve", R(CIs[:, j, :, 2:6], ck), R(tmpx[:, 0:ncol].rearrange("p (s i) -> p s i", i=SQ), "tmpx"),
                     R(c.ps[p1][:, 0:ncol].rearrange("p (s i) -> p s i", i=SQ), ("ps", p1)), ALU.mult)
                a3 = acc[:, 0:ncol].rearrange("p (s i) -> p s i", i=SQ)
                P.ts("dve", R(a3, "acc"), R(CIs[:, j, :, 2:6], ck), w2, None, ALU.mult)
                P.stt("dve", R(a3, "acc"), R(CIs[:, j, :, 1:5], ck), w1, R(a3, "acc"), ALU.mult, ALU.add)
                P.stt("dve", R(a3, "acc"), R(CIs[:, j, :, 0:4], ck), w0, R(a3, "acc"), ALU.mult, ALU.add)
                accv = acc[:, 0:ncol]
            p2 = c.next_ps(); wt, wk, wc = wsel(512 + j * 128); proj(c, p2, wt, wk, wc, 128, g)
            p3 = c.next_ps(); wt, wk, wc = wsel(768 + j * 128); proj(c, p3, wt, wk, wc, 128, g)
            silu_(c, R(sz[:, 0:ncol], "sz"), R(c.ps[p3][:, 0:ncol], ("ps", p3)))
            P.tt("dve", R(tz[:, 0:ncol], "tz"), R(accv, "acc"), R(sz[:, 0:ncol], "sz"), ALU.mult)
            P.tt("dve", R(mixA[:, j, 0:ncol], ("mixA", j)), R(tz[:, 0:ncol], "tz"), R(c.ps[p2][:, 0:ncol], ("ps", p2)), ALU.mult)
        outproj(c, l, g, lambda mc: R(mixA[:, mc, 0:GROUPS[g][1]], ("mixA", mc)), 2, 128)
    for j in range(2):
        P.copy("act", R(gat[:, j, 0:2], ("gat", j)), R(CI[:, j, T:T + 2], ("CI", j, 3)))
        P.copy("act", R(gat[:, j, 2:34].rearrange("p (s r) -> p s r", r=2), ("gat", j)), R(CIs[:, j, :, 4:6], ("CIs", j)))
        pi = c.next_ps()
        P.transpose(R(c.ps[pi][0:34, 0:128], ("ps", pi)), R(gat[:, j, :], ("gat", j)), c.ident)
        P.copy("dve", R(outa[0:34, j * 128:(j + 1) * 128], "outa"), R(c.ps[pi][0:34, 0:128], ("ps", pi)))
    P.dma("sp", c.ca_p[l], outa[0:2, :], reads=["outa"], final=True)
    P.dma("sp", c.ca_s[l], outa[2:34, :], reads=["outa"], final=True)
    P.barrier()


def final(c):
    P = c.P
    ada_vectors(c, DEPTH)
    A_ = c.Arena()
    yT = A_.f32(KC * 512).rearrange("p (k t) -> p k t", k=KC)
    A = {"sq": A_.bf16(KC * 512).rearrange("p (k t) -> p k t", k=KC), "rstd": A_.f32(512),
         "tmp": [A_.f32(512) for _ in range(2)], "tmps": A_.f32(KC * NS).rearrange("p (k t) -> p k t", k=KC)}
    ystg = [A_.f32(D) for _ in range(2)]
    si = 0
    for g in range(5):
        col0, ncol = GROUPS[g]
        rstd = rms_stats(c, A, g, "f")
        yk = ("yT",)
        if g < 4:
            for k in range(KC):
                tmp = A["tmp"][k % 2]; tk = ("ntmp", k % 2)
                P.tt("dve", R(tmp[:, 0:ncol], tk), R(c.xT[:, k, col0:col0 + ncol], xkey(g)), R(rstd[:, 0:ncol], "rstd"), ALU.mult)
                P.act(R(yT[:, k, 0:ncol], "yT"), R(tmp[:, 0:ncol], tk), AF.Identity,
                      bias=R(c.ada[:, k, 0:1], "ada"), scale=R(c.m1[:, k, 0:1], "m1"))
        else:
            ts_ = A["tmps"]
            P.tt("dve", R(ts_[:], "ntmps"), R(c.xT[:, :, col0:col0 + ncol], xkey(g)),
                 R(rstd[:, 0:ncol].unsqueeze(1).to_broadcast([128, KC, NS]), "rstd"), ALU.mult)
            v4 = ts_[:].rearrange("p k (s i) -> p k s i", i=SQ)
            m1b = c.m1[:, :, 1:17].unsqueeze(3).to_broadcast([128, KC, NSQ, SQ])
            shb = c.ada[:, 0:8, 1:17].unsqueeze(3).to_broadcast([128, KC, NSQ, SQ])
            P.tt("dve", R(v4, "ntmps"), R(v4, "ntmps"), R(m1b, "m1"), ALU.mult)
            P.tt("dve", R(yT[:, :, 0:NS].rearrange("p k (s i) -> p k s i", i=SQ), "yT"), R(v4, "ntmps"), R(shb, "ada"), ALU.add)
        for tt in range((ncol + 127) // 128):
            rows = min(128, ncol - tt * 128)
            stg = ystg[si % 2]; sk = ("ystg", si % 2); si += 1
            for half in range(2):
                pi = c.next_ps()
                for kk in range(4):
                    k = half * 4 + kk
                    P.transpose(R(c.ps[pi][0:rows, kk * 128:(kk + 1) * 128], ("ps", pi)),
                                R(yT[:, k, tt * 128:tt * 128 + rows], "yT"), c.ident)
                P.copy("act" if half == 0 else "dve", R(stg[0:rows, half * 512:(half + 1) * 512], sk), R(c.ps[pi][0:rows, :], ("ps", pi)))
            if g < 4:
                dst = c.yp[col0 + tt * 128:col0 + tt * 128 + rows, :]
            else:
                dst = c.ys
            P.dma("sp", dst, stg[0:rows, :], reads=[sk], final=True)


_PHASES = ("A", "B", "C")
_NC_CACHE = {}


def _host_consts():
    cst = np.zeros((128, NCST), np.float32)
    cst[:, C_IDENT:C_IDENT + 128] = np.eye(128, dtype=np.float32)
    for m in range(64):
        cst[64 + m, C_SHIFT + m] = 1.0
    cst[:, C_ONES:C_ONES + 128] = 1.0
    k = np.arange(128)[:, None]
    q = np.arange(128)[None, :]
    cst[:, C_MPREV:C_MPREV + 128] = np.where(k >= q, 0.0, NEGM)
    cst[:, C_MCUR:C_MCUR + 128] = np.where(k <= q, 0.0, NEGM)
    cst[0:64, C_BLK:C_BLK + 64] = 1.0
    cst[64:128, C_BLK + 64:C_BLK + 128] = 1.0
    for h in range(6):
        cst[h, C_EH + h * 64:C_EH + (h + 1) * 64] = 1.0
    cst[0:64, C_OFFD:C_OFFD + 64] = 1.0 - np.eye(64, dtype=np.float32)
    j64 = np.arange(64)[:, None]
    i64 = np.arange(64)[None, :]
    cst[0:64, C_MSEQ:C_MSEQ + 64] = np.where((j64 // 4 == i64 // 4) & (j64 <= i64), 0.0, NEGM)
    cst[0:64, C_SEQM:C_SEQM + 16] = (j64 // 4 == np.arange(16)[None, :]).astype(np.float32)
    cst[:, C_SCAN64:C_SCAN64 + 128] = (np.arange(128) % 64 != 0).astype(np.float32)[None, :]
    cst[:, C_SCAN4:C_SCAN4 + 64] = (np.arange(64) % 4 != 0).astype(np.float32)[None, :]
    cst[0:64, C_MDIAG:C_MDIAG + 64] = np.where(j64 == i64, 0.0, NEGM)
    cst[:, C_MS128:C_MS128 + 4] = np.where(np.arange(128)[:, None] >= np.arange(4)[None, :], 0.0, NEGM)
    half = 8
    inv_freq = (500000.0 ** (-np.arange(half, dtype=np.float32) * np.float32(2.0 / 16))).astype(np.float32)
    rope = np.zeros((128, 17, 16), np.float32)
    for tt in range(17):
        if tt < 16:
            pos = (tt * 128 + np.arange(128)).astype(np.float32)
        else:
            pos = (T + (np.arange(128) % SQ)).astype(np.float32)
        ang = pos[:, None] * inv_freq[None, :]
        rope[:, tt, 0:8] = np.cos(ang)
        rope[:, tt, 8:16] = np.sin(ang)
    return cst, rope


def _fm(v):
    v = np.asarray(v, np.float32)
    return np.ascontiguousarray(v.reshape(-1, 128).T)


def _host_vecT(b_ada, norm_w, final_norm_w, b_ada_final, conv_a_w, conv_b_w, gdn_norm_w, a_log, dt_bias):
    vt = np.zeros((128, NV), np.float32)
    for l in range(DEPTH):
        vt[:, V_BADA + l * 24:V_BADA + (l + 1) * 24] = _fm(b_ada[l])
        vt[:, V_NORMW + l * 8:V_NORMW + (l + 1) * 8] = _fm(norm_w[l])
        for tap in range(3):
            vt[:, V_CAW + (l * 3 + tap) * 2:V_CAW + (l * 3 + tap) * 2 + 2] = _fm(conv_a_w[l, tap])
        for tap in range(4):
            vt[:, V_CBW + (l * 4 + tap) * 9:V_CBW + (l * 4 + tap) * 9 + 9] = _fm(conv_b_w[l, tap])
        vt[:, V_GNW + l] = np.tile(np.asarray(gdn_norm_w[l], np.float32), 2)
        vt[0:6, V_ALOG + l] = a_log[l]
        vt[0:6, V_DTB + l] = dt_bias[l]
    vt[:, V_FNORM:V_FNORM + 8] = _fm(final_norm_w)
    vt[:, V_BADAF:V_BADAF + 16] = _fm(b_ada_final)
    return vt


def kernel(x_prompt, x_sample, state_conv_a, state_conv_b, state_gdn, cache_kv_w128, cache_kv_w512,
           cache_kv_w2048, c_prompt, c_sample, w_in, w_out, w_ada, b_ada, norm_w, conv_a_w, conv_b_w,
           a_log, dt_bias, gdn_norm_w, final_norm_w, w_ada_final, b_ada_final, _phases=None, _depth=DEPTH):
    phases = tuple(_phases) if _phases is not None else _PHASES
    f = lambda a: np.ascontiguousarray(np.asarray(a, dtype=np.float32))
    key = (phases, _depth)
    if key not in _NC_CACHE:
        _NC_CACHE[key] = build_program(phases, _depth)
    nc = _NC_CACHE[key]
    cst, rope = _host_consts()
    vt = _host_vecT(f(b_ada), f(norm_w), f(final_norm_w), f(b_ada_final), f(conv_a_w), f(conv_b_w), f(gdn_norm_w),
                    f(a_log), f(dt_bias))
    w_in, w_out, w_ada, w_adaf = f(w_in), f(w_out), f(w_ada), f(w_ada_final)
    x_prompt, x_sample = f(x_prompt), f(x_sample)
    c_prompt, c_sample = f(c_prompt), f(c_sample)
    sca, scb, sgd = f(state_conv_a), f(state_conv_b), f(state_gdn)
    k128, k512, k2048 = f(cache_kv_w128), f(cache_kv_w512), f(cache_kv_w2048)
    in_maps = []
    for i in range(NCORE):
        ss = slice(i * NSQ, (i + 1) * NSQ)
        call = np.concatenate([c_prompt[i:i + 1], c_sample[ss]], axis=0)
        cT = np.ascontiguousarray(call.reshape(17, KC, 128).transpose(2, 1, 0))
        in_maps.append({
            "xp": x_prompt[i], "xs": np.ascontiguousarray(x_sample[ss].reshape(NS, D)), "cT": cT,
            "sca": np.ascontiguousarray(sca[:, ss].reshape(DEPTH, NSQ * 2, 256)),
            "scb": np.ascontiguousarray(scb[:, ss].reshape(DEPTH, NSQ * 3, 1152)),
            "sgd": np.ascontiguousarray(sgd[:, ss]),
            "ck128": np.ascontiguousarray(k128[:, ss].reshape(DEPTH, NSQ, 128, 256)),
            "ck512": np.ascontiguousarray(k512[:, ss].reshape(DEPTH, NSQ, 512, 256)),
            "ck2048": np.ascontiguousarray(k2048[:, ss].reshape(DEPTH, NSQ, 2048, 256)),
            "w_in": w_in, "w_out": w_out, "w_ada": w_ada, "w_adaf": w_adaf,
            "vecT": vt, "cst": cst, "rope": rope,
        })
    res = run_bass_kernel_spmd(nc, in_maps, core_ids=list(range(NCORE)))
    rs = res.results
    cat = lambda name: np.stack([r[name] for r in rs], axis=0)
    y_p = cat("yp")
    y_s = cat("ys").reshape(NCORE * NSQ, SQ, D)
    ca_p = cat("ca_p").transpose(1, 0, 2, 3)
    ca_s = cat("ca_s").reshape(NCORE, DEPTH, NSQ, 2, 256).transpose(1, 0, 2, 3, 4).reshape(DEPTH, NCORE * NSQ, 2, 256)
    cb_p = cat("cb_p").transpose(1, 0, 2, 3)
    cb_s = cat("cb_s").reshape(NCORE, DEPTH, NSQ, 3, 1152).transpose(1, 0, 2, 3, 4).reshape(DEPTH, NCORE * NSQ, 3, 1152)
    gd_p = cat("gd_p").transpose(1, 0, 2, 3, 4)
    gd_s = cat("gd_s").transpose(1, 0, 2, 3, 4, 5).reshape(DEPTH, NCORE * NSQ, 6, 64, 64)
    outs = [y_p, y_s, ca_p, ca_s, cb_p, cb_s, gd_p, gd_s]
    for gi, win in enumerate((128, 512, 2048)):
        name = "kv%d" % win
        kp = cat(name + "_p").transpose(1, 0, 2, 3).reshape(DEPTH, NCORE, win, 2, 2, 64)
        ks = cat(name + "_s").reshape(NCORE, DEPTH, NSQ, SQ, 256).transpose(1, 0, 2, 3, 4).reshape(DEPTH, NCORE * NSQ, SQ, 2, 2, 64)
        outs += [kp, ks]
    return tuple(np.ascontiguousarray(o.astype(np.float32)) for o in outs)


def branch_b(c, l):
    P = c.P
    A_ = c.Arena()
    f32 = A_.f32

    def t3(n_mid, n_in, dt=F32):
        a = f32(n_mid * n_in)[0:64]
        return a.rearrange("p (a b) -> p a b", a=n_mid)

    _pre = f32(131)
    pre = [_pre, _pre]
    pres = f32(NSQ * 7).rearrange("p (s t) -> p s t", t=7)
    halo = f32(27).rearrange("p (b t) -> p b t", t=3)
    acc = f32(128); act_ = f32(128); rinv = f32(128)
    sqb = A_.bf16(128)
    nrm = act_
    hq_raw = f32(768)[0:64]; hk_raw = f32(768)[0:64]
    HQ = hq_raw.rearrange("p (a b) -> p a b", a=6)
    HK = hk_raw.rearrange("p (a b) -> p a b", a=6)
    HV = t3(6, 128, F32R)
    HZ = t3(6, 128)
    szt = f32(128)
    G = f32(128); BETA = f32(128); GC = f32(128); EG = f32(128); DL = f32(128); NGC = f32(128); tmpd = G
    EGL = f32(16); nA = f32(1)
    EGLB = t3(6, 16)
    OT = t3(6, 128)
    mixB = A_.bf16(6 * 128)[0:64].rearrange("p (h t) -> p h t", h=6)
    sqo = hk_raw[:, 0:384].bitcast(BF16).rearrange("p (h t) -> p h t", h=6)
    rso = hq_raw.rearrange("p (a b) -> p a b", a=6)
    S = t3(6, 64, F32R)
    SS = t3(NSQ, 64)
    KDblk = t3(NSQ, 64, F32R)
    U = {}
    for nm in ("decT", "LT0", "kbT", "kdT", "vbT", "VN"):
        U[nm] = t3(3, 64)
    U["RT"] = U["LT0"]
    _nm_i = [0]

    def rt3():
        o = _nm_i[0]; _nm_i[0] += 192
        return c.nmr[0:64, o:o + 192].rearrange("p (a b) -> p a b", a=3)
    for nm in ("LT", "L", "P0", "P1", "PT0", "PT1", "X0", "X1", "Rr"):
        U[nm] = rt3()
    SC = [{nm: t3(3, 64) for nm in ("kbgT", "qgT", "aT", "VB", "KD")} for _ in range(2)]
    for sc_ in SC:
        sc_["TinvT"] = rt3()

    def Fv(ap):
        return ap.bitcast(F32)
    stg = f32(384)
    gatB = f32(9 * 51).rearrange("p (b t) -> p b t", b=9)
    c.optmp = f32(NS).rearrange("p (s i) -> p s i", i=SQ)

    def F(ap):
        return ap

    identr = R(c.cstf[0:64, 0:64], "cstf")
    shiftr = R(c.cstf[:, C_SHIFT:C_SHIFT + 64], "cstf")
    ident64 = R(c.cstf[0:64, 0:64], "cstf")
    identb64 = R(c.cstb[0:64, 0:64], "cstb")

    def EH(h):
        return R(c.cstf[0:6, C_EH + h * 64:C_EH + (h + 1) * 64], "cstf")

    load_wo(c, l, 256, 6, 64)
    wt = [c.wload(c.w_in[l][:, 1024 + i * 384:1024 + (i + 1) * 384], 384) for i in range(4)]
    P.dma("pool", c.wab[:], c.w_in[l][:, 2560:2572].rearrange("(k p) c -> p k c", p=128), writes=["wab"])

    P.copy("dve", R(S[:], "S"), R(c.zero[0:64, 0:384].rearrange("p (h t) -> p h t", h=6), "zero"))
    P.memset("dve", R(halo[:], "halo"), 0.0)
    P.act(R(nA[0:6, :], "nA"), R(c.vec[0:6, V_ALOG + l:V_ALOG + l + 1], "vec"), AF.Exp)
    P.ts("dve", R(nA[0:6, :], "nA"), R(nA[0:6, :], "nA"), -1.0, None, ALU.mult)
    for b3 in range(3):
        P.dma("sp", stg[0:NSQ * 3, :], c.scb[l][:, b3 * 384:(b3 + 1) * 384], writes=["stgB"])
        for bb in range(3):
            blk = b3 * 3 + bb
            pi = c.next_ps()
            P.transpose(R(c.ps[pi][:, 0:48], ("ps", pi)), R(stg[0:48, bb * 128:(bb + 1) * 128], "stgB"), R(c.cstf[0:48, 0:48], "cstf"))
            P.copy("act", R(gatB[:, blk, 0:48], ("gatB", blk)), R(c.ps[pi][:, 0:48], ("ps", pi)))

    groups = [(gb * 128, 128, gb // 4) for gb in range(16)] + [(T, NS, 4)]
    for gi, (col0, ncol, g5) in enumerate(groups):
        smp = (g5 == 4)
        nch = 1 if smp else 2
        cols = (col0, ncol)
        for blk in range(9):
            typ, sub = blk // 3, blk % 3
            wtile, wkey = wt[typ]
            pi = c.next_ps()
            proj(c, pi, wtile, wkey, sub * 128, 128, g5, cols=cols)
            vb = V_CBW + (l * 4) * 9 + blk
            wtap = [R(c.vec[:, vb + 9 * tap:vb + 9 * tap + 1], "vec") for tap in range(4)]
            if not smp:
                pr = pre[0]; pk = ("pre", 0)
                P.copy("dve", R(pr[:, 0:3], pk), R(halo[:, blk, :], ("halo", blk)))
                P.copy("act", R(pr[:, 3:3 + ncol], pk), R(c.ps[pi][:, 0:ncol], ("ps", pi)))
                P.copy("dve", R(halo[:, blk, :], ("halo", blk)), R(pr[:, ncol:ncol + 3], pk))
                P.ts("dve", R(acc[:, 0:ncol], "accB"), R(pr[:, 3:3 + ncol], pk), wtap[3], None, ALU.mult)
                for tap in range(3):
                    P.stt("dve", R(acc[:, 0:ncol], "accB"), R(pr[:, tap:tap + ncol], pk), wtap[tap], R(acc[:, 0:ncol], "accB"), ALU.mult, ALU.add)
            else:
                pk = ("pres",)
                P.copy("dve", R(pres[:, :, 0:3], pk), R(gatB[:, blk, 0:48].rearrange("p (s r) -> p s r", r=3), ("gatB", blk)))
                P.copy("act", R(pres[:, :, 3:7], pk), R(c.ps[pi][:, 0:ncol].rearrange("p (s i) -> p s i", i=SQ), ("ps", pi)))
                a3 = acc[:, 0:ncol].rearrange("p (s i) -> p s i", i=SQ)
                P.ts("dve", R(a3, "accB"), R(pres[:, :, 3:7], pk), wtap[3], None, ALU.mult)
                for tap in range(3):
                    P.stt("dve", R(a3, "accB"), R(pres[:, :, tap:tap + 4], pk), wtap[tap], R(a3, "accB"), ALU.mult, ALU.add)
                P.copy("act", R(gatB[:, blk, 3:51].rearrange("p (s r) -> p s r", r=3), ("gatB", blk)), R(pres[:, :, 4:7], pk))
                P.copy("act", R(gatB[:, blk, 0:3], ("gatB", blk)), R(halo[:, blk, :], ("halo", blk)))
            silu_(c, R(act_[:, 0:ncol], "actB"), R(acc[:, 0:ncol], "accB"))
            if typ < 2:
                P.tt("dve", R(sqb[:, 0:ncol], "sqb"), R(act_[:, 0:ncol], "actB"), R(act_[:, 0:ncol], "actB"), ALU.mult)
                p2 = c.next_ps()
                P.mm(R(c.ps[p2][:, 0:ncol], ("ps", p2)), R(c.cstb[:, C_BLK:C_BLK + 128], "cstb"), R(sqb[:, 0:ncol], "sqb"))
                rsqrt(c, R(rinv[:, 0:ncol], "rinvB"), R(c.ps[p2][:, 0:ncol], ("ps", p2)), 1.0, 1e-6)
                P.stt("dve", R(nrm[:, 0:ncol], "actB"), R(act_[:, 0:ncol], "actB"), 0.125 if typ == 0 else 1.0,
                      R(rinv[:, 0:ncol], "rinvB"), ALU.mult, ALU.mult)
            H = (HQ, HK, HV)[typ]
            hk = ("H", typ)
            P.copy("act", R(H[:, 2 * sub, 0:ncol], hk), R(nrm[0:64, 0:ncol], "actB"))
            p3 = c.next_ps()
            P.mm(R(c.ps[p3][0:64, 0:ncol], ("ps", p3)), shiftr, R(nrm[:, 0:ncol], "actB"))
            P.copy("act", R(H[:, 2 * sub + 1, 0:ncol], hk), R(c.ps[p3][0:64, 0:ncol], ("ps", p3)))
        for sub in range(3):
            pi = c.next_ps()
            proj(c, pi, wt[3][0], wt[3][1], sub * 128, 128, g5, cols=cols)
            silu_(c, R(szt[:, 0:ncol], "szt"), R(c.ps[pi][:, 0:ncol], ("ps", pi)))
            P.copy("dve", R(HZ[:, 2 * sub, 0:ncol], "HZ"), R(F(szt[0:64, 0:ncol]), "szt"))
            p3 = c.next_ps()
            P.mm(R(c.ps[p3][0:64, 0:ncol], ("ps", p3)), shiftr, R(szt[:, 0:ncol], "szt"))
            P.copy("act", R(HZ[:, 2 * sub + 1, 0:ncol], "HZ"), R(c.ps[p3][0:64, 0:ncol], ("ps", p3)))
        pa = c.next_ps(); pb = c.next_ps()
        for k in range(KC):
            P.mm(R(c.ps[pa][0:6, 0:ncol], ("ps", pa)), R(c.wab[:, k, 0:6], "wab"), R(c.hnT[:, k, col0:col0 + ncol], hkey(g5)),
                 start=(k == 0), stop=(k == KC - 1))
        for k in range(KC):
            P.mm(R(c.ps[pb][0:6, 0:ncol], ("ps", pb)), R(c.wab[:, k, 6:12], "wab"), R(c.hnT[:, k, col0:col0 + ncol], hkey(g5)),
                 start=(k == 0), stop=(k == KC - 1))
        rk = "rowsB"
        P.act(R(G[0:6, 0:ncol], rk), R(c.ps[pa][0:6, 0:ncol], ("ps", pa)), AF.Exp, bias=R(c.vec[0:6, V_DTB + l:V_DTB + l + 1], "vec"))
        P.act(R(G[0:6, 0:ncol], rk), R(G[0:6, 0:ncol], rk), AF.Ln, bias=1.0)
        P.ts("dve", R(G[0:6, 0:ncol], rk), R(G[0:6, 0:ncol], rk), R(nA[0:6, 0:1], "nA"), None, ALU.mult)
        sigmoid_(c, R(BETA[0:6, 0:ncol], rk), R(c.ps[pb][0:6, 0:ncol], ("ps", pb)))
        scm = c.cstf[0:6, C_SCAN4:C_SCAN4 + 64] if smp else c.cstf[0:6, C_SCAN64:C_SCAN64 + 128]
        gco, go, sco = GC[0:6, 0:ncol], G[0:6, 0:ncol], scm
        P.op("dve", lambda e, gco=gco, go=go, sco=sco: e.tensor_tensor_scan(gco, sco, go, 0.0, ALU.mult, ALU.add), reads=[rk, "cstf"], writes=[rk])
        P.act(R(EG[0:6, 0:ncol], rk), R(GC[0:6, 0:ncol], rk), AF.Exp)
        P.ts("dve", R(NGC[0:6, 0:ncol], rk), R(GC[0:6, 0:ncol], rk), -1.0, None, ALU.mult)
        clen = SQ if smp else 64
        nseg = ncol // clen
        gc3 = GC[0:6, 0:ncol].rearrange("p (n t) -> p n t", t=clen)
        glb = gc3[:, :, clen - 1:clen].to_broadcast([6, nseg, clen])
        P.tt("dve", R(tmpd[0:6, 0:ncol].rearrange("p (n t) -> p n t", t=clen), rk), R(glb, rk), R(gc3, rk), ALU.subtract)
        P.act(R(DL[0:6, 0:ncol], rk), R(tmpd[0:6, 0:ncol], rk), AF.Exp)
        P.act(R(EGL[0:6, 0:nseg], rk), R(gc3[:, :, clen - 1], rk), AF.Exp)
        pe_ = c.next_ps()
        for h in range(6):
            P.mm(R(c.ps[pe_][0:64, h * 16:h * 16 + nseg], ("ps", pe_)), EH(h), R(EGL[0:6, 0:nseg], rk))
        P.copy("dve", R(EGLB[:, :, 0:nseg], "EGLB"), R(c.ps[pe_][0:64, 0:96].rearrange("p (h n) -> p h n", h=6)[:, :, 0:nseg], ("ps", pe_)))

        uk = lambda nm: ("U", nm)
        maskap = c.cstb[0:64, 704:768] if smp else c.cstb[0:64, C_MCUR:C_MCUR + 64]
        offd = c.cstf[0:64, C_OFFD:C_OFFD + 64].unsqueeze(1).to_broadcast([64, 3, 64])
        idb = c.cstf[0:64, 0:64].unsqueeze(1).to_broadcast([64, 3, 64])
        nlev = 1 if smp else 5
        v3 = lambda pi_, lo=0: R(c.ps[pi_][0:64, lo:lo + 192].rearrange("p (a b) -> p a b", a=3), ("ps", pi_))

        def pre1(ch, hb, st_):
            cc = slice(ch * 64, ch * 64 + 64)
            heads = [hb * 3 + i for i in range(3)]
            hs = slice(hb * 3, hb * 3 + 3)
            sck = lambda nm: ("SC", st_, nm)
            SCt = SC[st_]
            pd = c.next_ps()
            for i, h in enumerate(heads):
                o = R(c.ps[pd][0:64, i * 64:(i + 1) * 64], ("ps", pd))
                P.mm(o, identb64, R(maskap, "cstb"), start=True, stop=False)
                P.mm(o, EH(h), R(GC[0:6, cc], rk), start=False, stop=False)
                P.mm(o, R(NGC[0:6, cc], rk), EH(h), start=False, stop=True)
            P.act(R(U["decT"][:].rearrange("p a b -> p (a b)"), uk("decT")), R(c.ps[pd][0:64, 0:192], ("ps", pd)), AF.Exp)
            pbb = c.next_ps(); pb2 = c.next_ps()
            for qi, row in enumerate((BETA, EG, DL)):
                pq_ = pbb if qi < 2 else pb2
                for i, h in enumerate(heads):
                    P.mm(R(c.ps[pq_][0:64, ((qi % 2) * 3 + i) * 64:((qi % 2) * 3 + i + 1) * 64], ("ps", pq_)), EH(h), R(row[0:6, cc], rk))

            def bview(qi):
                pq_ = pbb if qi < 2 else pb2
                return v3(pq_, (qi % 2) * 192)
            P.tt("dve", R(U["kbT"][:], uk("kbT")), R(HK[:, hs, cc], ("H", 1)), bview(0), ALU.mult)
            P.tt("dve", R(U["vbT"][:], uk("vbT")), R(HV[:, hs, cc], ("H", 2)), bview(0), ALU.mult)
            P.tt("dve", R(SCt["kbgT"][:], sck("kbgT")), R(U["kbT"][:], uk("kbT")), bview(1), ALU.mult)
            P.tt("dve", R(SCt["qgT"][:], sck("qgT")), R(HQ[:, hs, cc], ("H", 0)), bview(1), ALU.mult)
            P.tt("dve", R(U["kdT"][:], uk("kdT")), R(HK[:, hs, cc], ("H", 1)), bview(2), ALU.mult)
            pk_ = c.next_ps()
            for i, h in enumerate(heads):
                P.mm(R(c.ps[pk_][0:64, i * 64:(i + 1) * 64], ("ps", pk_)), R(HK[:, h, cc], ("H", 1)), R(U["kbT"][:, i, :], uk("kbT")))
                P.mm(R(c.ps[pk_][0:64, 192 + i * 64:192 + (i + 1) * 64], ("ps", pk_)), R(HK[:, h, cc], ("H", 1)), R(HQ[:, h, cc], ("H", 0)))
            P.tt("dve", R(U["LT0"][:], uk("LT0")), v3(pk_, 0), R(U["decT"][:], uk("decT")), ALU.mult)
            P.tt("dve", R(U["LT"][:], uk("LT")), R(U["LT0"][:], uk("LT0")), R(offd, "cstf"), ALU.mult)
            P.tt("dve", R(SCt["aT"][:], sck("aT")), v3(pk_, 192), R(U["decT"][:], uk("decT")), ALU.mult)
            pl = c.next_ps()
            for i in range(3):
                P.transpose(R(c.ps[pl][0:64, i * 64:(i + 1) * 64], ("ps", pl)), R(U["LT0"][:, i, :], uk("LT0")), ident64)
            P.tt("dve", R(U["L"][:], uk("L")), v3(pl), R(offd, "cstf"), ALU.mult)
            P.stt("dve", R(U["X0"][:], uk("X0")), R(Fv(U["LT"][:]), uk("LT")), -1.0, R(idb, "cstf"), ALU.mult, ALU.add)
            pv = c.next_ps()
            for i in range(3):
                P.transpose(R(c.ps[pv][0:64, i * 64:(i + 1) * 64], ("ps", pv)), R(U["vbT"][:, i, :], uk("vbT")), ident64)
                P.transpose(R(c.ps[pv][0:64, 192 + i * 64:192 + (i + 1) * 64], ("ps", pv)), R(U["kdT"][:, i, :], uk("kdT")), ident64)
            P.copy("act", R(SCt["VB"][:], sck("VB")), v3(pv, 0))
            P.copy("act", R(SCt["KD"][:], sck("KD")), v3(pv, 192))
            return {"P": "L", "PT": "LT", "X": "X0"}

        def neumann(st_, state, lev):
            sck = lambda nm: ("SC", st_, nm)
            Pc, PTc, Xc = state["P"], state["PT"], state["X"]
            Pn = "P%d" % (lev % 2); PTn = "PT%d" % (lev % 2); Xn = "X%d" % ((lev + 1) % 2)
            last = (lev == nlev - 1)
            pp = c.next_ps()
            for i in range(3):
                P.mm(R(c.ps[pp][0:64, i * 64:(i + 1) * 64], ("ps", pp)), R(U[PTc][:, i, :], uk(PTc)), R(U[Pc][:, i, :], uk(Pc)))
            P.copy("act", R(U[Pn][:], uk(Pn)), v3(pp))
            if not last:
                pt_ = c.next_ps()
                for i in range(3):
                    P.mm(R(c.ps[pt_][0:64, i * 64:(i + 1) * 64], ("ps", pt_)), R(U[Pc][:, i, :], uk(Pc)), R(U[PTc][:, i, :], uk(PTc)))
                P.copy("act", R(U[PTn][:], uk(PTn)), v3(pt_))
            px = c.next_ps()
            for i in range(3):
                P.mm(R(c.ps[px][0:64, i * 64:(i + 1) * 64], ("ps", px)), R(U[Pn][:, i, :], uk(Pn)), R(U[Xc][:, i, :], uk(Xc)))
            dst = R(SC[st_]["TinvT"][:], sck("TinvT")) if last else R(U[Xn][:], uk(Xn))
            P.tt("dve", dst, R(Fv(U[Xc][:]), uk(Xc)), v3(px), ALU.add)
            state["P"], state["PT"], state["X"] = Pn, PTn, Xn

        def scan_steps(ch, hb, st_):
            cc = slice(ch * 64, ch * 64 + 64)
            heads = [hb * 3 + i for i in range(3)]
            hs = slice(hb * 3, hb * 3 + 3)
            sck = lambda nm: ("SC", st_, nm)
            SCt = SC[st_]

            def s_g():
                pr_ = c.next_ps()
                for i, h in enumerate(heads):
                    P.mm(R(c.ps[pr_][0:64, i * 64:(i + 1) * 64], ("ps", pr_)), R(SCt["kbgT"][:, i, :], sck("kbgT")), R(S[:, h, :], ("S", h)))
                P.tt("dve", R(U["Rr"][:], uk("Rr")), R(SCt["VB"][:], sck("VB")), v3(pr_), ALU.subtract)

            def s_h():
                pn = c.next_ps()
                for i in range(3):
                    P.mm(R(c.ps[pn][0:64, i * 64:(i + 1) * 64], ("ps", pn)), R(SCt["TinvT"][:, i, :], sck("TinvT")), R(U["Rr"][:, i, :], uk("Rr")))
                P.copy("act", R(U["VN"][:], uk("VN")), v3(pn))

            def s_i():
                po = c.next_ps()
                for i, h in enumerate(heads):
                    o = R(c.ps[po][0:64, i * 64:(i + 1) * 64], ("ps", po))
                    P.mm(o, R(S[:, h, :], ("S", h)), R(SCt["qgT"][:, i, :], sck("qgT")), start=True, stop=False)
                    P.mm(o, R(U["VN"][:, i, :], uk("VN")), R(SCt["aT"][:, i, :], sck("aT")), start=False, stop=True)
                P.copy("act", R(OT[:, hs, cc], "OT"), v3(po))

            def s_j():
                pS = c.next_ps()
                for i, h in enumerate(heads):
                    P.mm(R(c.ps[pS][0:64, i * 64:(i + 1) * 64], ("ps", pS)), R(SCt["KD"][:, i, :], sck("KD")), R(U["VN"][:, i, :], uk("VN")))
                for i, h in enumerate(heads):
                    P.stt("dve", R(S[:, h, :], ("S", h)), R(S[:, h, :], ("S", h)), R(EGLB[:, h, ch:ch + 1], "EGLB"),
                          R(c.ps[pS][0:64, i * 64:(i + 1) * 64], ("ps", pS)), ALU.mult, ALU.add)
            return [s_g, s_h, s_i, s_j]

        if not smp:
            units = [(ch, hb) for ch in range(nch) for hb in range(2)]
            st0 = pre1(units[0][0], units[0][1], 0)
            for lev in range(nlev):
                neumann(0, st0, lev)
            for ui in range(1, len(units)):
                st_ = ui % 2
                state = pre1(units[ui][0], units[ui][1], st_)
                steps = scan_steps(units[ui - 1][0], units[ui - 1][1], 1 - st_)
                for lev in range(nlev):
                    neumann(st_, state, lev)
                    if lev < len(steps):
                        steps[lev]()
                for fn in steps[nlev:]:
                    fn()
            for fn in scan_steps(units[-1][0], units[-1][1], (len(units) - 1) % 2):
                fn()
        else:
            ch = 0
            cc = slice(0, 64)
            for hb in range(2):
                heads = [hb * 3 + i for i in range(3)]
                st_ = 0
                sck = lambda nm: ("SC", 0, nm)
                SCt = SC[0]
                state = pre1(ch, hb, 0)
                for lev in range(nlev):
                    neumann(0, state, lev)
                for i, h in enumerate(heads):
                    P.dma("sp", SS[:], c.sgd[l][:, h].rearrange("s k v -> k s v"), writes=["SS"])
                    pr_ = c.next_ps()
                    for s_ in range(NSQ):
                        P.mm(R(c.ps[pr_][0:64, s_ * 4:(s_ + 1) * 4], ("ps", pr_)), R(SS[:, s_, :], "SS"),
                             R(SCt["kbgT"][:, i, s_ * 4:(s_ + 1) * 4], sck("kbgT")))
                    P.tt("dve", R(U["RT"][:, i, :], uk("LT0")), R(U["vbT"][:, i, :], uk("vbT")), R(c.ps[pr_][0:64, 0:64], ("ps", pr_)), ALU.subtract)
                    pq = c.next_ps()
                    P.transpose(R(c.ps[pq][0:64, 0:64], ("ps", pq)), R(U["RT"][:, i, :], uk("LT0")), ident64)
                    P.copy("act", R(U["Rr"][:, i, :], uk("Rr")), R(c.ps[pq][0:64, 0:64], ("ps", pq)))
                    pn = c.next_ps()
                    P.mm(R(c.ps[pn][0:64, 0:64], ("ps", pn)), R(SCt["TinvT"][:, i, :], sck("TinvT")), R(U["Rr"][:, i, :], uk("Rr")))
                    P.copy("act", R(U["VN"][:, i, :], uk("VN")), R(c.ps[pn][0:64, 0:64], ("ps", pn)))
                    po = c.next_ps()
                    P.mm(R(c.ps[po][0:64, 0:64], ("ps", po)), R(U["VN"][:, i, :], uk("VN")), R(SCt["aT"][:, i, :], sck("aT")), start=True, stop=False)
                    for s_ in range(NSQ):
                        P.mm(R(c.ps[po][0:64, s_ * 4:(s_ + 1) * 4], ("ps", po)), R(SS[:, s_, :], "SS"),
                             R(SCt["qgT"][:, i, s_ * 4:(s_ + 1) * 4], sck("qgT")), start=False, stop=(s_ == NSQ - 1))
                    P.copy("act", R(OT[:, h, cc], "OT"), R(c.ps[po][0:64, 0:64], ("ps", po)))
                    seqm = c.cstf[0:64, C_SEQM:C_SEQM + 16].unsqueeze(2).to_broadcast([64, NSQ, 64])
                    kdb = SCt["KD"][:, i, :].unsqueeze(1).to_broadcast([64, NSQ, 64])
                    P.tt("dve", R(KDblk[:], "KDblk"), R(kdb, sck("KD")), R(seqm, "cstf"), ALU.mult)
                    eglb = EGLB[:, h, 0:NSQ].unsqueeze(2).to_broadcast([64, NSQ, 64])
                    P.tt("dve", R(SS[:], "SS"), R(SS[:], "SS"), R(eglb, "EGLB"), ALU.mult)
                    for half in range(2):
                        pS = c.next_ps()
                        for s8 in range(8):
                            s_ = half * 8 + s8
                            P.mm(R(c.ps[pS][0:64, s8 * 64:(s8 + 1) * 64], ("ps", pS)), R(KDblk[:, s_, :], "KDblk"), R(U["VN"][:, i, :], uk("VN")))
                        P.tt("dve", R(SS[:, half * 8:half * 8 + 8, :], "SS"), R(SS[:, half * 8:half * 8 + 8, :], "SS"),
                             R(c.ps[pS][0:64, :].rearrange("p (s v) -> p s v", s=8), ("ps", pS)), ALU.add)
                    P.dma("sp", c.gd_s[l][:, h].rearrange("s k v -> k s v"), SS[:], reads=["SS"], final=True)
        P.tt("dve", R(sqo[:, :, 0:ncol], ("H", 1)), R(OT[:, :, 0:ncol], "OT"), R(OT[:, :, 0:ncol], "OT"), ALU.mult)
        for hb in range(2):
            pg = c.next_ps()
            for i in range(3):
                P.mm(R(c.ps[pg][0:64, i * 128:i * 128 + ncol], ("ps", pg)), R(c.cstb[0:64, C_ONES:C_ONES + 64], "cstb"), R(sqo[:, hb * 3 + i, 0:ncol], ("H", 1)))
            rsqrt(c, R(rso[:, hb * 3:hb * 3 + 3, 0:ncol], ("H", 0)),
                  R(c.ps[pg][0:64, 0:384].rearrange("p (a b) -> p a b", a=3)[:, :, 0:ncol], ("ps", pg)), 1.0 / 64, EPS)
        P.tt("dve", R(rso[:, :, 0:ncol], ("H", 0)), R(rso[:, :, 0:ncol], ("H", 0)), R(OT[:, :, 0:ncol], "OT"), ALU.mult)
        P.stt("dve", R(mixB[:, :, 0:ncol], "mixB"), R(rso[:, :, 0:ncol], ("H", 0)), R(c.vec[0:64, V_GNW + l:V_GNW + l + 1], "vec"),
              R(HZ[:, :, 0:ncol], "HZ"), ALU.mult, ALU.mult)
        outproj(c, l, g5, lambda mc: R(mixB[:, mc, 0:ncol], "mixB"), 6, 64, cols=cols)
    P.dma("sp", c.gd_p[l].rearrange("h k v -> k h v"), F(S[:]), reads=[("S", h) for h in range(6)], final=True)
    for b3 in range(3):
        for bb in range(3):
            blk = b3 * 3 + bb
            pi = c.next_ps()
            P.transpose(R(c.ps[pi][0:51, 0:128], ("ps", pi)), R(gatB[:, blk, :], ("gatB", blk)), c.ident)
            P.copy("dve", R(stg[0:51, bb * 128:(bb + 1) * 128], "stgB"), R(c.ps[pi][0:51, 0:128], ("ps", pi)))
        P.dma("sp", c.cb_p[l][:, b3 * 384:(b3 + 1) * 384], stg[0:3, :], reads=["stgB"], final=True)
        P.dma("sp", c.cb_s[l][:, b3 * 384:(b3 + 1) * 384], stg[3:51, :], reads=["stgB"], final=True)
    P.barrier()


CDIL = (1, 4, 16)
CWIN = (128, 512, 2048)


def _rope(c, buf3, rows, tt, rt, key):
    P = c.P
    nh = buf3.shape[1]
    x1 = buf3[:, :, 0:8]; x2 = buf3[:, :, 8:16]
    cos = c.ropet[0:rows, tt, 0:8].unsqueeze(1).to_broadcast([rows, nh, 8])
    sin = c.ropet[0:rows, tt, 8:16].unsqueeze(1).to_broadcast([rows, nh, 8])
    t = [r_[0:rows, 0:nh * 8].rearrange("p (h e) -> p h e", e=8) for r_ in rt]
    P.tt("dve", R(t[0], "ropet0"), R(x1, key), R(cos, "rope"), ALU.mult)
    P.tt("dve", R(t[1], "ropet1"), R(x2, key), R(sin, "rope"), ALU.mult)
    P.tt("dve", R(t[2], "ropet2"), R(x2, key), R(cos, "rope"), ALU.mult)
    P.tt("dve", R(t[3], "ropet3"), R(x1, key), R(sin, "rope"), ALU.mult)
    P.tt("dve", R(x1, key), R(t[0], "ropet0"), R(t[1], "ropet1"), ALU.subtract)
    P.tt("dve", R(x2, key), R(t[2], "ropet2"), R(t[3], "ropet3"), ALU.add)


def branch_c(c, l):
    P = c.P
    A_ = c.Arena(); f32 = A_.f32
    ZS = f32(768)[0:64].rearrange("p (h t) -> p h t", h=6)
    KT = A_.bf16(6 * T)[0:64].rearrange("p (h t) -> p h t", h=6)
    VS = A_.bf16(48 * 128).rearrange("p (n x) -> p n x", x=128)
    kv_raw = f32(768)
    kvst = kv_raw.rearrange("p (g x) -> p g x", g=3)
    qsb = kv_raw[:, 0:384]
    rt = [f32(48) for _ in range(4)]
    QTt = A_.bf16(6 * 128)[0:64].rearrange("p (h t) -> p h t", h=6)
    PT = [A_.bf16(512) for _ in range(2)]
    dtot = f32(256)[0:64].rearrange("p (h t) -> p h t", h=2)
    onorm = kv_raw[0:64].rearrange("p (h t) -> p h t", h=6)
    mixC = A_.bf16(768)[0:64].rearrange("p (h t) -> p h t", h=6)

    load_wo(c, l, 640, 6, 64)
    wk_t, wk_k = c.wload(c.w_in[l][:, 2956:3340], 384)
    wv_t, wv_k = c.wload(c.w_in[l][:, 3340:3724], 384)
    wq_t, wq_k = c.wload(c.w_in[l][:, 2572:2956], 384)
    wz_t, wz_k = c.wload(c.w_in[l][:, 3724:4108], 384)
    ident = c.ident

    def tokproj(pi, wt_, wk_, tt, rows, ncols=384, c0=0):
        col0 = tt * 128
        g5 = min(tt // 4, 4)
        for k in range(KC):
            P.mm(R(c.ps[pi][0:rows, 0:ncols], ("ps", pi)), R(c.hnT[:, k, col0:col0 + rows], hkey(g5)), R(wt_[:, k, c0:c0 + ncols], wk_),
                 start=(k == 0), stop=(k == KC - 1))

    def kv_tile(tt, kvst_, rt_, ksT_=None, vtok_=None):
        rows = 128 if tt < 16 else NS
        pk = c.next_ps(); tokproj(pk, wk_t, wk_k, tt, rows)
        pv = c.next_ps(); tokproj(pv, wv_t, wv_k, tt, rows)
        P.copy("act", R(kvst_[0:rows, :, 0:128], "kvst"), R(c.ps[pk][0:rows, 0:384].rearrange("p (g x) -> p g x", g=3), ("ps", pk)))
        P.copy("act", R(kvst_[0:rows, :, 128:256], "kvst"), R(c.ps[pv][0:rows, 0:384].rearrange("p (g x) -> p g x", g=3), ("ps", pv)))
        for gi in range(3):
            _rope(c, kvst_[0:rows, gi, 0:128].rearrange("p (h d) -> p h d", h=2), rows, tt, rt_, "kvst")
        for gi in range(3):
            if tt < 16:
                lo = T - CWIN[gi]
                if tt * 128 >= lo:
                    P.dma("sp", c.kv_p[gi][l][tt * 128 - lo:tt * 128 - lo + 128, :], kvst_[:, gi, :], reads=["kvst"], final=True)
            else:
                P.dma("sp", c.kv_s[gi][l], kvst_[0:NS, gi, :], reads=["kvst"], final=True)
        for half in range(2):
            pt = c.next_ps()
            for i in range(3):
                hx = half * 3 + i
                gi, h = hx // 2, hx % 2
                P.transpose(R(c.ps[pt][0:64, i * 128:i * 128 + rows], ("ps", pt)), R(kvst_[0:rows, gi, h * 64:(h + 1) * 64], "kvst"),
                            R(c.cstf[0:rows, 0:rows], "cstf"))
            src = c.ps[pt][0:64, 0:384].rearrange("p (h t) -> p h t", h=3)[:, :, 0:rows]
            if tt < 16:
                P.copy("act", R(KT[:, half * 3:half * 3 + 3, tt * 128:tt * 128 + 128], ("KT", tt)), R(src, ("ps", pt)))
            else:
                P.copy("act", R(ksT_[:, half * 3:half * 3 + 3, :], "ksT"), R(src, ("ps", pt)))
        if tt == 16:
            P.copy("dve", R(vtok_[:], "vtok"), R(kvst_[0:NS, :, 128:256], "kvst"))

    for tt in range(16):
        kv_tile(tt, kvst, rt)
    for gi in range(3):
        dil = CDIL[gi]; nb = 16 // dil
        for q4 in range(4):
            pi = c.next_ps()
            for j in range(4):
                st_ = q4 * 4 + j
                r, n = st_ // nb, st_ % nb
                lo = r + dil * n * 128
                for k in range(KC):
                    P.mm(R(c.ps[pi][:, j * 128:(j + 1) * 128], ("ps", pi)), (c.hnT[:, k, lo:lo + dil * 127 + 1:dil], tuple(hkey(g) for g in range(4))),
                         R(wv_t[:, k, gi * 128:(gi + 1) * 128], wv_k), start=(k == 0), stop=(k == KC - 1))
            P.copy("act" if q4 % 2 == 0 else "dve", R(VS[:, gi * 16 + q4 * 4:gi * 16 + q4 * 4 + 4, :], ("VS", gi)),
                   R(c.ps[pi][:].rearrange("p (n x) -> p n x", x=128), ("ps", pi)))

    c.ps_n = 4; c.ps_i = 0
    psO = [4, 5]; psD = [6, 7]
    onesb64 = R(c.cstb[:, C_ONES:C_ONES + 64], "cstb")
    mcur = c.cstb[:, C_MCUR:C_MCUR + 128]; mprev = c.cstb[:, C_MPREV:C_MPREV + 128]
    allkt = tuple(("KT", t_) for t_ in range(16))

    def q_and_z(tt, rows, QT_dst, qkey, Z_dst, qsb=qsb, rt=rt):
        pq = c.next_ps(); tokproj(pq, wq_t, wq_k, tt, rows)
        P.copy("act", R(qsb[0:rows, :], "kvst"), R(c.ps[pq][0:rows, 0:384], ("ps", pq)))
        _rope(c, qsb[0:rows, :].rearrange("p (h d) -> p h d", h=6), rows, tt, rt, "kvst")
        for half in range(2):
            pt = c.next_ps()
            for i in range(3):
                hx = half * 3 + i
                P.transpose(R(c.ps[pt][0:64, i * 128:i * 128 + rows], ("ps", pt)), R(qsb[0:rows, hx * 64:(hx + 1) * 64], "kvst"),
                            R(c.cstf[0:rows, 0:rows], "cstf"))
            P.copy("dve", R(QT_dst[:, half * 3:half * 3 + 3, 0:rows], qkey),
                   R(c.ps[pt][0:64, 0:384].rearrange("p (h t) -> p h t", h=3)[:, :, 0:rows], ("ps", pt)))
        col0 = tt * 128; g5 = min(tt // 4, 4)
        for half in range(2):
            pz = c.next_ps()
            for i in range(3):
                hx = half * 3 + i
                for k in range(KC):
                    P.mm(R(c.ps[pz][0:64, i * 128:i * 128 + rows], ("ps", pz)), R(wz_t[:, k, hx * 64:(hx + 1) * 64], wz_k),
                         R(c.hnT[:, k, col0:col0 + rows], hkey(g5)), start=(k == 0), stop=(k == KC - 1))
            silu_(c, R(Z_dst[:, half * 3:half * 3 + 3, 0:rows], "ZS"),
                  R(c.ps[pz][0:64, 0:384].rearrange("p (h t) -> p h t", h=3)[:, :, 0:rows], ("ps", pz)))

    for tt in range(16):
        q_and_z(tt, 128, QTt, "QTt", ZS)

        def pv_den(hx, ocols, vs_tile, h, pt_ap, ptkey, first, last):
            o = c.ps[psO[hx // 3]][0:64, (hx % 3) * 128 + ocols[0]:(hx % 3) * 128 + ocols[0] + ocols[1]]
            d = c.ps[psD[hx // 3]][0:64, (hx % 3) * 128 + ocols[0]:(hx % 3) * 128 + ocols[0] + ocols[1]]
            P.mm(R(o, ("ps", psO[hx // 3])), R(VS[:, vs_tile, h * 64:(h + 1) * 64], ("VS", vs_tile // 16)), R(pt_ap, ptkey), start=first, stop=last)
            P.mm(R(d, ("ps", psD[hx // 3])), onesb64, R(pt_ap, ptkey), start=first, stop=last)

        for gi in range(3):
            dil = CDIL[gi]; nb = 16 // dil; nq = 128 // dil
            n = tt // dil
            qo = (tt % dil) * nq
            kbl = [0] if n == 0 else [0, 1]
            pS = c.next_ps()
            ptile = PT[gi % 2]; ptk = ("PT", gi % 2)
            slots = {}
            for kbi in kbl:
                kb = n - kbi
                for h in range(2):
                    hx = gi * 2 + h
                    for r in range(dil):
                        col = ((kbi * 2 + h) * dil + r) * nq
                        slots[(kbi, h, r)] = col
                        klo = r + dil * kb * 128
                        o = R(c.ps[pS][:, col:col + nq], ("ps", pS))
                        P.mm(o, (KT[:, hx, klo:klo + dil * 127 + 1:dil], allkt), R(QTt[:, hx, r:128:dil], "QTt"), start=True, stop=False)
                        msk = (mcur if kbi == 0 else mprev)[:, qo:qo + nq]
                        P.mm(o, c.identb, R(msk, "cstb"), start=False, stop=True)
            used = len(kbl) * 2 * dil * nq
            P.act(R(ptile[:, 0:used], ptk), R(c.ps[pS][:, 0:used], ("ps", pS)), AF.Exp, scale=0.125)
            for h in range(2):
                hx = gi * 2 + h
                for r in range(dil):
                    for ki, kbi in enumerate(kbl):
                        kb = n - kbi
                        col = slots[(kbi, h, r)]
                        pv_den(hx, (r * nq, nq), gi * 16 + r * nb + kb, h, ptile[:, col:col + nq], ptk, ki == 0, ki == len(kbl) - 1)
        def nat(ap, dil):
            if dil == 1:
                return ap
            return ap.rearrange("p (r m) -> p m r", r=dil)
        for h in range(2):
            dv_ = dtot[:, h, :]
            P.copy("dve", R(dv_, "dtot"), R(c.ps[psD[0]][0:64, h * 128:(h + 1) * 128], ("ps", psD[0])))
            for gi in (1, 2):
                hx = gi * 2 + h
                dil = CDIL[gi]
                src = c.ps[psD[hx // 3]][0:64, (hx % 3) * 128:(hx % 3) * 128 + 128]
                dvw = dv_.rearrange("p (m r) -> p m r", r=dil)
                P.tt("dve", R(dvw, "dtot"), R(dvw, "dtot"), R(nat(src, dil), ("ps", psD[hx // 3])), ALU.add)
            P.op("dve", lambda e, dv_=dv_: e.reciprocal(dv_, dv_), reads=["dtot"], writes=["dtot"])
        for hx in range(6):
            gi, h = hx // 2, hx % 2
            dil = CDIL[gi]
            src = c.ps[psO[hx // 3]][0:64, (hx % 3) * 128:(hx % 3) * 128 + 128]
            if dil == 1:
                P.tt("dve", R(onorm[:, hx, :], "kvst"), R(src, ("ps", psO[hx // 3])), R(dtot[:, h, :], "dtot"), ALU.mult)
            else:
                P.tt("dve", R(onorm[:, hx, :].rearrange("p (m r) -> p m r", r=dil), "kvst"), R(nat(src, dil), ("ps", psO[hx // 3])),
                     R(dtot[:, h, :].rearrange("p (m r) -> p m r", r=dil), "dtot"), ALU.mult)
        P.tt("dve", R(mixC[:], "mixC"), R(onorm[:], "kvst"), R(ZS[:], "ZS"), ALU.mult)
        outproj(c, l, tt // 4, lambda mc: R(mixC[:, mc, :], "mixC"), 6, 64, cols=(tt * 128, 128))
    c.ps_n = 8; c.ps_i = 0

    P.barrier()
    B_ = c.Arena()
    b32 = B_.f32
    kvst_s = b32(768).rearrange("p (g x) -> p g x", g=3)
    qsb_s = b32(384)
    rt_s = [b32(48) for _ in range(4)]
    ksT = b32(384)[0:64].rearrange("p (h t) -> p h t", h=6)
    qsT = b32(384)[0:64].rearrange("p (h t) -> p h t", h=6)
    vtok = b32(384)[0:64].rearrange("p (g x) -> p g x", g=3)
    ZSs = b32(384)[0:64].rearrange("p (h t) -> p h t", h=6)
    c.optmp = b32(NS).rearrange("p (s i) -> p s i", i=SQ)
    kv_tile(16, kvst_s, rt_s, ksT, vtok)
    q_and_z(16, NS, qsT, "qsT", ZSs, qsb=qsb_s, rt=rt_s)
    CK = [b32(9 * 256).rearrange("p (b x) -> p b x", b=9) for _ in range(2)]
    KcT = b32(18 * 128)[0:64].rearrange("p (b t) -> p b t", b=18)
    PTn = b32(384)[0:64].rearrange("p (h t) -> p h t", h=6)
    PTc = b32(24)
    dts = b32(128)[0:64].rearrange("p (h t) -> p h t", h=2)
    ons = b32(384)[0:64].rearrange("p (h t) -> p h t", h=6)
    mixS = B_.bf16(384)[0:64].rearrange("p (h t) -> p h t", h=6)
    ones64f = R(c.cstf[0:64, C_ONES:C_ONES + 64], "cstf")
    ones128f = R(c.cstf[:, C_ONES:C_ONES + 64], "cstf")
    pO = 6; pD = 7
    c.ps_n = 6
    pn_ = c.next_ps()
    for hx in range(6):
        gi = hx // 2
        o = R(c.ps[pn_][0:64, hx * 64:(hx + 1) * 64], ("ps", pn_))
        P.mm(o, R(ksT[:, hx, :], "ksT"), R(qsT[:, hx, :], "qsT"), start=True, stop=False)
        msk = c.cstb[0:64, 704:768] if gi == 0 else c.cstb[0:64, 768:832]
        P.mm(o, R(c.cstb[0:64, 0:64], "cstb"), R(msk, "cstb"), start=False, stop=True)
    P.act(R(PTn[:].rearrange("p h t -> p (h t)"), "PTn"), R(c.ps[pn_][0:64, 0:384], ("ps", pn_)), AF.Exp, scale=0.125)
    for hx in range(6):
        gi, h = hx // 2, hx % 2
        P.mm(R(c.ps[pO][0:64, hx * 64:(hx + 1) * 64], ("ps", pO)), R(vtok[:, gi, h * 64:(h + 1) * 64], "vtok"), R(PTn[:, hx, :], "PTn"), start=True, stop=False)
        P.mm(R(c.ps[pD][0:64, hx * 64:(hx + 1) * 64], ("ps", pD)), ones64f, R(PTn[:, hx, :], "PTn"), start=True, stop=False)
    for s_ in range(NSQ):
        ck = CK[s_ % 2]; ckk = ("CK", s_ % 2)
        P.dma("sp", ck[:, 0, :], c.ck128[l][s_], writes=[ckk])
        P.dma("sp", ck[:, 1:5, :], c.ck512[l][s_].rearrange("(m i) x -> m i x", i=4), writes=[ckk])
        P.dma("sp", ck[:, 5:9, :], c.ck2048[l][s_].rearrange("(m i) x -> m i x", i=16)[:, 0:4, :], writes=[ckk])
        for q5 in range(5):
            pt = c.next_ps()
            nn = 4 if q5 < 4 else 2
            for j in range(nn):
                bh = q5 * 4 + j
                blk, h = bh // 2, bh % 2
                P.transpose(R(c.ps[pt][0:64, j * 128:(j + 1) * 128], ("ps", pt)), R(ck[:, blk, h * 64:(h + 1) * 64], ckk), ident)
            P.copy("act" if q5 % 2 == 0 else "dve", R(KcT[:, q5 * 4:q5 * 4 + nn, :], "KcT"),
                   R(c.ps[pt][0:64, 0:nn * 128].rearrange("p (b t) -> p b t", b=nn), ("ps", pt)))
        psc = c.next_ps()
        for h in range(2):
            o = R(c.ps[psc][:, h * 4:h * 4 + 4], ("ps", psc))
            P.mm(o, R(KcT[:, h, :], "KcT"), R(qsT[:, h, s_ * 4:s_ * 4 + 4], "qsT"), start=True, stop=False)
            P.mm(o, c.identb, R(c.cstb[:, 832:836], "cstb"), start=False, stop=True)
            for gi in (1, 2):
                for i in range(SQ):
                    blk = (1 if gi == 1 else 5) + i
                    col = gi * 8 + h * 4 + i
                    P.mm(R(c.ps[psc][:, col:col + 1], ("ps", psc)), R(KcT[:, blk * 2 + h, :], "KcT"),
                         R(qsT[:, gi * 2 + h, s_ * 4 + i:s_ * 4 + i + 1], "qsT"))
        P.act(R(PTc[:, 0:24], "PTc"), R(c.ps[psc][:, 0:24], ("ps", psc)), AF.Exp, scale=0.125)
        for h in range(2):
            for gi in range(3):
                hx = gi * 2 + h
                if gi == 0:
                    items = [(0, s_ * 4, 4, h * 4)]
                else:
                    items = [((1 if gi == 1 else 5) + i, s_ * 4 + i, 1, gi * 8 + h * 4 + i) for i in range(SQ)]
                for (blk, ocol, w_, pcol) in items:
                    P.mm(R(c.ps[pO][0:64, hx * 64 + ocol:hx * 64 + ocol + w_], ("ps", pO)), R(ck[:, blk, 128 + h * 64:128 + (h + 1) * 64], ckk),
                         R(PTc[:, pcol:pcol + w_], "PTc"), start=False, stop=True)
                    P.mm(R(c.ps[pD][0:64, hx * 64 + ocol:hx * 64 + ocol + w_], ("ps", pD)), ones128f,
                         R(PTc[:, pcol:pcol + w_], "PTc"), start=False, stop=True)
    for h in range(2):
        P.copy("dve", R(dts[:, h, :], "dts"), R(c.ps[pD][0:64, h * 64:(h + 1) * 64], ("ps", pD)))
        for gi in (1, 2):
            hx = gi * 2 + h
            P.tt("dve", R(dts[:, h, :], "dts"), R(dts[:, h, :], "dts"), R(c.ps[pD][0:64, hx * 64:(hx + 1) * 64], ("ps", pD)), ALU.add)
    dall = dts[:]
    P.op("dve", lambda e: e.reciprocal(dall, dall), reads=["dts"], writes=["dts"])
    for hx in range(6):
        P.tt("dve", R(ons[:, hx, :], "ons"), R(c.ps[pO][0:64, hx * 64:(hx + 1) * 64], ("ps", pO)), R(dts[:, hx % 2, :], "dts"), ALU.mult)
    P.tt("dve", R(mixS[:], "mixS"), R(ons[:], "ons"), R(ZSs[:, :, 0:NS], "ZS"), ALU.mult)
    c.ps_n = 8; c.ps_i = 0
    outproj(c, l, 4, lambda mc: R(mixS[:, mc, :], "mixS"), 6, 64, cols=(T, NS))
    P.barrier()
```

```python
import contextlib
import numpy as np
import concourse.bass as bass
import concourse.mybir as mybir
from concourse.bass_utils import run_bass_kernel_spmd

F32 = mybir.dt.float32
F32R = mybir.dt.float32r
BF16 = mybir.dt.bfloat16
AF = mybir.ActivationFunctionType
ALU = mybir.AluOpType
AX = mybir.AxisListType


class Prog:
    COMPUTE = ("pe", "act", "dve", "pool")
    ALL = ("pe", "act", "dve", "pool", "sp")

    def __init__(self, nc, stack, n_dma_sems=24):
        self.nc = nc
        self.stack = stack
        self.streams = {e: [] for e in self.ALL}
        self.esem = {e: stack.enter_context(nc.semaphore("prog_" + e)) for e in self.COMPUTE}
        self.ecount = {e: 0 for e in self.COMPUTE}
        self.known = {e: {} for e in self.ALL}
        self.res = {}
        self.dsem = {}
        for q in ("sp", "act", "pool"):
            self.dsem[q] = [[stack.enter_context(nc.semaphore("dq_%s_%d" % (q, i))), 0] for i in range(n_dma_sems)]
        self.dnext = {q: 0 for q in self.dsem}
        self.final_tokens = []

    def _deps(self, eng, reads, writes, same_engine_sem=None):
        toks = []
        for k in reads:
            r = self.res.get(k)
            if r and r["w"] is not None:
                toks.append(r["w"])
        for k in writes:
            r = self.res.get(k)
            if r:
                if r["w"] is not None:
                    toks.append(r["w"])
                toks.extend(r["r"])
        return toks

    def _record(self, tok, reads, writes):
        for k in reads:
            r = self.res.setdefault(k, {"w": None, "r": []})
            r["r"].append(tok)
        for k in writes:
            self.res[k] = {"w": tok, "r": []}

    def _waits(self, eng, toks, skip_sem=None):
        need = {}
        for (sem, val) in toks:
            if skip_sem is not None and sem is skip_sem:
                continue
            sid = id(sem)
            if self.known[eng].get(sid, 0) >= val:
                continue
            if sid not in need or need[sid][1] < val:
                need[sid] = (sem, val)
        for sid, (sem, val) in need.items():
            self.known[eng][sid] = val
        return list(need.values())

    def op(self, eng, fn, reads=(), writes=()):
        toks = self._deps(eng, reads, writes)
        own = self.esem[eng]
        if eng == "pe":
            waits = self._waits(eng, toks, skip_sem=own)
        else:
            waits = self._waits(eng, toks)
        self.ecount[eng] += 1
        tok = (own, self.ecount[eng])
        self.streams[eng].append((waits, fn, (own, 1)))
        self._record(tok, reads, writes)
        return tok

    def dma(self, q, out, in_, reads=(), writes=(), final=False, **kw):
        pool = self.dsem[q]
        i = self.dnext[q]
        self.dnext[q] = (i + 1) % len(pool)
        sem, val = pool[i]
        toks = self._deps(q, reads, writes)
        if val > 0:
            toks = toks + [(sem, val)]
        eng = {"sp": "sp", "act": "act", "pool": "pool"}[q]
        waits = self._waits(eng, toks)
        pool[i][1] = val + 16
        tok = (sem, val + 16)

        def fn(e, out=out, in_=in_, kw=kw):
            return e.dma_start(out, in_, **kw)
        self.streams[eng].append((waits, fn, (sem, 16)))
        self._record(tok, reads, writes)
        if final:
            self.final_tokens.append(tok)
        return tok


    @staticmethod
    def _k(*xs):
        out = []
        for x in xs:
            if isinstance(x, tuple):
                out.extend(x[1])
        return out

    @staticmethod
    def _a(x):
        return x[0] if isinstance(x, tuple) else x

    def mm(self, out, lhsT, rhs, start=True, stop=True):
        o, l, r = out[0], lhsT[0], rhs[0]
        return self.op("pe", lambda e: e.matmul(o, l, r, start=start, stop=stop),
                       reads=self._k(lhsT, rhs), writes=self._k(out))

    def transpose(self, out, in_, ident):
        o, i, d = out[0], in_[0], ident[0]
        return self.op("pe", lambda e: e.transpose(o, i, d), reads=self._k(in_, ident), writes=self._k(out))

    def act(self, out, in_, func, bias=0.0, scale=1.0, eng="act"):
        o, i, b, sc = out[0], in_[0], self._a(bias), self._a(scale)
        return self.op(eng, lambda e: e.activation(o, i, func, bias=b, scale=sc),
                       reads=self._k(in_, bias, scale), writes=self._k(out))

    def copy(self, eng, out, in_):
        o, i = out[0], in_[0]
        if eng == "act":
            return self.op(eng, lambda e: e.copy(o, i), reads=self._k(in_), writes=self._k(out))
        return self.op(eng, lambda e: e.tensor_copy(o, i), reads=self._k(in_), writes=self._k(out))

    def tt(self, eng, out, in0, in1, op):
        o, a, b = out[0], in0[0], in1[0]
        return self.op(eng, lambda e: e.tensor_tensor(o, a, b, op), reads=self._k(in0, in1), writes=self._k(out))

    def ts(self, eng, out, in0, s1, s2, op0, op1=None):
        o, a, x1, x2 = out[0], in0[0], self._a(s1), self._a(s2)
        if op1 is None:
            return self.op(eng, lambda e: e.tensor_scalar(o, a, x1, None, op0), reads=self._k(in0, s1), writes=self._k(out))
        return self.op(eng, lambda e: e.tensor_scalar(o, a, x1, x2, op0, op1), reads=self._k(in0, s1, s2), writes=self._k(out))

    def stt(self, eng, out, in0, scalar, in1, op0, op1):
        o, a, sc, b = out[0], in0[0], self._a(scalar), in1[0]
        return self.op(eng, lambda e: e.scalar_tensor_tensor(o, a, sc, b, op0, op1),
                       reads=self._k(in0, scalar, in1), writes=self._k(out))

    def memset(self, eng, out, val):
        o = out[0]
        return self.op(eng, lambda e: e.memset(o, val), writes=self._k(out))

    def barrier(self):
        toks = [(self.esem[e], self.ecount[e]) for e in self.COMPUTE if self.ecount[e] > 0]
        for q in self.dsem:
            for sem, val in self.dsem[q]:
                if val > 0:
                    toks.append((sem, val))
        for e in self.ALL:
            if e == "pool":
                continue
            waits = self._waits(e, toks, skip_sem=self.esem.get(e))
            if waits:
                self.streams[e].append((waits, None, None))
        self.res = {k: v for k, v in self.res.items()
                    if k in ("wo", "wab") or (isinstance(k, tuple) and len(k) == 2 and k[0] == "ring")}

    def new_epoch(self):
        self.epoch = getattr(self, "epoch", 0) + 1
        for e in self.COMPUTE:
            self.esem[e] = self.stack.enter_context(self.nc.semaphore("prog_%s_%d" % (e, self.epoch)))
            self.ecount[e] = 0

    def emit(self):
        nc = self.nc
        fin = self._waits("sp", self.final_tokens)
        streams = self.streams

        def run(ename, e):
            for (waits, fn, inc) in streams[ename]:
                for (sem, val) in waits:
                    e.wait_ge(sem, val)
                if fn is None:
                    continue
                ins = fn(e)
                if inc is not None:
                    ins.then_inc(inc[0], inc[1])
            if ename == "sp":
                for (sem, val) in fin:
                    e.wait_ge(sem, val)

        with nc.Block() as block:
            @block.sync
            def _(e):
                run("sp", e)

            @block.tensor
            def _(e):
                run("pe", e)

            @block.scalar
            def _(e):
                run("act", e)

            @block.vector
            def _(e):
                run("dve", e)

            @block.gpsimd
            def _(e):
                run("pool", e)


D = 1024
KC = 8
T = 2048
NSQ = 16
SQ = 4
NS = NSQ * SQ
NTOK = T + NS
DEPTH = 4
INW = 4108
NCORE = 8
GROUPS = [(0, 512), (512, 512), (1024, 512), (1536, 512), (2048, 64)]
WSLOT = 384
EPS = 1e-6

V_BADA = 0
V_NORMW = 96
V_FNORM = 128
V_BADAF = 136
V_CAW = 152
V_CBW = 176
V_GNW = 320
V_ALOG = 324
V_DTB = 328
NV = 332

C_IDENT = 0
C_SHIFT = 128
C_ONES = 192
C_MPREV = 320
C_MCUR = 448
C_BLK = 576
C_EH = 704
C_OFFD = 1088
C_MSEQ = 1152
C_SEQM = 1216
C_SCAN64 = 1232
C_SCAN4 = 1360
C_MDIAG = 1424
C_MS128 = 1488
NCST = 1492
NEGM = -240000.0


def R(ap, *keys):
    return (ap, keys)


class Ctx:
    pass


def build_program(phases=("A",), depth=DEPTH):
    nc = bass.Bass("TRN2", target_bir_lowering=False)
    st = contextlib.ExitStack()
    with st:
        P = Prog(nc, st)
        c = Ctx()
        c.nc, c.P, c.st = nc, P, st

        def din(name, shape):
            return nc.dram_tensor(name, list(shape), F32, kind="ExternalInput").ap()

        def dout(name, shape):
            return nc.dram_tensor(name, list(shape), F32, kind="ExternalOutput").ap()

        c.xp = din("xp", (T, D)); c.xs = din("xs", (NS, D))
        c.cT = din("cT", (128, KC, 17))
        c.sca = din("sca", (DEPTH, NSQ * 2, 256)); c.scb = din("scb", (DEPTH, NSQ * 3, 1152))
        c.sgd = din("sgd", (DEPTH, NSQ, 6, 64, 64))
        c.ck128 = din("ck128", (DEPTH, NSQ, 128, 256)); c.ck512 = din("ck512", (DEPTH, NSQ, 512, 256))
        c.ck2048 = din("ck2048", (DEPTH, NSQ, 2048, 256))
        c.w_in = din("w_in", (DEPTH, D, INW)); c.w_out = din("w_out", (DEPTH, D, D))
        c.w_ada = din("w_ada", (DEPTH, D, 3 * D)); c.w_adaf = din("w_adaf", (D, 2 * D))
        c.vecT = din("vecT", (128, NV)); c.cst = din("cst", (128, NCST))
        c.rope = din("rope", (128, 17, 16))
        c.yp = dout("yp", (T, D)); c.ys = dout("ys", (NS, D))
        c.ca_p = dout("ca_p", (DEPTH, 2, 256)); c.ca_s = dout("ca_s", (DEPTH, NSQ * 2, 256))
        c.cb_p = dout("cb_p", (DEPTH, 3, 1152)); c.cb_s = dout("cb_s", (DEPTH, NSQ * 3, 1152))
        c.gd_p = dout("gd_p", (DEPTH, 6, 64, 64)); c.gd_s = dout("gd_s", (DEPTH, NSQ, 6, 64, 64))
        c.kv_p = [dout("kv128_p", (DEPTH, 128, 256)), dout("kv512_p", (DEPTH, 512, 256)), dout("kv2048_p", (DEPTH, 2048, 256))]
        c.kv_s = [dout("kv128_s", (DEPTH, NS, 256)), dout("kv512_s", (DEPTH, NS, 256)), dout("kv2048_s", (DEPTH, NS, 256))]

        def sb(name, shape, dt):
            return st.enter_context(nc.sbuf_tensor(name, list(shape), dt))

        c.xT = sb("xT", (128, KC, NTOK), F32)
        c.hnT = sb("hnT", (128, KC, NTOK), BF16)
        c.ring = [sb("ring%d" % i, (128, KC, WSLOT), BF16) for i in range(4)]
        c.wab = sb("wab", (128, KC, 12), BF16)
        c.wo = sb("wo", (128, 6, D), BF16)
        c.vec = sb("vec", (128, NV), F32)
        c.cstf = sb("cstf", (128, NCST), F32)
        c.cstb = sb("cstb", (128, 836), BF16)
        c.ropet = sb("ropet", (128, 17, 16), F32)
        c.cTf = sb("cTf", (128, KC, 17), F32)
        c.cTb = sb("cTb", (128, KC, 17), BF16)
        c.ada = sb("ada", (128, 24, 17), F32)
        c.m1 = sb("m1", (128, KC, 17), F32)
        c.g1 = sb("g1", (128, KC, 17), F32)
        ARENA_F32 = 14592 - 2112
        c.arena = sb("arena", (128, ARENA_F32), F32)
        c.nmr = sb("nmr", (128, 2112), F32R)
        c.ps = [st.enter_context(nc.psum_tensor("ps%d" % i, [128, 512], F32)) for i in range(8)]
        c.ps_i = 0
        c.ring_i = 0
        c.phases = phases

        c.ps_n = 8

        def next_ps():
            i = c.ps_i % c.ps_n
            c.ps_i = (i + 1) % c.ps_n
            return i
        c.next_ps = next_ps

        class Arena:
            def __init__(self):
                self.off = 0

            def f32(self, n):
                o = self.off
                self.off += n
                c.arena_max = max(getattr(c, "arena_max", 0), self.off)
                assert self.off <= ARENA_F32, ("arena overflow", self.off)
                return c.arena[:, o:o + n]

            def bf16(self, n):
                assert n % 2 == 0
                return self.f32(n // 2).bitcast(BF16)

            def f32r(self, n):
                return self.f32(n).bitcast(F32R)
        c.Arena = Arena

        def wload(src, ncols):
            i = c.ring_i
            c.ring_i = (i + 1) % 4
            key = ("ring", i)
            P.dma("pool", c.ring[i][:, :, 0:ncols], src.rearrange("(k p) c -> p k c", p=128), writes=[key])
            return c.ring[i], key
        c.wload = wload

        P.dma("sp", c.vec[:], c.vecT, writes=["vec"])
        P.dma("sp", c.cstf[:], c.cst, writes=["cstf"])
        P.dma("sp", c.ropet[:], c.rope, writes=["rope"])
        P.dma("sp", c.cTf[:], c.cT, writes=["cTf"])
        P.copy("dve", R(c.cstb[:, 0:704], "cstb"), R(c.cstf[:, 0:704], "cstf"))
        P.copy("dve", R(c.cstb[:, 704:768], "cstb"), R(c.cstf[:, C_MSEQ:C_MSEQ + 64], "cstf"))
        P.copy("dve", R(c.cstb[:, 768:836], "cstb"), R(c.cstf[:, C_MDIAG:C_MDIAG + 68], "cstf"))
        P.copy("dve", R(c.cTb[:], "cTb"), R(c.cTf[:], "cTf"))
        for i in range(8):
            P.memset("dve", R(c.ps[i][:], ("ps", i)), 0.0)
        c.zero = sb("zero", (128, 384), F32)
        P.memset("dve", R(c.zero[:], "zero"), 0.0)
        c.ident = R(c.cstf[:, C_IDENT:C_IDENT + 128], "cstf")
        c.identb = R(c.cstb[:, C_IDENT:C_IDENT + 128], "cstb")
        c.onesb = R(c.cstb[:, C_ONES:C_ONES + 128], "cstb")

        load_x(c)
        for l in range(depth):
            layer(c, l)
        final(c)
        P.emit()
    return nc


def xkey(g):
    return ("x", g)


def hkey(g):
    return ("hn", g)


def load_x(c):
    P = c.P
    A = c.Arena()
    stg = [A.f32(D) for _ in range(2)]
    for tt in range(17):
        rows = 128 if tt < 16 else NS
        g = min(tt // 4, 4)
        s = stg[tt % 2]
        skey = ("xstg", tt % 2)
        src = c.xp[tt * 128:(tt + 1) * 128, :] if tt < 16 else c.xs
        P.dma("sp", s[0:rows, :], src, writes=[skey])
        for half in range(2):
            pi = c.next_ps()
            for kk in range(4):
                k = half * 4 + kk
                P.transpose(R(c.ps[pi][:, kk * 128:kk * 128 + rows], ("ps", pi)),
                            R(s[0:rows, k * 128:(k + 1) * 128], skey), R(c.cstf[0:rows, C_IDENT:C_IDENT + rows], "cstf"))
            col0 = tt * 128
            dst = c.xT[:, half * 4:half * 4 + 4, col0:col0 + rows]
            srcp = c.ps[pi][:].rearrange("p (k t) -> p k t", k=4)[:, :, 0:rows]
            P.copy("act" if half == 0 else "dve", R(dst, xkey(g)), R(srcp, ("ps", pi)))
    P.barrier()


def ada_vectors(c, l):
    P = c.P
    final_ = (l == DEPTH)
    ncol = 2 * D if final_ else 3 * D
    nj = ncol // 128
    pi = c.next_ps()
    pst = c.ps[pi][:, 0:24 * 17].rearrange("p (j s) -> p j s", s=17)
    for t0 in range(0, ncol, WSLOT):
        nc_ = min(WSLOT, ncol - t0)
        src = (c.w_adaf if final_ else c.w_ada[l])[:, t0:t0 + nc_]
        wt, wk = c.wload(src, nc_)
        for jj in range(nc_ // 128):
            j = t0 // 128 + jj
            for k in range(KC):
                P.mm(R(pst[:, j, :], ("ps", pi)), R(wt[:, k, jj * 128:(jj + 1) * 128], wk), R(c.cTb[:, k, :], "cTb"),
                     start=(k == 0), stop=(k == KC - 1))
    vb = V_BADAF if final_ else V_BADA + l * 24
    bias = c.vec[:, vb:vb + nj].unsqueeze(2).to_broadcast([128, nj, 17])
    P.tt("dve", R(c.ada[:, 0:nj, :], "ada"), R(pst[:, 0:nj, :], ("ps", pi)), R(bias, "vec"), ALU.add)
    nw0 = V_FNORM if final_ else V_NORMW + l * 8
    nw = c.vec[:, nw0:nw0 + KC].unsqueeze(2).to_broadcast([128, KC, 17])
    P.stt("dve", R(c.m1[:], "m1"), R(c.ada[:, 8:16, :], "ada"), 1.0, R(nw, "vec"), ALU.add, ALU.mult)
    if not final_:
        P.ts("dve", R(c.g1[:], "g1"), R(c.ada[:, 16:24, :], "ada"), 1.0, None, ALU.add)


def rsqrt(c, out, in_, scale, bias):
    P = c.P
    P.act(out, in_, AF.Ln, bias=bias, scale=scale)
    P.act(out, out, AF.Exp, scale=-0.5)


def sigmoid_(c, out, in_):
    P = c.P
    P.act(out, in_, AF.Exp, scale=-1.0)
    P.act(out, out, AF.Ln, bias=1.0)
    P.act(out, out, AF.Exp, scale=-1.0)


def silu_(c, out, in_):
    sigmoid_(c, out, in_)
    c.P.tt("dve", out, in_, out, ALU.mult)


def rms_stats(c, A, g, tag):
    P = c.P
    col0, ncol = GROUPS[g]
    sq = A["sq"]
    P.act(R(sq[:, :, 0:ncol], "sq"), R(c.xT[:, :, col0:col0 + ncol], xkey(g)), AF.Square)
    pi = c.next_ps()
    for k in range(KC):
        P.mm(R(c.ps[pi][:, 0:ncol], ("ps", pi)), c.onesb, R(sq[:, k, 0:ncol], "sq"), start=(k == 0), stop=(k == KC - 1))
    rstd = A["rstd"]
    rsqrt(c, R(rstd[:, 0:ncol], "rstd"), R(c.ps[pi][:, 0:ncol], ("ps", pi)), 1.0 / D, EPS)
    return rstd


def norm_phase(c, l, out_fn):
    P = c.P
    A_ = c.Arena()
    A = {"sq": A_.bf16(KC * 512).rearrange("p (k t) -> p k t", k=KC), "rstd": A_.f32(512),
         "tmp": [A_.f32(512) for _ in range(2)], "tmps": A_.f32(KC * NS).rearrange("p (k t) -> p k t", k=KC)}
    for g in range(5):
        col0, ncol = GROUPS[g]
        rstd = rms_stats(c, A, g, "n")
        if g < 4:
            for k in range(KC):
                tmp = A["tmp"][k % 2]
                tk = ("ntmp", k % 2)
                P.tt("dve", R(tmp[:, 0:ncol], tk), R(c.xT[:, k, col0:col0 + ncol], xkey(g)), R(rstd[:, 0:ncol], "rstd"), ALU.mult)
                P.act(out_fn(g, k, ncol), R(tmp[:, 0:ncol], tk), AF.Identity,
                      bias=R(c.ada[:, k, 0:1], "ada"), scale=R(c.m1[:, k, 0:1], "m1"))
        else:
            ts_ = A["tmps"]
            P.tt("dve", R(ts_[:], "ntmps"), R(c.xT[:, :, col0:col0 + ncol], xkey(g)),
                 R(rstd[:, 0:ncol].unsqueeze(1).to_broadcast([128, KC, NS]), "rstd"), ALU.mult)
            v4 = ts_[:].rearrange("p k (s i) -> p k s i", i=SQ)
            m1b = c.m1[:, :, 1:17].unsqueeze(3).to_broadcast([128, KC, NSQ, SQ])
            shb = c.ada[:, 0:8, 1:17].unsqueeze(3).to_broadcast([128, KC, NSQ, SQ])
            P.tt("dve", R(v4, "ntmps"), R(v4, "ntmps"), R(m1b, "m1"), ALU.mult)
            for k in range(KC):
                o = out_fn(g, k, ncol)
                P.tt("dve", (o[0].rearrange("p (s i) -> p s i", i=SQ), o[1]), R(v4[:, k], "ntmps"), R(shb[:, k], "ada"), ALU.add)
    P.barrier()


def proj(c, pi, wt, wk, wc0, m, g, ncol_override=None, cols=None):
    P = c.P
    col0, ncol = GROUPS[g] if cols is None else cols
    for k in range(KC):
        P.mm(R(c.ps[pi][0:m, 0:ncol], ("ps", pi)), R(wt[:, k, wc0:wc0 + m], wk), R(c.hnT[:, k, col0:col0 + ncol], hkey(g)),
             start=(k == 0), stop=(k == KC - 1))


def outproj(c, l, g, mix_fn, nchunk, kpart, cols=None):
    P = c.P
    col0, ncol = GROUPS[g] if cols is None else cols
    for dc in range(KC):
        pi = c.next_ps()
        for mc in range(nchunk):
            P.mm(R(c.ps[pi][:, 0:ncol], ("ps", pi)), R(c.wo[0:kpart, mc, dc * 128:(dc + 1) * 128], "wo"), mix_fn(mc),
                 start=(mc == 0), stop=(mc == nchunk - 1))
        xs = c.xT[:, dc, col0:col0 + ncol]
        if g < 4:
            P.stt("dve", R(xs, xkey(g)), R(c.ps[pi][:, 0:ncol], ("ps", pi)), R(c.g1[:, dc, 0:1], "g1"), R(xs, xkey(g)), ALU.mult, ALU.add)
        else:
            x3 = xs.rearrange("p (s i) -> p s i", i=SQ)
            p3 = c.ps[pi][:, 0:ncol].rearrange("p (s i) -> p s i", i=SQ)
            g1b = c.g1[:, dc, 1:17].unsqueeze(2).to_broadcast([128, NSQ, SQ])
            tmp = c.optmp
            P.tt("dve", R(tmp, "optmp"), R(p3, ("ps", pi)), R(g1b, "g1"), ALU.mult)
            P.tt("dve", R(x3, xkey(g)), R(x3, xkey(g)), R(tmp, "optmp"), ALU.add)


def load_wo(c, l, r0, nchunk, kpart):
    P = c.P
    src = c.w_out[l][r0:r0 + nchunk * kpart, :].rearrange("(j p) d -> p j d", p=kpart)
    P.dma("pool", c.wo[0:kpart, 0:nchunk, :], src, writes=["wo"])


def layer(c, l):
    P = c.P
    ada_vectors(c, l)
    norm_phase(c, l, lambda g, k, ncol: R(c.hnT[:, k, GROUPS[g][0]:GROUPS[g][0] + ncol], hkey(g)))
    if "A" in c.phases:
        branch_a(c, l)
    if "B" in c.phases:
        branch_b(c, l)
    if "C" in c.phases:
        branch_c(c, l)


def branch_a(c, l):
    P = c.P
    A_ = c.Arena()
    CI = A_.f32(2 * (2 + T)).rearrange("p (j t) -> p j t", j=2)
    CIs = A_.f32(2 * NSQ * 6).rearrange("p (j s t) -> p j s t", j=2, s=NSQ)
    tmpx = A_.f32(512); sz = A_.f32(512); acc = A_.f32(512); tz = A_.f32(512)
    mixA = A_.bf16(2 * 512).rearrange("p (j t) -> p j t", j=2)
    c.optmp = A_.f32(NS).rearrange("p (s i) -> p s i", i=SQ)
    sin_ = A_.f32(256)
    gat = A_.f32(2 * 34).rearrange("p (j t) -> p j t", j=2)
    outa = A_.f32(256)
    load_wo(c, l, 0, 2, 128)
    tiles = []
    for t0 in (0, 384, 768):
        ncols = min(384, 1024 - t0)
        tiles.append(c.wload(c.w_in[l][:, t0:t0 + ncols], ncols))

    def wsel(col):
        ti = col // 384
        return tiles[ti][0], tiles[ti][1], col - ti * 384

    P.memset("dve", R(CI[:, :, 0:2], "CIh"), 0.0)
    P.dma("sp", sin_[0:NSQ * 2, :], c.sca[l], writes=["sin"])
    for j in range(2):
        pi = c.next_ps()
        P.transpose(R(c.ps[pi][:, 0:32], ("ps", pi)), R(sin_[0:32, j * 128:(j + 1) * 128], "sin"), R(c.cstf[0:32, 0:32], "cstf"))
        P.copy("act", R(CIs[:, j, :, 0:2], ("CIs", j)), R(c.ps[pi][:, 0:32].rearrange("p (s r) -> p s r", r=2), ("ps", pi)))
    for g in range(5):
        col0, ncol = GROUPS[g]
        for j in range(2):
            p0 = c.next_ps(); wt, wk, wc = wsel(j * 128); proj(c, p0, wt, wk, wc, 128, g)
            p1 = c.next_ps(); wt, wk, wc = wsel(256 + j * 128); proj(c, p1, wt, wk, wc, 128, g)
            P.copy("act", R(tmpx[:, 0:ncol], "tmpx"), R(c.ps[p0][:, 0:ncol], ("ps", p0)))
            vb = V_CAW + (l * 3) * 2 + j
            w0 = R(c.vec[:, vb:vb + 1], "vec"); w1 = R(c.vec[:, vb + 2:vb + 3], "vec"); w2 = R(c.vec[:, vb + 4:vb + 5], "vec")
            if g < 4:
                ck = ("CI", j, g)
                P.tt("dve", R(CI[:, j, 2 + col0:2 + col0 + ncol], ck), R(tmpx[:, 0:ncol], "tmpx"), R(c.ps[p1][:, 0:ncol], ("ps", p1)), ALU.mult)
                rd = [ck, ("CI", j, g - 1), "CIh"]
                P.ts("dve", R(acc[:, 0:ncol], "acc"), (CI[:, j, col0 + 2:col0 + 2 + ncol], rd), w2, None, ALU.mult)
                P.stt("dve", R(acc[:, 0:ncol], "acc"), (CI[:, j, col0 + 1:col0 + 1 + ncol], rd), w1, R(acc[:, 0:ncol], "acc"), ALU.mult, ALU.add)
                P.stt("dve", R(acc[:, 0:ncol], "acc"), (CI[:, j, col0:col0 + ncol], rd), w0, R(acc[:, 0:ncol], "acc"), ALU.mult, ALU.add)
                accv = acc[:, 0:ncol]
            else:
                ck = ("CIs", j)
                P.tt("dve", R(CIs[:, j, :, 2:6], ck), R(tmpx[:, 0:ncol].rearrange("p (s i) -> p s i", i=SQ), "tmpx"),
                     R(c.ps[p1][:, 0:ncol].rearrange("p (s i) -> p s i", i=SQ), ("ps", p1)), ALU.mult)
                a3 = acc[:, 0:ncol].rearrange("p (s i) -> p s i", i=SQ)
                P.ts("dve", R(a3, "acc"), R(CIs[:, j, :, 2:6], ck), w2, None, ALU.mult)
                P.stt("dve", R(a3, "acc"), R(CIs[:, j, :, 1:5], ck), w1, R(a3, "acc"), ALU.mult, ALU.add)
                P.stt("dve", R(a3, "acc"), R(CIs[:, j, :, 0:4], ck), w0, R(a3, "acc"), ALU.mult, ALU.add)
                accv = acc[:, 0:ncol]
            p2 = c.next_ps(); wt, wk, wc = wsel(512 + j * 128); proj(c, p2, wt, wk, wc, 128, g)
            p3 = c.next_ps(); wt, wk, wc = wsel(768 + j * 128); proj(c, p3, wt, wk, wc, 128, g)
            silu_(c, R(sz[:, 0:ncol], "sz"), R(c.ps[p3][:, 0:ncol], ("ps", p3)))
            P.tt("dve", R(tz[:, 0:ncol], "tz"), R(accv, "acc"), R(sz[:, 0:ncol], "sz"), ALU.mult)
            P.tt("dve", R(mixA[:, j, 0:ncol], ("mixA", j)), R(tz[:, 0:ncol], "tz"), R(c.ps[p2][:, 0:ncol], ("ps", p2)), ALU.mult)
        outproj(c, l, g, lambda mc: R(mixA[:, mc, 0:GROUPS[g][1]], ("mixA", mc)), 2, 128)
    for j in range(2):
        P.copy("act", R(gat[:, j, 0:2], ("gat", j)), R(CI[:, j, T:T + 2], ("CI", j, 3)))
        P.copy("act", R(gat[:, j, 2:34].rearrange("p (s r) -> p s r", r=2), ("gat", j)), R(CIs[:, j, :, 4:6], ("CIs", j)))
        pi = c.next_ps()
        P.transpose(R(c.ps[pi][0:34, 0:128], ("ps", pi)), R(gat[:, j, :], ("gat", j)), c.ident)
        P.copy("dve", R(outa[0:34, j * 128:(j + 1) * 128], "outa"), R(c.ps[pi][0:34, 0:128], ("ps", pi)))
    P.dma("sp", c.ca_p[l], outa[0:2, :], reads=["outa"], final=True)
    P.dma("sp", c.ca_s[l], outa[2:34, :], reads=["outa"], final=True)
    P.barrier()


def final(c):
    P = c.P
    ada_vectors(c, DEPTH)
    A_ = c.Arena()
    yT = A_.f32(KC * 512).rearrange("p (k t) -> p k t", k=KC)
    A = {"sq": A_.bf16(KC * 512).rearrange("p (k t) -> p k t", k=KC), "rstd": A_.f32(512),
         "tmp": [A_.f32(512) for _ in range(2)], "tmps": A_.f32(KC * NS).rearrange("p (k t) -> p k t", k=KC)}
    ystg = [A_.f32(D) for _ in range(2)]
    si = 0
    for g in range(5):
        col0, ncol = GROUPS[g]
        rstd = rms_stats(c, A, g, "f")
        yk = ("yT",)
        if g < 4:
            for k in range(KC):
                tmp = A["tmp"][k % 2]; tk = ("ntmp", k % 2)
                P.tt("dve", R(tmp[:, 0:ncol], tk), R(c.xT[:, k, col0:col0 + ncol], xkey(g)), R(rstd[:, 0:ncol], "rstd"), ALU.mult)
                P.act(R(yT[:, k, 0:ncol], "yT"), R(tmp[:, 0:ncol], tk), AF.Identity,
                      bias=R(c.ada[:, k, 0:1], "ada"), scale=R(c.m1[:, k, 0:1], "m1"))
        else:
            ts_ = A["tmps"]
            P.tt("dve", R(ts_[:], "ntmps"), R(c.xT[:, :, col0:col0 + ncol], xkey(g)),
                 R(rstd[:, 0:ncol].unsqueeze(1).to_broadcast([128, KC, NS]), "rstd"), ALU.mult)
            v4 = ts_[:].rearrange("p k (s i) -> p k s i", i=SQ)
            m1b = c.m1[:, :, 1:17].unsqueeze(3).to_broadcast([128, KC, NSQ, SQ])
            shb = c.ada[:, 0:8, 1:17].unsqueeze(3).to_broadcast([128, KC, NSQ, SQ])
            P.tt("dve", R(v4, "ntmps"), R(v4, "ntmps"), R(m1b, "m1"), ALU.mult)
            P.tt("dve", R(yT[:, :, 0:NS].rearrange("p k (s i) -> p k s i", i=SQ), "yT"), R(v4, "ntmps"), R(shb, "ada"), ALU.add)
        for tt in range((ncol + 127) // 128):
            rows = min(128, ncol - tt * 128)
            stg = ystg[si % 2]; sk = ("ystg", si % 2); si += 1
            for half in range(2):
                pi = c.next_ps()
                for kk in range(4):
                    k = half * 4 + kk
                    P.transpose(R(c.ps[pi][0:rows, kk * 128:(kk + 1) * 128], ("ps", pi)),
                                R(yT[:, k, tt * 128:tt * 128 + rows], "yT"), c.ident)
                P.copy("act" if half == 0 else "dve", R(stg[0:rows, half * 512:(half + 1) * 512], sk), R(c.ps[pi][0:rows, :], ("ps", pi)))
            if g < 4:
                dst = c.yp[col0 + tt * 128:col0 + tt * 128 + rows, :]
            else:
                dst = c.ys
            P.dma("sp", dst, stg[0:rows, :], reads=[sk], final=True)


_PHASES = ("A", "B", "C")
_NC_CACHE = {}


def _host_consts():
    cst = np.zeros((128, NCST), np.float32)
    cst[:, C_IDENT:C_IDENT + 128] = np.eye(128, dtype=np.float32)
    for m in range(64):
        cst[64 + m, C_SHIFT + m] = 1.0
    cst[:, C_ONES:C_ONES + 128] = 1.0
    k = np.arange(128)[:, None]
    q = np.arange(128)[None, :]
    cst[:, C_MPREV:C_MPREV + 128] = np.where(k >= q, 0.0, NEGM)
    cst[:, C_MCUR:C_MCUR + 128] = np.where(k <= q, 0.0, NEGM)
    cst[0:64, C_BLK:C_BLK + 64] = 1.0
    cst[64:128, C_BLK + 64:C_BLK + 128] = 1.0
    for h in range(6):
        cst[h, C_EH + h * 64:C_EH + (h + 1) * 64] = 1.0
    cst[0:64, C_OFFD:C_OFFD + 64] = 1.0 - np.eye(64, dtype=np.float32)
    j64 = np.arange(64)[:, None]
    i64 = np.arange(64)[None, :]
    cst[0:64, C_MSEQ:C_MSEQ + 64] = np.where((j64 // 4 == i64 // 4) & (j64 <= i64), 0.0, NEGM)
    cst[0:64, C_SEQM:C_SEQM + 16] = (j64 // 4 == np.arange(16)[None, :]).astype(np.float32)
    cst[:, C_SCAN64:C_SCAN64 + 128] = (np.arange(128) % 64 != 0).astype(np.float32)[None, :]
    cst[:, C_SCAN4:C_SCAN4 + 64] = (np.arange(64) % 4 != 0).astype(np.float32)[None, :]
    cst[0:64, C_MDIAG:C_MDIAG + 64] = np.where(j64 == i64, 0.0, NEGM)
    cst[:, C_MS128:C_MS128 + 4] = np.where(np.arange(128)[:, None] >= np.arange(4)[None, :], 0.0, NEGM)
    half = 8
    inv_freq = (500000.0 ** (-np.arange(half, dtype=np.float32) * np.float32(2.0 / 16))).astype(np.float32)
    rope = np.zeros((128, 17, 16), np.float32)
    for tt in range(17):
        if tt < 16:
            pos = (tt * 128 + np.arange(128)).astype(np.float32)
        else:
            pos = (T + (np.arange(128) % SQ)).astype(np.float32)
        ang = pos[:, None] * inv_freq[None, :]
        rope[:, tt, 0:8] = np.cos(ang)
        rope[:, tt, 8:16] = np.sin(ang)
    return cst, rope


def _fm(v):
    v = np.asarray(v, np.float32)
    return np.ascontiguousarray(v.reshape(-1, 128).T)


def _host_vecT(b_ada, norm_w, final_norm_w, b_ada_final, conv_a_w, conv_b_w, gdn_norm_w, a_log, dt_bias):
    vt = np.zeros((128, NV), np.float32)
    for l in range(DEPTH):
        vt[:, V_BADA + l * 24:V_BADA + (l + 1) * 24] = _fm(b_ada[l])
        vt[:, V_NORMW + l * 8:V_NORMW + (l + 1) * 8] = _fm(norm_w[l])
        for tap in range(3):
            vt[:, V_CAW + (l * 3 + tap) * 2:V_CAW + (l * 3 + tap) * 2 + 2] = _fm(conv_a_w[l, tap])
        for tap in range(4):
            vt[:, V_CBW + (l * 4 + tap) * 9:V_CBW + (l * 4 + tap) * 9 + 9] = _fm(conv_b_w[l, tap])
        vt[:, V_GNW + l] = np.tile(np.asarray(gdn_norm_w[l], np.float32), 2)
        vt[0:6, V_ALOG + l] = a_log[l]
        vt[0:6, V_DTB + l] = dt_bias[l]
    vt[:, V_FNORM:V_FNORM + 8] = _fm(final_norm_w)
    vt[:, V_BADAF:V_BADAF + 16] = _fm(b_ada_final)
    return vt


def kernel(x_prompt, x_sample, state_conv_a, state_conv_b, state_gdn, cache_kv_w128, cache_kv_w512,
           cache_kv_w2048, c_prompt, c_sample, w_in, w_out, w_ada, b_ada, norm_w, conv_a_w, conv_b_w,
           a_log, dt_bias, gdn_norm_w, final_norm_w, w_ada_final, b_ada_final, _phases=None, _depth=DEPTH):
    phases = tuple(_phases) if _phases is not None else _PHASES
    f = lambda a: np.ascontiguousarray(np.asarray(a, dtype=np.float32))
    key = (phases, _depth)
    if key not in _NC_CACHE:
        _NC_CACHE[key] = build_program(phases, _depth)
    nc = _NC_CACHE[key]
    cst, rope = _host_consts()
    vt = _host_vecT(f(b_ada), f(norm_w), f(final_norm_w), f(b_ada_final), f(conv_a_w), f(conv_b_w), f(gdn_norm_w),
                    f(a_log), f(dt_bias))
    w_in, w_out, w_ada, w_adaf = f(w_in), f(w_out), f(w_ada), f(w_ada_final)
    x_prompt, x_sample = f(x_prompt), f(x_sample)
    c_prompt, c_sample = f(c_prompt), f(c_sample)
    sca, scb, sgd = f(state_conv_a), f(state_conv_b), f(state_gdn)
    k128, k512, k2048 = f(cache_kv_w128), f(cache_kv_w512), f(cache_kv_w2048)
    in_maps = []
    for i in range(NCORE):
        ss = slice(i * NSQ, (i + 1) * NSQ)
        call = np.concatenate([c_prompt[i:i + 1], c_sample[ss]], axis=0)
        cT = np.ascontiguousarray(call.reshape(17, KC, 128).transpose(2, 1, 0))
        in_maps.append({
            "xp": x_prompt[i], "xs": np.ascontiguousarray(x_sample[ss].reshape(NS, D)), "cT": cT,
            "sca": np.ascontiguousarray(sca[:, ss].reshape(DEPTH, NSQ * 2, 256)),
            "scb": np.ascontiguousarray(scb[:, ss].reshape(DEPTH, NSQ * 3, 1152)),
            "sgd": np.ascontiguousarray(sgd[:, ss]),
            "ck128": np.ascontiguousarray(k128[:, ss].reshape(DEPTH, NSQ, 128, 256)),
            "ck512": np.ascontiguousarray(k512[:, ss].reshape(DEPTH, NSQ, 512, 256)),
            "ck2048": np.ascontiguousarray(k2048[:, ss].reshape(DEPTH, NSQ, 2048, 256)),
            "w_in": w_in, "w_out": w_out, "w_ada": w_ada, "w_adaf": w_adaf,
            "vecT": vt, "cst": cst, "rope": rope,
        })
    res = run_bass_kernel_spmd(nc, in_maps, core_ids=list(range(NCORE)))
    rs = res.results
    cat = lambda name: np.stack([r[name] for r in rs], axis=0)
    y_p = cat("yp")
    y_s = cat("ys").reshape(NCORE * NSQ, SQ, D)
    ca_p = cat("ca_p").transpose(1, 0, 2, 3)
    ca_s = cat("ca_s").reshape(NCORE, DEPTH, NSQ, 2, 256).transpose(1, 0, 2, 3, 4).reshape(DEPTH, NCORE * NSQ, 2, 256)
    cb_p = cat("cb_p").transpose(1, 0, 2, 3)
    cb_s = cat("cb_s").reshape(NCORE, DEPTH, NSQ, 3, 1152).transpose(1, 0, 2, 3, 4).reshape(DEPTH, NCORE * NSQ, 3, 1152)
    gd_p = cat("gd_p").transpose(1, 0, 2, 3, 4)
    gd_s = cat("gd_s").transpose(1, 0, 2, 3, 4, 5).reshape(DEPTH, NCORE * NSQ, 6, 64, 64)
    outs = [y_p, y_s, ca_p, ca_s, cb_p, cb_s, gd_p, gd_s]
    for gi, win in enumerate((128, 512, 2048)):
        name = "kv%d" % win
        kp = cat(name + "_p").transpose(1, 0, 2, 3).reshape(DEPTH, NCORE, win, 2, 2, 64)
        ks = cat(name + "_s").reshape(NCORE, DEPTH, NSQ, SQ, 256).transpose(1, 0, 2, 3, 4).reshape(DEPTH, NCORE * NSQ, SQ, 2, 2, 64)
        outs += [kp, ks]
    return tuple(np.ascontiguousarray(o.astype(np.float32)) for o in outs)


def branch_b(c, l):
    P = c.P
    A_ = c.Arena()
    f32 = A_.f32

    def t3(n_mid, n_in, dt=F32):
        a = f32(n_mid * n_in)[0:64]
        return a.rearrange("p (a b) -> p a b", a=n_mid)

    _pre = f32(131)
    pre = [_pre, _pre]
    pres = f32(NSQ * 7).rearrange("p (s t) -> p s t", t=7)
    halo = f32(27).rearrange("p (b t) -> p b t", t=3)
    acc = f32(128); act_ = f32(128); rinv = f32(128)
    sqb = A_.bf16(128)
    nrm = act_
    hq_raw = f32(768)[0:64]; hk_raw = f32(768)[0:64]
    HQ = hq_raw.rearrange("p (a b) -> p a b", a=6)
    HK = hk_raw.rearrange("p (a b) -> p a b", a=6)
    HV = t3(6, 128, F32R)
    HZ = t3(6, 128)
    szt = f32(128)
    G = f32(128); BETA = f32(128); GC = f32(128); EG = f32(128); DL = f32(128); NGC = f32(128); tmpd = G
    EGL = f32(16); nA = f32(1)
    EGLB = t3(6, 16)
    OT = t3(6, 128)
    mixB = A_.bf16(6 * 128)[0:64].rearrange("p (h t) -> p h t", h=6)
    sqo = hk_raw[:, 0:384].bitcast(BF16).rearrange("p (h t) -> p h t", h=6)
    rso = hq_raw.rearrange("p (a b) -> p a b", a=6)
    S = t3(6, 64, F32R)
    SS = t3(NSQ, 64)
    KDblk = t3(NSQ, 64, F32R)
    U = {}
    for nm in ("decT", "LT0", "kbT", "kdT", "vbT", "VN"):
        U[nm] = t3(3, 64)
    U["RT"] = U["LT0"]
    _nm_i = [0]

    def rt3():
        o = _nm_i[0]; _nm_i[0] += 192
        return c.nmr[0:64, o:o + 192].rearrange("p (a b) -> p a b", a=3)
    for nm in ("LT", "L", "P0", "P1", "PT0", "PT1", "X0", "X1", "Rr"):
        U[nm] = rt3()
    SC = [{nm: t3(3, 64) for nm in ("kbgT", "qgT", "aT", "VB", "KD")} for _ in range(2)]
    for sc_ in SC:
        sc_["TinvT"] = rt3()

    def Fv(ap):
        return ap.bitcast(F32)
    stg = f32(384)
    gatB = f32(9 * 51).rearrange("p (b t) -> p b t", b=9)
    c.optmp = f32(NS).rearrange("p (s i) -> p s i", i=SQ)

    def F(ap):
        return ap

    identr = R(c.cstf[0:64, 0:64], "cstf")
    shiftr = R(c.cstf[:, C_SHIFT:C_SHIFT + 64], "cstf")
    ident64 = R(c.cstf[0:64, 0:64], "cstf")
    identb64 = R(c.cstb[0:64, 0:64], "cstb")

    def EH(h):
        return R(c.cstf[0:6, C_EH + h * 64:C_EH + (h + 1) * 64], "cstf")

    load_wo(c, l, 256, 6, 64)
    wt = [c.wload(c.w_in[l][:, 1024 + i * 384:1024 + (i + 1) * 384], 384) for i in range(4)]
    P.dma("pool", c.wab[:], c.w_in[l][:, 2560:2572].rearrange("(k p) c -> p k c", p=128), writes=["wab"])

    P.copy("dve", R(S[:], "S"), R(c.zero[0:64, 0:384].rearrange("p (h t) -> p h t", h=6), "zero"))
    P.memset("dve", R(halo[:], "halo"), 0.0)
    P.act(R(nA[0:6, :], "nA"), R(c.vec[0:6, V_ALOG + l:V_ALOG + l + 1], "vec"), AF.Exp)
    P.ts("dve", R(nA[0:6, :], "nA"), R(nA[0:6, :], "nA"), -1.0, None, ALU.mult)
    for b3 in range(3):
        P.dma("sp", stg[0:NSQ * 3, :], c.scb[l][:, b3 * 384:(b3 + 1) * 384], writes=["stgB"])
        for bb in range(3):
            blk = b3 * 3 + bb
            pi = c.next_ps()
            P.transpose(R(c.ps[pi][:, 0:48], ("ps", pi)), R(stg[0:48, bb * 128:(bb + 1) * 128], "stgB"), R(c.cstf[0:48, 0:48], "cstf"))
            P.copy("act", R(gatB[:, blk, 0:48], ("gatB", blk)), R(c.ps[pi][:, 0:48], ("ps", pi)))

    groups = [(gb * 128, 128, gb // 4) for gb in range(16)] + [(T, NS, 4)]
    for gi, (col0, ncol, g5) in enumerate(groups):
        smp = (g5 == 4)
        nch = 1 if smp else 2
        cols = (col0, ncol)
        for blk in range(9):
            typ, sub = blk // 3, blk % 3
            wtile, wkey = wt[typ]
            pi = c.next_ps()
            proj(c, pi, wtile, wkey, sub * 128, 128, g5, cols=cols)
            vb = V_CBW + (l * 4) * 9 + blk
            wtap = [R(c.vec[:, vb + 9 * tap:vb + 9 * tap + 1], "vec") for tap in range(4)]
            if not smp:
                pr = pre[0]; pk = ("pre", 0)
                P.copy("dve", R(pr[:, 0:3], pk), R(halo[:, blk, :], ("halo", blk)))
                P.copy("act", R(pr[:, 3:3 + ncol], pk), R(c.ps[pi][:, 0:ncol], ("ps", pi)))
                P.copy("dve", R(halo[:, blk, :], ("halo", blk)), R(pr[:, ncol:ncol + 3], pk))
                P.ts("dve", R(acc[:, 0:ncol], "accB"), R(pr[:, 3:3 + ncol], pk), wtap[3], None, ALU.mult)
                for tap in range(3):
                    P.stt("dve", R(acc[:, 0:ncol], "accB"), R(pr[:, tap:tap + ncol], pk), wtap[tap], R(acc[:, 0:ncol], "accB"), ALU.mult, ALU.add)
            else:
                pk = ("pres",)
                P.copy("dve", R(pres[:, :, 0:3], pk), R(gatB[:, blk, 0:48].rearrange("p (s r) -> p s r", r=3), ("gatB", blk)))
                P.copy("act", R(pres[:, :, 3:7], pk), R(c.ps[pi][:, 0:ncol].rearrange("p (s i) -> p s i", i=SQ), ("ps", pi)))
                a3 = acc[:, 0:ncol].rearrange("p (s i) -> p s i", i=SQ)
                P.ts("dve", R(a3, "accB"), R(pres[:, :, 3:7], pk), wtap[3], None, ALU.mult)
                for tap in range(3):
                    P.stt("dve", R(a3, "accB"), R(pres[:, :, tap:tap + 4], pk), wtap[tap], R(a3, "accB"), ALU.mult, ALU.add)
                P.copy("act", R(gatB[:, blk, 3:51].rearrange("p (s r) -> p s r", r=3), ("gatB", blk)), R(pres[:, :, 4:7], pk))
                P.copy("act", R(gatB[:, blk, 0:3], ("gatB", blk)), R(halo[:, blk, :], ("halo", blk)))
            silu_(c, R(act_[:, 0:ncol], "actB"), R(acc[:, 0:ncol], "accB"))
            if typ < 2:
                P.tt("dve", R(sqb[:, 0:ncol], "sqb"), R(act_[:, 0:ncol], "actB"), R(act_[:, 0:ncol], "actB"), ALU.mult)
                p2 = c.next_ps()
                P.mm(R(c.ps[p2][:, 0:ncol], ("ps", p2)), R(c.cstb[:, C_BLK:C_BLK + 128], "cstb"), R(sqb[:, 0:ncol], "sqb"))
                rsqrt(c, R(rinv[:, 0:ncol], "rinvB"), R(c.ps[p2][:, 0:ncol], ("ps", p2)), 1.0, 1e-6)
                P.stt("dve", R(nrm[:, 0:ncol], "actB"), R(act_[:, 0:ncol], "actB"), 0.125 if typ == 0 else 1.0,
                      R(rinv[:, 0:ncol], "rinvB"), ALU.mult, ALU.mult)
            H = (HQ, HK, HV)[typ]
            hk = ("H", typ)
            P.copy("act", R(H[:, 2 * sub, 0:ncol], hk), R(nrm[0:64, 0:ncol], "actB"))
            p3 = c.next_ps()
            P.mm(R(c.ps[p3][0:64, 0:ncol], ("ps", p3)), shiftr, R(nrm[:, 0:ncol], "actB"))
            P.copy("act", R(H[:, 2 * sub + 1, 0:ncol], hk), R(c.ps[p3][0:64, 0:ncol], ("ps", p3)))
        for sub in range(3):
            pi = c.next_ps()
            proj(c, pi, wt[3][0], wt[3][1], sub * 128, 128, g5, cols=cols)
            silu_(c, R(szt[:, 0:ncol], "szt"), R(c.ps[pi][:, 0:ncol], ("ps", pi)))
            P.copy("dve", R(HZ[:, 2 * sub, 0:ncol], "HZ"), R(F(szt[0:64, 0:ncol]), "szt"))
            p3 = c.next_ps()
            P.mm(R(c.ps[p3][0:64, 0:ncol], ("ps", p3)), shiftr, R(szt[:, 0:ncol], "szt"))
            P.copy("act", R(HZ[:, 2 * sub + 1, 0:ncol], "HZ"), R(c.ps[p3][0:64, 0:ncol], ("ps", p3)))
        pa = c.next_ps(); pb = c.next_ps()
        for k in range(KC):
            P.mm(R(c.ps[pa][0:6, 0:ncol], ("ps", pa)), R(c.wab[:, k, 0:6], "wab"), R(c.hnT[:, k, col0:col0 + ncol], hkey(g5)),
                 start=(k == 0), stop=(k == KC - 1))
        for k in range(KC):
            P.mm(R(c.ps[pb][0:6, 0:ncol], ("ps", pb)), R(c.wab[:, k, 6:12], "wab"), R(c.hnT[:, k, col0:col0 + ncol], hkey(g5)),
                 start=(k == 0), stop=(k == KC - 1))
        rk = "rowsB"
        P.act(R(G[0:6, 0:ncol], rk), R(c.ps[pa][0:6, 0:ncol], ("ps", pa)), AF.Exp, bias=R(c.vec[0:6, V_DTB + l:V_DTB + l + 1], "vec"))
        P.act(R(G[0:6, 0:ncol], rk), R(G[0:6, 0:ncol], rk), AF.Ln, bias=1.0)
        P.ts("dve", R(G[0:6, 0:ncol], rk), R(G[0:6, 0:ncol], rk), R(nA[0:6, 0:1], "nA"), None, ALU.mult)
        sigmoid_(c, R(BETA[0:6, 0:ncol], rk), R(c.ps[pb][0:6, 0:ncol], ("ps", pb)))
        scm = c.cstf[0:6, C_SCAN4:C_SCAN4 + 64] if smp else c.cstf[0:6, C_SCAN64:C_SCAN64 + 128]
        gco, go, sco = GC[0:6, 0:ncol], G[0:6, 0:ncol], scm
        P.op("dve", lambda e, gco=gco, go=go, sco=sco: e.tensor_tensor_scan(gco, sco, go, 0.0, ALU.mult, ALU.add), reads=[rk, "cstf"], writes=[rk])
        P.act(R(EG[0:6, 0:ncol], rk), R(GC[0:6, 0:ncol], rk), AF.Exp)
        P.ts("dve", R(NGC[0:6, 0:ncol], rk), R(GC[0:6, 0:ncol], rk), -1.0, None, ALU.mult)
        clen = SQ if smp else 64
        nseg = ncol // clen
        gc3 = GC[0:6, 0:ncol].rearrange("p (n t) -> p n t", t=clen)
        glb = gc3[:, :, clen - 1:clen].to_broadcast([6, nseg, clen])
        P.tt("dve", R(tmpd[0:6, 0:ncol].rearrange("p (n t) -> p n t", t=clen), rk), R(glb, rk), R(gc3, rk), ALU.subtract)
        P.act(R(DL[0:6, 0:ncol], rk), R(tmpd[0:6, 0:ncol], rk), AF.Exp)
        P.act(R(EGL[0:6, 0:nseg], rk), R(gc3[:, :, clen - 1], rk), AF.Exp)
        pe_ = c.next_ps()
        for h in range(6):
            P.mm(R(c.ps[pe_][0:64, h * 16:h * 16 + nseg], ("ps", pe_)), EH(h), R(EGL[0:6, 0:nseg], rk))
        P.copy("dve", R(EGLB[:, :, 0:nseg], "EGLB"), R(c.ps[pe_][0:64, 0:96].rearrange("p (h n) -> p h n", h=6)[:, :, 0:nseg], ("ps", pe_)))

        uk = lambda nm: ("U", nm)
        maskap = c.cstb[0:64, 704:768] if smp else c.cstb[0:64, C_MCUR:C_MCUR + 64]
        offd = c.cstf[0:64, C_OFFD:C_OFFD + 64].unsqueeze(1).to_broadcast([64, 3, 64])
        idb = c.cstf[0:64, 0:64].unsqueeze(1).to_broadcast([64, 3, 64])
        nlev = 1 if smp else 5
        v3 = lambda pi_, lo=0: R(c.ps[pi_][0:64, lo:lo + 192].rearrange("p (a b) -> p a b", a=3), ("ps", pi_))

        def pre1(ch, hb, st_):
            cc = slice(ch * 64, ch * 64 + 64)
            heads = [hb * 3 + i for i in range(3)]
            hs = slice(hb * 3, hb * 3 + 3)
            sck = lambda nm: ("SC", st_, nm)
            SCt = SC[st_]
            pd = c.next_ps()
            for i, h in enumerate(heads):
                o = R(c.ps[pd][0:64, i * 64:(i + 1) * 64], ("ps", pd))
                P.mm(o, identb64, R(maskap, "cstb"), start=True, stop=False)
                P.mm(o, EH(h), R(GC[0:6, cc], rk), start=False, stop=False)
                P.mm(o, R(NGC[0:6, cc], rk), EH(h), start=False, stop=True)
            P.act(R(U["decT"][:].rearrange("p a b -> p (a b)"), uk("decT")), R(c.ps[pd][0:64, 0:192], ("ps", pd)), AF.Exp)
            pbb = c.next_ps(); pb2 = c.next_ps()
            for qi, row in enumerate((BETA, EG, DL)):
                pq_ = pbb if qi < 2 else pb2
                for i, h in enumerate(heads):
                    P.mm(R(c.ps[pq_][0:64, ((qi % 2) * 3 + i) * 64:((qi % 2) * 3 + i + 1) * 64], ("ps", pq_)), EH(h), R(row[0:6, cc], rk))

            def bview(qi):
                pq_ = pbb if qi < 2 else pb2
                return v3(pq_, (qi % 2) * 192)
            P.tt("dve", R(U["kbT"][:], uk("kbT")), R(HK[:, hs, cc], ("H", 1)), bview(0), ALU.mult)
            P.tt("dve", R(U["vbT"][:], uk("vbT")), R(HV[:, hs, cc], ("H", 2)), bview(0), ALU.mult)
            P.tt("dve", R(SCt["kbgT"][:], sck("kbgT")), R(U["kbT"][:], uk("kbT")), bview(1), ALU.mult)
            P.tt("dve", R(SCt["qgT"][:], sck("qgT")), R(HQ[:, hs, cc], ("H", 0)), bview(1), ALU.mult)
            P.tt("dve", R(U["kdT"][:], uk("kdT")), R(HK[:, hs, cc], ("H", 1)), bview(2), ALU.mult)
            pk_ = c.next_ps()
            for i, h in enumerate(heads):
                P.mm(R(c.ps[pk_][0:64, i * 64:(i + 1) * 64], ("ps", pk_)), R(HK[:, h, cc], ("H", 1)), R(U["kbT"][:, i, :], uk("kbT")))
                P.mm(R(c.ps[pk_][0:64, 192 + i * 64:192 + (i + 1) * 64], ("ps", pk_)), R(HK[:, h, cc], ("H", 1)), R(HQ[:, h, cc], ("H", 0)))
            P.tt("dve", R(U["LT0"][:], uk("LT0")), v3(pk_, 0), R(U["decT"][:], uk("decT")), ALU.mult)
            P.tt("dve", R(U["LT"][:], uk("LT")), R(U["LT0"][:], uk("LT0")), R(offd, "cstf"), ALU.mult)
            P.tt("dve", R(SCt["aT"][:], sck("aT")), v3(pk_, 192), R(U["decT"][:], uk("decT")), ALU.mult)
            pl = c.next_ps()
            for i in range(3):
                P.transpose(R(c.ps[pl][0:64, i * 64:(i + 1) * 64], ("ps", pl)), R(U["LT0"][:, i, :], uk("LT0")), ident64)
            P.tt("dve", R(U["L"][:], uk("L")), v3(pl), R(offd, "cstf"), ALU.mult)
            P.stt("dve", R(U["X0"][:], uk("X0")), R(Fv(U["LT"][:]), uk("LT")), -1.0, R(idb, "cstf"), ALU.mult, ALU.add)
            pv = c.next_ps()
            for i in range(3):
                P.transpose(R(c.ps[pv][0:64, i * 64:(i + 1) * 64], ("ps", pv)), R(U["vbT"][:, i, :], uk("vbT")), ident64)
                P.transpose(R(c.ps[pv][0:64, 192 + i * 64:192 + (i + 1) * 64], ("ps", pv)), R(U["kdT"][:, i, :], uk("kdT")), ident64)
            P.copy("act", R(SCt["VB"][:], sck("VB")), v3(pv, 0))
            P.copy("act", R(SCt["KD"][:], sck("KD")), v3(pv, 192))
            return {"P": "L", "PT": "LT", "X": "X0"}

        def neumann(st_, state, lev):
            sck = lambda nm: ("SC", st_, nm)
            Pc, PTc, Xc = state["P"], state["PT"], state["X"]
            Pn = "P%d" % (lev % 2); PTn = "PT%d" % (lev % 2); Xn = "X%d" % ((lev + 1) % 2)
            last = (lev == nlev - 1)
            pp = c.next_ps()
            for i in range(3):
                P.mm(R(c.ps[pp][0:64, i * 64:(i + 1) * 64], ("ps", pp)), R(U[PTc][:, i, :], uk(PTc)), R(U[Pc][:, i, :], uk(Pc)))
            P.copy("act", R(U[Pn][:], uk(Pn)), v3(pp))
            if not last:
                pt_ = c.next_ps()
                for i in range(3):
                    P.mm(R(c.ps[pt_][0:64, i * 64:(i + 1) * 64], ("ps", pt_)), R(U[Pc][:, i, :], uk(Pc)), R(U[PTc][:, i, :], uk(PTc)))
                P.copy("dve", R(U[PTn][:], uk(PTn)), v3(pt_))
            px = c.next_ps()
            for i in range(3):
                P.mm(R(c.ps[px][0:64, i * 64:(i + 1) * 64], ("ps", px)), R(U[Pn][:, i, :], uk(Pn)), R(U[Xc][:, i, :], uk(Xc)))
            dst = R(SC[st_]["TinvT"][:], sck("TinvT")) if last else R(U[Xn][:], uk(Xn))
            P.tt("dve", dst, R(Fv(U[Xc][:]), uk(Xc)), v3(px), ALU.add)
            state["P"], state["PT"], state["X"] = Pn, PTn, Xn

        def scan_steps(ch, hb, st_):
            cc = slice(ch * 64, ch * 64 + 64)
            heads = [hb * 3 + i for i in range(3)]
            hs = slice(hb * 3, hb * 3 + 3)
            sck = lambda nm: ("SC", st_, nm)
            SCt = SC[st_]

            def s_g():
                pr_ = c.next_ps()
                for i, h in enumerate(heads):
                    P.mm(R(c.ps[pr_][0:64, i * 64:(i + 1) * 64], ("ps", pr_)), R(SCt["kbgT"][:, i, :], sck("kbgT")), R(S[:, h, :], ("S", h)))
                P.tt("dve", R(U["Rr"][:], uk("Rr")), R(SCt["VB"][:], sck("VB")), v3(pr_), ALU.subtract)

            def s_h():
                pn = c.next_ps()
                for i in range(3):
                    P.mm(R(c.ps[pn][0:64, i * 64:(i + 1) * 64], ("ps", pn)), R(SCt["TinvT"][:, i, :], sck("TinvT")), R(U["Rr"][:, i, :], uk("Rr")))
                P.copy("act", R(U["VN"][:], uk("VN")), v3(pn))

            def s_i():
                po = c.next_ps()
                for i, h in enumerate(heads):
                    o = R(c.ps[po][0:64, i * 64:(i + 1) * 64], ("ps", po))
                    P.mm(o, R(S[:, h, :], ("S", h)), R(SCt["qgT"][:, i, :], sck("qgT")), start=True, stop=False)
                    P.mm(o, R(U["VN"][:, i, :], uk("VN")), R(SCt["aT"][:, i, :], sck("aT")), start=False, stop=True)
                P.copy("act", R(OT[:, hs, cc], "OT"), v3(po))

            def s_j():
                pS = c.next_ps()
                for i, h in enumerate(heads):
                    P.mm(R(c.ps[pS][0:64, i * 64:(i + 1) * 64], ("ps", pS)), R(SCt["KD"][:, i, :], sck("KD")), R(U["VN"][:, i, :], uk("VN")))
                for i, h in enumerate(heads):
                    P.stt("dve", R(S[:, h, :], ("S", h)), R(S[:, h, :], ("S", h)), R(EGLB[:, h, ch:ch + 1], "EGLB"),
                          R(c.ps[pS][0:64, i * 64:(i + 1) * 64], ("ps", pS)), ALU.mult, ALU.add)
            return [s_g, s_h, s_i, s_j]

        if not smp:
            units = [(ch, hb) for ch in range(nch) for hb in range(2)]
            st0 = pre1(units[0][0], units[0][1], 0)
            for lev in range(nlev):
                neumann(0, st0, lev)
            for ui in range(1, len(units)):
                st_ = ui % 2
                state = pre1(units[ui][0], units[ui][1], st_)
                steps = scan_steps(units[ui - 1][0], units[ui - 1][1], 1 - st_)
                for lev in range(nlev):
                    neumann(st_, state, lev)
                    if lev < len(steps):
                        steps[lev]()
                for fn in steps[nlev:]:
                    fn()
            for fn in scan_steps(units[-1][0], units[-1][1], (len(units) - 1) % 2):
                fn()
        else:
            ch = 0
            cc = slice(0, 64)
            for hb in range(2):
                heads = [hb * 3 + i for i in range(3)]
                st_ = 0
                sck = lambda nm: ("SC", 0, nm)
                SCt = SC[0]
                state = pre1(ch, hb, 0)
                for lev in range(nlev):
                    neumann(0, state, lev)
                for i, h in enumerate(heads):
                    P.dma("sp", SS[:], c.sgd[l][:, h].rearrange("s k v -> k s v"), writes=["SS"])
                    pr_ = c.next_ps()
                    for s_ in range(NSQ):
                        P.mm(R(c.ps[pr_][0:64, s_ * 4:(s_ + 1) * 4], ("ps", pr_)), R(SS[:, s_, :], "SS"),
                             R(SCt["kbgT"][:, i, s_ * 4:(s_ + 1) * 4], sck("kbgT")))
                    P.tt("dve", R(U["RT"][:, i, :], uk("LT0")), R(U["vbT"][:, i, :], uk("vbT")), R(c.ps[pr_][0:64, 0:64], ("ps", pr_)), ALU.subtract)
                    pq = c.next_ps()
                    P.transpose(R(c.ps[pq][0:64, 0:64], ("ps", pq)), R(U["RT"][:, i, :], uk("LT0")), ident64)
                    P.copy("act", R(U["Rr"][:, i, :], uk("Rr")), R(c.ps[pq][0:64, 0:64], ("ps", pq)))
                    pn = c.next_ps()
                    P.mm(R(c.ps[pn][0:64, 0:64], ("ps", pn)), R(SCt["TinvT"][:, i, :], sck("TinvT")), R(U["Rr"][:, i, :], uk("Rr")))
                    P.copy("act", R(U["VN"][:, i, :], uk("VN")), R(c.ps[pn][0:64, 0:64], ("ps", pn)))
                    po = c.next_ps()
                    P.mm(R(c.ps[po][0:64, 0:64], ("ps", po)), R(U["VN"][:, i, :], uk("VN")), R(SCt["aT"][:, i, :], sck("aT")), start=True, stop=False)
                    for s_ in range(NSQ):
                        P.mm(R(c.ps[po][0:64, s_ * 4:(s_ + 1) * 4], ("ps", po)), R(SS[:, s_, :], "SS"),
                             R(SCt["qgT"][:, i, s_ * 4:(s_ + 1) * 4], sck("qgT")), start=False, stop=(s_ == NSQ - 1))
                    P.copy("act", R(OT[:, h, cc], "OT"), R(c.ps[po][0:64, 0:64], ("ps", po)))
                    seqm = c.cstf[0:64, C_SEQM:C_SEQM + 16].unsqueeze(2).to_broadcast([64, NSQ, 64])
                    kdb = SCt["KD"][:, i, :].unsqueeze(1).to_broadcast([64, NSQ, 64])
                    P.tt("dve", R(KDblk[:], "KDblk"), R(kdb, sck("KD")), R(seqm, "cstf"), ALU.mult)
                    eglb = EGLB[:, h, 0:NSQ].unsqueeze(2).to_broadcast([64, NSQ, 64])
                    P.tt("dve", R(SS[:], "SS"), R(SS[:], "SS"), R(eglb, "EGLB"), ALU.mult)
                    for half in range(2):
                        pS = c.next_ps()
                        for s8 in range(8):
                            s_ = half * 8 + s8
                            P.mm(R(c.ps[pS][0:64, s8 * 64:(s8 + 1) * 64], ("ps", pS)), R(KDblk[:, s_, :], "KDblk"), R(U["VN"][:, i, :], uk("VN")))
                        P.tt("dve", R(SS[:, half * 8:half * 8 + 8, :], "SS"), R(SS[:, half * 8:half * 8 + 8, :], "SS"),
                             R(c.ps[pS][0:64, :].rearrange("p (s v) -> p s v", s=8), ("ps", pS)), ALU.add)
                    P.dma("sp", c.gd_s[l][:, h].rearrange("s k v -> k s v"), SS[:], reads=["SS"], final=True)
        P.tt("dve", R(sqo[:, :, 0:ncol], ("H", 1)), R(OT[:, :, 0:ncol], "OT"), R(OT[:, :, 0:ncol], "OT"), ALU.mult)
        for hb in range(2):
            pg = c.next_ps()
            for i in range(3):
                P.mm(R(c.ps[pg][0:64, i * 128:i * 128 + ncol], ("ps", pg)), R(c.cstb[0:64, C_ONES:C_ONES + 64], "cstb"), R(sqo[:, hb * 3 + i, 0:ncol], ("H", 1)))
            rsqrt(c, R(rso[:, hb * 3:hb * 3 + 3, 0:ncol], ("H", 0)),
                  R(c.ps[pg][0:64, 0:384].rearrange("p (a b) -> p a b", a=3)[:, :, 0:ncol], ("ps", pg)), 1.0 / 64, EPS)
        P.tt("dve", R(rso[:, :, 0:ncol], ("H", 0)), R(rso[:, :, 0:ncol], ("H", 0)), R(OT[:, :, 0:ncol], "OT"), ALU.mult)
        P.stt("dve", R(mixB[:, :, 0:ncol], "mixB"), R(rso[:, :, 0:ncol], ("H", 0)), R(c.vec[0:64, V_GNW + l:V_GNW + l + 1], "vec"),
              R(HZ[:, :, 0:ncol], "HZ"), ALU.mult, ALU.mult)
        outproj(c, l, g5, lambda mc: R(mixB[:, mc, 0:ncol], "mixB"), 6, 64, cols=cols)
    P.dma("sp", c.gd_p[l].rearrange("h k v -> k h v"), F(S[:]), reads=[("S", h) for h in range(6)], final=True)
    for b3 in range(3):
        for bb in range(3):
            blk = b3 * 3 + bb
            pi = c.next_ps()
            P.transpose(R(c.ps[pi][0:51, 0:128], ("ps", pi)), R(gatB[:, blk, :], ("gatB", blk)), c.ident)
            P.copy("dve", R(stg[0:51, bb * 128:(bb + 1) * 128], "stgB"), R(c.ps[pi][0:51, 0:128], ("ps", pi)))
        P.dma("sp", c.cb_p[l][:, b3 * 384:(b3 + 1) * 384], stg[0:3, :], reads=["stgB"], final=True)
        P.dma("sp", c.cb_s[l][:, b3 * 384:(b3 + 1) * 384], stg[3:51, :], reads=["stgB"], final=True)
    P.barrier()


CDIL = (1, 4, 16)
CWIN = (128, 512, 2048)


def _rope(c, buf3, rows, tt, rt, key):
    P = c.P
    nh = buf3.shape[1]
    x1 = buf3[:, :, 0:8]; x2 = buf3[:, :, 8:16]
    cos = c.ropet[0:rows, tt, 0:8].unsqueeze(1).to_broadcast([rows, nh, 8])
    sin = c.ropet[0:rows, tt, 8:16].unsqueeze(1).to_broadcast([rows, nh, 8])
    t = [r_[0:rows, 0:nh * 8].rearrange("p (h e) -> p h e", e=8) for r_ in rt]
    P.tt("dve", R(t[0], "ropet0"), R(x1, key), R(cos, "rope"), ALU.mult)
    P.tt("dve", R(t[1], "ropet1"), R(x2, key), R(sin, "rope"), ALU.mult)
    P.tt("dve", R(t[2], "ropet2"), R(x2, key), R(cos, "rope"), ALU.mult)
    P.tt("dve", R(t[3], "ropet3"), R(x1, key), R(sin, "rope"), ALU.mult)
    P.tt("dve", R(x1, key), R(t[0], "ropet0"), R(t[1], "ropet1"), ALU.subtract)
    P.tt("dve", R(x2, key), R(t[2], "ropet2"), R(t[3], "ropet3"), ALU.add)


def branch_c(c, l):
    P = c.P
    A_ = c.Arena(); f32 = A_.f32
    ZS = f32(768)[0:64].rearrange("p (h t) -> p h t", h=6)
    KT = A_.bf16(6 * T)[0:64].rearrange("p (h t) -> p h t", h=6)
    VS = A_.bf16(48 * 128).rearrange("p (n x) -> p n x", x=128)
    kv_raw = f32(768)
    kvst = kv_raw.rearrange("p (g x) -> p g x", g=3)
    qsb = kv_raw[:, 0:384]
    rt = [f32(48) for _ in range(4)]
    QTt = A_.bf16(6 * 128)[0:64].rearrange("p (h t) -> p h t", h=6)
    PT = [A_.bf16(512) for _ in range(2)]
    dtot = f32(256)[0:64].rearrange("p (h t) -> p h t", h=2)
    onorm = kv_raw[0:64].rearrange("p (h t) -> p h t", h=6)
    mixC = A_.bf16(768)[0:64].rearrange("p (h t) -> p h t", h=6)

    load_wo(c, l, 640, 6, 64)
    wk_t, wk_k = c.wload(c.w_in[l][:, 2956:3340], 384)
    wv_t, wv_k = c.wload(c.w_in[l][:, 3340:3724], 384)
    wq_t, wq_k = c.wload(c.w_in[l][:, 2572:2956], 384)
    wz_t, wz_k = c.wload(c.w_in[l][:, 3724:4108], 384)
    ident = c.ident

    def tokproj(pi, wt_, wk_, tt, rows, ncols=384, c0=0):
        col0 = tt * 128
        g5 = min(tt // 4, 4)
        for k in range(KC):
            P.mm(R(c.ps[pi][0:rows, 0:ncols], ("ps", pi)), R(c.hnT[:, k, col0:col0 + rows], hkey(g5)), R(wt_[:, k, c0:c0 + ncols], wk_),
                 start=(k == 0), stop=(k == KC - 1))

    def kv_tile(tt, kvst_, rt_, ksT_=None, vtok_=None):
        rows = 128 if tt < 16 else NS
        pk = c.next_ps(); tokproj(pk, wk_t, wk_k, tt, rows)
        pv = c.next_ps(); tokproj(pv, wv_t, wv_k, tt, rows)
        P.copy("act", R(kvst_[0:rows, :, 0:128], "kvst"), R(c.ps[pk][0:rows, 0:384].rearrange("p (g x) -> p g x", g=3), ("ps", pk)))
        P.copy("act", R(kvst_[0:rows, :, 128:256], "kvst"), R(c.ps[pv][0:rows, 0:384].rearrange("p (g x) -> p g x", g=3), ("ps", pv)))
        for gi in range(3):
            _rope(c, kvst_[0:rows, gi, 0:128].rearrange("p (h d) -> p h d", h=2), rows, tt, rt_, "kvst")
        for gi in range(3):
            if tt < 16:
                lo = T - CWIN[gi]
                if tt * 128 >= lo:
                    P.dma("sp", c.kv_p[gi][l][tt * 128 - lo:tt * 128 - lo + 128, :], kvst_[:, gi, :], reads=["kvst"], final=True)
            else:
                P.dma("sp", c.kv_s[gi][l], kvst_[0:NS, gi, :], reads=["kvst"], final=True)
        for half in range(2):
            pt = c.next_ps()
            for i in range(3):
                hx = half * 3 + i
                gi, h = hx // 2, hx % 2
                P.transpose(R(c.ps[pt][0:64, i * 128:i * 128 + rows], ("ps", pt)), R(kvst_[0:rows, gi, h * 64:(h + 1) * 64], "kvst"),
                            R(c.cstf[0:rows, 0:rows], "cstf"))
            src = c.ps[pt][0:64, 0:384].rearrange("p (h t) -> p h t", h=3)[:, :, 0:rows]
            if tt < 16:
                P.copy("act", R(KT[:, half * 3:half * 3 + 3, tt * 128:tt * 128 + 128], ("KT", tt)), R(src, ("ps", pt)))
            else:
                P.copy("act", R(ksT_[:, half * 3:half * 3 + 3, :], "ksT"), R(src, ("ps", pt)))
        if tt == 16:
            P.copy("dve", R(vtok_[:], "vtok"), R(kvst_[0:NS, :, 128:256], "kvst"))

    for tt in range(16):
        kv_tile(tt, kvst, rt)
    for gi in range(3):
        dil = CDIL[gi]; nb = 16 // dil
        for q4 in range(4):
            pi = c.next_ps()
            for j in range(4):
                st_ = q4 * 4 + j
                r, n = st_ // nb, st_ % nb
                lo = r + dil * n * 128
                for k in range(KC):
                    P.mm(R(c.ps[pi][:, j * 128:(j + 1) * 128], ("ps", pi)), (c.hnT[:, k, lo:lo + dil * 127 + 1:dil], tuple(hkey(g) for g in range(4))),
                         R(wv_t[:, k, gi * 128:(gi + 1) * 128], wv_k), start=(k == 0), stop=(k == KC - 1))
            P.copy("act" if q4 % 2 == 0 else "dve", R(VS[:, gi * 16 + q4 * 4:gi * 16 + q4 * 4 + 4, :], ("VS", gi)),
                   R(c.ps[pi][:].rearrange("p (n x) -> p n x", x=128), ("ps", pi)))

    c.ps_n = 4; c.ps_i = 0
    psO = [4, 5]; psD = [6, 7]
    onesb64 = R(c.cstb[:, C_ONES:C_ONES + 64], "cstb")
    mcur = c.cstb[:, C_MCUR:C_MCUR + 128]; mprev = c.cstb[:, C_MPREV:C_MPREV + 128]
    allkt = tuple(("KT", t_) for t_ in range(16))

    def q_and_z(tt, rows, QT_dst, qkey, Z_dst, qsb=qsb, rt=rt):
        pq = c.next_ps(); tokproj(pq, wq_t, wq_k, tt, rows)
        P.copy("act", R(qsb[0:rows, :], "kvst"), R(c.ps[pq][0:rows, 0:384], ("ps", pq)))
        _rope(c, qsb[0:rows, :].rearrange("p (h d) -> p h d", h=6), rows, tt, rt, "kvst")
        for half in range(2):
            pt = c.next_ps()
            for i in range(3):
                hx = half * 3 + i
                P.transpose(R(c.ps[pt][0:64, i * 128:i * 128 + rows], ("ps", pt)), R(qsb[0:rows, hx * 64:(hx + 1) * 64], "kvst"),
                            R(c.cstf[0:rows, 0:rows], "cstf"))
            P.copy("dve", R(QT_dst[:, half * 3:half * 3 + 3, 0:rows], qkey),
                   R(c.ps[pt][0:64, 0:384].rearrange("p (h t) -> p h t", h=3)[:, :, 0:rows], ("ps", pt)))
        col0 = tt * 128; g5 = min(tt // 4, 4)
        for half in range(2):
            pz = c.next_ps()
            for i in range(3):
                hx = half * 3 + i
                for k in range(KC):
                    P.mm(R(c.ps[pz][0:64, i * 128:i * 128 + rows], ("ps", pz)), R(wz_t[:, k, hx * 64:(hx + 1) * 64], wz_k),
                         R(c.hnT[:, k, col0:col0 + rows], hkey(g5)), start=(k == 0), stop=(k == KC - 1))
            silu_(c, R(Z_dst[:, half * 3:half * 3 + 3, 0:rows], "ZS"),
                  R(c.ps[pz][0:64, 0:384].rearrange("p (h t) -> p h t", h=3)[:, :, 0:rows], ("ps", pz)))

    for tt in range(16):
        q_and_z(tt, 128, QTt, "QTt", ZS)

        def pv_den(hx, ocols, vs_tile, h, pt_ap, ptkey, first, last):
            o = c.ps[psO[hx // 3]][0:64, (hx % 3) * 128 + ocols[0]:(hx % 3) * 128 + ocols[0] + ocols[1]]
            d = c.ps[psD[hx // 3]][0:64, (hx % 3) * 128 + ocols[0]:(hx % 3) * 128 + ocols[0] + ocols[1]]
            P.mm(R(o, ("ps", psO[hx // 3])), R(VS[:, vs_tile, h * 64:(h + 1) * 64], ("VS", vs_tile // 16)), R(pt_ap, ptkey), start=first, stop=last)
            P.mm(R(d, ("ps", psD[hx // 3])), onesb64, R(pt_ap, ptkey), start=first, stop=last)

        for gi in range(3):
            dil = CDIL[gi]; nb = 16 // dil; nq = 128 // dil
            n = tt // dil
            qo = (tt % dil) * nq
            kbl = [0] if n == 0 else [0, 1]
            pS = c.next_ps()
            ptile = PT[gi % 2]; ptk = ("PT", gi % 2)
            slots = {}
            for kbi in kbl:
                kb = n - kbi
                for h in range(2):
                    hx = gi * 2 + h
                    for r in range(dil):
                        col = ((kbi * 2 + h) * dil + r) * nq
                        slots[(kbi, h, r)] = col
                        klo = r + dil * kb * 128
                        o = R(c.ps[pS][:, col:col + nq], ("ps", pS))
                        P.mm(o, (KT[:, hx, klo:klo + dil * 127 + 1:dil], allkt), R(QTt[:, hx, r:128:dil], "QTt"), start=True, stop=False)
                        msk = (mcur if kbi == 0 else mprev)[:, qo:qo + nq]
                        P.mm(o, c.identb, R(msk, "cstb"), start=False, stop=True)
            used = len(kbl) * 2 * dil * nq
            P.act(R(ptile[:, 0:used], ptk), R(c.ps[pS][:, 0:used], ("ps", pS)), AF.Exp, scale=0.125)
            for h in range(2):
                hx = gi * 2 + h
                for r in range(dil):
                    for ki, kbi in enumerate(kbl):
                        kb = n - kbi
                        col = slots[(kbi, h, r)]
                        pv_den(hx, (r * nq, nq), gi * 16 + r * nb + kb, h, ptile[:, col:col + nq], ptk, ki == 0, ki == len(kbl) - 1)
        def nat(ap, dil):
            if dil == 1:
                return ap
            return ap.rearrange("p (r m) -> p m r", r=dil)
        for h in range(2):
            dv_ = dtot[:, h, :]
            P.copy("dve", R(dv_, "dtot"), R(c.ps[psD[0]][0:64, h * 128:(h + 1) * 128], ("ps", psD[0])))
            for gi in (1, 2):
                hx = gi * 2 + h
                dil = CDIL[gi]
                src = c.ps[psD[hx // 3]][0:64, (hx % 3) * 128:(hx % 3) * 128 + 128]
                dvw = dv_.rearrange("p (m r) -> p m r", r=dil)
                P.tt("dve", R(dvw, "dtot"), R(dvw, "dtot"), R(nat(src, dil), ("ps", psD[hx // 3])), ALU.add)
            P.op("dve", lambda e, dv_=dv_: e.reciprocal(dv_, dv_), reads=["dtot"], writes=["dtot"])
        for hx in range(6):
            gi, h = hx // 2, hx % 2
            dil = CDIL[gi]
            src = c.ps[psO[hx // 3]][0:64, (hx % 3) * 128:(hx % 3) * 128 + 128]
            if dil == 1:
                P.tt("dve", R(onorm[:, hx, :], "kvst"), R(src, ("ps", psO[hx // 3])), R(dtot[:, h, :], "dtot"), ALU.mult)
            else:
                P.tt("dve", R(onorm[:, hx, :].rearrange("p (m r) -> p m r", r=dil), "kvst"), R(nat(src, dil), ("ps", psO[hx // 3])),
                     R(dtot[:, h, :].rearrange("p (m r) -> p m r", r=dil), "dtot"), ALU.mult)
        P.tt("dve", R(mixC[:], "mixC"), R(onorm[:], "kvst"), R(ZS[:], "ZS"), ALU.mult)
        outproj(c, l, tt // 4, lambda mc: R(mixC[:, mc, :], "mixC"), 6, 64, cols=(tt * 128, 128))
    c.ps_n = 8; c.ps_i = 0

    P.barrier()
    B_ = c.Arena()
    b32 = B_.f32
    kvst_s = b32(768).rearrange("p (g x) -> p g x", g=3)
    qsb_s = b32(384)
    rt_s = [b32(48) for _ in range(4)]
    ksT = b32(384)[0:64].rearrange("p (h t) -> p h t", h=6)
    qsT = b32(384)[0:64].rearrange("p (h t) -> p h t", h=6)
    vtok = b32(384)[0:64].rearrange("p (g x) -> p g x", g=3)
    ZSs = b32(384)[0:64].rearrange("p (h t) -> p h t", h=6)
    c.optmp = b32(NS).rearrange("p (s i) -> p s i", i=SQ)
    kv_tile(16, kvst_s, rt_s, ksT, vtok)
    q_and_z(16, NS, qsT, "qsT", ZSs, qsb=qsb_s, rt=rt_s)
    CK = [b32(9 * 256).rearrange("p (b x) -> p b x", b=9) for _ in range(2)]
    KcT = b32(18 * 128)[0:64].rearrange("p (b t) -> p b t", b=18)
    PTn = b32(384)[0:64].rearrange("p (h t) -> p h t", h=6)
    PTc = b32(24)
    dts = b32(128)[0:64].rearrange("p (h t) -> p h t", h=2)
    ons = b32(384)[0:64].rearrange("p (h t) -> p h t", h=6)
    mixS = B_.bf16(384)[0:64].rearrange("p (h t) -> p h t", h=6)
    ones64f = R(c.cstf[0:64, C_ONES:C_ONES + 64], "cstf")
    ones128f = R(c.cstf[:, C_ONES:C_ONES + 64], "cstf")
    pO = 6; pD = 7
    c.ps_n = 6
    pn_ = c.next_ps()
    for hx in range(6):
        gi = hx // 2
        o = R(c.ps[pn_][0:64, hx * 64:(hx + 1) * 64], ("ps", pn_))
        P.mm(o, R(ksT[:, hx, :], "ksT"), R(qsT[:, hx, :], "qsT"), start=True, stop=False)
        msk = c.cstb[0:64, 704:768] if gi == 0 else c.cstb[0:64, 768:832]
        P.mm(o, R(c.cstb[0:64, 0:64], "cstb"), R(msk, "cstb"), start=False, stop=True)
    P.act(R(PTn[:].rearrange("p h t -> p (h t)"), "PTn"), R(c.ps[pn_][0:64, 0:384], ("ps", pn_)), AF.Exp, scale=0.125)
    for hx in range(6):
        gi, h = hx // 2, hx % 2
        P.mm(R(c.ps[pO][0:64, hx * 64:(hx + 1) * 64], ("ps", pO)), R(vtok[:, gi, h * 64:(h + 1) * 64], "vtok"), R(PTn[:, hx, :], "PTn"), start=True, stop=False)
        P.mm(R(c.ps[pD][0:64, hx * 64:(hx + 1) * 64], ("ps", pD)), ones64f, R(PTn[:, hx, :], "PTn"), start=True, stop=False)
    for s_ in range(NSQ):
        ck = CK[s_ % 2]; ckk = ("CK", s_ % 2)
        P.dma("sp", ck[:, 0, :], c.ck128[l][s_], writes=[ckk])
        P.dma("sp", ck[:, 1:5, :], c.ck512[l][s_].rearrange("(m i) x -> m i x", i=4), writes=[ckk])
        P.dma("sp", ck[:, 5:9, :], c.ck2048[l][s_].rearrange("(m i) x -> m i x", i=16)[:, 0:4, :], writes=[ckk])
        for q5 in range(5):
            pt = c.next_ps()
            nn = 4 if q5 < 4 else 2
            for j in range(nn):
                bh = q5 * 4 + j
                blk, h = bh // 2, bh % 2
                P.transpose(R(c.ps[pt][0:64, j * 128:(j + 1) * 128], ("ps", pt)), R(ck[:, blk, h * 64:(h + 1) * 64], ckk), ident)
            P.copy("act" if q5 % 2 == 0 else "dve", R(KcT[:, q5 * 4:q5 * 4 + nn, :], "KcT"),
                   R(c.ps[pt][0:64, 0:nn * 128].rearrange("p (b t) -> p b t", b=nn), ("ps", pt)))
        psc = c.next_ps()
        for h in range(2):
            o = R(c.ps[psc][:, h * 4:h * 4 + 4], ("ps", psc))
            P.mm(o, R(KcT[:, h, :], "KcT"), R(qsT[:, h, s_ * 4:s_ * 4 + 4], "qsT"), start=True, stop=False)
            P.mm(o, c.identb, R(c.cstb[:, 832:836], "cstb"), start=False, stop=True)
            for gi in (1, 2):
                for i in range(SQ):
                    blk = (1 if gi == 1 else 5) + i
                    col = gi * 8 + h * 4 + i
                    P.mm(R(c.ps[psc][:, col:col + 1], ("ps", psc)), R(KcT[:, blk * 2 + h, :], "KcT"),
                         R(qsT[:, gi * 2 + h, s_ * 4 + i:s_ * 4 + i + 1], "qsT"))
        P.act(R(PTc[:, 0:24], "PTc"), R(c.ps[psc][:, 0:24], ("ps", psc)), AF.Exp, scale=0.125)
        for h in range(2):
            for gi in range(3):
                hx = gi * 2 + h
                if gi == 0:
                    items = [(0, s_ * 4, 4, h * 4)]
                else:
                    items = [((1 if gi == 1 else 5) + i, s_ * 4 + i, 1, gi * 8 + h * 4 + i) for i in range(SQ)]
                for (blk, ocol, w_, pcol) in items:
                    P.mm(R(c.ps[pO][0:64, hx * 64 + ocol:hx * 64 + ocol + w_], ("ps", pO)), R(ck[:, blk, 128 + h * 64:128 + (h + 1) * 64], ckk),
                         R(PTc[:, pcol:pcol + w_], "PTc"), start=False, stop=True)
                    P.mm(R(c.ps[pD][0:64, hx * 64 + ocol:hx * 64 + ocol + w_], ("ps", pD)), ones128f,
                         R(PTc[:, pcol:pcol + w_], "PTc"), start=False, stop=True)
    for h in range(2):
        P.copy("dve", R(dts[:, h, :], "dts"), R(c.ps[pD][0:64, h * 64:(h + 1) * 64], ("ps", pD)))
        for gi in (1, 2):
            hx = gi * 2 + h
            P.tt("dve", R(dts[:, h, :], "dts"), R(dts[:, h, :], "dts"), R(c.ps[pD][0:64, hx * 64:(hx + 1) * 64], ("ps", pD)), ALU.add)
    dall = dts[:]
    P.op("dve", lambda e: e.reciprocal(dall, dall), reads=["dts"], writes=["dts"])
    for hx in range(6):
        P.tt("dve", R(ons[:, hx, :], "ons"), R(c.ps[pO][0:64, hx * 64:(hx + 1) * 64], ("ps", pO)), R(dts[:, hx % 2, :], "dts"), ALU.mult)
    P.tt("dve", R(mixS[:], "mixS"), R(ons[:], "ons"), R(ZSs[:, :, 0:NS], "ZS"), ALU.mult)
    c.ps_n = 8; c.ps_i = 0
    outproj(c, l, 4, lambda mc: R(mixS[:, mc, :], "mixS"), 6, 64, cols=(T, NS))
    P.barrier()
```
